# Optimizing a Trainium2 kernel written in Bass

```python
import jax, jax.numpy as jnp
from jax import lax
import numpy as np

D_MODEL = 1024
BATCH = 8
SEQ = 2048
DEPTH = 1
DEC_BATCH = 128
DEC_SEQ = 1
PAST_LEN = 16384
PAGE_SIZE = 128

MIX_A = D_MODEL // 2
MIX_B = D_MODEL - MIX_A
HA = 4
DK_A = MIX_A // HA
DV_A = MIX_A // HA
HS_B = 64
HB = MIX_B // HS_B
CONV_W = 4
CHUNK = 64
LORA_W = 64
LORA_A = 64
RMS_EPS = 1e-6
GN_EPS = 64e-5
A_QKV = 3 * MIX_A
A_W = A_QKV + MIX_A + 2 * HA
B_W = 4 * MIX_B + LORA_W + LORA_A
P_W = A_W + B_W

kernel_name = "hymba_gdn_rwkv7_step"

F32 = jnp.float32


def rmsnorm(x, w):
    x32 = x.astype(F32)
    return x32 * lax.rsqrt(jnp.mean(x32 * x32, -1, keepdims=True) + RMS_EPS) * w.astype(F32)


def l2norm(x):
    return x * lax.rsqrt(jnp.sum(x * x, -1, keepdims=True) + 1e-12)


def short_conv(buf, u, w):
    t = u.shape[1]
    full = jnp.concatenate([buf, u], axis=1)
    out = sum(full[:, i:i + t] * w[i] for i in range(CONV_W))
    return jax.nn.silu(out), full[:, full.shape[1] - (CONV_W - 1):]


def gated_delta_chunked(q, k, v, g, beta, s0):
    bsz, t, h, dk = q.shape
    t_pad = -(-t // CHUNK) * CHUNK
    pad = t_pad - t
    padt = lambda z: jnp.pad(z, [(0, 0), (0, pad)] + [(0, 0)] * (z.ndim - 2))
    q, k, v, g, beta = (padt(z) for z in (q, k, v, g, beta))
    n = t_pad // CHUNK

    def blocks(z):
        z = z.reshape((bsz, n, CHUNK) + z.shape[2:])
        return jnp.swapaxes(jnp.moveaxis(z, 3, 2), 0, 1)

    qb, kb, vb, gb, bb = (blocks(z) for z in (q, k, v, g, beta))
    G = jnp.cumsum(gb, axis=-1)
    idx = jnp.arange(CHUNK)
    strict = idx[:, None] > idx[None, :]
    incl = idx[:, None] >= idx[None, :]
    diff = G[..., :, None] - G[..., None, :]
    decay = jnp.where(incl, jnp.exp(jnp.where(incl, diff, 0.0)), 0.0)
    kk = jnp.einsum('nbhck,nbhdk->nbhcd', kb, kb)
    amat = jnp.where(strict, bb[..., :, None] * decay * kk, 0.0) + jnp.eye(CHUNK, dtype=F32)
    rhs = jnp.concatenate([(bb * jnp.exp(G))[..., None] * kb, bb[..., None] * vb], axis=-1)
    sol = lax.linalg.triangular_solve(amat, rhs, left_side=True, lower=True, unit_diagonal=True)
    wmat, umat = sol[..., :dk], sol[..., dk:]
    qk = decay * jnp.einsum('nbhck,nbhdk->nbhcd', qb, kb)
    qg = qb * jnp.exp(G)[..., None]
    kdec = kb * jnp.exp(G[..., -1:] - G)[..., None]
    glast = jnp.exp(G[..., -1])

    def step(s, xs):
        w_c, u_c, qg_c, qk_c, kd_c, gl_c = xs
        u = u_c - jnp.einsum('bhck,bhkv->bhcv', w_c, s)
        o = jnp.einsum('bhck,bhkv->bhcv', qg_c, s) + jnp.einsum('bhcd,bhdv->bhcv', qk_c, u)
        s = gl_c[..., None, None] * s + jnp.einsum('bhck,bhcv->bhkv', kd_c, u)
        return s, o

    s, o = lax.scan(step, s0, (wmat, umat, qg, qk, kdec, glast))
    o = jnp.transpose(o, (1, 0, 3, 2, 4)).reshape(bsz, t_pad, h, -1)[:, :t]
    return o, s


def rwkv7_scan(r, w, k, v, kk, a, s0):
    def step(s, xs):
        r_t, w_t, k_t, v_t, kk_t, a_t = xs
        sa = jnp.einsum('bhvk,bhk->bhv', s, -kk_t)
        s = (s * w_t[:, :, None, :] + sa[..., None] * (kk_t * a_t)[:, :, None, :]
             + v_t[..., None] * k_t[:, :, None, :])
        return s, jnp.einsum('bhvk,bhk->bhv', s, r_t)

    xs = tuple(jnp.swapaxes(z, 0, 1) for z in (r, w, k, v, kk, a))
    s, y = lax.scan(step, s0, xs)
    return jnp.swapaxes(y, 0, 1), s


def mixer_layer(x, s_a, s_conv, s_b, s_shift, norm_w, w_in, conv_w, a_log, dt_bias, a_norm_w,
                mu, w0, w2, a0, a2, k_k, k_a, r_k, ln_w, ln_b, w_out):
    bsz, t, _ = x.shape
    xn = rmsnorm(x, norm_w).astype(x.dtype)
    proj = jnp.einsum('btd,dp->btp', xn, w_in).astype(F32)
    pa, pb = proj[..., :A_W], proj[..., A_W:]

    qkv, conv_new = short_conv(s_conv.astype(F32), pa[..., :A_QKV], conv_w.astype(F32))
    q = l2norm(qkv[..., :MIX_A].reshape(bsz, t, HA, DK_A)) * (DK_A ** -0.5)
    k = l2norm(qkv[..., MIX_A:2 * MIX_A].reshape(bsz, t, HA, DK_A))
    v = qkv[..., 2 * MIX_A:].reshape(bsz, t, HA, DV_A)
    gate_a = pa[..., A_QKV:A_QKV + MIX_A]
    beta = jax.nn.sigmoid(pa[..., A_QKV + MIX_A:A_QKV + MIX_A + HA])
    g = -jnp.exp(a_log.astype(F32)) * jax.nn.softplus(pa[..., A_QKV + MIX_A + HA:] + dt_bias.astype(F32))
    o_a, s_a_new = gated_delta_chunked(q, k, v, g, beta, s_a.astype(F32))
    o_a = rmsnorm(o_a, a_norm_w).reshape(bsz, t, MIX_A) * jax.nn.silu(gate_a)

    prev = jnp.concatenate([s_shift.astype(F32)[:, None], pb[:, :-1]], axis=1)
    xb = pb + (prev - pb) * mu.astype(F32)
    shift_new = pb[:, -1]
    hd = lambda z: z.reshape(bsz, t, HB, HS_B)
    r = xb[..., :MIX_B]
    kr = xb[..., MIX_B:2 * MIX_B]
    vr = xb[..., 2 * MIX_B:3 * MIX_B]
    gate_b = xb[..., 3 * MIX_B:4 * MIX_B]
    wd = xb[..., 4 * MIX_B:4 * MIX_B + LORA_W]
    ad = xb[..., 4 * MIX_B + LORA_W:]
    w_ll = -jax.nn.softplus(-(w0.astype(F32) + jnp.tanh(wd) @ w2.astype(F32))) - 0.5
    wdec = jnp.exp(-jnp.exp(w_ll))
    a = jax.nn.sigmoid(a0.astype(F32) + ad @ a2.astype(F32))
    kk = l2norm(hd(kr * k_k.astype(F32)))
    kmod = kr * (1.0 + (a - 1.0) * k_a.astype(F32))
    r_h, k_h, v_h = hd(r), hd(kmod), hd(vr)
    y_b, s_b_new = rwkv7_scan(r_h, hd(wdec), k_h, v_h, kk, hd(a), s_b.astype(F32))
    mean = jnp.mean(y_b, -1, keepdims=True)
    var = jnp.mean(jnp.square(y_b - mean), -1, keepdims=True)
    gn = ((y_b - mean) * lax.rsqrt(var + GN_EPS)).reshape(bsz, t, MIX_B) * ln_w.astype(F32) + ln_b.astype(F32)
    bonus = jnp.sum(r_h * k_h * r_k.astype(F32), -1, keepdims=True) * v_h
    o_b = (gn + bonus.reshape(bsz, t, MIX_B)) * jax.nn.silu(gate_b)

    mix = jnp.concatenate([o_a, o_b], axis=-1).astype(x.dtype)
    h = x + jnp.einsum('btm,md->btd', mix, w_out)
    dt = x.dtype
    return h, (s_a_new.astype(dt), conv_new.astype(dt), s_b_new.astype(dt), shift_new.astype(dt))


def setup_inputs(seed: int = 0) -> dict:
    key = jax.random.key(seed)
    ks = jax.random.split(key, 32)
    nrm = lambda i, shape, s=1.0: jax.random.normal(ks[i], shape, F32) * s
    return {
        "x_prompt": nrm(0, (BATCH, SEQ, D_MODEL)),
        "x_sample": nrm(1, (DEC_BATCH, DEC_SEQ, D_MODEL)),
        "state_a_mat": nrm(2, (DEPTH, DEC_BATCH, HA, DK_A, DV_A), 0.5),
        "state_a_conv": nrm(3, (DEPTH, DEC_BATCH, CONV_W - 1, A_QKV)),
        "state_b_mat": nrm(4, (DEPTH, DEC_BATCH, HB, HS_B, HS_B), 0.1),
        "state_b_shift": nrm(5, (DEPTH, DEC_BATCH, B_W)),
        "norm_w": 1.0 + nrm(6, (DEPTH, D_MODEL), 0.01),
        "w_in": nrm(7, (DEPTH, D_MODEL, P_W), D_MODEL ** -0.5),
        "conv_w": nrm(8, (DEPTH, CONV_W, A_QKV), CONV_W ** -0.5),
        "a_log": jnp.log(jax.random.uniform(ks[9], (DEPTH, HA), F32, 1.0, 16.0)),
        "dt_bias": nrm(10, (DEPTH, HA), 0.1),
        "a_norm_w": 1.0 + nrm(11, (DEPTH, DV_A), 0.01),
        "mu": jax.random.uniform(ks[12], (DEPTH, B_W), F32, 0.0, 1.0),
        "w0": nrm(13, (DEPTH, MIX_B), 0.5),
        "w2": nrm(14, (DEPTH, LORA_W, MIX_B), 0.5 * LORA_W ** -0.5),
        "a0": nrm(15, (DEPTH, MIX_B), 0.1),
        "a2": nrm(16, (DEPTH, LORA_A, MIX_B), LORA_A ** -0.5),
        "k_k": 0.85 + nrm(17, (DEPTH, MIX_B), 0.02),
        "k_a": 1.0 + nrm(18, (DEPTH, MIX_B), 0.02),
        "r_k": nrm(19, (DEPTH, HB, HS_B), 0.1),
        "ln_w": 1.0 + nrm(20, (DEPTH, MIX_B), 0.01),
        "ln_b": nrm(21, (DEPTH, MIX_B), 0.01),
        "w_out": nrm(22, (DEPTH, D_MODEL, D_MODEL), D_MODEL ** -0.5),
        "final_norm_w": 1.0 + nrm(23, (D_MODEL,), 0.01),
    }


def reference(x_prompt, x_sample, state_a_mat, state_a_conv, state_b_mat, state_b_shift,
              norm_w, w_in, conv_w, a_log, dt_bias, a_norm_w, mu, w0, w2, a0, a2, k_k, k_a,
              r_k, ln_w, ln_b, w_out, final_norm_w):
    hp, hs = x_prompt, x_sample
    dt = x_prompt.dtype
    p_amat, p_aconv, p_bmat, p_bshift = [], [], [], []
    s_amat, s_aconv, s_bmat, s_bshift = [], [], [], []
    for l in range(DEPTH):
        wts = (norm_w[l], w_in[l], conv_w[l], a_log[l], dt_bias[l], a_norm_w[l], mu[l], w0[l], w2[l],
               a0[l], a2[l], k_k[l], k_a[l], r_k[l], ln_w[l], ln_b[l], w_out[l])
        z_a = jnp.zeros((BATCH, HA, DK_A, DV_A), dt)
        z_c = jnp.zeros((BATCH, CONV_W - 1, A_QKV), dt)
        z_b = jnp.zeros((BATCH, HB, HS_B, HS_B), dt)
        z_s = jnp.zeros((BATCH, B_W), dt)
        hp, (pa, pc, pbm, pbs) = mixer_layer(hp, z_a, z_c, z_b, z_s, *wts)
        hs, (sa, sc, sbm, sbs) = mixer_layer(hs, state_a_mat[l], state_a_conv[l], state_b_mat[l],
                                             state_b_shift[l], *wts)
        p_amat.append(pa); p_aconv.append(pc); p_bmat.append(pbm); p_bshift.append(pbs)
        s_amat.append(sa); s_aconv.append(sc); s_bmat.append(sbm); s_bshift.append(sbs)
    y_prompt = rmsnorm(hp, final_norm_w).astype(dt)
    y_sample = rmsnorm(hs, final_norm_w).astype(dt)
    return (y_prompt, y_sample,
            jnp.stack(p_amat), jnp.stack(p_aconv), jnp.stack(p_bmat), jnp.stack(p_bshift),
            jnp.stack(s_amat), jnp.stack(s_aconv), jnp.stack(s_bmat), jnp.stack(s_bshift))
```

```python
import contextlib
import numpy as np
import concourse.bass as bass
import concourse.mybir as mybir
from concourse.bass_utils import run_bass_kernel_spmd

F32 = mybir.dt.float32
BF16 = mybir.dt.bfloat16
ALU = mybir.AluOpType
AF = mybir.ActivationFunctionType
AX = mybir.AxisListType


class _Rec:
    def __init__(self):
        self.call = None

    def __getattr__(self, name):
        def f(*a, **k):
            self.call = (name, a, k)
            return self
        return f


class Prog:
    ENGS = ["pe", "dve", "act", "pool", "sp"]

    def __init__(self, nc, n_dma_sems=24):
        self.nc = nc
        self.stack = contextlib.ExitStack()
        self.items = {e: [] for e in self.ENGS}
        self.cnt = {e: 0 for e in self.ENGS}
        self.sem = {e: self.stack.enter_context(nc.semaphore("s_" + e)) for e in ["pe", "dve", "act", "pool"]}
        self.dsem = [self.stack.enter_context(nc.semaphore("d%d" % i)) for i in range(n_dma_sems)]
        self.dval = [0] * n_dma_sems
        self.dma_i = 0
        self.n_sw = 8
        self.sw_i = 0
        self.seen = {e: {} for e in self.ENGS}
        self.lastw = {}
        self.readers = {}
        self.n_ops = 0
        self.capture = None
        self.oplist = []
        self.warm_ops = None
        self.n_fill = 0

    def sb(self, name, shape, dtype):
        return self.stack.enter_context(self.nc.sbuf_tensor(name, list(shape), dtype))

    def ps(self, name, shape, dtype):
        return self.stack.enter_context(self.nc.psum_tensor(name, list(shape), dtype))

    _VEC_OPS = ("tensor_tensor", "tensor_copy", "memset")

    def op(self, eng, fn, reads=(), writes=(), is_dma=False, alts=None):
        rec = _Rec()
        fn(rec)
        al = {}
        if alts:
            for e2, fn2 in alts:
                r2 = _Rec()
                fn2(r2)
                al[e2] = r2.call
        if ALT[0] and not is_dma and eng in ("dve", "pool") and rec.call[0] in self._VEC_OPS \
                and not any(k[:2] in ("PF", "PQ", "PB") for k in tuple(reads) + tuple(writes)):
            al.setdefault("pool" if eng == "dve" else "dve", rec.call)
        item = (eng, rec.call, tuple(reads), tuple(writes), is_dma, al)
        if self.capture is not None:
            self.capture.append(item)
            return
        self.oplist.append(item)

    def copy(self, out, in_, reads=(), writes=()):
        self.op("dve", lambda e: e.tensor_copy(out=out, in_=in_), reads, writes,
                alts=[("act", lambda e: e.activation(out=out, in_=in_, func=AF.Copy))] if ALT[0] else None)

    def capture_begin(self):
        self.capture = []

    def capture_end(self):
        c, self.capture = self.capture, None
        return c

    def replay(self, streams):
        items = []
        for si, st in enumerate(streams):
            n = max(len(st), 1)
            for i, it in enumerate(st):
                items.append(((i + 0.5) / n, si, i, it))
        items.sort(key=lambda t: (t[0], t[1], t[2]))
        for _, _, _, it in items:
            self.oplist.append(it)

    def _op(self, eng, call, reads=(), writes=(), is_dma=False):
        if self.n_ops >= MAXOPS[0]:
            return
        need = {}

        def add(tok, same_ok):
            if tok is None:
                return
            key, h, v, teng = tok
            if teng == eng and eng == "pe" and not is_dma_tok(tok) and not same_ok:
                return
            if need.get(key, (None, 0))[1] < v:
                need[key] = (h, v)

        def is_dma_tok(tok):
            return tok[0].startswith("d#")

        for k in reads:
            add(self.lastw.get(k), True)
        for k in writes:
            add(self.lastw.get(k), False)
            for tok in self.readers.get(k, {}).values():
                add(tok, False)
        if is_dma:
            nh = len(self.dsem) - self.n_sw
            if eng == "pool":
                slot = nh + self.sw_i % self.n_sw
                self.sw_i += 1
            else:
                slot = self.dma_i % nh
                self.dma_i += 1
            if self.dval[slot] > 0:
                add(("d#%d" % slot, self.dsem[slot], self.dval[slot], None), True)
            self.dval[slot] += 16
            tok = ("d#%d" % slot, self.dsem[slot], self.dval[slot], None)
            inc = 16
        else:
            self.cnt[eng] += 1
            tok = (eng, self.sem[eng], self.cnt[eng], eng)
            inc = 1
        waits = []
        for key, (h, v) in need.items():
            if self.seen[eng].get(key, 0) < v:
                self.seen[eng][key] = v
                waits.append((h, v))
        for k in writes:
            self.lastw[k] = tok
            self.readers[k] = {}
        for k in reads:
            if k in writes:
                continue
            self.readers.setdefault(k, {})[tok[0]] = tok
        self.items[eng].append((waits, call, tok[1], inc))
        self.n_ops += 1

    def pe(self, fn, reads=(), writes=()):
        self.op("pe", fn, reads, writes)

    def dve(self, fn, reads=(), writes=()):
        self.op("dve", fn, reads, writes)

    def act(self, fn, reads=(), writes=()):
        self.op("act", fn, reads, writes)

    def pool(self, fn, reads=(), writes=()):
        self.op("pool", fn, reads, writes)

    def dma(self, out, in_, reads=(), writes=(), q="sp"):
        self.op(q, lambda e: e.dma_start(out=out, in_=in_), reads, writes, is_dma=True)

    @staticmethod
    def _fd(call):
        name, a_, k_ = call
        out = k_.get("out", a_[0] if a_ else None)
        try:
            shp = list(out.shape)
            n = 1
            for d in shp[1:]:
                n *= int(d)
            return max(n, 1), int(shp[0])
        except Exception:
            return 128, 128

    @staticmethod
    def _act_set(call):
        if call[0] != "activation":
            return None
        f = str(call[2].get("func", "")).split(".")[-1]
        if f in ("Exp", "Ln"):
            return "E"
        if f in ("Silu", "Sigmoid", "Tanh"):
            return f
        return None

    def _dur(self, eng, call, is_dma):
        fd, npart = self._fd(call)
        if is_dma:
            byt = fd * npart * 4
            return 0.15, 2.0 + byt / 120e3
        if eng == "pe":
            t = (0.06 + fd / 1200.0) * PE_SCALE[0]
            return t, t + 0.25 + LAT_EXTRA[0]
        if eng == "dve":
            t = 0.16 + fd / 960.0
            if call[0] == "scalar_tensor_tensor":
                t = 0.16 + fd / 480.0
            return t, t + 0.1 + LAT_EXTRA[0]
        if eng == "act":
            t = 0.22 + fd / 1200.0
            return t, t + 0.1 + LAT_EXTRA[0]
        t = 0.3 + fd / 600.0
        return t, t + 0.1 + LAT_EXTRA[0]

    def schedule(self):
        import heapq
        ops = self.oplist
        n = len(ops)
        if MAXOPS[0] < n:
            ops = ops[:MAXOPS[0]]
            n = len(ops)
        preds = [None] * n
        preds_ps = [None] * n
        lastw, readers = {}, {}
        for i, (eng, call, reads, writes, is_dma, _al) in enumerate(ops):
            ps = set()
            pp = set()
            for k in reads:
                if k in lastw:
                    ps.add(lastw[k])
            for k in writes:
                if k in lastw:
                    ps.add(lastw[k])
                    if k[:2] in ("PF", "PQ", "PB"):
                        pp.add(lastw[k])
                ps.update(readers.get(k, ()))
                if k[:2] in ("PF", "PQ", "PB"):
                    pp.update(readers.get(k, ()))
            ps.discard(i)
            pp.discard(i)
            preds[i] = ps
            preds_ps[i] = pp
            for k in writes:
                lastw[k] = i
                readers[k] = set()
            for k in reads:
                if k not in writes:
                    readers.setdefault(k, set()).add(i)
        succs = [[] for _ in range(n)]
        npred = [0] * n
        for i in range(n):
            npred[i] = len(preds[i])
            for p in preds[i]:
                succs[p].append(i)
        chosen = None
        if not SCHED[0]:
            order = list(range(n))
        else:
            durs = [self._dur(ops[i][0], ops[i][1], ops[i][4]) for i in range(n)]
            tail = [0.0] * n
            for i in range(n - 1, -1, -1):
                t = 0.0
                for sidx in succs[i]:
                    if tail[sidx] > t:
                        t = tail[sidx]
                tail[i] = t + durs[i][1]
            done_t = [0.0] * n
            ready_t = [0.0] * n
            free = {e: 0.0 for e in self.ENGS}
            pend = {e: [] for e in self.ENGS}
            avail = {e: [] for e in self.ENGS}
            act_set = [None]
            act_pick = [None]
            chosen = [None] * n
            engs_of = [[ops[i][0]] + list(ops[i][5].keys()) for i in range(n)]

            def push_ready(i, rt):
                for e2 in engs_of[i]:
                    heapq.heappush(pend[e2], (rt, i))

            for i in range(n):
                if npred[i] == 0:
                    push_ready(i, 0.0)
            order = []
            left = n
            while left:
                best = None
                for e in self.ENGS:
                    f = free[e]
                    while pend[e] and (pend[e][0][0] <= f or chosen[pend[e][0][1]] is not None):
                        j_ = heapq.heappop(pend[e])[1]
                        if chosen[j_] is None:
                            heapq.heappush(avail[e], ((-tail[j_] if PRIO[0] else 0.0), j_))
                    while avail[e] and chosen[avail[e][0][1]] is not None:
                        heapq.heappop(avail[e])
                    if avail[e] and e == "act" and ACT_TABLES[0]:
                        peek = []
                        while avail[e] and len(peek) < 8:
                            it_ = heapq.heappop(avail[e])
                            if chosen[it_[1]] is None:
                                peek.append(it_)
                        pick = peek[0]
                        for it_ in peek:
                            cs_ = self._act_set(ops[it_[1]][1] if ops[it_[1]][0] == "act" else ops[it_[1]][5]["act"])
                            if ACT_TABLES[0] == 2 and (cs_ is None or cs_ == act_set[0]):
                                pick = it_
                                break
                        for it_ in peek:
                            heapq.heappush(avail[e], it_)
                        act_pick[0] = pick
                        cand = (f, pick[1], e, True)
                    elif avail[e]:
                        cand = (f, avail[e][0][1], e, True)
                    elif pend[e]:
                        cand = (pend[e][0][0], pend[e][0][1], e, False)
                    else:
                        continue
                    i_ = cand[1]
                    pen = 0.0
                    if e != ops[i_][0]:
                        pen = max(0.0, self._dur(e, ops[i_][5][e], False)[0] - self._dur(ops[i_][0], ops[i_][1], False)[0])
                    key = (cand[0] + pen, cand[1])
                    if best is None or key < best[0]:
                        best = (key, cand)
                st, i, e, from_avail = best[1]
                if from_avail and e == "act" and ACT_TABLES[0]:
                    tmp_ = []
                    while avail[e]:
                        it_ = heapq.heappop(avail[e])
                        if it_[1] == i:
                            break
                        tmp_.append(it_)
                    for it_ in tmp_:
                        heapq.heappush(avail[e], it_)
                elif from_avail:
                    heapq.heappop(avail[e])
                else:
                    heapq.heappop(pend[e])
                chosen[i] = e
                call_i = ops[i][1] if e == ops[i][0] else ops[i][5][e]
                if e == "pe" and KEEPWARM[0] and self.warm_ops is not None and call_i[0] == "matmul" \
                        and call_i[2].get("start", True) and st - free[e] > 0.7:
                    tp = free[e]
                    for p in preds_ps[i]:
                        tp = max(tp, done_t[p])
                    nfill = min(int((st - tp - 0.25) / 0.17), KEEPWARM[0])
                    if nfill > 0:
                        order.append(("fill", i, nfill))
                        self.n_fill += nfill
                occ, lat = self._dur(e, call_i, ops[i][4])
                if e == "act" and ACT_TABLES[0]:
                    cs_ = self._act_set(call_i)
                    if cs_ is not None and cs_ != act_set[0]:
                        occ += 1.3
                        lat += 1.3
                        act_set[0] = cs_
                if SCHED_TRACE is not None:
                    SCHED_TRACE.append((i, e, st, occ, lat, free[e], ready_t[i]))
                free[e] = st + occ
                done_t[i] = st + lat
                order.append(i)
                left -= 1
                for sidx in succs[i]:
                    ready_t[sidx] = max(ready_t[sidx], (st + occ) if (e == "pe" and ops[sidx][0] == "pe") else done_t[i])
                    npred[sidx] -= 1
                    if npred[sidx] == 0:
                        push_ready(sidx, ready_t[sidx])
            self.est_us = max(done_t) if n else 0.0
        toks = [None] * n
        eng_of = [(chosen[i] if chosen is not None and chosen[i] is not None else ops[i][0]) for i in range(n)]
        for i in order:
            if isinstance(i, tuple):
                _, ri, nfill = i
                out_ap = ops[ri][1][1][0] if ops[ri][1][1] else ops[ri][1][2].get("out")
                try:
                    shp = list(out_ap.shape)
                    if len(shp) != 2 or shp[1] < 64 or str(out_ap.dtype) != str(F32):
                        continue
                    ncol = min(int(shp[1]), 128)
                    dcall = ("matmul", (out_ap[:, 0:ncol],), dict(lhsT=self.warm_ops[:, 0:int(shp[0])], rhs=self.warm_ops[:, 0:ncol], start=True, stop=True))
                except Exception:
                    continue
                for _ in range(nfill):
                    self._emit("pe", dcall, False, [(toks[p], eng_of[p]) for p in preds_ps[ri]])
                continue
            eng, call, reads, writes, is_dma, al = ops[i]
            e = eng_of[i]
            toks[i] = self._emit(e, call if e == eng else al[e], is_dma, [(toks[p], eng_of[p]) for p in preds[i]])

    def _emit(self, eng, call, is_dma, pred_toks):
        need = {}

        def add(tok):
            key, h, v, teng = tok
            if need.get(key, (None, 0))[1] < v:
                need[key] = (h, v)

        for tok, peng in pred_toks:
            if peng == "pe" and eng == "pe" and not tok[0].startswith("d#"):
                continue
            add(tok)
        if is_dma:
            nh = len(self.dsem) - self.n_sw
            if eng == "pool":
                slot = nh + self.sw_i % self.n_sw
                self.sw_i += 1
            else:
                slot = self.dma_i % nh
                self.dma_i += 1
            if self.dval[slot] > 0:
                add(("d#%d" % slot, self.dsem[slot], self.dval[slot], None))
            self.dval[slot] += 16
            tok = ("d#%d" % slot, self.dsem[slot], self.dval[slot], None)
            inc = 16
        else:
            self.cnt[eng] += 1
            tok = (eng, self.sem[eng], self.cnt[eng], eng)
            inc = 1
        waits = []
        for key, (h, v) in need.items():
            if self.seen[eng].get(key, 0) < v:
                self.seen[eng][key] = v
                waits.append((h, v))
        self.items[eng].append((waits, call, tok[1], inc))
        self.n_ops += 1
        return tok

    def finish(self):
        nc = self.nc
        self.schedule()
        final = [(self.dsem[i], self.dval[i]) for i in range(len(self.dsem)) if self.dval[i] > 0]

        def emit(name, e, tail=False):
            for waits, fn, h, inc in self.items[name]:
                for (wh, wv) in waits:
                    e.wait_ge(wh, wv)
                name_, a_, k_ = fn
                getattr(e, name_)(*a_, **k_).then_inc(h, inc)
            if tail:
                for (wh, wv) in final:
                    e.wait_ge(wh, wv)

        with nc.Block() as block:
            @block.tensor
            def _(e):
                emit("pe", e)

            @block.vector
            def _(e):
                emit("dve", e)

            @block.scalar
            def _(e):
                emit("act", e)

            @block.gpsimd
            def _(e):
                emit("pool", e)

            @block.sync
            def _(e):
                emit("sp", e, tail=True)
        self.stack.close()


D = 1024
PW = 4232
C0 = 0.6065306597126334
NCONST = 9
LASTP = None
MAXOPS = [10 ** 9]
SCHED = [True]
PRIO = [True]
ALT = [True]
KEEPWARM = [0]
ACT_TABLES = [2]
PE_SCALE = [0.6]
LAT_EXTRA = [0.25]
DECODE_LAST = [False]
SCHED_TRACE = None
TRACE_OPS = None


def host_consts():
    p = np.arange(128)
    same = (p[:, None] // 64) == (p[None, :] // 64)
    c = np.zeros((NCONST, 128, 128), np.float32)
    c[0] = np.eye(128)
    c[1] = 1.0
    c[2] = same & (p[:, None] <= p[None, :])
    c[3] = same & (p[:, None] > p[None, :])
    c[4] = same & (p[:, None] < p[None, :])
    c[5] = same
    c[6] = (p[:, None] % 64) == (p[None, :] % 64)
    c[7][:, 0] = p < 64
    c[7][:, 1] = p >= 64
    c[8] = -c[4]
    return c


def build(T=2048, NS=16, TB=512):
    nc = bass.Bass("TRN2", target_bir_lowering=False)

    def din(name, shape):
        return nc.dram_tensor(name, list(shape), F32, kind="ExternalInput").ap()

    def dout(name, shape):
        return nc.dram_tensor(name, list(shape), F32, kind="ExternalOutput").ap()

    xp = din("xp", [T, D])
    w_in = din("w_in", [D, PW])
    w_out = din("w_out", [D, D])
    vrows = din("vrows", [128, 128])
    rowp = din("rowp", [2048 + 128 + 1024 + 8])
    w2d = din("w2", [64, 512])
    a2d = din("a2", [64, 512])
    constd = din("consts", [NCONST, 128, 128])
    resetd = din("resetm", [128, 512])
    yp = dout("yp", [T, D])
    p_amat = dout("p_amat", [4, 128, 128])
    p_aconv = dout("p_aconv", [36, 128])
    p_bmat = dout("p_bmat", [8, 64, 64])
    p_bshift = dout("p_bshift", [17, 128])

    xs_d = din("xs", [NS, D])
    sa_mat_d = din("sa_mat", [NS, 4, 128, 128])
    sa_conv_d = din("sa_conv", [NS, 3, 1536])
    sb_mat_d = din("sb_mat", [NS, 8, 64, 64])
    sb_shift_d = din("sb_shift", [NS, 2176])
    ys = dout("ys", [NS, D])
    s_amat = dout("s_amat", [NS, 4, 128, 128])
    s_aconv = dout("s_aconv", [NS, 3, 1536])
    s_bmat = dout("s_bmat", [NS, 8, 64, 64])
    s_bshift = dout("s_bshift", [NS, 2176])

    def dscr(name, shape):
        return nc.dram_tensor(name, list(shape), F32, kind="Internal").ap()

    scr_q, scr_k, scr_v = dscr("scr_q", [NS, 512]), dscr("scr_k", [NS, 512]), dscr("scr_v", [NS, 512])
    scr_ab = dscr("scr_ab", [NS, 8])
    scr_o = dscr("scr_o", [64, 128])
    scr_b6 = [dscr("scr_b%d" % i, [NS, 512]) for i in range(6)]
    scr_y = dscr("scr_y", [128, 64])

    P = Prog(nc)
    global LASTP
    LASTP = P
    NTB = TB // 128
    assert T % TB == 0

    cf = P.sb("cf", [128, NCONST * 128], F32)
    for i in range(NCONST):
        P.dma(cf[:, i * 128:(i + 1) * 128], constd[i], writes=["cf"])
    CF = lambda i: cf[:, i * 128:(i + 1) * 128]
    IDF, ONESF, BT, MGT, STRICT, BL, PAIRS, NSTRICT = CF(0), CF(1), CF(2), CF(3), CF(4), CF(5), CF(6), CF(8)
    INCL = BT
    CHIND = cf[:, 7 * 128:7 * 128 + 2]
    idb = P.sb("idb", [128, 128], BF16)
    P.dve(lambda e: e.tensor_copy(out=idb[:, :], in_=IDF), reads=["cf"], writes=["idb"])
    P.warm_ops = idb
    onesb = P.sb("onesb", [128, 128], BF16)
    P.dve(lambda e: e.tensor_copy(out=onesb[:, :], in_=ONESF), reads=["cf"], writes=["onesb"])
    blb = P.sb("blb", [128, 128], BF16)
    P.dve(lambda e: e.tensor_copy(out=blb[:, :], in_=BL), reads=["cf"], writes=["blb"])
    resetm = P.sb("resetm_sb", [128, 512], F32)
    P.dma(resetm[:, :], resetd[:, :], writes=["resetm"])

    PF = [P.ps("PF%d" % i, [128, 512], F32) for i in range(4)]
    PQ = [P.ps("PQ%d" % i, [128, 512], F32) for i in range(2)]
    PB = [P.ps("PB%d" % i, [128, 1024], BF16) for i in range(2)]

    vr_t = P.sb("vr_t", [128, 128], F32)
    P.dma(vr_t[:, :], vrows[:, :], writes=["vr_t"])
    pcol = P.sb("pcol", [128, 128], F32)
    P.pe(lambda e: e.transpose(out=PF[0][:, 0:128], in_=vr_t[:, :], identity=IDF), reads=["vr_t", "cf"], writes=["PF0"])
    P.dve(lambda e: e.tensor_copy(out=pcol[:, :], in_=PF[0][:, 0:128]), reads=["PF0"], writes=["pcol"])
    col = lambda i: pcol[:, i:i + 1]
    nwa = P.sb("nwa", [128, 8], F32)
    P.dve(lambda e: e.tensor_scalar(out=nwa[:, :], in0=pcol[:, 73:81], scalar1=-1.0, scalar2=None, op0=ALU.mult), reads=["pcol"], writes=["nwa"])
    omu = P.sb("omu", [128, 17], F32)
    P.dve(lambda e: e.tensor_scalar(out=omu[:, :], in0=pcol[:, 56:73], scalar1=-1.0, scalar2=1.0, op0=ALU.mult, op1=ALU.add), reads=["pcol"], writes=["omu"])
    NW0, CW0, MU0, W00, A00, KK0, KA0, RK0 = 0, 8, 56, 73, 77, 81, 85, 89

    RL = 2048 + 128 + 1024 + 8
    rp = P.sb("rp", [128, RL], F32)
    P.dma(rp[:, :], rowp.partition_broadcast(128), writes=["rp"])
    LNW, LNB, ANW, FNW = rp[:, 0:512], rp[:, 512:1024], rp[:, 2048:2176], rp[:, 2176:3200]
    ALOG, DTB = rp[:, 3200:3204], rp[:, 3204:3208]
    nega = P.sb("nega", [128, 4], F32)
    P.act(lambda e: e.activation(out=nega[:, :], in_=ALOG, func=AF.Exp), reads=["rp"], writes=["nega"])
    P.dve(lambda e: e.tensor_scalar(out=nega[:, :], in0=nega[:, :], scalar1=-1.0, scalar2=None, op0=ALU.mult), reads=["nega"], writes=["nega"])

    w2a2 = P.sb("w2a2", [128, 512], F32)
    P.dma(w2a2[0:64, :], w2d[:, :], writes=["w2a2"])
    P.dma(w2a2[64:128, :], a2d[:, :], writes=["w2a2"])

    woutb = P.sb("woutb", [128, 8 * 1024], BF16)
    NWB = 4
    wbf = [P.sb("wbf%d" % i, [128, 1024], BF16) for i in range(NWB)]
    cast_i = [0]

    def cast(out, in_, reads, writes):
        i = cast_i[0]
        cast_i[0] += 1
        if i % 3 != 2:
            P.act(lambda e: e.activation(out=out, in_=in_, func=AF.Copy), reads=reads, writes=writes)
        else:
            P.dve(lambda e: e.tensor_copy(out=out, in_=in_), reads=reads, writes=writes)

    for kc in range(8):
        P.dma(woutb[:, kc * 1024:(kc + 1) * 1024], w_out[kc * 128:(kc + 1) * 128, :], writes=["woutb"], q="pool")

    Sa = P.sb("Sa", [128, 512], F32)
    Sab = P.sb("Sab", [128, 512], BF16)
    Hb = P.sb("Hb", [128, 512], F32)
    Hbb = P.sb("Hbb", [128, 512], BF16)
    for t_, n_ in ((Sa, "Sa"), (Sab, "Sab"), (Hb, "Hb"), (Hbb, "Hbb")):
        P.pool(lambda e, t_=t_: e.memset(t_[:, :], 0.0), writes=[n_])
    ccar = P.sb("ccar", [128, 36], F32)
    P.pool(lambda e: e.memset(ccar[:, :], 0.0), writes=["ccar"])
    bcar = P.sb("bcar", [128, 17], F32)
    P.pool(lambda e: e.memset(bcar[:, :], 0.0), writes=["bcar"])

    xt = [P.sb("xt%d" % i, [128, D], F32) for i in range(2)]
    xs_ = P.sb("xs_", [128, D], F32)
    st4 = P.sb("st4", [128, 8], F32)
    xnT = P.sb("xnT", [128, 8 * TB], BF16)
    cb = [P.sb("cb%d" % i, [128, TB + 4], F32) for i in range(2)]
    FT = [P.sb("FT%d" % i, [128, 512], F32) for i in range(10)]
    HT = [P.sb("HT%d" % i, [128, 512], BF16) for i in range(40)]
    mixB4 = P.sb("mixB4", [128, (TB // 128) * 512], BF16)
    ctmp = P.sb("ctmp", [128, 512], F32)
    mixT = P.sb("mixT", [128, 1024], BF16)
    BLK = [P.sb("BLK%d" % i, [128, (8 if i in (0, 3, 4) else 4) * TB], BF16) for i in range(5)]
    P.pool(lambda e: e.memset(BLK[3][:, :], 0.0), writes=["BLK3"])
    P.pool(lambda e: e.memset(BLK[4][:, :], 0.0), writes=["BLK4"])
    mixA2 = [P.sb("mixA%d" % i, [128, NTB * 512], BF16) for i in range(2)]
    pce = P.sb("pce", [128, 4 * (TB // 64)], F32)
    bon = P.sb("bon", [128, 4 * TB], BF16)

    def rsqrt_act(out, in_, scale, eps, reads, writes):
        P.act(lambda e: e.activation(out=out, in_=in_, func=AF.Ln, scale=scale, bias=eps), reads=reads, writes=writes)
        P.act(lambda e: e.activation(out=out, in_=out, func=AF.Exp, scale=-0.5), reads=writes, writes=writes)

    def b3(ap, h, n):
        return ap.rearrange("p (h n) -> p h n", h=h)

    SPJ = P.sb("SPJ", [128, 35 * NS], F32)
    xnTs = P.sb("xnTs", [128, 8 * NS], BF16)
    arena = P.sb("arena", [128, 4096], F32)

    class _Sub:
        def __init__(self, off):
            self.off = off

        def __getitem__(self, key):
            rows, cols = key
            lo = 0 if cols.start is None else cols.start
            hi = 512 if cols.stop is None else cols.stop
            return arena[rows, self.off + lo:self.off + hi]

    SF = [_Sub(i * 512) for i in range(8)]
    SAMPLE_LOCALS = dict(locals())
    if not DECODE_LAST[0]:
        sample_path(SAMPLE_LOCALS)
    arenab = arena[:, :].bitcast(BF16)
    P.pool(lambda e: e.memset(st4[:, 7:8], 0.0), reads=["SF%d" % i for i in range(8)], writes=["BAR", "BVB", "BSG", "st4"])

    nblk = T // TB
    stream_b = None
    for blk in range(nblk):
        t0 = blk * TB
        last_blk = blk == nblk - 1
        for tb in range(NTB):
            xa = xt[tb % 2]
            xk = "xt%d" % (tb % 2)
            P.dma(xa[:, :], xp[t0 + tb * 128:t0 + (tb + 1) * 128, :], writes=[xk])
            P.act(lambda e, xa=xa: e.activation(out=xs_[:, :], in_=xa[:, :], func=AF.Square, accum_out=st4[:, 0:1]), reads=[xk], writes=["xs_", "st4"])
            rsqrt_act(st4[:, 0:1], st4[:, 0:1], 1.0 / D, 1e-6, ["st4"], ["st4"])
            P.act(lambda e, xa=xa: e.activation(out=xs_[:, :], in_=xa[:, :], func=AF.Copy, scale=st4[:, 0:1]), reads=[xk, "st4"], writes=["xs_"])
            for half in range(2):
                pf = PF[half]
                pk = "PF%d" % half
                for q in range(4):
                    kc = half * 4 + q
                    P.pe(lambda e, pf=pf, q=q, kc=kc: e.transpose(out=pf[:, q * 128:(q + 1) * 128], in_=xs_[:, kc * 128:(kc + 1) * 128], identity=IDF), reads=["xs_", "cf"], writes=[pk])
                out3 = xnT[:, :].rearrange("p (k t) -> p k t", k=8)[:, half * 4:(half + 1) * 4, tb * 128:(tb + 1) * 128]
                in3 = pf[:, :].rearrange("p (k t) -> p k t", k=4)
                nw3 = pcol[:, NW0 + half * 4:NW0 + half * 4 + 4].unsqueeze(2).to_broadcast([128, 4, 128])
                P.dve(lambda e, out3=out3, in3=in3, nw3=nw3: e.tensor_tensor(out=out3, in0=in3, in1=nw3, op=ALU.mult), reads=[pk, "pcol"], writes=["xnT"])

        wchunk_i = [0]

        def proj_chunk(c0, ncols, pf, pk):
            s = wchunk_i[0] % NWB
            wchunk_i[0] += 1
            bk = "wbf%d" % s
            src = w_in[:, c0:c0 + ncols].rearrange("(k p) n -> p k n", p=128)
            dst = wbf[s][:, 0:8 * ncols].rearrange("p (k n) -> p k n", k=8)
            P.dma(dst, src, writes=[bk], q="pool")
            for kc in range(8):
                P.pe(lambda e, kc=kc, s=s: e.matmul(pf[0:ncols, 0:TB], lhsT=wbf[s][:, kc * ncols:(kc + 1) * ncols], rhs=xnT[:, kc * TB:(kc + 1) * TB], start=(kc == 0), stop=(kc == 7)), reads=[bk, "xnT"], writes=[pk])

        mixA, mxk = mixA2[blk % 2], "mixA%d" % (blk % 2)
        KQ, VT, SGA = BLK[0], BLK[1], BLK[2]
        for c in range(12):
            pf, pk = PF[c % 2], "PF%d" % (c % 2)
            cbuf, ck = cb[c % 2], "cb%d" % (c % 2)
            proj_chunk(c * 128, 128, pf, pk)
            car3 = ccar[:, :].rearrange("p (i c) -> p i c", i=3)[:, :, c]
            P.pool(lambda e, cbuf=cbuf, car3=car3: e.tensor_copy(out=cbuf[:, 0:3], in_=car3), reads=["ccar"], writes=[ck])
            P.act(lambda e, cbuf=cbuf, pf=pf: e.activation(out=cbuf[:, 3:3 + TB], in_=pf[:, 0:TB], func=AF.Copy), reads=[pk], writes=[ck])
            P.pool(lambda e, cbuf=cbuf, car3=car3: e.tensor_copy(out=car3, in_=cbuf[:, TB:TB + 3]), reads=[ck], writes=["ccar"])
            acc, ak = FT[c % 2], "FT%d" % (c % 2)
            P.op("dve", lambda e, cbuf=cbuf, acc=acc, c=c: e.tensor_scalar(out=acc[:, 0:TB], in0=cbuf[:, 0:TB], scalar1=col(CW0 + c * 4), scalar2=None, op0=ALU.mult), [ck, "pcol"], [ak],
                 alts=[("act", lambda e, cbuf=cbuf, acc=acc, c=c: e.activation(out=acc[:, 0:TB], in_=cbuf[:, 0:TB], func=AF.Copy, scale=col(CW0 + c * 4)))] if ALT[0] else None)
            P.dve(lambda e, cbuf=cbuf, acc=acc, c=c: e.scalar_tensor_tensor(out=acc[:, 0:TB], in0=cbuf[:, 1:1 + TB], scalar=col(CW0 + c * 4 + 1), in1=acc[:, 0:TB], op0=ALU.mult, op1=ALU.add), reads=[ck, "pcol", ak], writes=[ak])
            P.op("dve", lambda e, cbuf=cbuf, c=c: e.tensor_scalar(out=ctmp[:, 0:TB], in0=cbuf[:, 2:2 + TB], scalar1=col(CW0 + c * 4 + 2), scalar2=None, op0=ALU.mult), [ck, "pcol"], ["ctmp"],
                 alts=[("act", lambda e, cbuf=cbuf, c=c: e.activation(out=ctmp[:, 0:TB], in_=cbuf[:, 2:2 + TB], func=AF.Copy, scale=col(CW0 + c * 4 + 2)))] if ALT[0] else None)
            P.dve(lambda e, cbuf=cbuf, c=c: e.scalar_tensor_tensor(out=ctmp[:, 0:TB], in0=cbuf[:, 3:3 + TB], scalar=col(CW0 + c * 4 + 3), in1=ctmp[:, 0:TB], op0=ALU.mult, op1=ALU.add), reads=[ck, "pcol", "ctmp"], writes=["ctmp"])
            P.dve(lambda e, acc=acc: e.tensor_tensor(out=acc[:, 0:TB], in0=acc[:, 0:TB], in1=ctmp[:, 0:TB], op=ALU.add), reads=[ak, "ctmp"], writes=[ak])
            P.act(lambda e, acc=acc: e.activation(out=acc[:, 0:TB], in_=acc[:, 0:TB], func=AF.Silu), reads=[ak], writes=[ak])
            if c < 8:
                h = c % 4
                isq = c < 4
                sq, sqk = HT[c % 2], "HT%d" % (c % 2)
                P.act(lambda e, acc=acc, sq=sq: e.activation(out=sq[:, 0:TB], in_=acc[:, 0:TB], func=AF.Square), reads=[ak], writes=[sqk])
                p2, p2k = PF[2 + c % 2], "PF%d" % (2 + c % 2)
                P.pe(lambda e, p2=p2, sq=sq: e.matmul(p2[:, 0:TB], lhsT=onesb[:, :], rhs=sq[:, 0:TB], start=True, stop=True), reads=[sqk, "onesb"], writes=[p2k])
                ri, rik = FT[2 + c % 2], "FT%d" % (2 + c % 2)
                rsqrt_act(ri[:, 0:TB], p2[:, 0:TB], 1.0, 1e-12, [p2k], [rik])
                dst = KQ[:, h * 2 * TB:(h + 1) * 2 * TB].rearrange("p (t s n) -> p t s n", t=NTB, s=2)[:, :, 1 if isq else 0, :]
                P.dve(lambda e, acc=acc, ri=ri, dst=dst, isq=isq: e.scalar_tensor_tensor(out=dst, in0=acc[:, 0:TB].rearrange("p (t n) -> p t n", t=NTB), scalar=(128 ** -0.5 if isq else 1.0), in1=ri[:, 0:TB].rearrange("p (t n) -> p t n", t=NTB), op0=ALU.mult, op1=ALU.mult), reads=[ak, rik], writes=["BLK0"])
            else:
                h = c - 8
                P.dve(lambda e, acc=acc, h=h: e.tensor_copy(out=VT[:, h * TB:(h + 1) * TB], in_=acc[:, 0:TB]), reads=[ak], writes=["BLK1"])
        if last_blk:
            P.pe(lambda e: e.transpose(out=PF[0][0:36, 0:128], in_=ccar[:, :], identity=IDF), reads=["ccar", "cf"], writes=["PF0"])
            P.dve(lambda e: e.tensor_copy(out=FT[0][0:36, 0:128], in_=PF[0][0:36, 0:128]), reads=["PF0"], writes=["FT0"])
            P.dma(p_aconv[:, :], FT[0][0:36, 0:128], reads=["FT0"], writes=["p_aconv"])
        for c in range(4):
            pf, pk = PF[c % 2], "PF%d" % (c % 2)
            proj_chunk(1536 + c * 128, 128, pf, pk)
            P.act(lambda e, pf=pf, c=c: e.activation(out=SGA[:, c * TB:(c + 1) * TB], in_=pf[:, 0:TB], func=AF.Silu), reads=[pk], writes=["BLK2"])
        bdT = FT[4]
        proj_chunk(2048, 8, PF[0], "PF0")
        P.act(lambda e: e.activation(out=bdT[0:8, 0:TB], in_=PF[0][0:8, 0:TB], func=AF.Copy), reads=["PF0"], writes=["FT4"])

        P.capture_begin()
        for tb in range(NTB):
            cs = slice(tb * 128, (tb + 1) * 128)
            kq = lambda h, s: KQ[:, h * 2 * TB + tb * 256 + s * 128: h * 2 * TB + tb * 256 + (s + 1) * 128]
            kq2 = lambda h: KQ[:, h * 2 * TB + tb * 256: h * 2 * TB + (tb + 1) * 256]
            sc = FT[5]
            P.pe(lambda e, cs=cs: e.transpose(out=PF[0][:, 0:8], in_=bdT[0:8, cs], identity=IDF[0:8, 0:8]), reads=["FT4", "cf"], writes=["PF0"])
            P.act(lambda e: e.activation(out=sc[:, 0:4], in_=PF[0][:, 0:4], func=AF.Exp, scale=-1.0), reads=["PF0"], writes=["FT5"])
            P.dve(lambda e: e.tensor_scalar(out=sc[:, 0:4], in0=sc[:, 0:4], scalar1=1.0, scalar2=None, op0=ALU.add), reads=["FT5"], writes=["FT5"])
            P.dve(lambda e: e.reciprocal(out=sc[:, 0:4], in_=sc[:, 0:4]), reads=["FT5"], writes=["FT5"])
            P.dve(lambda e: e.tensor_tensor(out=sc[:, 4:8], in0=PF[0][:, 4:8], in1=DTB, op=ALU.add), reads=["PF0", "rp"], writes=["FT5"])
            P.act(lambda e: e.activation(out=sc[:, 4:8], in_=sc[:, 4:8], func=AF.Exp), reads=["FT5"], writes=["FT5"])
            P.act(lambda e: e.activation(out=sc[:, 4:8], in_=sc[:, 4:8], func=AF.Ln, bias=1.0), reads=["FT5"], writes=["FT5"])
            P.dve(lambda e: e.tensor_tensor(out=sc[:, 4:8], in0=sc[:, 4:8], in1=nega[:, :], op=ALU.mult), reads=["FT5", "nega"], writes=["FT5"])
            P.pe(lambda e: e.matmul(PF[0][:, 8:12], lhsT=BT, rhs=sc[:, 4:8], start=True, stop=True), reads=["FT5", "cf"], writes=["PF0"])
            P.pe(lambda e: e.matmul(PF[0][:, 12:16], lhsT=BL, rhs=sc[:, 4:8], start=True, stop=True), reads=["FT5", "cf"], writes=["PF0"])
            P.dve(lambda e: e.tensor_copy(out=sc[:, 8:16], in_=PF[0][:, 8:16]), reads=["PF0"], writes=["FT5"])
            P.act(lambda e: e.activation(out=sc[:, 16:20], in_=sc[:, 8:12], func=AF.Exp), reads=["FT5"], writes=["FT5"])
            P.dve(lambda e: e.tensor_tensor(out=sc[:, 20:24], in0=sc[:, 12:16], in1=sc[:, 8:12], op=ALU.subtract), reads=["FT5"], writes=["FT5"])
            P.act(lambda e: e.activation(out=sc[:, 20:24], in_=sc[:, 20:24], func=AF.Exp), reads=["FT5"], writes=["FT5"])
            gm3 = sc[:, 32:40].rearrange("p (c h) -> p c h", c=2)
            P.dve(lambda e: e.tensor_tensor(out=gm3, in0=sc[:, 4:8].unsqueeze(1).to_broadcast([128, 2, 4]), in1=CHIND.unsqueeze(2).to_broadcast([128, 2, 4]), op=ALU.mult), reads=["FT5", "cf"], writes=["FT5"])
            P.pe(lambda e: e.matmul(PF[0][:, 16:24], lhsT=ONESF, rhs=sc[:, 32:40], start=True, stop=True), reads=["FT5", "cf"], writes=["PF0"])
            P.act(lambda e: e.activation(out=sc[:, 24:32], in_=PF[0][:, 16:24], func=AF.Exp), reads=["PF0"], writes=["FT5"])
            beta_b = sc[:, 0:4].unsqueeze(2).to_broadcast([128, 4, 128])
            Kg, Kd, Vtm = HT[2], HT[3], HT[4]
            for h in range(4):
                P.pe(lambda e, h=h: e.transpose(out=PB[0][:, h * 128:(h + 1) * 128], in_=kq(h, 0), identity=idb[:, :]), reads=["BLK0", "idb"], writes=["PB0"])
                P.pe(lambda e, h=h: e.transpose(out=PB[0][:, 512 + h * 128:512 + (h + 1) * 128], in_=VT[:, h * TB + tb * 128:h * TB + (tb + 1) * 128], identity=idb[:, :]), reads=["BLK1", "idb"], writes=["PB0"])
            P.dve(lambda e: e.tensor_tensor(out=b3(Kg[:, :], 4, 128), in0=b3(PB[0][:, 0:512], 4, 128), in1=sc[:, 16:20].unsqueeze(2).to_broadcast([128, 4, 128]), op=ALU.mult), reads=["PB0", "FT5"], writes=["HT2"])
            P.dve(lambda e: e.tensor_tensor(out=b3(Kd[:, :], 4, 128), in0=b3(PB[0][:, 0:512], 4, 128), in1=sc[:, 20:24].unsqueeze(2).to_broadcast([128, 4, 128]), op=ALU.mult), reads=["PB0", "FT5"], writes=["HT3"])
            P.dve(lambda e: e.tensor_copy(out=Vtm[:, :], in_=PB[0][:, 512:1024]), reads=["PB0"], writes=["HT4"])
            MG, E = FT[6], FT[7]
            P.pool(lambda e: e.tensor_tensor(out=b3(MG[:, :], 4, 128), in0=MGT.unsqueeze(1).to_broadcast([128, 4, 128]), in1=sc[:, 4:8].unsqueeze(2).to_broadcast([128, 4, 128]), op=ALU.mult), reads=["cf", "FT5"], writes=["FT6"])
            for h in range(4):
                P.pe(lambda e, h=h: e.matmul(PF[0][:, h * 128:(h + 1) * 128], lhsT=MG[:, h * 128:(h + 1) * 128], rhs=BT, start=True, stop=True), reads=["FT6", "cf"], writes=["PF0"])
            P.act(lambda e: e.activation(out=E[:, :], in_=PF[0][:, :], func=AF.Exp), reads=["PF0"], writes=["FT7"])
            for h in range(4):
                P.pe(lambda e, h=h: e.matmul((PQ[0] if h < 2 else PF[1])[:, (h % 2) * 256:(h % 2 + 1) * 256], lhsT=kq(h, 0), rhs=kq2(h), start=True, stop=True), reads=["BLK0"], writes=["PQ0" if h < 2 else "PF1"])
            XQ = [HT[5], HT[6]]
            XQa, XQb = (HT[5], HT[6]), (HT[7], HT[8])
            Nn = [HT[9], HT[10]]
            QKT = HT[11]
            E2 = FT[8]
            P.pool(lambda e: e.tensor_tensor(out=b3(E2[:, :], 4, 128), in0=b3(E[:, :], 4, 128), in1=INCL.unsqueeze(1).to_broadcast([128, 4, 128]), op=ALU.mult), reads=["FT7", "cf"], writes=["FT8"])
            pdh = [(PQ[0], "PQ0"), (PF[1], "PF1")]
            pd2 = lambda i: pdh[i][0][:, :].rearrange("p (h s n) -> p h s n", h=2, s=2)
            h2 = lambda t_, i: t_[:, i * 256:(i + 1) * 256].rearrange("p (h n) -> p h n", h=2)
            for i in range(2):
                P.dve(lambda e, i=i: e.tensor_tensor(out=h2(QKT, i), in0=h2(E2, i), in1=pd2(i)[:, :, 1, :], op=ALU.mult), reads=["FT8", pdh[i][1]], writes=["HT11"])
            P.pool(lambda e: e.tensor_tensor(out=b3(E2[:, :], 4, 128), in0=b3(E2[:, :], 4, 128), in1=STRICT.unsqueeze(1).to_broadcast([128, 4, 128]), op=ALU.mult), reads=["FT8", "cf"], writes=["FT8"])
            P.pool(lambda e: e.tensor_tensor(out=b3(E2[:, :], 4, 128), in0=b3(E2[:, :], 4, 128), in1=beta_b, op=ALU.mult), reads=["FT8", "FT5"], writes=["FT8"])
            X0 = FT[9]
            for i in range(2):
                P.dve(lambda e, i=i: e.tensor_tensor(out=h2(X0, i), in0=h2(E2, i), in1=pd2(i)[:, :, 0, :], op=ALU.mult), reads=["FT8", pdh[i][1]], writes=["FT9"])
            resA = dict(X=([HT[5], HT[6]], ["HT5", "HT6"]), Q=([HT[7], HT[8]], ["HT7", "HT8"]), N=([HT[9], HT[10]], ["HT9", "HT10"]),
                        SQ=(PQ[0], "PQ0"), G0=(PF[0], "PF0"), G1=(PF[1], "PF1"), T=(PB[0], "PB0"))
            Tinv = inverse_chain(P, X0, "FT9", resA, IDF, idb)
            TinvT, tk = Tinv
            WT, qgT = HT[12], HT[13]
            U0 = FT[6]
            for h in range(4):
                P.pe(lambda e, h=h: e.matmul(PF[0][:, h * 128:(h + 1) * 128], lhsT=Kg[:, h * 128:(h + 1) * 128], rhs=TinvT[:, h * 128:(h + 1) * 128], start=True, stop=True), reads=["HT2", tk], writes=["PF0"])
            P.copy(WT[:, :], PF[0][:, :], reads=["PF0"], writes=["HT12"])
            for h in range(4):
                P.pe(lambda e, h=h: e.matmul(PF[1][:, h * 128:(h + 1) * 128], lhsT=TinvT[:, h * 128:(h + 1) * 128], rhs=Vtm[:, h * 128:(h + 1) * 128], start=True, stop=True), reads=["HT4", tk], writes=["PF1"])
            P.copy(U0[:, :], PF[1][:, :], reads=["PF1"], writes=["FT6"])
            Dg = FT[7]
            P.pool(lambda e: e.tensor_tensor(out=b3(Dg[:, :], 4, 128), in0=IDF.unsqueeze(1).to_broadcast([128, 4, 128]), in1=sc[:, 16:20].unsqueeze(2).to_broadcast([128, 4, 128]), op=ALU.mult), reads=["cf", "FT5"], writes=["FT7"])
            for h in range(4):
                P.pe(lambda e, h=h: e.matmul(PF[0][:, h * 128:(h + 1) * 128], lhsT=ONESF, rhs=Dg[:, h * 128:(h + 1) * 128], start=True, stop=True), reads=["FT7", "cf"], writes=["PF0"])
            q4 = KQ[:, :].rearrange("p (h t s n) -> p h t s n", h=4, t=NTB, s=2)[:, :, tb, 1, :]
            P.dve(lambda e, q4=q4: e.tensor_tensor(out=b3(qgT[:, :], 4, 128), in0=q4, in1=b3(PF[0][:, :], 4, 128), op=ALU.mult), reads=["BLK0", "PF0"], writes=["HT13"])
            ub = HT[2 + 0]
            ub = HT[5]
            osb = FT[8]
            for c in range(2):
                r0, r1 = c * 64, c * 64 + 64
                for h in range(4):
                    P.pe(lambda e, h=h: e.matmul(PF[0][:, h * 128:(h + 1) * 128], lhsT=WT[:, h * 128:(h + 1) * 128], rhs=Sab[:, h * 128:(h + 1) * 128], start=True, stop=True), reads=["HT12", "Sab"], writes=["PF0"])
                tmpu = FT[9]
                P.dve(lambda e, r0=r0, r1=r1: e.tensor_tensor(out=tmpu[r0:r1, :], in0=U0[r0:r1, :], in1=PF[0][r0:r1, :], op=ALU.subtract), reads=["FT6", "PF0"], writes=["FT9"])
                P.dve(lambda e, r0=r0, r1=r1: e.tensor_tensor(out=b3(ub[r0:r1, :], 4, 128), in0=b3(tmpu[r0:r1, :], 4, 128), in1=sc[r0:r1, 0:4].unsqueeze(2).to_broadcast([64, 4, 128]), op=ALU.mult), reads=["FT9", "FT5"], writes=["HT5"])
                for h in range(4):
                    P.pe(lambda e, h=h: e.matmul(PF[1][:, h * 128:(h + 1) * 128], lhsT=qgT[:, h * 128:(h + 1) * 128], rhs=Sab[:, h * 128:(h + 1) * 128], start=True, stop=False), reads=["HT13", "Sab"], writes=["PF1"])
                    P.pe(lambda e, h=h, r0=r0, r1=r1: e.matmul(PF[1][:, h * 128:(h + 1) * 128], lhsT=QKT[r0:r1, h * 128:(h + 1) * 128], rhs=ub[r0:r1, h * 128:(h + 1) * 128], start=False, stop=True), reads=["HT11", "HT5"], writes=["PF1"])
                P.copy(osb[r0:r1, :], PF[1][r0:r1, :], reads=["PF1"], writes=["FT8"])
                for h in range(4):
                    P.pe(lambda e, h=h, r0=r0, r1=r1: e.matmul(PF[0][:, h * 128:(h + 1) * 128], lhsT=Kd[r0:r1, h * 128:(h + 1) * 128], rhs=ub[r0:r1, h * 128:(h + 1) * 128], start=True, stop=True), reads=["HT3", "HT5"], writes=["PF0"])
                P.dve(lambda e, c=c: e.tensor_tensor(out=b3(Sa[:, :], 4, 128), in0=b3(Sa[:, :], 4, 128), in1=sc[:, 24 + c * 4:28 + c * 4].unsqueeze(2).to_broadcast([128, 4, 128]), op=ALU.mult), reads=["Sa", "FT5"], writes=["Sa"])
                P.dve(lambda e: e.tensor_tensor(out=Sa[:, :], in0=Sa[:, :], in1=PF[0][:, :], op=ALU.add), reads=["Sa", "PF0"], writes=["Sa"])
                P.copy(Sab[:, :], Sa[:, :], reads=["Sa"], writes=["Sab"])
            sq = FT[9]
            P.pool(lambda e: e.tensor_tensor(out=sq[:, :], in0=osb[:, :], in1=osb[:, :], op=ALU.mult), reads=["FT8"], writes=["FT9"])
            P.dve(lambda e: e.tensor_reduce(out=sc[:, 40:44], in_=b3(sq[:, :], 4, 128), axis=AX.X, op=ALU.add), reads=["FT9"], writes=["FT5"])
            rsqrt_act(sc[:, 40:44], sc[:, 40:44], 1.0 / 128, 1e-6, ["FT5"], ["FT5"])
            P.dve(lambda e: e.tensor_tensor(out=b3(osb[:, :], 4, 128), in0=b3(osb[:, :], 4, 128), in1=sc[:, 40:44].unsqueeze(2).to_broadcast([128, 4, 128]), op=ALU.mult), reads=["FT8", "FT5"], writes=["FT8"])
            P.pool(lambda e: e.tensor_tensor(out=b3(osb[:, :], 4, 128), in0=b3(osb[:, :], 4, 128), in1=ANW.unsqueeze(1).to_broadcast([128, 4, 128]), op=ALU.mult), reads=["FT8", "rp"], writes=["FT8"])
            for c4 in range(4):
                P.pe(lambda e, c4=c4, cs=cs: e.transpose(out=PB[0][:, c4 * 128:(c4 + 1) * 128], in_=SGA[:, c4 * TB + tb * 128:c4 * TB + (tb + 1) * 128], identity=idb[:, :]), reads=["BLK2", "idb"], writes=["PB0"])
            P.dve(lambda e: e.tensor_tensor(out=mixA[:, tb * 512:(tb + 1) * 512], in0=osb[:, :], in1=PB[0][:, 0:512], op=ALU.mult), reads=["FT8", "PB0"], writes=[mxk])
        if last_blk:
            P.dma(p_amat.rearrange("h k v -> k h v"), b3(Sa[:, :], 4, 128), reads=["Sa"], writes=["p_amat"])
        stream_a = P.capture_end()
        P.replay([stream_a] + ([stream_b] if stream_b is not None else []))


        AR, BKL, BKH, VBT, SGB = arenab[:, 0:8 * TB], BLK[3], BLK[4], arenab[:, 4096:4096 + 4 * TB], arenab[:, 6144:6144 + 4 * TB]
        NCH = TB // 64
        ar = lambda pr, tb_, s_: AR[:, pr * 2 * TB + tb_ * 256 + s_ * 128: pr * 2 * TB + tb_ * 256 + (s_ + 1) * 128]
        bkh = lambda hh, pr, tb_, s_: (BKL, BKH)[hh][:, pr * 2 * TB + tb_ * 256 + s_ * 128: pr * 2 * TB + tb_ * 256 + (s_ + 1) * 128]
        ar_dst = lambda pr, s_: AR[:, pr * 2 * TB:(pr + 1) * 2 * TB].rearrange("p (t s n) -> p t s n", t=NTB, s=2)[:, :, s_, :]
        bk_dst = lambda hh, pr, s_: (BKL, BKH)[hh][64 * hh:64 * hh + 64, pr * 2 * TB:(pr + 1) * 2 * TB].rearrange("p (t s n) -> p t s n", t=NTB, s=2)[:, :, s_, :]
        t3 = lambda ap: ap.rearrange("p (t n) -> p t n", t=NTB)
        bci = [0]

        def b_chunk(j, dstT, dk):
            i = bci[0] % 2
            bci[0] += 1
            pf, pk = PF[i], "PF%d" % i
            cbuf, ck = cb[i], "cb%d" % i
            proj_chunk(2056 + j * 128, 128, pf, pk)
            P.pool(lambda e: e.tensor_copy(out=cbuf[:, 0:1], in_=bcar[:, j:j + 1]), reads=["bcar"], writes=[ck])
            P.act(lambda e: e.activation(out=cbuf[:, 1:1 + TB], in_=pf[:, 0:TB], func=AF.Copy), reads=[pk], writes=[ck])
            P.pool(lambda e: e.tensor_copy(out=bcar[:, j:j + 1], in_=cbuf[:, TB:TB + 1]), reads=[ck], writes=["bcar"])
            P.op("dve", lambda e: e.tensor_scalar(out=dstT[:, 0:TB], in0=cbuf[:, 0:TB], scalar1=col(MU0 + j), scalar2=None, op0=ALU.mult), [ck, "pcol"], [dk],
                 alts=[("act", lambda e: e.activation(out=dstT[:, 0:TB], in_=cbuf[:, 0:TB], func=AF.Copy, scale=col(MU0 + j)))] if ALT[0] else None)
            P.dve(lambda e: e.scalar_tensor_tensor(out=dstT[:, 0:TB], in0=cbuf[:, 1:1 + TB], scalar=omu[:, j:j + 1], in1=dstT[:, 0:TB], op0=ALU.mult, op1=ALU.add), reads=[dk, ck, "omu"], writes=[dk])

        xb16 = FT[0]
        b_chunk(16, xb16, "FT0")
        P.act(lambda e: e.activation(out=xb16[0:64, 0:TB], in_=xb16[0:64, 0:TB], func=AF.Tanh), reads=["FT0"], writes=["FT0"])
        for pr in range(4):
            sg, aic, Lsg, Pt, Pinv, Pm1, xr, xk, kmod = FT[1], FT[2], FT[3], FT[4], FT[5], FT[6], FT[7], FT[8], FT[9]
            P.pe(lambda e, pr=pr: e.matmul(PF[2][:, 0:TB], lhsT=w2a2[0:64, pr * 128:(pr + 1) * 128], rhs=xb16[0:64, 0:TB], start=True, stop=True), reads=["w2a2", "FT0"], writes=["PF2"])
            P.act(lambda e, pr=pr: e.activation(out=sg[:, 0:TB], in_=PF[2][:, 0:TB], func=AF.Exp, scale=-1.0, bias=nwa[:, pr:pr + 1]), reads=["PF2", "nwa"], writes=["FT1"])
            P.dve(lambda e: e.tensor_scalar(out=sg[:, 0:TB], in0=sg[:, 0:TB], scalar1=1.0, scalar2=None, op0=ALU.add), reads=["FT1"], writes=["FT1"])
            P.dve(lambda e: e.reciprocal(out=sg[:, 0:TB], in_=sg[:, 0:TB]), reads=["FT1"], writes=["FT1"])
            P.pe(lambda e, pr=pr: e.matmul(PF[3][:, 0:TB], lhsT=w2a2[64:128, pr * 128:(pr + 1) * 128], rhs=xb16[64:128, 0:TB], start=True, stop=True), reads=["w2a2", "FT0"], writes=["PF3"])
            P.act(lambda e, pr=pr: e.activation(out=aic[:, 0:TB], in_=PF[3][:, 0:TB], func=AF.Exp, scale=-1.0, bias=nwa[:, 4 + pr:5 + pr]), reads=["PF3", "nwa"], writes=["FT2"])
            P.dve(lambda e: e.tensor_scalar(out=aic[:, 0:TB], in0=aic[:, 0:TB], scalar1=1.0, scalar2=None, op0=ALU.add), reads=["FT2"], writes=["FT2"])
            P.dve(lambda e: e.reciprocal(out=aic[:, 0:TB], in_=aic[:, 0:TB]), reads=["FT2"], writes=["FT2"])
            P.dve(lambda e: e.tensor_tensor_scan(out=Lsg[:, 0:TB], data0=resetm[:, 0:TB], data1=sg[:, 0:TB], initial=0.0, op0=ALU.mult, op1=ALU.add), reads=["FT1", "resetm"], writes=["FT3"])
            P.act(lambda e: e.activation(out=Pt[:, 0:TB], in_=Lsg[:, 0:TB], func=AF.Exp, scale=-C0), reads=["FT3"], writes=["FT4"])
            P.act(lambda e: e.activation(out=Pinv[:, 0:TB], in_=Lsg[:, 0:TB], func=AF.Exp, scale=C0), reads=["FT3"], writes=["FT5"])
            P.dve(lambda e: e.tensor_tensor(out=Pm1[:, 0:TB], in0=Lsg[:, 0:TB], in1=sg[:, 0:TB], op=ALU.subtract), reads=["FT3", "FT1"], writes=["FT6"])
            P.act(lambda e: e.activation(out=Pm1[:, 0:TB], in_=Pm1[:, 0:TB], func=AF.Exp, scale=-C0), reads=["FT6"], writes=["FT6"])
            P.pool(lambda e, pr=pr: e.tensor_copy(out=pce[:, pr * NCH:(pr + 1) * NCH], in_=Pt[:, 0:TB].rearrange("p (c n) -> p c n", n=64)[:, :, 63]), reads=["FT4"], writes=["pce"])
            b_chunk(pr, xr, "FT7")
            P.dve(lambda e, pr=pr: e.tensor_tensor(out=ar_dst(pr, 1), in0=t3(xr[:, 0:TB]), in1=t3(Pt[:, 0:TB]), op=ALU.mult), reads=["FT7", "FT4"], writes=["BAR"])
            b_chunk(4 + pr, xk, "FT8")
            P.dve(lambda e, pr=pr: e.scalar_tensor_tensor(out=kmod[:, 0:TB], in0=aic[:, 0:TB], scalar=-1.0, in1=col(KA0 + pr).to_broadcast([128, TB]), op0=ALU.add, op1=ALU.mult), reads=["FT2", "pcol"], writes=["FT9"])
            P.dve(lambda e: e.scalar_tensor_tensor(out=kmod[:, 0:TB], in0=kmod[:, 0:TB], scalar=1.0, in1=xk[:, 0:TB], op0=ALU.add, op1=ALU.mult), reads=["FT9", "FT8"], writes=["FT9"])
            rkb = HT[1]
            P.dve(lambda e, pr=pr: e.scalar_tensor_tensor(out=rkb[:, 0:TB], in0=xr[:, 0:TB], scalar=col(RK0 + pr), in1=kmod[:, 0:TB], op0=ALU.mult, op1=ALU.mult), reads=["FT7", "FT9", "pcol"], writes=["HT1"])
            P.pe(lambda e: e.matmul(PF[3][:, 0:TB], lhsT=blb[:, :], rhs=rkb[:, 0:TB], start=True, stop=True), reads=["HT1", "blb"], writes=["PF3"])
            P.dve(lambda e, pr=pr: e.tensor_scalar(out=xk[:, 0:TB], in0=xk[:, 0:TB], scalar1=col(KK0 + pr), scalar2=None, op0=ALU.mult), reads=["FT8", "pcol"], writes=["FT8"])
            sqb = HT[0]
            P.act(lambda e: e.activation(out=sqb[:, 0:TB], in_=xk[:, 0:TB], func=AF.Square), reads=["FT8"], writes=["HT0"])
            P.pe(lambda e: e.matmul(PF[2][:, 0:TB], lhsT=blb[:, :], rhs=sqb[:, 0:TB], start=True, stop=True), reads=["HT0", "blb"], writes=["PF2"])
            rsqrt_act(xr[:, 0:TB], PF[2][:, 0:TB], 1.0, 1e-12, ["PF2"], ["FT7"])
            P.dve(lambda e: e.tensor_tensor(out=xk[:, 0:TB], in0=xk[:, 0:TB], in1=xr[:, 0:TB], op=ALU.mult), reads=["FT8", "FT7"], writes=["FT8"])
            P.dve(lambda e, pr=pr: e.scalar_tensor_tensor(out=ar_dst(pr, 0), in0=t3(xk[:, 0:TB]), scalar=-1.0, in1=t3(Pm1[:, 0:TB]), op0=ALU.mult, op1=ALU.mult), reads=["FT8", "FT6"], writes=["BAR"])
            P.pool(lambda e: e.tensor_tensor(out=xk[:, 0:TB], in0=xk[:, 0:TB], in1=aic[:, 0:TB], op=ALU.mult), reads=["FT8", "FT2"], writes=["FT8"])
            for hh in range(2):
                hs = slice(64 * hh, 64 * hh + 64)
                P.dve(lambda e, pr=pr, hh=hh, hs=hs: e.tensor_tensor(out=bk_dst(hh, pr, 0), in0=t3(xk[hs, 0:TB]), in1=t3(Pinv[hs, 0:TB]), op=ALU.mult), reads=["FT8", "FT5"], writes=["BLK%d" % (3 + hh)])
                P.dve(lambda e, pr=pr, hh=hh, hs=hs: e.tensor_tensor(out=bk_dst(hh, pr, 1), in0=t3(kmod[hs, 0:TB]), in1=t3(Pinv[hs, 0:TB]), op=ALU.mult), reads=["FT9", "FT5"], writes=["BLK%d" % (3 + hh)])
            b_chunk(8 + pr, xr, "FT7")
            P.act(lambda e, pr=pr: e.activation(out=VBT[:, pr * TB:(pr + 1) * TB], in_=xr[:, 0:TB], func=AF.Copy), reads=["FT7"], writes=["BVB"])
            P.dve(lambda e, pr=pr: e.tensor_tensor(out=bon[:, pr * TB:(pr + 1) * TB], in0=xr[:, 0:TB], in1=PF[3][:, 0:TB], op=ALU.mult), reads=["FT7", "PF3"], writes=["bon"])
            b_chunk(12 + pr, xk, "FT8")
            P.act(lambda e, pr=pr: e.activation(out=SGB[:, pr * TB:(pr + 1) * TB], in_=xk[:, 0:TB], func=AF.Silu), reads=["FT8"], writes=["BSG"])
        if last_blk:
            P.pe(lambda e: e.transpose(out=PF[0][0:17, 0:128], in_=bcar[:, :], identity=IDF), reads=["bcar", "cf"], writes=["PF0"])
            P.dve(lambda e: e.tensor_copy(out=FT[0][0:17, 0:128], in_=PF[0][0:17, 0:128]), reads=["PF0"], writes=["FT0"])
            P.dma(p_bshift[:, :], FT[0][0:17, 0:128], reads=["FT0"], writes=["p_bshift"])

        P.capture_begin()
        for tb in range(NTB):
            aTM, bTM, kTM, vTM = HT[0], HT[1], HT[26], HT[27]
            bonTM, bonk = (HT[28], "HT28") if tb % 2 == 0 else (HT[38], "HT38")
            sgTM, sgk = (HT[29], "HT29") if tb % 2 == 0 else (HT[39], "HT39")
            def emit_tm(tb=tb, aTM=aTM, bTM=bTM, kTM=kTM, vTM=vTM, bonTM=bonTM, bonk=bonk, sgTM=sgTM, sgk=sgk):
                for s_, dstt, dkey in ((0, bTM, "HT1"), (1, kTM, "HT26")):
                    for pr in range(4):
                        P.pe(lambda e, pr=pr, s_=s_: e.matmul(PF[2][:, pr * 128:(pr + 1) * 128], lhsT=bkh(0, pr, tb, s_), rhs=idb[:, :], start=True, stop=False), reads=["BLK3", "idb"], writes=["PF2"])
                        P.pe(lambda e, pr=pr, s_=s_: e.matmul(PF[2][:, pr * 128:(pr + 1) * 128], lhsT=bkh(1, pr, tb, s_), rhs=idb[:, :], start=False, stop=True), reads=["BLK4", "idb"], writes=["PF2"])
                    P.copy(dstt[:, :], PF[2][:, :], reads=["PF2"], writes=[dkey])
                srcs = [(lambda pr: ar(pr, tb, 0), "BAR", aTM, "HT0"), (lambda pr: VBT[:, pr * TB + tb * 128:pr * TB + (tb + 1) * 128], "BVB", vTM, "HT27"),
                        (lambda pr: bon[:, pr * TB + tb * 128:pr * TB + (tb + 1) * 128], "bon", bonTM, bonk),
                        (lambda pr: SGB[:, pr * TB + tb * 128:pr * TB + (tb + 1) * 128], "BSG", sgTM, sgk)]
                for si, (srcf, skey, dstt, dkey) in enumerate(srcs):
                    half = (si % 2) * 512
                    for pr in range(4):
                        P.pe(lambda e, pr=pr, srcf=srcf, half=half: e.transpose(out=PB[1][:, half + pr * 128:half + (pr + 1) * 128], in_=srcf(pr), identity=idb[:, :]), reads=[skey, "idb"], writes=["PB1"])
                    if True:
                        P.dve(lambda e, dstt=dstt, half=half: e.tensor_copy(out=dstt[:, :], in_=PB[1][:, half:half + 512]), reads=["PB1"], writes=[dkey])
            WmT, U0b, yb = HT[30], FT[0], FT[1]
            Yab = [HT[14], HT[15]]
            Yak = [HT[16], HT[17]]
            Aak = [HT[18], HT[19]]
            AV = HT[20]
            Ub = HT[21]
            for hb in range(2):
                for hl in range(4):
                    h = hb * 4 + hl
                    pr, p0 = h // 2, 64 * (h % 2)
                    P.pe(lambda e, hl=hl, pr=pr, h=h: e.matmul(PF[2 + hl // 2][:, (hl % 2) * 256:(hl % 2 + 1) * 256], lhsT=bkh(h % 2, pr, tb, 0), rhs=AR[:, pr * 2 * TB + tb * 256: pr * 2 * TB + (tb + 1) * 256], start=True, stop=True), reads=["BAR", "BLK3", "BLK4"], writes=["PF%d" % (2 + hl // 2)])
                pd2 = lambda i: PF[2 + i][:, :].rearrange("p (h s n) -> p h s n", h=2, s=2)
                h2 = lambda t_, i: t_[:, i * 256:(i + 1) * 256].rearrange("p (h n) -> p h n", h=2)
                m2 = lambda m_: m_.unsqueeze(1).to_broadcast([128, 2, 128])
                X0 = FT[3]
                for i in range(2):
                    P.dve(lambda e, i=i: e.tensor_tensor(out=h2(X0, i), in0=pd2(i)[:, :, 0, :], in1=m2(NSTRICT), op=ALU.mult), reads=["PF%d" % (2 + i), "cf"], writes=["FT3"])
                    P.dve(lambda e, hb=hb, i=i: e.tensor_tensor(out=h2(Yab[hb], i), in0=pd2(i)[:, :, 1, :], in1=m2(INCL), op=ALU.mult), reads=["PF%d" % (2 + i), "cf"], writes=["HT%d" % (14 + hb)])
                for hl in range(4):
                    h = hb * 4 + hl
                    pr, p0 = h // 2, 64 * (h % 2)
                    P.pe(lambda e, hl=hl, pr=pr, h=h: e.matmul(PF[2 + hl // 2][:, (hl % 2) * 256:(hl % 2 + 1) * 256], lhsT=bkh(h % 2, pr, tb, 1), rhs=AR[:, pr * 2 * TB + tb * 256: pr * 2 * TB + (tb + 1) * 256], start=True, stop=True), reads=["BAR", "BLK3", "BLK4"], writes=["PF%d" % (2 + hl // 2)])
                for i in range(2):
                    P.dve(lambda e, hb=hb, i=i: e.tensor_tensor(out=h2(Aak[hb], i), in0=pd2(i)[:, :, 0, :], in1=m2(STRICT), op=ALU.mult), reads=["PF%d" % (2 + i), "cf"], writes=["HT%d" % (18 + hb)])
                    P.dve(lambda e, hb=hb, i=i: e.tensor_tensor(out=h2(Yak[hb], i), in0=pd2(i)[:, :, 1, :], in1=m2(INCL), op=ALU.mult), reads=["PF%d" % (2 + i), "cf"], writes=["HT%d" % (16 + hb)])
                resB = dict(X=([HT[32], HT[33]], ["HT32", "HT33"]), Q=([HT[34], HT[35]], ["HT34", "HT35"]), N=([HT[36], HT[37]], ["HT36", "HT37"]),
                            SQ=(PQ[1], "PQ1"), G0=(PF[2], "PF2"), G1=(PF[3], "PF3"), T=(PB[1], "PB1"))
                TinvT, tk = inverse_chain(P, X0, "FT3", resB, IDF, idb)
                if hb == 0:
                    emit_tm()
                for hl in range(4):
                    h = hb * 4 + hl
                    pr, p0 = h // 2, 64 * (h % 2)
                    P.pe(lambda e, hl=hl, pr=pr: e.matmul(PF[2][:, hl * 128:(hl + 1) * 128], lhsT=aTM[:, pr * 128:(pr + 1) * 128], rhs=TinvT[:, hl * 128:(hl + 1) * 128], start=True, stop=True), reads=["HT0", tk], writes=["PF2"])
                    P.pe(lambda e, hl=hl, h=h, hb=hb: e.matmul(PF[3][:, hl * 64:(hl + 1) * 64], lhsT=Aak[hb][:, hl * 128:(hl + 1) * 128], rhs=vTM[:, h * 64:(h + 1) * 64], start=True, stop=True), reads=["HT%d" % (18 + hb), "HT27"], writes=["PF3"])
                for hh in range(2):
                    p0 = 64 * hh
                    src = PF[2][p0:p0 + 64, :].rearrange("p (a b n) -> p a b n", a=2, b=2)[:, :, hh, :]
                    dst = WmT[p0:p0 + 64, hb * 256:(hb + 1) * 256].rearrange("p (a n) -> p a n", a=2)
                    P.copy(dst, src, reads=["PF2"], writes=["HT30"])
                P.copy(AV[:, 0:256], PF[3][:, 0:256], reads=["PF3"], writes=["HT20"])
                for hl in range(4):
                    P.pe(lambda e, hl=hl: e.matmul(PF[3][:, 256 + hl * 64:256 + (hl + 1) * 64], lhsT=TinvT[:, hl * 128:(hl + 1) * 128], rhs=AV[:, hl * 64:(hl + 1) * 64], start=True, stop=True), reads=[tk, "HT20"], writes=["PF3"])
                P.copy(U0b[:, hb * 256:(hb + 1) * 256], PF[3][:, 256:512], reads=["PF3"], writes=["FT0"])
            for c in range(2):
                r0, r1 = c * 64, c * 64 + 64
                ci = tb * 2 + c
                hsl = lambda h: slice((h // 2) * 128 + (h % 2) * 64, (h // 2) * 128 + (h % 2) * 64 + 64)
                for h in range(8):
                    pr, p0 = h // 2, 64 * (h % 2)
                    P.pe(lambda e, h=h, pr=pr, p0=p0: e.matmul(PF[2][:, h * 64:(h + 1) * 64], lhsT=WmT[:, pr * 128:(pr + 1) * 128], rhs=Hbb[:, hsl(h)], start=True, stop=True), reads=["HT30", "Hbb"], writes=["PF2"])
                P.dve(lambda e, r0=r0, r1=r1: e.tensor_tensor(out=Ub[r0:r1, :], in0=U0b[r0:r1, :], in1=PF[2][r0:r1, :], op=ALU.add), reads=["FT0", "PF2"], writes=["HT21"])
                for h in range(8):
                    pr, p0, hb, hl = h // 2, 64 * (h % 2), h // 4, h % 4
                    P.pe(lambda e, h=h, pr=pr, p0=p0: e.matmul(PF[3][:, h * 64:(h + 1) * 64], lhsT=ar(pr, tb, 1), rhs=Hbb[:, hsl(h)], start=True, stop=False), reads=["BAR", "Hbb"], writes=["PF3"])
                    P.pe(lambda e, h=h, hb=hb, hl=hl, r0=r0, r1=r1: e.matmul(PF[3][:, h * 64:(h + 1) * 64], lhsT=Yab[hb][r0:r1, hl * 128:(hl + 1) * 128], rhs=Ub[r0:r1, h * 64:(h + 1) * 64], start=False, stop=False), reads=["HT%d" % (14 + hb), "HT21"], writes=["PF3"])
                    P.pe(lambda e, h=h, hb=hb, hl=hl, r0=r0, r1=r1: e.matmul(PF[3][:, h * 64:(h + 1) * 64], lhsT=Yak[hb][r0:r1, hl * 128:(hl + 1) * 128], rhs=vTM[r0:r1, h * 64:(h + 1) * 64], start=False, stop=True), reads=["HT%d" % (16 + hb), "HT27"], writes=["PF3"])
                P.copy(yb[r0:r1, :], PF[3][r0:r1, :], reads=["PF3"], writes=["FT1"])
                for pr in range(4):
                    P.pe(lambda e, pr=pr, r0=r0, r1=r1: e.matmul(PQ[1][:, pr * 128:(pr + 1) * 128], lhsT=bTM[r0:r1, pr * 128:(pr + 1) * 128], rhs=Ub[r0:r1, pr * 128:(pr + 1) * 128], start=True, stop=False), reads=["HT1", "HT21"], writes=["PQ1"])
                    P.pe(lambda e, pr=pr, r0=r0, r1=r1: e.matmul(PQ[1][:, pr * 128:(pr + 1) * 128], lhsT=kTM[r0:r1, pr * 128:(pr + 1) * 128], rhs=vTM[r0:r1, pr * 128:(pr + 1) * 128], start=False, stop=True), reads=["HT26", "HT27"], writes=["PQ1"])
                P.dve(lambda e: e.tensor_tensor(out=Hb[:, :], in0=Hb[:, :], in1=PQ[1][:, 0:512], op=ALU.add), reads=["Hb", "PQ1"], writes=["Hb"])
                pc3 = pce[:, :].rearrange("p (a c) -> p a c", a=4)[:, :, ci]
                P.dve(lambda e, pc3=pc3: e.tensor_tensor(out=b3(Hb[:, :], 4, 128), in0=b3(Hb[:, :], 4, 128), in1=pc3.unsqueeze(2).to_broadcast([128, 4, 128]), op=ALU.mult), reads=["Hb", "pce"], writes=["Hb"])
                P.pool(lambda e: e.tensor_tensor(out=b3(Hbb[:, :], 4, 128), in0=b3(Hb[:, :], 4, 128), in1=BL.unsqueeze(1).to_broadcast([128, 4, 128]), op=ALU.mult), reads=["Hb", "cf"], writes=["Hbb"])
            y3 = b3(yb[:, :], 8, 64)
            stt = FT[2]
            P.dve(lambda e: e.tensor_reduce(out=stt[:, 0:8], in_=y3, axis=AX.X, op=ALU.add), reads=["FT1"], writes=["FT2"])
            P.dve(lambda e: e.tensor_scalar(out=stt[:, 0:8], in0=stt[:, 0:8], scalar1=1.0 / 64, scalar2=None, op0=ALU.mult), reads=["FT2"], writes=["FT2"])
            P.dve(lambda e: e.tensor_tensor(out=y3, in0=y3, in1=stt[:, 0:8].unsqueeze(2).to_broadcast([128, 8, 64]), op=ALU.subtract), reads=["FT1", "FT2"], writes=["FT1"])
            sq2 = FT[0]
            P.pool(lambda e: e.tensor_tensor(out=sq2[:, :], in0=yb[:, :], in1=yb[:, :], op=ALU.mult), reads=["FT1"], writes=["FT0"])
            P.dve(lambda e: e.tensor_reduce(out=stt[:, 8:16], in_=b3(sq2[:, :], 8, 64), axis=AX.X, op=ALU.add), reads=["FT0"], writes=["FT2"])
            rsqrt_act(stt[:, 8:16], stt[:, 8:16], 1.0 / 64, 64e-5, ["FT2"], ["FT2"])
            P.dve(lambda e: e.tensor_tensor(out=y3, in0=y3, in1=stt[:, 8:16].unsqueeze(2).to_broadcast([128, 8, 64]), op=ALU.mult), reads=["FT1", "FT2"], writes=["FT1"])
            P.pool(lambda e: e.tensor_tensor(out=yb[:, :], in0=yb[:, :], in1=LNW, op=ALU.mult), reads=["FT1", "rp"], writes=["FT1"])
            P.pool(lambda e: e.tensor_tensor(out=yb[:, :], in0=yb[:, :], in1=LNB, op=ALU.add), reads=["FT1", "rp"], writes=["FT1"])
            P.dve(lambda e: e.tensor_tensor(out=yb[:, :], in0=yb[:, :], in1=bonTM[:, :], op=ALU.add), reads=["FT1", bonk], writes=["FT1"])
            P.dve(lambda e: e.tensor_tensor(out=mixB4[:, tb * 512:(tb + 1) * 512], in0=yb[:, :], in1=sgTM[:, :], op=ALU.mult), reads=["FT1", sgk], writes=["mixB4_%d" % tb])
        for tb in range(NTB):
            for c8 in range(8):
                if c8 < 4:
                    P.pe(lambda e, c8=c8: e.transpose(out=PB[1][:, c8 * 128:(c8 + 1) * 128], in_=mixA[:, tb * 512 + c8 * 128:tb * 512 + (c8 + 1) * 128], identity=idb[:, :]), reads=[mxk, "idb"], writes=["PB1"])
                else:
                    P.pe(lambda e, c8=c8: e.transpose(out=PB[1][:, c8 * 128:(c8 + 1) * 128], in_=mixB4[:, tb * 512 + (c8 - 4) * 128:tb * 512 + (c8 - 3) * 128], identity=idb[:, :]), reads=["mixB4_%d" % tb, "idb"], writes=["PB1"])
            P.dve(lambda e: e.tensor_copy(out=mixT[:, :], in_=PB[1][:, :]), reads=["PB1"], writes=["mixT"])
            hx = xt[tb % 2]
            hk = "xt%d" % (tb % 2)
            P.dma(hx[:, :], xp[t0 + tb * 128:t0 + (tb + 1) * 128, :], writes=[hk])
            for n in range(2):
                for kc in range(8):
                    P.pe(lambda e, n=n, kc=kc: e.matmul(PQ[1][:, :], lhsT=mixT[:, kc * 128:(kc + 1) * 128], rhs=woutb[:, kc * 1024 + n * 512:kc * 1024 + (n + 1) * 512], start=(kc == 0), stop=(kc == 7)), reads=["mixT", "woutb"], writes=["PQ1"])
                P.dve(lambda e, n=n, hx=hx: e.tensor_tensor(out=hx[:, n * 512:(n + 1) * 512], in0=hx[:, n * 512:(n + 1) * 512], in1=PQ[1][:, :], op=ALU.add), reads=[hk, "PQ1"], writes=[hk])
            P.act(lambda e, hx=hx: e.activation(out=xs_[:, :], in_=hx[:, :], func=AF.Square, accum_out=st4[:, 4:5]), reads=[hk], writes=["xs_", "st4"])
            rsqrt_act(st4[:, 4:5], st4[:, 4:5], 1.0 / D, 1e-6, ["st4"], ["st4"])
            P.dve(lambda e, hx=hx: e.scalar_tensor_tensor(out=xs_[:, :], in0=hx[:, :], scalar=st4[:, 4:5], in1=FNW, op0=ALU.mult, op1=ALU.mult), reads=[hk, "st4", "rp"], writes=["xs_"])
            P.dma(yp[t0 + tb * 128:t0 + (tb + 1) * 128, :], xs_[:, :], reads=["xs_"], writes=["yp%d_%d" % (blk, tb)])
        if last_blk:
            for pr in range(4):
                P.pe(lambda e, pr=pr: e.transpose(out=PF[2][:, pr * 128:(pr + 1) * 128], in_=Hb[:, pr * 128:(pr + 1) * 128], identity=IDF), reads=["Hb", "cf"], writes=["PF2"])
            P.dve(lambda e: e.tensor_copy(out=FT[4][:, :], in_=PF[2][:, :]), reads=["PF2"], writes=["FT4"])
            for h in range(8):
                pr, p0 = h // 2, 64 * (h % 2)
                P.dma(p_bmat[h], FT[4][p0:p0 + 64, pr * 128 + p0:pr * 128 + p0 + 64], reads=["FT4"], writes=["p_bmat%d" % h])
        stream_b = P.capture_end()
        if last_blk:
            P.replay([stream_b])

    if DECODE_LAST[0]:
        P.pool(lambda e: e.memset(st4[:, 7:8], 0.0), reads=["BAR", "BVB", "BSG"], writes=["SF%d" % i for i in range(8)] + ["st4"])
        sample_path(SAMPLE_LOCALS)
    P.finish()
    return nc


def inverse_chain(P, X0, x0key, res, IDF, idb):
    Xt, Xk = res["X"]
    Qt, Qk = res["Q"]
    Nt, Nk = res["N"]
    (SQ, sqk), (G0, g0k), (G1, g1k), (TT, ttk) = res["SQ"], res["G0"], res["G1"], res["T"]

    def mm4(out_ps, okey, lhs, lkey, rhs, rkey):
        for h in range(4):
            P.pe(lambda e, h=h: e.matmul(out_ps[:, h * 128:(h + 1) * 128], lhsT=lhs[:, h * 128:(h + 1) * 128], rhs=rhs[:, h * 128:(h + 1) * 128], start=True, stop=True), reads=[lkey, rkey], writes=[okey])

    P.copy(Xt[0][:, :], X0[:, :], reads=[x0key], writes=[Xk[0]])
    P.dve(lambda e: e.tensor_tensor(out=Qt[0][:, :].rearrange("p (h n) -> p h n", h=4), in0=IDF.unsqueeze(1).to_broadcast([128, 4, 128]), in1=X0[:, :].rearrange("p (h n) -> p h n", h=4), op=ALU.subtract), reads=["cf", x0key], writes=[Qk[0]])
    for h in range(4):
        P.pe(lambda e, h=h: e.transpose(out=TT[:, h * 128:(h + 1) * 128], in_=Xt[0][:, h * 128:(h + 1) * 128], identity=idb[:, :]), reads=[Xk[0], "idb"], writes=[ttk])
    P.dve(lambda e: e.tensor_copy(out=Nt[0][:, :], in_=TT[:, 0:512]), reads=[ttk], writes=[Nk[0]])
    mm4(SQ, sqk, Nt[0], Nk[0], Xt[0], Xk[0])
    mm4(G0, g0k, Xt[0], Xk[0], Nt[0], Nk[0])
    P.copy(Xt[1][:, :], SQ[:, 0:512], reads=[sqk], writes=[Xk[1]])
    P.copy(Nt[1][:, :], G0[:, 0:512], reads=[g0k], writes=[Nk[1]])
    xi, ni, qi = 1, 1, 0
    for m in range(5):
        mm4(G1, g1k, Nt[ni], Nk[ni], Qt[qi], Qk[qi])
        P.dve(lambda e, qi=qi: e.tensor_tensor(out=Qt[1 - qi][:, :], in0=Qt[qi][:, :], in1=G1[:, 0:512], op=ALU.add), reads=[Qk[qi], g1k], writes=[Qk[1 - qi]])
        qi = 1 - qi
        if m < 4:
            mm4(SQ, sqk, Nt[ni], Nk[ni], Xt[xi], Xk[xi])
            mm4(G0, g0k, Xt[xi], Xk[xi], Nt[ni], Nk[ni])
            P.copy(Xt[1 - xi][:, :], SQ[:, 0:512], reads=[sqk], writes=[Xk[1 - xi]])
            P.copy(Nt[1 - ni][:, :], G0[:, 0:512], reads=[g0k], writes=[Nk[1 - ni]])
            xi, ni = 1 - xi, 1 - ni
    return Qt[qi], Qk[qi]


_PQ = {}


def PBt_f32(PD, FT):
    return _PQ["t"]


def group_b_block(L):
    pass


def pack_inputs(inp):
    g = lambda k: np.asarray(inp[k], np.float32)
    vrows = np.zeros((128, 128), np.float32)
    vrows[0:8] = g("norm_w")[0].reshape(8, 128)
    cw = g("conv_w")[0]
    for c in range(12):
        for i in range(4):
            vrows[8 + c * 4 + i] = cw[i, c * 128:(c + 1) * 128]
    vrows[56:73] = g("mu")[0].reshape(17, 128)
    vrows[73:77] = g("w0")[0].reshape(4, 128)
    vrows[77:81] = g("a0")[0].reshape(4, 128)
    vrows[81:85] = g("k_k")[0].reshape(4, 128)
    vrows[85:89] = g("k_a")[0].reshape(4, 128)
    vrows[89:93] = g("r_k")[0].reshape(4, 128)
    rowp = np.concatenate([g("ln_w")[0], g("ln_b")[0], np.zeros(1024, np.float32), g("a_norm_w")[0],
                           g("final_norm_w"), g("a_log")[0], g("dt_bias")[0]]).astype(np.float32)
    p = np.arange(512)
    resetm = np.broadcast_to((p % 64 != 0).astype(np.float32), (128, 512)).copy()
    return dict(w_in=g("w_in")[0], w_out=g("w_out")[0], vrows=vrows, rowp=rowp, w2=g("w2")[0], a2=g("a2")[0],
                consts=host_consts(), resetm=resetm)


def kernel(**inputs):
    n = 8
    packed = pack_inputs(inputs)
    xp = np.ascontiguousarray(np.asarray(inputs["x_prompt"], np.float32))
    nc = build(T=2048, NS=16, TB=512)
    in_maps = []
    for i in range(n):
        m = dict(packed)
        m["xp"] = xp[i]
        sl = slice(i * 16, (i + 1) * 16)
        m["xs"] = np.ascontiguousarray(np.asarray(inputs["x_sample"], np.float32)[sl, 0])
        m["sa_mat"] = np.ascontiguousarray(np.asarray(inputs["state_a_mat"], np.float32)[0, sl])
        m["sa_conv"] = np.ascontiguousarray(np.asarray(inputs["state_a_conv"], np.float32)[0, sl])
        m["sb_mat"] = np.ascontiguousarray(np.asarray(inputs["state_b_mat"], np.float32)[0, sl])
        m["sb_shift"] = np.ascontiguousarray(np.asarray(inputs["state_b_shift"], np.float32)[0, sl])
        in_maps.append(m)
    res = run_bass_kernel_spmd(nc, in_maps, core_ids=list(range(n)))
    r = res.results
    f = lambda k: [np.asarray(r[i][k], np.float32) for i in range(n)]
    y_prompt = np.stack(f("yp"))
    p_amat = np.stack(f("p_amat"))[None]
    p_aconv = np.stack([a.reshape(3, 12, 128).reshape(3, 1536) for a in f("p_aconv")])[None]
    p_bmat = np.stack(f("p_bmat"))[None]
    p_bshift = np.stack([a.reshape(2176) for a in f("p_bshift")])[None]
    y_sample = np.concatenate(f("ys"))[:, None, :]
    s_amat = np.concatenate(f("s_amat"))[None]
    s_aconv = np.concatenate(f("s_aconv"))[None]
    s_bmat = np.concatenate(f("s_bmat"))[None]
    s_bshift = np.concatenate(f("s_bshift"))[None]
    return (y_prompt, y_sample, p_amat, p_aconv, p_bmat, p_bshift, s_amat, s_aconv, s_bmat, s_bshift)


def sample_path(L):
    P, NS = L["P"], L["NS"]
    PF, PQ, PB = L["PF"], L["PQ"], L["PB"]
    FT, HT, SF, SPJ, xnTs = L["FT"], L["HT"], L["SF"], L["SPJ"], L["xnTs"]
    xt, xs_, st4, pcol, col = L["xt"], L["xs_"], L["st4"], L["pcol"], L["col"]
    IDF, ONESF, BL, PAIRS, idb = L["IDF"], L["ONESF"], L["BL"], L["PAIRS"], L["idb"]
    wbf, w_in, woutb, w2a2 = L["wbf"], L["w_in"], L["woutb"], L["w2a2"]
    rsqrt_act, b3 = L["rsqrt_act"], L["b3"]
    LNW, LNB, ANW, FNW, DTB, nega = L["LNW"], L["LNB"], L["ANW"], L["FNW"], L["DTB"], L["nega"]
    NW0, CW0, MU0, W00, A00, KK0, KA0, RK0 = 0, 8, 56, 73, 77, 81, 85, 89
    xs_d, sa_mat_d, sa_conv_d, sb_mat_d, sb_shift_d = L["xs_d"], L["sa_mat_d"], L["sa_conv_d"], L["sb_mat_d"], L["sb_shift_d"]
    ys, s_amat, s_aconv, s_bmat, s_bshift = L["ys"], L["s_amat"], L["s_aconv"], L["s_bmat"], L["s_bshift"]
    scr_q, scr_k, scr_v, scr_ab, scr_o, scr_b6, scr_y = L["scr_q"], L["scr_k"], L["scr_v"], L["scr_ab"], L["scr_o"], L["scr_b6"], L["scr_y"]
    N = NS
    R = slice(0, N)

    xa = xt[0]
    P.dma(xa[R, :], xs_d[:, :], writes=["xt0"])
    P.act(lambda e: e.activation(out=xs_[R, :], in_=xa[R, :], func=AF.Square, accum_out=st4[R, 0:1]), reads=["xt0"], writes=["xs_", "st4"])
    rsqrt_act(st4[R, 0:1], st4[R, 0:1], 1.0 / D, 1e-6, ["st4"], ["st4"])
    P.act(lambda e: e.activation(out=xs_[R, :], in_=xa[R, :], func=AF.Copy, scale=st4[R, 0:1]), reads=["xt0", "st4"], writes=["xs_"])
    for half in range(2):
        pf, pk = PF[half], "PF%d" % half
        for q in range(4):
            kc = half * 4 + q
            P.pe(lambda e, pf=pf, q=q, kc=kc: e.transpose(out=pf[:, q * N:(q + 1) * N], in_=xs_[R, kc * 128:(kc + 1) * 128], identity=IDF[R, R]), reads=["xs_", "cf"], writes=[pk])
        P.dve(lambda e, pf=pf, half=half: e.tensor_tensor(out=xnTs[:, half * 4 * N:(half + 1) * 4 * N].rearrange("p (k t) -> p k t", k=4), in0=pf[:, 0:4 * N].rearrange("p (k t) -> p k t", k=4), in1=pcol[:, NW0 + half * 4:NW0 + half * 4 + 4].unsqueeze(2).to_broadcast([128, 4, N]), op=ALU.mult), reads=[pk, "pcol"], writes=["xnTs"])

    chunks = [(c * 128, 128) for c in range(16)] + [(2048, 8)] + [(2056 + j * 128, 128) for j in range(17)]
    for idx, (c0, ncols) in enumerate(chunks):
        s = idx % len(wbf)
        bk_ = "wbf%d" % s
        pf, pk = PF[idx % 2], "PF%d" % (idx % 2)
        P.dma(wbf[s][:, 0:8 * ncols].rearrange("p (k n) -> p k n", k=8), w_in[:, c0:c0 + ncols].rearrange("(k p) n -> p k n", p=128), writes=[bk_], q="pool")
        for kc in range(8):
            P.pe(lambda e, kc=kc, s=s, pf=pf, ncols=ncols: e.matmul(pf[0:ncols, 0:N], lhsT=wbf[s][:, kc * ncols:(kc + 1) * ncols], rhs=xnTs[:, kc * N:(kc + 1) * N], start=(kc == 0), stop=(kc == 7)), reads=[bk_, "xnTs"], writes=[pk])
        P.act(lambda e, pf=pf, ncols=ncols, idx=idx: e.activation(out=SPJ[0:ncols, idx * N:(idx + 1) * N], in_=pf[0:ncols, 0:N], func=AF.Copy), reads=[pk], writes=["SPJ"])
    sp = lambda i, j=None: SPJ[:, i * N:((i + 1) if j is None else j) * N]

    def to_tm(srcs, skey, dst, dkey, ps, pskey):
        for i, src in enumerate(srcs):
            P.pe(lambda e, i=i, src=src: e.transpose(out=ps[R, i * 128:(i + 1) * 128], in_=src, identity=IDF), reads=[skey, "cf"], writes=[pskey])
        n = len(srcs) * 128
        P.dve(lambda e: e.tensor_copy(out=dst[R, 0:n], in_=ps[R, 0:n]), reads=[pskey], writes=[dkey])

    cv = sa_conv_d.rearrange("b i c -> (b i) c")
    P.dma(xt[1][0:3 * N, 0:1024], cv[:, 0:1024], writes=["xt1"])
    P.dma(FT[0][0:3 * N, 0:512], cv[:, 1024:1536], writes=["FT0"])
    for c in range(12):
        src = xt[1][0:3 * N, c * 128:(c + 1) * 128] if c < 8 else FT[0][0:3 * N, (c - 8) * 128:(c - 7) * 128]
        po = c * 3 * N if c < 10 else 512 + (c - 10) * 3 * N
        pq_ = PQ[0] if c < 10 else PQ[1]
        po = po % 512
        P.pe(lambda e, c=c, src=src, po=po, pq_=pq_: e.transpose(out=pq_[:, po:po + 3 * N], in_=src, identity=IDF[0:3 * N, 0:3 * N]), reads=["xt1", "FT0", "cf"], writes=["PQ0", "PQ1"])
    P.dve(lambda e: e.tensor_copy(out=xs_[:, 0:30 * N], in_=PQ[0][:, 0:30 * N]), reads=["PQ0"], writes=["xs_"])
    P.dve(lambda e: e.tensor_copy(out=xs_[:, 30 * N:36 * N], in_=PQ[1][:, 0:6 * N]), reads=["PQ1"], writes=["xs_"])
    acc = FT[2]
    for c in range(12):
        cs3 = xs_[:, c * 3 * N:(c + 1) * 3 * N].rearrange("p (b i) -> p b i", i=3)
        tmp3 = FT[1][:, 0:3 * N].rearrange("p (b i) -> p b i", i=3)
        P.dve(lambda e, cs3=cs3, tmp3=tmp3, c=c: e.tensor_tensor(out=tmp3, in0=cs3, in1=pcol[:, CW0 + 4 * c:CW0 + 4 * c + 3].unsqueeze(1).to_broadcast([128, N, 3]), op=ALU.mult), reads=["xs_", "pcol"], writes=["FT1"])
        P.dve(lambda e, tmp3=tmp3, c=c: e.tensor_reduce(out=acc[:, c * N:(c + 1) * N], in_=tmp3, axis=AX.X, op=ALU.add), reads=["FT1"], writes=["FT2"])
        P.dve(lambda e, c=c: e.scalar_tensor_tensor(out=acc[:, c * N:(c + 1) * N], in0=sp(c), scalar=col(CW0 + 4 * c + 3), in1=acc[:, c * N:(c + 1) * N], op0=ALU.mult, op1=ALU.add), reads=["SPJ", "pcol", "FT2"], writes=["FT2"])
    P.act(lambda e: e.activation(out=acc[:, 0:12 * N], in_=acc[:, 0:12 * N], func=AF.Silu), reads=["FT2"], writes=["FT2"])
    P.act(lambda e: e.activation(out=FT[3][:, 0:8 * N], in_=acc[:, 0:8 * N], func=AF.Square), reads=["FT2"], writes=["FT3"])
    P.pe(lambda e: e.matmul(PF[2][:, 0:8 * N], lhsT=ONESF, rhs=FT[3][:, 0:8 * N], start=True, stop=True), reads=["FT3", "cf"], writes=["PF2"])
    rsqrt_act(FT[3][:, 0:8 * N], PF[2][:, 0:8 * N], 1.0, 1e-12, ["PF2"], ["FT3"])
    P.dve(lambda e: e.scalar_tensor_tensor(out=acc[:, 0:4 * N], in0=acc[:, 0:4 * N], scalar=128 ** -0.5, in1=FT[3][:, 0:4 * N], op0=ALU.mult, op1=ALU.mult), reads=["FT2", "FT3"], writes=["FT2"])
    P.dve(lambda e: e.tensor_tensor(out=acc[:, 4 * N:8 * N], in0=acc[:, 4 * N:8 * N], in1=FT[3][:, 4 * N:8 * N], op=ALU.mult), reads=["FT2", "FT3"], writes=["FT2"])
    for g, (scr, sf, sname) in enumerate(((scr_q, 0, "scr_q"), (scr_k, 1, "scr_k"), (scr_v, 2, "scr_v"))):
        to_tm([acc[:, (g * 4 + i) * N:(g * 4 + i + 1) * N] for i in range(4)], "FT2", SF[sf], "SF%d" % sf, PF[3], "PF3")
        P.dma(scr[:, :], SF[sf][R, 0:512], reads=["SF%d" % sf], writes=[sname])
    P.dma(s_aconv[:, 0:2, :], sa_conv_d[:, 1:3, :], writes=["s_aconv"])
    for g in range(3):
        to_tm([sp(g * 4 + i) for i in range(4)], "SPJ", SF[3], "SF3", PF[3], "PF3")
        P.dma(s_aconv[:, 2, g * 512:(g + 1) * 512], SF[3][R, 0:512], reads=["SF3"], writes=["s_aconv2_%d" % g])
    sga = FT[3]
    P.act(lambda e: e.activation(out=sga[:, 0:4 * N], in_=sp(12, 16), func=AF.Silu), reads=["SPJ"], writes=["FT3"])
    sgaTM = SF[4]
    to_tm([sga[:, i * N:(i + 1) * N] for i in range(4)], "FT3", sgaTM, "SF4", PF[3], "PF3")
    sc = SF[5]
    P.pe(lambda e: e.transpose(out=PF[3][R, 0:8], in_=SPJ[0:8, 16 * N:17 * N], identity=IDF[0:8, 0:8]), reads=["SPJ", "cf"], writes=["PF3"])
    ab3 = sc[R, 16:24].rearrange("p (h s) -> p h s", s=2)
    P.act(lambda e: e.activation(out=ab3[:, :, 1], in_=PF[3][R, 0:4], func=AF.Sigmoid), reads=["PF3"], writes=["SF5"])
    P.dve(lambda e: e.tensor_tensor(out=sc[R, 4:8], in0=PF[3][R, 4:8], in1=DTB[R, :], op=ALU.add), reads=["PF3", "rp"], writes=["SF5"])
    P.act(lambda e: e.activation(out=sc[R, 4:8], in_=sc[R, 4:8], func=AF.Exp), reads=["SF5"], writes=["SF5"])
    P.act(lambda e: e.activation(out=sc[R, 4:8], in_=sc[R, 4:8], func=AF.Ln, bias=1.0), reads=["SF5"], writes=["SF5"])
    P.dve(lambda e: e.tensor_tensor(out=sc[R, 4:8], in0=sc[R, 4:8], in1=nega[R, :], op=ALU.mult), reads=["SF5", "nega"], writes=["SF5"])
    P.act(lambda e: e.activation(out=ab3[:, :, 0], in_=sc[R, 4:8], func=AF.Exp), reads=["SF5"], writes=["SF5"])
    P.dma(scr_ab[:, :], sc[R, 16:24], reads=["SF5"], writes=["scr_ab"])
    VP = FT[8]
    kP, qP, vP, abP = VP[:, 0:64], VP[:, 64:128], VP[:, 128:256], VP[:, 256:260]
    for hi in range(2):
        hs = slice(hi * 64, hi * 64 + 64)
        P.dma(VP[hs, 0:64], scr_k.rearrange("b (h t l) -> (b h) t l", h=4, t=2)[:, hi, :], reads=["scr_k"], writes=["FT8"])
        P.dma(VP[hs, 64:128], scr_q.rearrange("b (h t l) -> (b h) t l", h=4, t=2)[:, hi, :], reads=["scr_q"], writes=["FT8"])
        P.dma(VP[hs, 128:256], scr_v.rearrange("b (h v) -> (b h) v", h=4), reads=["scr_v"], writes=["FT8"])
        P.dma(VP[hs, 256:258], scr_ab.rearrange("b (h s) -> (b h) s", s=2), reads=["scr_ab"], writes=["FT8"])
    P.dve(lambda e: e.tensor_scalar(out=VP[:, 258:259], in0=VP[:, 256:257], scalar1=-1.0, scalar2=None, op0=ALU.mult), reads=["FT8"], writes=["FT8"])
    sav = sa_mat_d.rearrange("b h (t l) v -> (b h) t l v", t=2)
    sov = s_amat.rearrange("b h (t l) v -> (b h) t l v", t=2)
    SLs, prods, red, accs = [(FT[4], "FT4"), (FT[3], "FT3")], [(FT[5], "FT5"), (FT[6], "FT6")], FT[2], FT[7]
    kS, uu, oacc = accs[:, 0:128], accs[:, 128:256], accs[:, 256:384]
    sl3 = lambda t_: t_[:, :].rearrange("p (l v) -> p l v", l=4)
    P.pool(lambda e: e.memset(accs[:, :], 0.0), writes=["FT7"])
    NSL = 16
    for j in range(NSL):
        (SL, slk), (prod, pdk) = SLs[j % 2], prods[j % 2]
        for hi in range(2):
            P.dma(sl3(SL)[hi * 64:hi * 64 + 64], sav[:, hi, 4 * j:4 * j + 4, :], writes=[slk])
        P.dve(lambda e, j=j: e.tensor_tensor(out=sl3(prod), in0=sl3(SL), in1=kP[:, 4 * j:4 * j + 4].unsqueeze(2).to_broadcast([128, 4, 128]), op=ALU.mult), reads=[slk, "FT8"], writes=[pdk])
        P.pool(lambda e, prod=prod: e.tensor_tensor(out=prod[:, 0:256], in0=prod[:, 0:256], in1=prod[:, 256:512], op=ALU.add), reads=[pdk], writes=[pdk])
        P.pool(lambda e, prod=prod: e.tensor_tensor(out=prod[:, 0:128], in0=prod[:, 0:128], in1=prod[:, 128:256], op=ALU.add), reads=[pdk], writes=[pdk])
        P.dve(lambda e: e.tensor_tensor(out=kS, in0=kS, in1=prod[:, 0:128], op=ALU.add), reads=["FT7", pdk], writes=["FT7"])
    P.pe(lambda e: e.matmul(PF[2][:, 0:128], lhsT=PAIRS, rhs=kS, start=True, stop=True), reads=["FT7", "cf"], writes=["PF2"])
    P.dve(lambda e: e.scalar_tensor_tensor(out=uu, in0=PF[2][:, 0:128], scalar=VP[:, 258:259], in1=vP, op0=ALU.mult, op1=ALU.add), reads=["PF2", "FT8", "FT7"], writes=["FT7"])
    P.dve(lambda e: e.tensor_scalar(out=uu, in0=uu, scalar1=VP[:, 257:258], scalar2=None, op0=ALU.mult), reads=["FT7", "FT8"], writes=["FT7"])
    for j in range(NSL):
        (SL, slk), (prod, pdk) = SLs[j % 2], prods[j % 2]
        for hi in range(2):
            P.dma(sl3(SL)[hi * 64:hi * 64 + 64], sav[:, hi, 4 * j:4 * j + 4, :], writes=[slk])
        P.pool(lambda e, j=j: e.tensor_tensor(out=sl3(prod), in0=kP[:, 4 * j:4 * j + 4].unsqueeze(2).to_broadcast([128, 4, 128]), in1=uu.unsqueeze(1).to_broadcast([128, 4, 128]), op=ALU.mult), reads=["FT8", "FT7"], writes=[pdk])
        P.dve(lambda e: e.scalar_tensor_tensor(out=SL[:, :], in0=SL[:, :], scalar=VP[:, 256:257], in1=prod[:, :], op0=ALU.mult, op1=ALU.add), reads=[slk, pdk, "FT8"], writes=[slk])
        for hi in range(2):
            P.dma(sov[:, hi, 4 * j:4 * j + 4, :], sl3(SL)[hi * 64:hi * 64 + 64], reads=[slk], writes=["s_amat%d_%d" % (j, hi)])
        P.dve(lambda e, j=j: e.tensor_tensor(out=sl3(prod), in0=sl3(SL), in1=qP[:, 4 * j:4 * j + 4].unsqueeze(2).to_broadcast([128, 4, 128]), op=ALU.mult), reads=[slk, "FT8"], writes=[pdk])
        P.pool(lambda e, prod=prod: e.tensor_tensor(out=prod[:, 0:256], in0=prod[:, 0:256], in1=prod[:, 256:512], op=ALU.add), reads=[pdk], writes=[pdk])
        P.pool(lambda e, prod=prod: e.tensor_tensor(out=prod[:, 0:128], in0=prod[:, 0:128], in1=prod[:, 128:256], op=ALU.add), reads=[pdk], writes=[pdk])
        P.dve(lambda e: e.tensor_tensor(out=oacc, in0=oacc, in1=prod[:, 0:128], op=ALU.add), reads=["FT7", pdk], writes=["FT7"])
    P.pe(lambda e: e.matmul(PF[2][:, 0:128], lhsT=PAIRS, rhs=oacc, start=True, stop=True), reads=["FT7", "cf"], writes=["PF2"])
    P.dve(lambda e: e.tensor_copy(out=red[0:64, 0:128], in_=PF[2][0:64, 0:128]), reads=["PF2"], writes=["FT2"])
    P.dma(scr_o[:, :], red[0:64, 0:128], reads=["FT2"], writes=["scr_o"])
    osb = SF[6]
    P.dma(osb[R, 0:512], scr_o.rearrange("(b h) v -> b (h v)", h=4), reads=["scr_o"], writes=["SF6"])
    sq = SF[7]
    P.pool(lambda e: e.tensor_tensor(out=sq[R, :], in0=osb[R, :], in1=osb[R, :], op=ALU.mult), reads=["SF6"], writes=["SF7"])
    P.dve(lambda e: e.tensor_reduce(out=sc[R, 40:44], in_=b3(sq[R, :], 4, 128), axis=AX.X, op=ALU.add), reads=["SF7"], writes=["SF5"])
    rsqrt_act(sc[R, 40:44], sc[R, 40:44], 1.0 / 128, 1e-6, ["SF5"], ["SF5"])
    P.dve(lambda e: e.tensor_tensor(out=b3(osb[R, :], 4, 128), in0=b3(osb[R, :], 4, 128), in1=sc[R, 40:44].unsqueeze(2).to_broadcast([N, 4, 128]), op=ALU.mult), reads=["SF6", "SF5"], writes=["SF6"])
    P.dve(lambda e: e.tensor_tensor(out=b3(osb[R, :], 4, 128), in0=b3(osb[R, :], 4, 128), in1=ANW[R, :].unsqueeze(1).to_broadcast([N, 4, 128]), op=ALU.mult), reads=["SF6", "rp"], writes=["SF6"])
    mixs = HT[22]
    P.dve(lambda e: e.tensor_tensor(out=mixs[R, :], in0=osb[R, :], in1=sgaTM[R, :], op=ALU.mult), reads=["SF6", "SF4"], writes=["HT22"])

    pbT = lambda j, k=None: SPJ[:, (17 + j) * N:(17 + (j + 1 if k is None else k)) * N]
    for g in range(5):
        n = 4 if g < 4 else 1
        P.dma(SF[0][R, 0:n * 128], sb_shift_d[:, g * 512:g * 512 + n * 128], writes=["SF0"])
        for i in range(n):
            P.pe(lambda e, g=g, i=i: e.transpose(out=PF[2][:, (g * 4 + i) * N:(g * 4 + i + 1) * N], in_=SF[0][R, i * 128:(i + 1) * 128], identity=IDF[R, R]), reads=["SF0", "cf"], writes=["PF2"])
        to_tm([pbT(g * 4 + i) for i in range(n)], "SPJ", SF[1], "SF1", PF[3], "PF3")
        P.dma(s_bshift[:, g * 512:g * 512 + n * 128], SF[1][R, 0:n * 128], reads=["SF1"], writes=["s_bshift%d" % g])
    xb = FT[9]
    xbj = lambda j, k=None: xb[:, j * N:(j + 1 if k is None else k) * N]
    mu3 = pcol[:, MU0:MU0 + 17].unsqueeze(2).to_broadcast([128, 17, N])
    x3 = xb[:, 0:17 * N].rearrange("p (j t) -> p j t", j=17)
    pb3 = SPJ[:, 17 * N:34 * N].rearrange("p (j t) -> p j t", j=17)
    P.dve(lambda e: e.tensor_tensor(out=xb[:, 0:17 * N], in0=PF[2][:, 0:17 * N], in1=SPJ[:, 17 * N:34 * N], op=ALU.subtract), reads=["PF2", "SPJ"], writes=["FT9"])
    P.dve(lambda e: e.tensor_tensor(out=x3, in0=x3, in1=mu3, op=ALU.mult), reads=["FT9", "pcol"], writes=["FT9"])
    P.dve(lambda e: e.tensor_tensor(out=xb[:, 0:17 * N], in0=xb[:, 0:17 * N], in1=SPJ[:, 17 * N:34 * N], op=ALU.add), reads=["FT9", "SPJ"], writes=["FT9"])
    P.act(lambda e: e.activation(out=xb[0:64, 16 * N:17 * N], in_=xb[0:64, 16 * N:17 * N], func=AF.Tanh), reads=["FT9"], writes=["FT9"])
    W = FT[0]
    wq = lambda qi, pr=None: W[:, (qi * 4 + (0 if pr is None else pr)) * N:(qi * 4 + (4 if pr is None else pr + 1)) * N]
    SG, AIC, KK, KM, BB, BON, SGT, WD = 0, 1, 2, 3, 4, 5, 6, 7
    for pr in range(4):
        P.pe(lambda e, pr=pr: e.matmul(PF[2][:, pr * N:(pr + 1) * N], lhsT=w2a2[0:64, pr * 128:(pr + 1) * 128], rhs=xb[0:64, 16 * N:17 * N], start=True, stop=True), reads=["w2a2", "FT9"], writes=["PF2"])
        P.pe(lambda e, pr=pr: e.matmul(PF[2][:, (4 + pr) * N:(5 + pr) * N], lhsT=w2a2[64:128, pr * 128:(pr + 1) * 128], rhs=xb[64:128, 16 * N:17 * N], start=True, stop=True), reads=["w2a2", "FT9"], writes=["PF2"])
    for pr in range(4):
        P.act(lambda e, pr=pr: e.activation(out=wq(SG, pr), in_=PF[2][:, pr * N:(pr + 1) * N], func=AF.Sigmoid, bias=col(W00 + pr)), reads=["PF2", "pcol"], writes=["FT0"])
        P.act(lambda e, pr=pr: e.activation(out=wq(AIC, pr), in_=PF[2][:, (4 + pr) * N:(5 + pr) * N], func=AF.Sigmoid, bias=col(A00 + pr)), reads=["PF2", "pcol"], writes=["FT0"])
    P.act(lambda e: e.activation(out=wq(WD), in_=wq(SG), func=AF.Exp, scale=-C0), reads=["FT0"], writes=["FT0"])
    pc3 = lambda c0_: pcol[:, c0_:c0_ + 4].unsqueeze(2).to_broadcast([128, 4, N])
    q3 = lambda ap: ap.rearrange("p (a t) -> p a t", a=4)
    rT, kT, vT, gT = xbj(0, 4), xbj(4, 8), xbj(8, 12), xbj(12, 16)
    P.dve(lambda e: e.scalar_tensor_tensor(out=q3(wq(KM)), in0=q3(wq(AIC)), scalar=-1.0, in1=pc3(KA0), op0=ALU.add, op1=ALU.mult), reads=["FT0", "pcol"], writes=["FT0"])
    P.dve(lambda e: e.scalar_tensor_tensor(out=wq(KM), in0=wq(KM), scalar=1.0, in1=kT, op0=ALU.add, op1=ALU.mult), reads=["FT0", "FT9"], writes=["FT0"])
    P.dve(lambda e: e.tensor_tensor(out=q3(wq(KK)), in0=q3(kT), in1=pc3(KK0), op=ALU.mult), reads=["FT9", "pcol"], writes=["FT0"])
    P.act(lambda e: e.activation(out=wq(BB), in_=wq(KK), func=AF.Square), reads=["FT0"], writes=["FT0"])
    P.pe(lambda e: e.matmul(PF[3][:, 0:4 * N], lhsT=BL, rhs=wq(BB), start=True, stop=True), reads=["FT0", "cf"], writes=["PF3"])
    rsqrt_act(wq(BB), PF[3][:, 0:4 * N], 1.0, 1e-12, ["PF3"], ["FT0"])
    P.dve(lambda e: e.tensor_tensor(out=wq(KK), in0=wq(KK), in1=wq(BB), op=ALU.mult), reads=["FT0"], writes=["FT0"])
    P.dve(lambda e: e.tensor_tensor(out=wq(BB), in0=wq(KK), in1=wq(AIC), op=ALU.mult), reads=["FT0"], writes=["FT0"])
    P.dve(lambda e: e.tensor_tensor(out=q3(wq(BON)), in0=q3(rT), in1=pc3(RK0), op=ALU.mult), reads=["FT9", "pcol"], writes=["FT0"])
    P.dve(lambda e: e.tensor_tensor(out=wq(BON), in0=wq(BON), in1=wq(KM), op=ALU.mult), reads=["FT0"], writes=["FT0"])
    P.pe(lambda e: e.matmul(PF[3][:, 0:4 * N], lhsT=BL, rhs=wq(BON), start=True, stop=True), reads=["FT0", "cf"], writes=["PF3"])
    P.dve(lambda e: e.tensor_tensor(out=wq(BON), in0=PF[3][:, 0:4 * N], in1=vT, op=ALU.mult), reads=["PF3", "FT9"], writes=["FT0"])
    P.act(lambda e: e.activation(out=wq(SGT), in_=gT, func=AF.Silu), reads=["FT9"], writes=["FT0"])
    P.dve(lambda e: e.tensor_scalar(out=wq(KK), in0=wq(KK), scalar1=-1.0, scalar2=None, op0=ALU.mult), reads=["FT0"], writes=["FT0"])
    tmsrc = [(wq(WD), "FT0"), (wq(KK), "FT0"), (wq(BB), "FT0"), (wq(KM), "FT0"), (rT, "FT9"), (vT, "FT9")]
    for i, (ap_, key_) in enumerate(tmsrc):
        sf = i % 2
        to_tm([ap_[:, pr * N:(pr + 1) * N] for pr in range(4)], key_, SF[sf], "SF%d" % sf, PF[3], "PF3")
        P.dma(scr_b6[i][:, :], SF[sf][R, 0:512], reads=["SF%d" % sf], writes=["scr_b%d" % i])
    bonTM, sgTM = SF[2], SF[3]
    to_tm([wq(BON, pr) for pr in range(4)], "FT0", bonTM, "SF2", PF[3], "PF3")
    to_tm([wq(SGT, pr) for pr in range(4)], "FT0", sgTM, "SF3", PF[3], "PF3")
    V6 = FT[1]
    for i in range(6):
        P.dma(V6[:, i * 64:(i + 1) * 64], scr_b6[i].rearrange("b (h k) -> (b h) k", h=8), reads=["scr_b%d" % i], writes=["FT1"])
    wP, aP, bP, kP2, rP, vP2 = [V6[:, i * 64:(i + 1) * 64] for i in range(6)]
    sbv = sb_mat_d.rearrange("b h v k -> (b h) (v k)")
    sbo = s_bmat.rearrange("b h v k -> (b h) (v k)")
    S1s, T1s, sa_t, yP = [(FT[4], "FT4"), (FT[3], "FT3")], [(FT[5], "FT5"), (FT[6], "FT6")], FT[2], FT[7]
    v8 = lambda t_: t_[:, :].rearrange("p (v k) -> p v k", v=8)
    kb = lambda ap: ap.unsqueeze(1).to_broadcast([128, 8, 64])
    for j in range(8):
        vsl = slice(8 * j, 8 * j + 8)
        (S1, s1k), (T1, t1k) = S1s[j % 2], T1s[j % 2]
        P.dma(S1[:, :], sbv[:, j * 512:(j + 1) * 512], writes=[s1k])
        P.pool(lambda e, S1=S1, T1=T1: e.tensor_tensor(out=v8(T1), in0=v8(S1), in1=kb(aP), op=ALU.mult), reads=[s1k, "FT1"], writes=[t1k])
        P.dve(lambda e, S1=S1, T1=T1: e.tensor_reduce(out=sa_t[:, 0:8], in_=v8(T1), axis=AX.X, op=ALU.add), reads=[t1k], writes=["FT2"])
        P.dve(lambda e, S1=S1, T1=T1: e.tensor_tensor(out=v8(S1), in0=v8(S1), in1=kb(wP), op=ALU.mult), reads=[s1k, "FT1"], writes=[s1k])
        P.pool(lambda e, S1=S1, T1=T1: e.tensor_tensor(out=v8(T1), in0=sa_t[:, 0:8].unsqueeze(2).to_broadcast([128, 8, 64]), in1=kb(bP), op=ALU.mult), reads=["FT2", "FT1"], writes=[t1k])
        P.dve(lambda e, S1=S1, T1=T1: e.tensor_tensor(out=S1[:, :], in0=S1[:, :], in1=T1[:, :], op=ALU.add), reads=[s1k, t1k], writes=[s1k])
        P.pool(lambda e, vsl=vsl, S1=S1, T1=T1: e.tensor_tensor(out=v8(T1), in0=vP2[:, vsl].unsqueeze(2).to_broadcast([128, 8, 64]), in1=kb(kP2), op=ALU.mult), reads=["FT1"], writes=[t1k])
        P.dve(lambda e, S1=S1, T1=T1: e.tensor_tensor(out=S1[:, :], in0=S1[:, :], in1=T1[:, :], op=ALU.add), reads=[s1k, t1k], writes=[s1k])
        P.dma(sbo[:, j * 512:(j + 1) * 512], S1[:, :], reads=[s1k], writes=["s_bmat%d" % j])
        P.pool(lambda e, S1=S1, T1=T1: e.tensor_tensor(out=v8(T1), in0=v8(S1), in1=kb(rP), op=ALU.mult), reads=[s1k, "FT1"], writes=[t1k])
        P.dve(lambda e, vsl=vsl, S1=S1, T1=T1: e.tensor_reduce(out=yP[:, vsl], in_=v8(T1), axis=AX.X, op=ALU.add), reads=[t1k], writes=["FT7"])
    P.dma(scr_y[:, :], yP[:, 0:64], reads=["FT7"], writes=["scr_y"])
    yb = SF[4]
    P.dma(yb[R, 0:512], scr_y.rearrange("(b h) v -> b (h v)", h=8), reads=["scr_y"], writes=["SF4"])
    y3 = b3(yb[R, :], 8, 64)
    stt = SF[5]
    P.dve(lambda e: e.tensor_reduce(out=stt[R, 0:8], in_=y3, axis=AX.X, op=ALU.add), reads=["SF4"], writes=["SF5"])
    P.dve(lambda e: e.tensor_scalar(out=stt[R, 0:8], in0=stt[R, 0:8], scalar1=1.0 / 64, scalar2=None, op0=ALU.mult), reads=["SF5"], writes=["SF5"])
    P.dve(lambda e: e.tensor_tensor(out=y3, in0=y3, in1=stt[R, 0:8].unsqueeze(2).to_broadcast([N, 8, 64]), op=ALU.subtract), reads=["SF4", "SF5"], writes=["SF4"])
    sq2 = SF[7]
    P.pool(lambda e: e.tensor_tensor(out=sq2[R, :], in0=yb[R, :], in1=yb[R, :], op=ALU.mult), reads=["SF4"], writes=["SF7"])
    P.dve(lambda e: e.tensor_reduce(out=stt[R, 8:16], in_=b3(sq2[R, :], 8, 64), axis=AX.X, op=ALU.add), reads=["SF7"], writes=["SF5"])
    rsqrt_act(stt[R, 8:16], stt[R, 8:16], 1.0 / 64, 64e-5, ["SF5"], ["SF5"])
    P.dve(lambda e: e.tensor_tensor(out=y3, in0=y3, in1=stt[R, 8:16].unsqueeze(2).to_broadcast([N, 8, 64]), op=ALU.mult), reads=["SF4", "SF5"], writes=["SF4"])
    P.dve(lambda e: e.tensor_tensor(out=yb[R, :], in0=yb[R, :], in1=LNW[R, :], op=ALU.mult), reads=["SF4", "rp"], writes=["SF4"])
    P.dve(lambda e: e.tensor_tensor(out=yb[R, :], in0=yb[R, :], in1=LNB[R, :], op=ALU.add), reads=["SF4", "rp"], writes=["SF4"])
    P.dve(lambda e: e.tensor_tensor(out=yb[R, :], in0=yb[R, :], in1=bonTM[R, :], op=ALU.add), reads=["SF4", "SF2"], writes=["SF4"])
    mixb = HT[23]
    P.dve(lambda e: e.tensor_tensor(out=mixb[R, :], in0=yb[R, :], in1=sgTM[R, :], op=ALU.mult), reads=["SF4", "SF3"], writes=["HT23"])
    for c8 in range(8):
        src = mixs[R, c8 * 128:(c8 + 1) * 128] if c8 < 4 else mixb[R, (c8 - 4) * 128:(c8 - 3) * 128]
        P.pe(lambda e, c8=c8, src=src: e.transpose(out=PB[1][:, c8 * N:(c8 + 1) * N], in_=src, identity=idb[R, R]), reads=["HT22", "HT23", "idb"], writes=["PB1"])
    mixTs = HT[24]
    P.dve(lambda e: e.tensor_copy(out=mixTs[:, 0:8 * N], in_=PB[1][:, 0:8 * N]), reads=["PB1"], writes=["HT24"])
    for n in range(2):
        for kc in range(8):
            P.pe(lambda e, n=n, kc=kc: e.matmul(PF[n][R, :], lhsT=mixTs[:, kc * N:(kc + 1) * N], rhs=woutb[:, kc * 1024 + n * 512:kc * 1024 + (n + 1) * 512], start=(kc == 0), stop=(kc == 7)), reads=["HT24", "woutb"], writes=["PF%d" % n])
        P.dve(lambda e, n=n: e.tensor_tensor(out=xa[R, n * 512:(n + 1) * 512], in0=xa[R, n * 512:(n + 1) * 512], in1=PF[n][R, :], op=ALU.add), reads=["xt0", "PF%d" % n], writes=["xt0"])
    P.act(lambda e: e.activation(out=xs_[R, :], in_=xa[R, :], func=AF.Square, accum_out=st4[R, 4:5]), reads=["xt0"], writes=["xs_", "st4"])
    rsqrt_act(st4[R, 4:5], st4[R, 4:5], 1.0 / D, 1e-6, ["st4"], ["st4"])
    P.dve(lambda e: e.scalar_tensor_tensor(out=xs_[R, :], in0=xa[R, :], scalar=st4[R, 4:5], in1=FNW[R, :], op0=ALU.mult, op1=ALU.mult), reads=["xt0", "st4", "rp"], writes=["xs_"])
    P.dma(ys[:, :], xs_[R, :], reads=["xs_"], writes=["ys"])
```

```python
import contextlib
import numpy as np
import concourse.bass as bass
import concourse.mybir as mybir
from concourse.bass_utils import run_bass_kernel_spmd

F32 = mybir.dt.float32
BF16 = mybir.dt.bfloat16
ALU = mybir.AluOpType
AF = mybir.ActivationFunctionType
AX = mybir.AxisListType


class _Rec:
    def __init__(self):
        self.call = None

    def __getattr__(self, name):
        def f(*a, **k):
            self.call = (name, a, k)
            return self
        return f


class Prog:
    ENGS = ["pe", "dve", "act", "pool", "sp"]

    def __init__(self, nc, n_dma_sems=24):
        self.nc = nc
        self.stack = contextlib.ExitStack()
        self.items = {e: [] for e in self.ENGS}
        self.cnt = {e: 0 for e in self.ENGS}
        self.sem = {e: self.stack.enter_context(nc.semaphore("s_" + e)) for e in ["pe", "dve", "act", "pool"]}
        self.dsem = [self.stack.enter_context(nc.semaphore("d%d" % i)) for i in range(n_dma_sems)]
        self.dval = [0] * n_dma_sems
        self.dma_i = 0
        self.n_sw = 8
        self.sw_i = 0
        self.seen = {e: {} for e in self.ENGS}
        self.lastw = {}
        self.readers = {}
        self.n_ops = 0
        self.capture = None
        self.oplist = []
        self.warm_ops = None
        self.n_fill = 0

    def sb(self, name, shape, dtype):
        return self.stack.enter_context(self.nc.sbuf_tensor(name, list(shape), dtype))

    def ps(self, name, shape, dtype):
        return self.stack.enter_context(self.nc.psum_tensor(name, list(shape), dtype))

    _VEC_OPS = ("tensor_tensor", "tensor_copy", "memset")

    def op(self, eng, fn, reads=(), writes=(), is_dma=False, alts=None):
        rec = _Rec()
        fn(rec)
        al = {}
        if alts:
            for e2, fn2 in alts:
                r2 = _Rec()
                fn2(r2)
                al[e2] = r2.call
        if ALT[0] and not is_dma and eng in ("dve", "pool") and rec.call[0] in self._VEC_OPS \
                and not any(k[:2] in ("PF", "PQ", "PB") for k in tuple(reads) + tuple(writes)):
            al.setdefault("pool" if eng == "dve" else "dve", rec.call)
        item = (eng, rec.call, tuple(reads), tuple(writes), is_dma, al)
        if self.capture is not None:
            self.capture.append(item)
            return
        self.oplist.append(item)

    def copy(self, out, in_, reads=(), writes=()):
        self.op("dve", lambda e: e.tensor_copy(out=out, in_=in_), reads, writes,
                alts=[("act", lambda e: e.activation(out=out, in_=in_, func=AF.Copy))] if ALT[0] else None)

    def capture_begin(self):
        self.capture = []

    def capture_end(self):
        c, self.capture = self.capture, None
        return c

    def replay(self, streams):
        items = []
        for si, st in enumerate(streams):
            n = max(len(st), 1)
            for i, it in enumerate(st):
                items.append(((i + 0.5) / n, si, i, it))
        items.sort(key=lambda t: (t[0], t[1], t[2]))
        for _, _, _, it in items:
            self.oplist.append(it)

    def _op(self, eng, call, reads=(), writes=(), is_dma=False):
        if self.n_ops >= MAXOPS[0]:
            return
        need = {}

        def add(tok, same_ok):
            if tok is None:
                return
            key, h, v, teng = tok
            if teng == eng and eng == "pe" and not is_dma_tok(tok) and not same_ok:
                return
            if need.get(key, (None, 0))[1] < v:
                need[key] = (h, v)

        def is_dma_tok(tok):
            return tok[0].startswith("d#")

        for k in reads:
            add(self.lastw.get(k), True)
        for k in writes:
            add(self.lastw.get(k), False)
            for tok in self.readers.get(k, {}).values():
                add(tok, False)
        if is_dma:
            nh = len(self.dsem) - self.n_sw
            if eng == "pool":
                slot = nh + self.sw_i % self.n_sw
                self.sw_i += 1
            else:
                slot = self.dma_i % nh
                self.dma_i += 1
            if self.dval[slot] > 0:
                add(("d#%d" % slot, self.dsem[slot], self.dval[slot], None), True)
            self.dval[slot] += 16
            tok = ("d#%d" % slot, self.dsem[slot], self.dval[slot], None)
            inc = 16
        else:
            self.cnt[eng] += 1
            tok = (eng, self.sem[eng], self.cnt[eng], eng)
            inc = 1
        waits = []
        for key, (h, v) in need.items():
            if self.seen[eng].get(key, 0) < v:
                self.seen[eng][key] = v
                waits.append((h, v))
        for k in writes:
            self.lastw[k] = tok
            self.readers[k] = {}
        for k in reads:
            if k in writes:
                continue
            self.readers.setdefault(k, {})[tok[0]] = tok
        self.items[eng].append((waits, call, tok[1], inc))
        self.n_ops += 1

    def pe(self, fn, reads=(), writes=()):
        self.op("pe", fn, reads, writes)

    def dve(self, fn, reads=(), writes=()):
        self.op("dve", fn, reads, writes)

    def act(self, fn, reads=(), writes=()):
        self.op("act", fn, reads, writes)

    def pool(self, fn, reads=(), writes=()):
        self.op("pool", fn, reads, writes)

    def dma(self, out, in_, reads=(), writes=(), q="sp"):
        self.op(q, lambda e: e.dma_start(out=out, in_=in_), reads, writes, is_dma=True)

    @staticmethod
    def _fd(call):
        name, a_, k_ = call
        out = k_.get("out", a_[0] if a_ else None)
        try:
            shp = list(out.shape)
            n = 1
            for d in shp[1:]:
                n *= int(d)
            return max(n, 1), int(shp[0])
        except Exception:
            return 128, 128

    @staticmethod
    def _act_set(call):
        if call[0] != "activation":
            return None
        f = str(call[2].get("func", "")).split(".")[-1]
        if f in ("Exp", "Ln"):
            return "E"
        if f in ("Silu", "Sigmoid", "Tanh"):
            return f
        return None

    def _dur(self, eng, call, is_dma):
        fd, npart = self._fd(call)
        if is_dma:
            byt = fd * npart * 4
            return 0.15, 2.0 + byt / 120e3
        if eng == "pe":
            t = (0.06 + fd / 1200.0) * PE_SCALE[0]
            return t, t + 0.25 + LAT_EXTRA[0]
        if eng == "dve":
            t = 0.16 + fd / 960.0
            if call[0] == "scalar_tensor_tensor":
                t = 0.16 + fd / 480.0
            return t, t + 0.1 + LAT_EXTRA[0]
        if eng == "act":
            t = 0.22 + fd / 1200.0
            return t, t + 0.1 + LAT_EXTRA[0]
        t = 0.3 + fd / 600.0
        return t, t + 0.1 + LAT_EXTRA[0]

    def schedule(self):
        import heapq
        ops = self.oplist
        n = len(ops)
        if MAXOPS[0] < n:
            ops = ops[:MAXOPS[0]]
            n = len(ops)
        preds = [None] * n
        preds_ps = [None] * n
        lastw, readers = {}, {}
        for i, (eng, call, reads, writes, is_dma, _al) in enumerate(ops):
            ps = set()
            pp = set()
            for k in reads:
                if k in lastw:
                    ps.add(lastw[k])
            for k in writes:
                if k in lastw:
                    ps.add(lastw[k])
                    if k[:2] in ("PF", "PQ", "PB"):
                        pp.add(lastw[k])
                ps.update(readers.get(k, ()))
                if k[:2] in ("PF", "PQ", "PB"):
                    pp.update(readers.get(k, ()))
            ps.discard(i)
            pp.discard(i)
            preds[i] = ps
            preds_ps[i] = pp
            for k in writes:
                lastw[k] = i
                readers[k] = set()
            for k in reads:
                if k not in writes:
                    readers.setdefault(k, set()).add(i)
        succs = [[] for _ in range(n)]
        npred = [0] * n
        for i in range(n):
            npred[i] = len(preds[i])
            for p in preds[i]:
                succs[p].append(i)
        chosen = None
        if not SCHED[0]:
            order = list(range(n))
        else:
            durs = [self._dur(ops[i][0], ops[i][1], ops[i][4]) for i in range(n)]
            tail = [0.0] * n
            for i in range(n - 1, -1, -1):
                t = 0.0
                for sidx in succs[i]:
                    if tail[sidx] > t:
                        t = tail[sidx]
                tail[i] = t + durs[i][1]
            done_t = [0.0] * n
            ready_t = [0.0] * n
            free = {e: 0.0 for e in self.ENGS}
            pend = {e: [] for e in self.ENGS}
            avail = {e: [] for e in self.ENGS}
            act_set = [None]
            act_pick = [None]
            chosen = [None] * n
            engs_of = [[ops[i][0]] + list(ops[i][5].keys()) for i in range(n)]

            def push_ready(i, rt):
                for e2 in engs_of[i]:
                    heapq.heappush(pend[e2], (rt, i))

            for i in range(n):
                if npred[i] == 0:
                    push_ready(i, 0.0)
            order = []
            left = n
            while left:
                best = None
                for e in self.ENGS:
                    f = free[e]
                    while pend[e] and (pend[e][0][0] <= f or chosen[pend[e][0][1]] is not None):
                        j_ = heapq.heappop(pend[e])[1]
                        if chosen[j_] is None:
                            heapq.heappush(avail[e], ((-tail[j_] if PRIO[0] else 0.0), j_))
                    while avail[e] and chosen[avail[e][0][1]] is not None:
                        heapq.heappop(avail[e])
                    if avail[e] and e == "act" and ACT_TABLES[0]:
                        peek = []
                        while avail[e] and len(peek) < 8:
                            it_ = heapq.heappop(avail[e])
                            if chosen[it_[1]] is None:
                                peek.append(it_)
                        pick = peek[0]
                        for it_ in peek:
                            cs_ = self._act_set(ops[it_[1]][1] if ops[it_[1]][0] == "act" else ops[it_[1]][5]["act"])
                            if ACT_TABLES[0] == 2 and (cs_ is None or cs_ == act_set[0]):
                                pick = it_
                                break
                        for it_ in peek:
                            heapq.heappush(avail[e], it_)
                        act_pick[0] = pick
                        cand = (f, pick[1], e, True)
                    elif avail[e]:
                        cand = (f, avail[e][0][1], e, True)
                    elif pend[e]:
                        cand = (pend[e][0][0], pend[e][0][1], e, False)
                    else:
                        continue
                    i_ = cand[1]
                    pen = 0.0
                    if e != ops[i_][0]:
                        pen = max(0.0, self._dur(e, ops[i_][5][e], False)[0] - self._dur(ops[i_][0], ops[i_][1], False)[0])
                    key = (cand[0] + pen, cand[1])
                    if best is None or key < best[0]:
                        best = (key, cand)
                st, i, e, from_avail = best[1]
                if from_avail and e == "act" and ACT_TABLES[0]:
                    tmp_ = []
                    while avail[e]:
                        it_ = heapq.heappop(avail[e])
                        if it_[1] == i:
                            break
                        tmp_.append(it_)
                    for it_ in tmp_:
                        heapq.heappush(avail[e], it_)
                elif from_avail:
                    heapq.heappop(avail[e])
                else:
                    heapq.heappop(pend[e])
                chosen[i] = e
                call_i = ops[i][1] if e == ops[i][0] else ops[i][5][e]
                if e == "pe" and KEEPWARM[0] and self.warm_ops is not None and call_i[0] == "matmul" \
                        and call_i[2].get("start", True) and st - free[e] > 0.7:
                    tp = free[e]
                    for p in preds_ps[i]:
                        tp = max(tp, done_t[p])
                    nfill = min(int((st - tp - 0.25) / 0.17), KEEPWARM[0])
                    if nfill > 0:
                        order.append(("fill", i, nfill))
                        self.n_fill += nfill
                occ, lat = self._dur(e, call_i, ops[i][4])
                if e == "act" and ACT_TABLES[0]:
                    cs_ = self._act_set(call_i)
                    if cs_ is not None and cs_ != act_set[0]:
                        occ += 1.3
                        lat += 1.3
                        act_set[0] = cs_
                if SCHED_TRACE is not None:
                    SCHED_TRACE.append((i, e, st, occ, lat, free[e], ready_t[i]))
                free[e] = st + occ
                done_t[i] = st + lat
                order.append(i)
                left -= 1
                for sidx in succs[i]:
                    ready_t[sidx] = max(ready_t[sidx], (st + occ) if (e == "pe" and ops[sidx][0] == "pe") else done_t[i])
                    npred[sidx] -= 1
                    if npred[sidx] == 0:
                        push_ready(sidx, ready_t[sidx])
            self.est_us = max(done_t) if n else 0.0
        toks = [None] * n
        eng_of = [(chosen[i] if chosen is not None and chosen[i] is not None else ops[i][0]) for i in range(n)]
        for i in order:
            if isinstance(i, tuple):
                _, ri, nfill = i
                out_ap = ops[ri][1][1][0] if ops[ri][1][1] else ops[ri][1][2].get("out")
                try:
                    shp = list(out_ap.shape)
                    if len(shp) != 2 or shp[1] < 64 or str(out_ap.dtype) != str(F32):
                        continue
                    ncol = min(int(shp[1]), 128)
                    dcall = ("matmul", (out_ap[:, 0:ncol],), dict(lhsT=self.warm_ops[:, 0:int(shp[0])], rhs=self.warm_ops[:, 0:ncol], start=True, stop=True))
                except Exception:
                    continue
                for _ in range(nfill):
                    self._emit("pe", dcall, False, [(toks[p], eng_of[p]) for p in preds_ps[ri]])
                continue
            eng, call, reads, writes, is_dma, al = ops[i]
            e = eng_of[i]
            toks[i] = self._emit(e, call if e == eng else al[e], is_dma, [(toks[p], eng_of[p]) for p in preds[i]])

    def _emit(self, eng, call, is_dma, pred_toks):
        need = {}

        def add(tok):
            key, h, v, teng = tok
            if need.get(key, (None, 0))[1] < v:
                need[key] = (h, v)

        for tok, peng in pred_toks:
            if peng == "pe" and eng == "pe" and not tok[0].startswith("d#"):
                continue
            add(tok)
        if is_dma:
            nh = len(self.dsem) - self.n_sw
            if eng == "pool":
                slot = nh + self.sw_i % self.n_sw
                self.sw_i += 1
            else:
                slot = self.dma_i % nh
                self.dma_i += 1
            if self.dval[slot] > 0:
                add(("d#%d" % slot, self.dsem[slot], self.dval[slot], None))
            self.dval[slot] += 16
            tok = ("d#%d" % slot, self.dsem[slot], self.dval[slot], None)
            inc = 16
        else:
            self.cnt[eng] += 1
            tok = (eng, self.sem[eng], self.cnt[eng], eng)
            inc = 1
        waits = []
        for key, (h, v) in need.items():
            if self.seen[eng].get(key, 0) < v:
                self.seen[eng][key] = v
                waits.append((h, v))
        self.items[eng].append((waits, call, tok[1], inc))
        self.n_ops += 1
        return tok

    def finish(self):
        nc = self.nc
        self.schedule()
        final = [(self.dsem[i], self.dval[i]) for i in range(len(self.dsem)) if self.dval[i] > 0]

        def emit(name, e, tail=False):
            for waits, fn, h, inc in self.items[name]:
                for (wh, wv) in waits:
                    e.wait_ge(wh, wv)
                name_, a_, k_ = fn
                getattr(e, name_)(*a_, **k_).then_inc(h, inc)
            if tail:
                for (wh, wv) in final:
                    e.wait_ge(wh, wv)

        with nc.Block() as block:
            @block.tensor
            def _(e):
                emit("pe", e)

            @block.vector
            def _(e):
                emit("dve", e)

            @block.scalar
            def _(e):
                emit("act", e)

            @block.gpsimd
            def _(e):
                emit("pool", e)

            @block.sync
            def _(e):
                emit("sp", e, tail=True)
        self.stack.close()


D = 1024
PW = 4232
C0 = 0.6065306597126334
NCONST = 9
LASTP = None
MAXOPS = [10 ** 9]
SCHED = [True]
PRIO = [True]
ALT = [True]
KEEPWARM = [0]
ACT_TABLES = [2]
PE_SCALE = [0.6]
LAT_EXTRA = [0.15]
DECODE_LAST = [False]
SCHED_TRACE = None
TRACE_OPS = None


def host_consts():
    p = np.arange(128)
    same = (p[:, None] // 64) == (p[None, :] // 64)
    c = np.zeros((NCONST, 128, 128), np.float32)
    c[0] = np.eye(128)
    c[1] = 1.0
    c[2] = same & (p[:, None] <= p[None, :])
    c[3] = same & (p[:, None] > p[None, :])
    c[4] = same & (p[:, None] < p[None, :])
    c[5] = same
    c[6] = (p[:, None] % 64) == (p[None, :] % 64)
    c[7][:, 0] = p < 64
    c[7][:, 1] = p >= 64
    c[8] = -c[4]
    return c


def build(T=2048, NS=16, TB=512):
    nc = bass.Bass("TRN2", target_bir_lowering=False)

    def din(name, shape):
        return nc.dram_tensor(name, list(shape), F32, kind="ExternalInput").ap()

    def dout(name, shape):
        return nc.dram_tensor(name, list(shape), F32, kind="ExternalOutput").ap()

    xp = din("xp", [T, D])
    w_in = din("w_in", [D, PW])
    w_out = din("w_out", [D, D])
    vrows = din("vrows", [128, 128])
    rowp = din("rowp", [2048 + 128 + 1024 + 8])
    w2d = din("w2", [64, 512])
    a2d = din("a2", [64, 512])
    constd = din("consts", [NCONST, 128, 128])
    resetd = din("resetm", [128, 512])
    yp = dout("yp", [T, D])
    p_amat = dout("p_amat", [4, 128, 128])
    p_aconv = dout("p_aconv", [36, 128])
    p_bmat = dout("p_bmat", [8, 64, 64])
    p_bshift = dout("p_bshift", [17, 128])

    xs_d = din("xs", [NS, D])
    sa_mat_d = din("sa_mat", [NS, 4, 128, 128])
    sa_conv_d = din("sa_conv", [NS, 3, 1536])
    sb_mat_d = din("sb_mat", [NS, 8, 64, 64])
    sb_shift_d = din("sb_shift", [NS, 2176])
    ys = dout("ys", [NS, D])
    s_amat = dout("s_amat", [NS, 4, 128, 128])
    s_aconv = dout("s_aconv", [NS, 3, 1536])
    s_bmat = dout("s_bmat", [NS, 8, 64, 64])
    s_bshift = dout("s_bshift", [NS, 2176])

    def dscr(name, shape):
        return nc.dram_tensor(name, list(shape), F32, kind="Internal").ap()

    scr_q, scr_k, scr_v = dscr("scr_q", [NS, 512]), dscr("scr_k", [NS, 512]), dscr("scr_v", [NS, 512])
    scr_ab = dscr("scr_ab", [NS, 8])
    scr_o = dscr("scr_o", [64, 128])
    scr_b6 = [dscr("scr_b%d" % i, [NS, 512]) for i in range(6)]
    scr_y = dscr("scr_y", [128, 64])

    P = Prog(nc)
    global LASTP
    LASTP = P
    NTB = TB // 128
    assert T % TB == 0

    cf = P.sb("cf", [128, NCONST * 128], F32)
    for i in range(NCONST):
        P.dma(cf[:, i * 128:(i + 1) * 128], constd[i], writes=["cf"])
    CF = lambda i: cf[:, i * 128:(i + 1) * 128]
    IDF, ONESF, BT, MGT, STRICT, BL, PAIRS, NSTRICT = CF(0), CF(1), CF(2), CF(3), CF(4), CF(5), CF(6), CF(8)
    INCL = BT
    CHIND = cf[:, 7 * 128:7 * 128 + 2]
    idb = P.sb("idb", [128, 128], BF16)
    P.dve(lambda e: e.tensor_copy(out=idb[:, :], in_=IDF), reads=["cf"], writes=["idb"])
    P.warm_ops = idb
    onesb = P.sb("onesb", [128, 128], BF16)
    P.dve(lambda e: e.tensor_copy(out=onesb[:, :], in_=ONESF), reads=["cf"], writes=["onesb"])
    blb = P.sb("blb", [128, 128], BF16)
    P.dve(lambda e: e.tensor_copy(out=blb[:, :], in_=BL), reads=["cf"], writes=["blb"])
    resetm = P.sb("resetm_sb", [128, 512], F32)
    P.dma(resetm[:, :], resetd[:, :], writes=["resetm"])

    PF = [P.ps("PF%d" % i, [128, 512], F32) for i in range(4)]
    PQ = [P.ps("PQ%d" % i, [128, 512], F32) for i in range(2)]
    PB = [P.ps("PB%d" % i, [128, 1024], BF16) for i in range(2)]

    vr_t = P.sb("vr_t", [128, 128], F32)
    P.dma(vr_t[:, :], vrows[:, :], writes=["vr_t"])
    pcol = P.sb("pcol", [128, 128], F32)
    P.pe(lambda e: e.transpose(out=PF[0][:, 0:128], in_=vr_t[:, :], identity=IDF), reads=["vr_t", "cf"], writes=["PF0"])
    P.dve(lambda e: e.tensor_copy(out=pcol[:, :], in_=PF[0][:, 0:128]), reads=["PF0"], writes=["pcol"])
    col = lambda i: pcol[:, i:i + 1]
    omu = P.sb("omu", [128, 17], F32)
    P.dve(lambda e: e.tensor_scalar(out=omu[:, :], in0=pcol[:, 56:73], scalar1=-1.0, scalar2=1.0, op0=ALU.mult, op1=ALU.add), reads=["pcol"], writes=["omu"])
    NW0, CW0, MU0, W00, A00, KK0, KA0, RK0 = 0, 8, 56, 73, 77, 81, 85, 89

    RL = 2048 + 128 + 1024 + 8
    rp = P.sb("rp", [128, RL], F32)
    P.dma(rp[:, :], rowp.partition_broadcast(128), writes=["rp"])
    LNW, LNB, ANW, FNW = rp[:, 0:512], rp[:, 512:1024], rp[:, 2048:2176], rp[:, 2176:3200]
    ALOG, DTB = rp[:, 3200:3204], rp[:, 3204:3208]
    nega = P.sb("nega", [128, 4], F32)
    P.act(lambda e: e.activation(out=nega[:, :], in_=ALOG, func=AF.Exp), reads=["rp"], writes=["nega"])
    P.dve(lambda e: e.tensor_scalar(out=nega[:, :], in0=nega[:, :], scalar1=-1.0, scalar2=None, op0=ALU.mult), reads=["nega"], writes=["nega"])

    w2a2 = P.sb("w2a2", [128, 512], F32)
    P.dma(w2a2[0:64, :], w2d[:, :], writes=["w2a2"])
    P.dma(w2a2[64:128, :], a2d[:, :], writes=["w2a2"])

    woutb = P.sb("woutb", [128, 8 * 1024], BF16)
    NWB = 4
    wbf = [P.sb("wbf%d" % i, [128, 1024], BF16) for i in range(NWB)]
    cast_i = [0]

    def cast(out, in_, reads, writes):
        i = cast_i[0]
        cast_i[0] += 1
        if i % 3 != 2:
            P.act(lambda e: e.activation(out=out, in_=in_, func=AF.Copy), reads=reads, writes=writes)
        else:
            P.dve(lambda e: e.tensor_copy(out=out, in_=in_), reads=reads, writes=writes)

    for kc in range(8):
        P.dma(woutb[:, kc * 1024:(kc + 1) * 1024], w_out[kc * 128:(kc + 1) * 128, :], writes=["woutb"], q="pool")

    Sa = P.sb("Sa", [128, 512], F32)
    Sab = P.sb("Sab", [128, 512], BF16)
    Hb = P.sb("Hb", [128, 512], F32)
    Hbb = P.sb("Hbb", [128, 512], BF16)
    for t_, n_ in ((Sa, "Sa"), (Sab, "Sab"), (Hb, "Hb"), (Hbb, "Hbb")):
        P.pool(lambda e, t_=t_: e.memset(t_[:, :], 0.0), writes=[n_])
    ccar = P.sb("ccar", [128, 36], F32)
    P.pool(lambda e: e.memset(ccar[:, :], 0.0), writes=["ccar"])
    bcar = P.sb("bcar", [128, 17], F32)
    P.pool(lambda e: e.memset(bcar[:, :], 0.0), writes=["bcar"])

    xt = [P.sb("xt%d" % i, [128, D], F32) for i in range(2)]
    xs_ = P.sb("xs_", [128, D], F32)
    st4 = P.sb("st4", [128, 8], F32)
    xnT = P.sb("xnT", [128, 8 * TB], BF16)
    cb = [P.sb("cb%d" % i, [128, TB + 4], F32) for i in range(2)]
    FT = [P.sb("FT%d" % i, [128, 512], F32) for i in range(10)]
    HT = [P.sb("HT%d" % i, [128, 512], BF16) for i in range(40)]
    mixB4 = P.sb("mixB4", [128, (TB // 128) * 512], BF16)
    ctmp = P.sb("ctmp", [128, 512], F32)
    mixT = P.sb("mixT", [128, 1024], BF16)
    BLK = [P.sb("BLK%d" % i, [128, (8 if i in (0, 3, 4) else 4) * TB], BF16) for i in range(5)]
    P.pool(lambda e: e.memset(BLK[3][:, :], 0.0), writes=["BLK3"])
    P.pool(lambda e: e.memset(BLK[4][:, :], 0.0), writes=["BLK4"])
    mixA2 = [P.sb("mixA%d" % i, [128, NTB * 512], BF16) for i in range(2)]
    pce = P.sb("pce", [128, 4 * (TB // 64)], F32)
    bon = P.sb("bon", [128, 4 * TB], BF16)

    def rsqrt_act(out, in_, scale, eps, reads, writes):
        P.act(lambda e: e.activation(out=out, in_=in_, func=AF.Ln, scale=scale, bias=eps), reads=reads, writes=writes)
        P.act(lambda e: e.activation(out=out, in_=out, func=AF.Exp, scale=-0.5), reads=writes, writes=writes)

    def b3(ap, h, n):
        return ap.rearrange("p (h n) -> p h n", h=h)

    SPJ = P.sb("SPJ", [128, 35 * NS], F32)
    xnTs = P.sb("xnTs", [128, 8 * NS], BF16)
    arena = P.sb("arena", [128, 4096], F32)

    class _Sub:
        def __init__(self, off):
            self.off = off

        def __getitem__(self, key):
            rows, cols = key
            lo = 0 if cols.start is None else cols.start
            hi = 512 if cols.stop is None else cols.stop
            return arena[rows, self.off + lo:self.off + hi]

    SF = [_Sub(i * 512) for i in range(8)]
    SAMPLE_LOCALS = dict(locals())
    if not DECODE_LAST[0]:
        sample_path(SAMPLE_LOCALS)
    arenab = arena[:, :].bitcast(BF16)
    P.pool(lambda e: e.memset(st4[:, 7:8], 0.0), reads=["SF%d" % i for i in range(8)], writes=["BAR", "BVB", "BSG", "st4"])

    nblk = T // TB
    stream_b = None
    for blk in range(nblk):
        t0 = blk * TB
        last_blk = blk == nblk - 1
        for tb in range(NTB):
            xa = xt[tb % 2]
            xk = "xt%d" % (tb % 2)
            P.dma(xa[:, :], xp[t0 + tb * 128:t0 + (tb + 1) * 128, :], writes=[xk])
            P.act(lambda e, xa=xa: e.activation(out=xs_[:, :], in_=xa[:, :], func=AF.Square, accum_out=st4[:, 0:1]), reads=[xk], writes=["xs_", "st4"])
            rsqrt_act(st4[:, 0:1], st4[:, 0:1], 1.0 / D, 1e-6, ["st4"], ["st4"])
            P.act(lambda e, xa=xa: e.activation(out=xs_[:, :], in_=xa[:, :], func=AF.Copy, scale=st4[:, 0:1]), reads=[xk, "st4"], writes=["xs_"])
            for half in range(2):
                pf = PF[half]
                pk = "PF%d" % half
                for q in range(4):
                    kc = half * 4 + q
                    P.pe(lambda e, pf=pf, q=q, kc=kc: e.transpose(out=pf[:, q * 128:(q + 1) * 128], in_=xs_[:, kc * 128:(kc + 1) * 128], identity=IDF), reads=["xs_", "cf"], writes=[pk])
                out3 = xnT[:, :].rearrange("p (k t) -> p k t", k=8)[:, half * 4:(half + 1) * 4, tb * 128:(tb + 1) * 128]
                in3 = pf[:, :].rearrange("p (k t) -> p k t", k=4)
                nw3 = pcol[:, NW0 + half * 4:NW0 + half * 4 + 4].unsqueeze(2).to_broadcast([128, 4, 128])
                P.dve(lambda e, out3=out3, in3=in3, nw3=nw3: e.tensor_tensor(out=out3, in0=in3, in1=nw3, op=ALU.mult), reads=[pk, "pcol"], writes=["xnT"])

        wchunk_i = [0]

        def proj_chunk(c0, ncols, pf, pk):
            s = wchunk_i[0] % NWB
            wchunk_i[0] += 1
            bk = "wbf%d" % s
            src = w_in[:, c0:c0 + ncols].rearrange("(k p) n -> p k n", p=128)
            dst = wbf[s][:, 0:8 * ncols].rearrange("p (k n) -> p k n", k=8)
            P.dma(dst, src, writes=[bk], q="pool")
            for kc in range(8):
                P.pe(lambda e, kc=kc, s=s: e.matmul(pf[0:ncols, 0:TB], lhsT=wbf[s][:, kc * ncols:(kc + 1) * ncols], rhs=xnT[:, kc * TB:(kc + 1) * TB], start=(kc == 0), stop=(kc == 7)), reads=[bk, "xnT"], writes=[pk])

        mixA, mxk = mixA2[blk % 2], "mixA%d" % (blk % 2)
        KQ, VT, SGA = BLK[0], BLK[1], BLK[2]
        for c in range(12):
            pf, pk = PF[c % 2], "PF%d" % (c % 2)
            cbuf, ck = cb[c % 2], "cb%d" % (c % 2)
            proj_chunk(c * 128, 128, pf, pk)
            car3 = ccar[:, :].rearrange("p (i c) -> p i c", i=3)[:, :, c]
            P.pool(lambda e, cbuf=cbuf, car3=car3: e.tensor_copy(out=cbuf[:, 0:3], in_=car3), reads=["ccar"], writes=[ck])
            P.act(lambda e, cbuf=cbuf, pf=pf: e.activation(out=cbuf[:, 3:3 + TB], in_=pf[:, 0:TB], func=AF.Copy), reads=[pk], writes=[ck])
            P.pool(lambda e, cbuf=cbuf, car3=car3: e.tensor_copy(out=car3, in_=cbuf[:, TB:TB + 3]), reads=[ck], writes=["ccar"])
            acc, ak = FT[c % 2], "FT%d" % (c % 2)
            P.op("dve", lambda e, cbuf=cbuf, acc=acc, c=c: e.tensor_scalar(out=acc[:, 0:TB], in0=cbuf[:, 0:TB], scalar1=col(CW0 + c * 4), scalar2=None, op0=ALU.mult), [ck, "pcol"], [ak],
                 alts=[("act", lambda e, cbuf=cbuf, acc=acc, c=c: e.activation(out=acc[:, 0:TB], in_=cbuf[:, 0:TB], func=AF.Copy, scale=col(CW0 + c * 4)))] if ALT[0] else None)
            P.dve(lambda e, cbuf=cbuf, acc=acc, c=c: e.scalar_tensor_tensor(out=acc[:, 0:TB], in0=cbuf[:, 1:1 + TB], scalar=col(CW0 + c * 4 + 1), in1=acc[:, 0:TB], op0=ALU.mult, op1=ALU.add), reads=[ck, "pcol", ak], writes=[ak])
            P.op("dve", lambda e, cbuf=cbuf, c=c: e.tensor_scalar(out=ctmp[:, 0:TB], in0=cbuf[:, 2:2 + TB], scalar1=col(CW0 + c * 4 + 2), scalar2=None, op0=ALU.mult), [ck, "pcol"], ["ctmp"],
                 alts=[("act", lambda e, cbuf=cbuf, c=c: e.activation(out=ctmp[:, 0:TB], in_=cbuf[:, 2:2 + TB], func=AF.Copy, scale=col(CW0 + c * 4 + 2)))] if ALT[0] else None)
            P.dve(lambda e, cbuf=cbuf, c=c: e.scalar_tensor_tensor(out=ctmp[:, 0:TB], in0=cbuf[:, 3:3 + TB], scalar=col(CW0 + c * 4 + 3), in1=ctmp[:, 0:TB], op0=ALU.mult, op1=ALU.add), reads=[ck, "pcol", "ctmp"], writes=["ctmp"])
            P.dve(lambda e, acc=acc: e.tensor_tensor(out=acc[:, 0:TB], in0=acc[:, 0:TB], in1=ctmp[:, 0:TB], op=ALU.add), reads=[ak, "ctmp"], writes=[ak])
            P.act(lambda e, acc=acc: e.activation(out=acc[:, 0:TB], in_=acc[:, 0:TB], func=AF.Silu), reads=[ak], writes=[ak])
            if c < 8:
                h = c % 4
                isq = c < 4
                sq, sqk = HT[c % 2], "HT%d" % (c % 2)
                P.act(lambda e, acc=acc, sq=sq: e.activation(out=sq[:, 0:TB], in_=acc[:, 0:TB], func=AF.Square), reads=[ak], writes=[sqk])
                p2, p2k = PF[2 + c % 2], "PF%d" % (2 + c % 2)
                P.pe(lambda e, p2=p2, sq=sq: e.matmul(p2[:, 0:TB], lhsT=onesb[:, :], rhs=sq[:, 0:TB], start=True, stop=True), reads=[sqk, "onesb"], writes=[p2k])
                ri, rik = FT[2 + c % 2], "FT%d" % (2 + c % 2)
                rsqrt_act(ri[:, 0:TB], p2[:, 0:TB], 1.0, 1e-12, [p2k], [rik])
                dst = KQ[:, h * 2 * TB:(h + 1) * 2 * TB].rearrange("p (t s n) -> p t s n", t=NTB, s=2)[:, :, 1 if isq else 0, :]
                P.dve(lambda e, acc=acc, ri=ri, dst=dst, isq=isq: e.scalar_tensor_tensor(out=dst, in0=acc[:, 0:TB].rearrange("p (t n) -> p t n", t=NTB), scalar=(128 ** -0.5 if isq else 1.0), in1=ri[:, 0:TB].rearrange("p (t n) -> p t n", t=NTB), op0=ALU.mult, op1=ALU.mult), reads=[ak, rik], writes=["BLK0"])
            else:
                h = c - 8
                P.dve(lambda e, acc=acc, h=h: e.tensor_copy(out=VT[:, h * TB:(h + 1) * TB], in_=acc[:, 0:TB]), reads=[ak], writes=["BLK1"])
        if last_blk:
            P.pe(lambda e: e.transpose(out=PF[0][0:36, 0:128], in_=ccar[:, :], identity=IDF), reads=["ccar", "cf"], writes=["PF0"])
            P.dve(lambda e: e.tensor_copy(out=FT[0][0:36, 0:128], in_=PF[0][0:36, 0:128]), reads=["PF0"], writes=["FT0"])
            P.dma(p_aconv[:, :], FT[0][0:36, 0:128], reads=["FT0"], writes=["p_aconv"])
        for c in range(4):
            pf, pk = PF[c % 2], "PF%d" % (c % 2)
            proj_chunk(1536 + c * 128, 128, pf, pk)
            P.act(lambda e, pf=pf, c=c: e.activation(out=SGA[:, c * TB:(c + 1) * TB], in_=pf[:, 0:TB], func=AF.Silu), reads=[pk], writes=["BLK2"])
        bdT = FT[4]
        proj_chunk(2048, 8, PF[0], "PF0")
        P.act(lambda e: e.activation(out=bdT[0:8, 0:TB], in_=PF[0][0:8, 0:TB], func=AF.Copy), reads=["PF0"], writes=["FT4"])

        P.capture_begin()
        for tb in range(NTB):
            cs = slice(tb * 128, (tb + 1) * 128)
            kq = lambda h, s: KQ[:, h * 2 * TB + tb * 256 + s * 128: h * 2 * TB + tb * 256 + (s + 1) * 128]
            kq2 = lambda h: KQ[:, h * 2 * TB + tb * 256: h * 2 * TB + (tb + 1) * 256]
            sc = FT[5]
            P.pe(lambda e, cs=cs: e.transpose(out=PF[0][:, 0:8], in_=bdT[0:8, cs], identity=IDF[0:8, 0:8]), reads=["FT4", "cf"], writes=["PF0"])
            P.act(lambda e: e.activation(out=sc[:, 0:4], in_=PF[0][:, 0:4], func=AF.Exp, scale=-1.0), reads=["PF0"], writes=["FT5"])
            P.dve(lambda e: e.tensor_scalar(out=sc[:, 0:4], in0=sc[:, 0:4], scalar1=1.0, scalar2=None, op0=ALU.add), reads=["FT5"], writes=["FT5"])
            P.dve(lambda e: e.reciprocal(out=sc[:, 0:4], in_=sc[:, 0:4]), reads=["FT5"], writes=["FT5"])
            P.dve(lambda e: e.tensor_tensor(out=sc[:, 4:8], in0=PF[0][:, 4:8], in1=DTB, op=ALU.add), reads=["PF0", "rp"], writes=["FT5"])
            P.act(lambda e: e.activation(out=sc[:, 4:8], in_=sc[:, 4:8], func=AF.Exp), reads=["FT5"], writes=["FT5"])
            P.act(lambda e: e.activation(out=sc[:, 4:8], in_=sc[:, 4:8], func=AF.Ln, bias=1.0), reads=["FT5"], writes=["FT5"])
            P.dve(lambda e: e.tensor_tensor(out=sc[:, 4:8], in0=sc[:, 4:8], in1=nega[:, :], op=ALU.mult), reads=["FT5", "nega"], writes=["FT5"])
            P.pe(lambda e: e.matmul(PF[0][:, 8:12], lhsT=BT, rhs=sc[:, 4:8], start=True, stop=True), reads=["FT5", "cf"], writes=["PF0"])
            P.pe(lambda e: e.matmul(PF[0][:, 12:16], lhsT=BL, rhs=sc[:, 4:8], start=True, stop=True), reads=["FT5", "cf"], writes=["PF0"])
            P.dve(lambda e: e.tensor_copy(out=sc[:, 8:16], in_=PF[0][:, 8:16]), reads=["PF0"], writes=["FT5"])
            P.act(lambda e: e.activation(out=sc[:, 16:20], in_=sc[:, 8:12], func=AF.Exp), reads=["FT5"], writes=["FT5"])
            P.dve(lambda e: e.tensor_tensor(out=sc[:, 20:24], in0=sc[:, 12:16], in1=sc[:, 8:12], op=ALU.subtract), reads=["FT5"], writes=["FT5"])
            P.act(lambda e: e.activation(out=sc[:, 20:24], in_=sc[:, 20:24], func=AF.Exp), reads=["FT5"], writes=["FT5"])
            gm3 = sc[:, 32:40].rearrange("p (c h) -> p c h", c=2)
            P.dve(lambda e: e.tensor_tensor(out=gm3, in0=sc[:, 4:8].unsqueeze(1).to_broadcast([128, 2, 4]), in1=CHIND.unsqueeze(2).to_broadcast([128, 2, 4]), op=ALU.mult), reads=["FT5", "cf"], writes=["FT5"])
            P.pe(lambda e: e.matmul(PF[0][:, 16:24], lhsT=ONESF, rhs=sc[:, 32:40], start=True, stop=True), reads=["FT5", "cf"], writes=["PF0"])
            P.act(lambda e: e.activation(out=sc[:, 24:32], in_=PF[0][:, 16:24], func=AF.Exp), reads=["PF0"], writes=["FT5"])
            beta_b = sc[:, 0:4].unsqueeze(2).to_broadcast([128, 4, 128])
            Kg, Kd, Vtm = HT[2], HT[3], HT[4]
            for h in range(4):
                P.pe(lambda e, h=h: e.transpose(out=PB[0][:, h * 128:(h + 1) * 128], in_=kq(h, 0), identity=idb[:, :]), reads=["BLK0", "idb"], writes=["PB0"])
                P.pe(lambda e, h=h: e.transpose(out=PB[0][:, 512 + h * 128:512 + (h + 1) * 128], in_=VT[:, h * TB + tb * 128:h * TB + (tb + 1) * 128], identity=idb[:, :]), reads=["BLK1", "idb"], writes=["PB0"])
            P.dve(lambda e: e.tensor_tensor(out=b3(Kg[:, :], 4, 128), in0=b3(PB[0][:, 0:512], 4, 128), in1=sc[:, 16:20].unsqueeze(2).to_broadcast([128, 4, 128]), op=ALU.mult), reads=["PB0", "FT5"], writes=["HT2"])
            P.dve(lambda e: e.tensor_tensor(out=b3(Kd[:, :], 4, 128), in0=b3(PB[0][:, 0:512], 4, 128), in1=sc[:, 20:24].unsqueeze(2).to_broadcast([128, 4, 128]), op=ALU.mult), reads=["PB0", "FT5"], writes=["HT3"])
            P.dve(lambda e: e.tensor_copy(out=Vtm[:, :], in_=PB[0][:, 512:1024]), reads=["PB0"], writes=["HT4"])
            MG, E = FT[6], FT[7]
            P.pool(lambda e: e.tensor_tensor(out=b3(MG[:, :], 4, 128), in0=MGT.unsqueeze(1).to_broadcast([128, 4, 128]), in1=sc[:, 4:8].unsqueeze(2).to_broadcast([128, 4, 128]), op=ALU.mult), reads=["cf", "FT5"], writes=["FT6"])
            for h in range(4):
                P.pe(lambda e, h=h: e.matmul(PF[0][:, h * 128:(h + 1) * 128], lhsT=MG[:, h * 128:(h + 1) * 128], rhs=BT, start=True, stop=True), reads=["FT6", "cf"], writes=["PF0"])
            P.act(lambda e: e.activation(out=E[:, :], in_=PF[0][:, :], func=AF.Exp), reads=["PF0"], writes=["FT7"])
            for h in range(4):
                P.pe(lambda e, h=h: e.matmul((PQ[0] if h < 2 else PF[1])[:, (h % 2) * 256:(h % 2 + 1) * 256], lhsT=kq(h, 0), rhs=kq2(h), start=True, stop=True), reads=["BLK0"], writes=["PQ0" if h < 2 else "PF1"])
            XQ = [HT[5], HT[6]]
            XQa, XQb = (HT[5], HT[6]), (HT[7], HT[8])
            Nn = [HT[9], HT[10]]
            QKT = HT[11]
            E2 = FT[8]
            P.pool(lambda e: e.tensor_tensor(out=b3(E2[:, :], 4, 128), in0=b3(E[:, :], 4, 128), in1=INCL.unsqueeze(1).to_broadcast([128, 4, 128]), op=ALU.mult), reads=["FT7", "cf"], writes=["FT8"])
            pdh = [(PQ[0], "PQ0"), (PF[1], "PF1")]
            pd2 = lambda i: pdh[i][0][:, :].rearrange("p (h s n) -> p h s n", h=2, s=2)
            h2 = lambda t_, i: t_[:, i * 256:(i + 1) * 256].rearrange("p (h n) -> p h n", h=2)
            for i in range(2):
                P.dve(lambda e, i=i: e.tensor_tensor(out=h2(QKT, i), in0=h2(E2, i), in1=pd2(i)[:, :, 1, :], op=ALU.mult), reads=["FT8", pdh[i][1]], writes=["HT11"])
            P.pool(lambda e: e.tensor_tensor(out=b3(E2[:, :], 4, 128), in0=b3(E2[:, :], 4, 128), in1=STRICT.unsqueeze(1).to_broadcast([128, 4, 128]), op=ALU.mult), reads=["FT8", "cf"], writes=["FT8"])
            P.pool(lambda e: e.tensor_tensor(out=b3(E2[:, :], 4, 128), in0=b3(E2[:, :], 4, 128), in1=beta_b, op=ALU.mult), reads=["FT8", "FT5"], writes=["FT8"])
            X0 = FT[9]
            for i in range(2):
                P.dve(lambda e, i=i: e.tensor_tensor(out=h2(X0, i), in0=h2(E2, i), in1=pd2(i)[:, :, 0, :], op=ALU.mult), reads=["FT8", pdh[i][1]], writes=["FT9"])
            resA = dict(X=([HT[5], HT[6]], ["HT5", "HT6"]), Q=([HT[7], HT[8]], ["HT7", "HT8"]), N=([HT[9], HT[10]], ["HT9", "HT10"]),
                        SQ=(PQ[0], "PQ0"), G0=(PF[0], "PF0"), G1=(PF[1], "PF1"), T=(PB[0], "PB0"))
            Tinv = inverse_chain(P, X0, "FT9", resA, IDF, idb)
            TinvT, tk = Tinv
            WT, qgT = HT[12], HT[13]
            U0 = FT[6]
            for h in range(4):
                P.pe(lambda e, h=h: e.matmul(PF[0][:, h * 128:(h + 1) * 128], lhsT=Kg[:, h * 128:(h + 1) * 128], rhs=TinvT[:, h * 128:(h + 1) * 128], start=True, stop=True), reads=["HT2", tk], writes=["PF0"])
            P.copy(WT[:, :], PF[0][:, :], reads=["PF0"], writes=["HT12"])
            for h in range(4):
                P.pe(lambda e, h=h: e.matmul(PF[1][:, h * 128:(h + 1) * 128], lhsT=TinvT[:, h * 128:(h + 1) * 128], rhs=Vtm[:, h * 128:(h + 1) * 128], start=True, stop=True), reads=["HT4", tk], writes=["PF1"])
            P.copy(U0[:, :], PF[1][:, :], reads=["PF1"], writes=["FT6"])
            Dg = FT[7]
            P.pool(lambda e: e.tensor_tensor(out=b3(Dg[:, :], 4, 128), in0=IDF.unsqueeze(1).to_broadcast([128, 4, 128]), in1=sc[:, 16:20].unsqueeze(2).to_broadcast([128, 4, 128]), op=ALU.mult), reads=["cf", "FT5"], writes=["FT7"])
            for h in range(4):
                P.pe(lambda e, h=h: e.matmul(PF[0][:, h * 128:(h + 1) * 128], lhsT=ONESF, rhs=Dg[:, h * 128:(h + 1) * 128], start=True, stop=True), reads=["FT7", "cf"], writes=["PF0"])
            q4 = KQ[:, :].rearrange("p (h t s n) -> p h t s n", h=4, t=NTB, s=2)[:, :, tb, 1, :]
            P.dve(lambda e, q4=q4: e.tensor_tensor(out=b3(qgT[:, :], 4, 128), in0=q4, in1=b3(PF[0][:, :], 4, 128), op=ALU.mult), reads=["BLK0", "PF0"], writes=["HT13"])
            ub = HT[2 + 0]
            ub = HT[5]
            osb = FT[8]
            for c in range(2):
                r0, r1 = c * 64, c * 64 + 64
                for h in range(4):
                    P.pe(lambda e, h=h: e.matmul(PF[0][:, h * 128:(h + 1) * 128], lhsT=WT[:, h * 128:(h + 1) * 128], rhs=Sab[:, h * 128:(h + 1) * 128], start=True, stop=True), reads=["HT12", "Sab"], writes=["PF0"])
                tmpu = FT[9]
                P.dve(lambda e, r0=r0, r1=r1: e.tensor_tensor(out=tmpu[r0:r1, :], in0=U0[r0:r1, :], in1=PF[0][r0:r1, :], op=ALU.subtract), reads=["FT6", "PF0"], writes=["FT9"])
                P.dve(lambda e, r0=r0, r1=r1: e.tensor_tensor(out=b3(ub[r0:r1, :], 4, 128), in0=b3(tmpu[r0:r1, :], 4, 128), in1=sc[r0:r1, 0:4].unsqueeze(2).to_broadcast([64, 4, 128]), op=ALU.mult), reads=["FT9", "FT5"], writes=["HT5"])
                for h in range(4):
                    P.pe(lambda e, h=h: e.matmul(PF[1][:, h * 128:(h + 1) * 128], lhsT=qgT[:, h * 128:(h + 1) * 128], rhs=Sab[:, h * 128:(h + 1) * 128], start=True, stop=False), reads=["HT13", "Sab"], writes=["PF1"])
                    P.pe(lambda e, h=h, r0=r0, r1=r1: e.matmul(PF[1][:, h * 128:(h + 1) * 128], lhsT=QKT[r0:r1, h * 128:(h + 1) * 128], rhs=ub[r0:r1, h * 128:(h + 1) * 128], start=False, stop=True), reads=["HT11", "HT5"], writes=["PF1"])
                P.copy(osb[r0:r1, :], PF[1][r0:r1, :], reads=["PF1"], writes=["FT8"])
                for h in range(4):
                    P.pe(lambda e, h=h, r0=r0, r1=r1: e.matmul(PF[0][:, h * 128:(h + 1) * 128], lhsT=Kd[r0:r1, h * 128:(h + 1) * 128], rhs=ub[r0:r1, h * 128:(h + 1) * 128], start=True, stop=True), reads=["HT3", "HT5"], writes=["PF0"])
                P.dve(lambda e, c=c: e.tensor_tensor(out=b3(Sa[:, :], 4, 128), in0=b3(Sa[:, :], 4, 128), in1=sc[:, 24 + c * 4:28 + c * 4].unsqueeze(2).to_broadcast([128, 4, 128]), op=ALU.mult), reads=["Sa", "FT5"], writes=["Sa"])
                P.dve(lambda e: e.tensor_tensor(out=Sa[:, :], in0=Sa[:, :], in1=PF[0][:, :], op=ALU.add), reads=["Sa", "PF0"], writes=["Sa"])
                P.copy(Sab[:, :], Sa[:, :], reads=["Sa"], writes=["Sab"])
            sq = FT[9]
            P.pool(lambda e: e.tensor_tensor(out=sq[:, :], in0=osb[:, :], in1=osb[:, :], op=ALU.mult), reads=["FT8"], writes=["FT9"])
            P.dve(lambda e: e.tensor_reduce(out=sc[:, 40:44], in_=b3(sq[:, :], 4, 128), axis=AX.X, op=ALU.add), reads=["FT9"], writes=["FT5"])
            rsqrt_act(sc[:, 40:44], sc[:, 40:44], 1.0 / 128, 1e-6, ["FT5"], ["FT5"])
            P.dve(lambda e: e.tensor_tensor(out=b3(osb[:, :], 4, 128), in0=b3(osb[:, :], 4, 128), in1=sc[:, 40:44].unsqueeze(2).to_broadcast([128, 4, 128]), op=ALU.mult), reads=["FT8", "FT5"], writes=["FT8"])
            P.pool(lambda e: e.tensor_tensor(out=b3(osb[:, :], 4, 128), in0=b3(osb[:, :], 4, 128), in1=ANW.unsqueeze(1).to_broadcast([128, 4, 128]), op=ALU.mult), reads=["FT8", "rp"], writes=["FT8"])
            for c4 in range(4):
                P.pe(lambda e, c4=c4, cs=cs: e.transpose(out=PB[0][:, c4 * 128:(c4 + 1) * 128], in_=SGA[:, c4 * TB + tb * 128:c4 * TB + (tb + 1) * 128], identity=idb[:, :]), reads=["BLK2", "idb"], writes=["PB0"])
            P.dve(lambda e: e.tensor_tensor(out=mixA[:, tb * 512:(tb + 1) * 512], in0=osb[:, :], in1=PB[0][:, 0:512], op=ALU.mult), reads=["FT8", "PB0"], writes=[mxk])
        if last_blk:
            P.dma(p_amat.rearrange("h k v -> k h v"), b3(Sa[:, :], 4, 128), reads=["Sa"], writes=["p_amat"])
        stream_a = P.capture_end()
        P.replay([stream_a] + ([stream_b] if stream_b is not None else []))


        AR, BKL, BKH, VBT, SGB = arenab[:, 0:8 * TB], BLK[3], BLK[4], arenab[:, 4096:4096 + 4 * TB], arenab[:, 6144:6144 + 4 * TB]
        NCH = TB // 64
        ar = lambda pr, tb_, s_: AR[:, pr * 2 * TB + tb_ * 256 + s_ * 128: pr * 2 * TB + tb_ * 256 + (s_ + 1) * 128]
        bkh = lambda hh, pr, tb_, s_: (BKL, BKH)[hh][:, pr * 2 * TB + tb_ * 256 + s_ * 128: pr * 2 * TB + tb_ * 256 + (s_ + 1) * 128]
        ar_dst = lambda pr, s_: AR[:, pr * 2 * TB:(pr + 1) * 2 * TB].rearrange("p (t s n) -> p t s n", t=NTB, s=2)[:, :, s_, :]
        bk_dst = lambda hh, pr, s_: (BKL, BKH)[hh][64 * hh:64 * hh + 64, pr * 2 * TB:(pr + 1) * 2 * TB].rearrange("p (t s n) -> p t s n", t=NTB, s=2)[:, :, s_, :]
        t3 = lambda ap: ap.rearrange("p (t n) -> p t n", t=NTB)
        bci = [0]

        def b_chunk(j, dstT, dk):
            i = bci[0] % 2
            bci[0] += 1
            pf, pk = PF[i], "PF%d" % i
            cbuf, ck = cb[i], "cb%d" % i
            proj_chunk(2056 + j * 128, 128, pf, pk)
            P.pool(lambda e: e.tensor_copy(out=cbuf[:, 0:1], in_=bcar[:, j:j + 1]), reads=["bcar"], writes=[ck])
            P.act(lambda e: e.activation(out=cbuf[:, 1:1 + TB], in_=pf[:, 0:TB], func=AF.Copy), reads=[pk], writes=[ck])
            P.pool(lambda e: e.tensor_copy(out=bcar[:, j:j + 1], in_=cbuf[:, TB:TB + 1]), reads=[ck], writes=["bcar"])
            P.op("dve", lambda e: e.tensor_scalar(out=dstT[:, 0:TB], in0=cbuf[:, 0:TB], scalar1=col(MU0 + j), scalar2=None, op0=ALU.mult), [ck, "pcol"], [dk],
                 alts=[("act", lambda e: e.activation(out=dstT[:, 0:TB], in_=cbuf[:, 0:TB], func=AF.Copy, scale=col(MU0 + j)))] if ALT[0] else None)
            P.dve(lambda e: e.scalar_tensor_tensor(out=dstT[:, 0:TB], in0=cbuf[:, 1:1 + TB], scalar=omu[:, j:j + 1], in1=dstT[:, 0:TB], op0=ALU.mult, op1=ALU.add), reads=[dk, ck, "omu"], writes=[dk])

        xb16 = FT[0]
        b_chunk(16, xb16, "FT0")
        P.act(lambda e: e.activation(out=xb16[0:64, 0:TB], in_=xb16[0:64, 0:TB], func=AF.Tanh), reads=["FT0"], writes=["FT0"])
        for pr in range(4):
            sg, aic, Lsg, Pt, Pinv, Pm1, xr, xk, kmod = FT[1], FT[2], FT[3], FT[4], FT[5], FT[6], FT[7], FT[8], FT[9]
            P.pe(lambda e, pr=pr: e.matmul(PF[2][:, 0:TB], lhsT=w2a2[0:64, pr * 128:(pr + 1) * 128], rhs=xb16[0:64, 0:TB], start=True, stop=True), reads=["w2a2", "FT0"], writes=["PF2"])
            P.act(lambda e, pr=pr: e.activation(out=sg[:, 0:TB], in_=PF[2][:, 0:TB], func=AF.Sigmoid, bias=col(W00 + pr)), reads=["PF2", "pcol"], writes=["FT1"])
            P.pe(lambda e, pr=pr: e.matmul(PF[3][:, 0:TB], lhsT=w2a2[64:128, pr * 128:(pr + 1) * 128], rhs=xb16[64:128, 0:TB], start=True, stop=True), reads=["w2a2", "FT0"], writes=["PF3"])
            P.act(lambda e, pr=pr: e.activation(out=aic[:, 0:TB], in_=PF[3][:, 0:TB], func=AF.Sigmoid, bias=col(A00 + pr)), reads=["PF3", "pcol"], writes=["FT2"])
            P.dve(lambda e: e.tensor_tensor_scan(out=Lsg[:, 0:TB], data0=resetm[:, 0:TB], data1=sg[:, 0:TB], initial=0.0, op0=ALU.mult, op1=ALU.add), reads=["FT1", "resetm"], writes=["FT3"])
            P.act(lambda e: e.activation(out=Pt[:, 0:TB], in_=Lsg[:, 0:TB], func=AF.Exp, scale=-C0), reads=["FT3"], writes=["FT4"])
            P.act(lambda e: e.activation(out=Pinv[:, 0:TB], in_=Lsg[:, 0:TB], func=AF.Exp, scale=C0), reads=["FT3"], writes=["FT5"])
            P.dve(lambda e: e.tensor_tensor(out=Pm1[:, 0:TB], in0=Lsg[:, 0:TB], in1=sg[:, 0:TB], op=ALU.subtract), reads=["FT3", "FT1"], writes=["FT6"])
            P.act(lambda e: e.activation(out=Pm1[:, 0:TB], in_=Pm1[:, 0:TB], func=AF.Exp, scale=-C0), reads=["FT6"], writes=["FT6"])
            P.pool(lambda e, pr=pr: e.tensor_copy(out=pce[:, pr * NCH:(pr + 1) * NCH], in_=Pt[:, 0:TB].rearrange("p (c n) -> p c n", n=64)[:, :, 63]), reads=["FT4"], writes=["pce"])
            b_chunk(pr, xr, "FT7")
            P.dve(lambda e, pr=pr: e.tensor_tensor(out=ar_dst(pr, 1), in0=t3(xr[:, 0:TB]), in1=t3(Pt[:, 0:TB]), op=ALU.mult), reads=["FT7", "FT4"], writes=["BAR"])
            b_chunk(4 + pr, xk, "FT8")
            P.dve(lambda e, pr=pr: e.scalar_tensor_tensor(out=kmod[:, 0:TB], in0=aic[:, 0:TB], scalar=-1.0, in1=col(KA0 + pr).to_broadcast([128, TB]), op0=ALU.add, op1=ALU.mult), reads=["FT2", "pcol"], writes=["FT9"])
            P.dve(lambda e: e.scalar_tensor_tensor(out=kmod[:, 0:TB], in0=kmod[:, 0:TB], scalar=1.0, in1=xk[:, 0:TB], op0=ALU.add, op1=ALU.mult), reads=["FT9", "FT8"], writes=["FT9"])
            rkb = HT[1]
            P.dve(lambda e, pr=pr: e.scalar_tensor_tensor(out=rkb[:, 0:TB], in0=xr[:, 0:TB], scalar=col(RK0 + pr), in1=kmod[:, 0:TB], op0=ALU.mult, op1=ALU.mult), reads=["FT7", "FT9", "pcol"], writes=["HT1"])
            P.pe(lambda e: e.matmul(PF[3][:, 0:TB], lhsT=blb[:, :], rhs=rkb[:, 0:TB], start=True, stop=True), reads=["HT1", "blb"], writes=["PF3"])
            P.dve(lambda e, pr=pr: e.tensor_scalar(out=xk[:, 0:TB], in0=xk[:, 0:TB], scalar1=col(KK0 + pr), scalar2=None, op0=ALU.mult), reads=["FT8", "pcol"], writes=["FT8"])
            sqb = HT[0]
            P.act(lambda e: e.activation(out=sqb[:, 0:TB], in_=xk[:, 0:TB], func=AF.Square), reads=["FT8"], writes=["HT0"])
            P.pe(lambda e: e.matmul(PF[2][:, 0:TB], lhsT=blb[:, :], rhs=sqb[:, 0:TB], start=True, stop=True), reads=["HT0", "blb"], writes=["PF2"])
            rsqrt_act(xr[:, 0:TB], PF[2][:, 0:TB], 1.0, 1e-12, ["PF2"], ["FT7"])
            P.dve(lambda e: e.tensor_tensor(out=xk[:, 0:TB], in0=xk[:, 0:TB], in1=xr[:, 0:TB], op=ALU.mult), reads=["FT8", "FT7"], writes=["FT8"])
            P.dve(lambda e, pr=pr: e.scalar_tensor_tensor(out=ar_dst(pr, 0), in0=t3(xk[:, 0:TB]), scalar=-1.0, in1=t3(Pm1[:, 0:TB]), op0=ALU.mult, op1=ALU.mult), reads=["FT8", "FT6"], writes=["BAR"])
            P.pool(lambda e: e.tensor_tensor(out=xk[:, 0:TB], in0=xk[:, 0:TB], in1=aic[:, 0:TB], op=ALU.mult), reads=["FT8", "FT2"], writes=["FT8"])
            for hh in range(2):
                hs = slice(64 * hh, 64 * hh + 64)
                P.dve(lambda e, pr=pr, hh=hh, hs=hs: e.tensor_tensor(out=bk_dst(hh, pr, 0), in0=t3(xk[hs, 0:TB]), in1=t3(Pinv[hs, 0:TB]), op=ALU.mult), reads=["FT8", "FT5"], writes=["BLK%d" % (3 + hh)])
                P.dve(lambda e, pr=pr, hh=hh, hs=hs: e.tensor_tensor(out=bk_dst(hh, pr, 1), in0=t3(kmod[hs, 0:TB]), in1=t3(Pinv[hs, 0:TB]), op=ALU.mult), reads=["FT9", "FT5"], writes=["BLK%d" % (3 + hh)])
            b_chunk(8 + pr, xr, "FT7")
            P.act(lambda e, pr=pr: e.activation(out=VBT[:, pr * TB:(pr + 1) * TB], in_=xr[:, 0:TB], func=AF.Copy), reads=["FT7"], writes=["BVB"])
            P.dve(lambda e, pr=pr: e.tensor_tensor(out=bon[:, pr * TB:(pr + 1) * TB], in0=xr[:, 0:TB], in1=PF[3][:, 0:TB], op=ALU.mult), reads=["FT7", "PF3"], writes=["bon"])
            b_chunk(12 + pr, xk, "FT8")
            P.act(lambda e, pr=pr: e.activation(out=SGB[:, pr * TB:(pr + 1) * TB], in_=xk[:, 0:TB], func=AF.Silu), reads=["FT8"], writes=["BSG"])
        if last_blk:
            P.pe(lambda e: e.transpose(out=PF[0][0:17, 0:128], in_=bcar[:, :], identity=IDF), reads=["bcar", "cf"], writes=["PF0"])
            P.dve(lambda e: e.tensor_copy(out=FT[0][0:17, 0:128], in_=PF[0][0:17, 0:128]), reads=["PF0"], writes=["FT0"])
            P.dma(p_bshift[:, :], FT[0][0:17, 0:128], reads=["FT0"], writes=["p_bshift"])

        P.capture_begin()
        for tb in range(NTB):
            aTM, bTM, kTM, vTM = HT[0], HT[1], HT[26], HT[27]
            bonTM, bonk = (HT[28], "HT28") if tb % 2 == 0 else (HT[38], "HT38")
            sgTM, sgk = (HT[29], "HT29") if tb % 2 == 0 else (HT[39], "HT39")
            def emit_tm(tb=tb, aTM=aTM, bTM=bTM, kTM=kTM, vTM=vTM, bonTM=bonTM, bonk=bonk, sgTM=sgTM, sgk=sgk):
                for s_, dstt, dkey in ((0, bTM, "HT1"), (1, kTM, "HT26")):
                    for pr in range(4):
                        P.pe(lambda e, pr=pr, s_=s_: e.matmul(PF[2][:, pr * 128:(pr + 1) * 128], lhsT=bkh(0, pr, tb, s_), rhs=idb[:, :], start=True, stop=False), reads=["BLK3", "idb"], writes=["PF2"])
                        P.pe(lambda e, pr=pr, s_=s_: e.matmul(PF[2][:, pr * 128:(pr + 1) * 128], lhsT=bkh(1, pr, tb, s_), rhs=idb[:, :], start=False, stop=True), reads=["BLK4", "idb"], writes=["PF2"])
                    P.copy(dstt[:, :], PF[2][:, :], reads=["PF2"], writes=[dkey])
                srcs = [(lambda pr: ar(pr, tb, 0), "BAR", aTM, "HT0"), (lambda pr: VBT[:, pr * TB + tb * 128:pr * TB + (tb + 1) * 128], "BVB", vTM, "HT27"),
                        (lambda pr: bon[:, pr * TB + tb * 128:pr * TB + (tb + 1) * 128], "bon", bonTM, bonk),
                        (lambda pr: SGB[:, pr * TB + tb * 128:pr * TB + (tb + 1) * 128], "BSG", sgTM, sgk)]
                for si, (srcf, skey, dstt, dkey) in enumerate(srcs):
                    half = (si % 2) * 512
                    for pr in range(4):
                        P.pe(lambda e, pr=pr, srcf=srcf, half=half: e.transpose(out=PB[1][:, half + pr * 128:half + (pr + 1) * 128], in_=srcf(pr), identity=idb[:, :]), reads=[skey, "idb"], writes=["PB1"])
                    if True:
                        P.dve(lambda e, dstt=dstt, half=half: e.tensor_copy(out=dstt[:, :], in_=PB[1][:, half:half + 512]), reads=["PB1"], writes=[dkey])
            WmT, U0b, yb = HT[30], FT[0], FT[1]
            Yab = [HT[14], HT[15]]
            Yak = [HT[16], HT[17]]
            Aak = [HT[18], HT[19]]
            AV = HT[20]
            Ub = HT[21]
            for hb in range(2):
                for hl in range(4):
                    h = hb * 4 + hl
                    pr, p0 = h // 2, 64 * (h % 2)
                    P.pe(lambda e, hl=hl, pr=pr, h=h: e.matmul(PF[2 + hl // 2][:, (hl % 2) * 256:(hl % 2 + 1) * 256], lhsT=bkh(h % 2, pr, tb, 0), rhs=AR[:, pr * 2 * TB + tb * 256: pr * 2 * TB + (tb + 1) * 256], start=True, stop=True), reads=["BAR", "BLK3", "BLK4"], writes=["PF%d" % (2 + hl // 2)])
                pd2 = lambda i: PF[2 + i][:, :].rearrange("p (h s n) -> p h s n", h=2, s=2)
                h2 = lambda t_, i: t_[:, i * 256:(i + 1) * 256].rearrange("p (h n) -> p h n", h=2)
                m2 = lambda m_: m_.unsqueeze(1).to_broadcast([128, 2, 128])
                X0 = FT[3]
                for i in range(2):
                    P.dve(lambda e, i=i: e.tensor_tensor(out=h2(X0, i), in0=pd2(i)[:, :, 0, :], in1=m2(NSTRICT), op=ALU.mult), reads=["PF%d" % (2 + i), "cf"], writes=["FT3"])
                    P.dve(lambda e, hb=hb, i=i: e.tensor_tensor(out=h2(Yab[hb], i), in0=pd2(i)[:, :, 1, :], in1=m2(INCL), op=ALU.mult), reads=["PF%d" % (2 + i), "cf"], writes=["HT%d" % (14 + hb)])
                for hl in range(4):
                    h = hb * 4 + hl
                    pr, p0 = h // 2, 64 * (h % 2)
                    P.pe(lambda e, hl=hl, pr=pr, h=h: e.matmul(PF[2 + hl // 2][:, (hl % 2) * 256:(hl % 2 + 1) * 256], lhsT=bkh(h % 2, pr, tb, 1), rhs=AR[:, pr * 2 * TB + tb * 256: pr * 2 * TB + (tb + 1) * 256], start=True, stop=True), reads=["BAR", "BLK3", "BLK4"], writes=["PF%d" % (2 + hl // 2)])
                for i in range(2):
                    P.dve(lambda e, hb=hb, i=i: e.tensor_tensor(out=h2(Aak[hb], i), in0=pd2(i)[:, :, 0, :], in1=m2(STRICT), op=ALU.mult), reads=["PF%d" % (2 + i), "cf"], writes=["HT%d" % (18 + hb)])
                    P.dve(lambda e, hb=hb, i=i: e.tensor_tensor(out=h2(Yak[hb], i), in0=pd2(i)[:, :, 1, :], in1=m2(INCL), op=ALU.mult), reads=["PF%d" % (2 + i), "cf"], writes=["HT%d" % (16 + hb)])
                resB = dict(X=([HT[32], HT[33]], ["HT32", "HT33"]), Q=([HT[34], HT[35]], ["HT34", "HT35"]), N=([HT[36], HT[37]], ["HT36", "HT37"]),
                            SQ=(PQ[1], "PQ1"), G0=(PF[2], "PF2"), G1=(PF[3], "PF3"), T=(PB[1], "PB1"))
                TinvT, tk = inverse_chain(P, X0, "FT3", resB, IDF, idb)
                if hb == 0:
                    emit_tm()
                for hl in range(4):
                    h = hb * 4 + hl
                    pr, p0 = h // 2, 64 * (h % 2)
                    P.pe(lambda e, hl=hl, pr=pr: e.matmul(PF[2][:, hl * 128:(hl + 1) * 128], lhsT=aTM[:, pr * 128:(pr + 1) * 128], rhs=TinvT[:, hl * 128:(hl + 1) * 128], start=True, stop=True), reads=["HT0", tk], writes=["PF2"])
                    P.pe(lambda e, hl=hl, h=h, hb=hb: e.matmul(PF[3][:, hl * 64:(hl + 1) * 64], lhsT=Aak[hb][:, hl * 128:(hl + 1) * 128], rhs=vTM[:, h * 64:(h + 1) * 64], start=True, stop=True), reads=["HT%d" % (18 + hb), "HT27"], writes=["PF3"])
                for hh in range(2):
                    p0 = 64 * hh
                    src = PF[2][p0:p0 + 64, :].rearrange("p (a b n) -> p a b n", a=2, b=2)[:, :, hh, :]
                    dst = WmT[p0:p0 + 64, hb * 256:(hb + 1) * 256].rearrange("p (a n) -> p a n", a=2)
                    P.copy(dst, src, reads=["PF2"], writes=["HT30"])
                P.copy(AV[:, 0:256], PF[3][:, 0:256], reads=["PF3"], writes=["HT20"])
                for hl in range(4):
                    P.pe(lambda e, hl=hl: e.matmul(PF[3][:, 256 + hl * 64:256 + (hl + 1) * 64], lhsT=TinvT[:, hl * 128:(hl + 1) * 128], rhs=AV[:, hl * 64:(hl + 1) * 64], start=True, stop=True), reads=[tk, "HT20"], writes=["PF3"])
                P.copy(U0b[:, hb * 256:(hb + 1) * 256], PF[3][:, 256:512], reads=["PF3"], writes=["FT0"])
            for c in range(2):
                r0, r1 = c * 64, c * 64 + 64
                ci = tb * 2 + c
                hsl = lambda h: slice((h // 2) * 128 + (h % 2) * 64, (h // 2) * 128 + (h % 2) * 64 + 64)
                for h in range(8):
                    pr, p0 = h // 2, 64 * (h % 2)
                    P.pe(lambda e, h=h, pr=pr, p0=p0: e.matmul(PF[2][:, h * 64:(h + 1) * 64], lhsT=WmT[:, pr * 128:(pr + 1) * 128], rhs=Hbb[:, hsl(h)], start=True, stop=True), reads=["HT30", "Hbb"], writes=["PF2"])
                P.dve(lambda e, r0=r0, r1=r1: e.tensor_tensor(out=Ub[r0:r1, :], in0=U0b[r0:r1, :], in1=PF[2][r0:r1, :], op=ALU.add), reads=["FT0", "PF2"], writes=["HT21"])
                for h in range(8):
                    pr, p0, hb, hl = h // 2, 64 * (h % 2), h // 4, h % 4
                    P.pe(lambda e, h=h, pr=pr, p0=p0: e.matmul(PF[3][:, h * 64:(h + 1) * 64], lhsT=ar(pr, tb, 1), rhs=Hbb[:, hsl(h)], start=True, stop=False), reads=["BAR", "Hbb"], writes=["PF3"])
                    P.pe(lambda e, h=h, hb=hb, hl=hl, r0=r0, r1=r1: e.matmul(PF[3][:, h * 64:(h + 1) * 64], lhsT=Yab[hb][r0:r1, hl * 128:(hl + 1) * 128], rhs=Ub[r0:r1, h * 64:(h + 1) * 64], start=False, stop=False), reads=["HT%d" % (14 + hb), "HT21"], writes=["PF3"])
                    P.pe(lambda e, h=h, hb=hb, hl=hl, r0=r0, r1=r1: e.matmul(PF[3][:, h * 64:(h + 1) * 64], lhsT=Yak[hb][r0:r1, hl * 128:(hl + 1) * 128], rhs=vTM[r0:r1, h * 64:(h + 1) * 64], start=False, stop=True), reads=["HT%d" % (16 + hb), "HT27"], writes=["PF3"])
                P.copy(yb[r0:r1, :], PF[3][r0:r1, :], reads=["PF3"], writes=["FT1"])
                for pr in range(4):
                    P.pe(lambda e, pr=pr, r0=r0, r1=r1: e.matmul(PQ[1][:, pr * 128:(pr + 1) * 128], lhsT=bTM[r0:r1, pr * 128:(pr + 1) * 128], rhs=Ub[r0:r1, pr * 128:(pr + 1) * 128], start=True, stop=False), reads=["HT1", "HT21"], writes=["PQ1"])
                    P.pe(lambda e, pr=pr, r0=r0, r1=r1: e.matmul(PQ[1][:, pr * 128:(pr + 1) * 128], lhsT=kTM[r0:r1, pr * 128:(pr + 1) * 128], rhs=vTM[r0:r1, pr * 128:(pr + 1) * 128], start=False, stop=True), reads=["HT26", "HT27"], writes=["PQ1"])
                P.dve(lambda e: e.tensor_tensor(out=Hb[:, :], in0=Hb[:, :], in1=PQ[1][:, 0:512], op=ALU.add), reads=["Hb", "PQ1"], writes=["Hb"])
                pc3 = pce[:, :].rearrange("p (a c) -> p a c", a=4)[:, :, ci]
                P.dve(lambda e, pc3=pc3: e.tensor_tensor(out=b3(Hb[:, :], 4, 128), in0=b3(Hb[:, :], 4, 128), in1=pc3.unsqueeze(2).to_broadcast([128, 4, 128]), op=ALU.mult), reads=["Hb", "pce"], writes=["Hb"])
                P.pool(lambda e: e.tensor_tensor(out=b3(Hbb[:, :], 4, 128), in0=b3(Hb[:, :], 4, 128), in1=BL.unsqueeze(1).to_broadcast([128, 4, 128]), op=ALU.mult), reads=["Hb", "cf"], writes=["Hbb"])
            y3 = b3(yb[:, :], 8, 64)
            stt = FT[2]
            P.dve(lambda e: e.tensor_reduce(out=stt[:, 0:8], in_=y3, axis=AX.X, op=ALU.add), reads=["FT1"], writes=["FT2"])
            P.dve(lambda e: e.tensor_scalar(out=stt[:, 0:8], in0=stt[:, 0:8], scalar1=1.0 / 64, scalar2=None, op0=ALU.mult), reads=["FT2"], writes=["FT2"])
            P.dve(lambda e: e.tensor_tensor(out=y3, in0=y3, in1=stt[:, 0:8].unsqueeze(2).to_broadcast([128, 8, 64]), op=ALU.subtract), reads=["FT1", "FT2"], writes=["FT1"])
            sq2 = FT[0]
            P.pool(lambda e: e.tensor_tensor(out=sq2[:, :], in0=yb[:, :], in1=yb[:, :], op=ALU.mult), reads=["FT1"], writes=["FT0"])
            P.dve(lambda e: e.tensor_reduce(out=stt[:, 8:16], in_=b3(sq2[:, :], 8, 64), axis=AX.X, op=ALU.add), reads=["FT0"], writes=["FT2"])
            rsqrt_act(stt[:, 8:16], stt[:, 8:16], 1.0 / 64, 64e-5, ["FT2"], ["FT2"])
            P.dve(lambda e: e.tensor_tensor(out=y3, in0=y3, in1=stt[:, 8:16].unsqueeze(2).to_broadcast([128, 8, 64]), op=ALU.mult), reads=["FT1", "FT2"], writes=["FT1"])
            P.pool(lambda e: e.tensor_tensor(out=yb[:, :], in0=yb[:, :], in1=LNW, op=ALU.mult), reads=["FT1", "rp"], writes=["FT1"])
            P.pool(lambda e: e.tensor_tensor(out=yb[:, :], in0=yb[:, :], in1=LNB, op=ALU.add), reads=["FT1", "rp"], writes=["FT1"])
            P.dve(lambda e: e.tensor_tensor(out=yb[:, :], in0=yb[:, :], in1=bonTM[:, :], op=ALU.add), reads=["FT1", bonk], writes=["FT1"])
            P.dve(lambda e: e.tensor_tensor(out=mixB4[:, tb * 512:(tb + 1) * 512], in0=yb[:, :], in1=sgTM[:, :], op=ALU.mult), reads=["FT1", sgk], writes=["mixB4_%d" % tb])
        for tb in range(NTB):
            for c8 in range(8):
                if c8 < 4:
                    P.pe(lambda e, c8=c8: e.transpose(out=PB[1][:, c8 * 128:(c8 + 1) * 128], in_=mixA[:, tb * 512 + c8 * 128:tb * 512 + (c8 + 1) * 128], identity=idb[:, :]), reads=[mxk, "idb"], writes=["PB1"])
                else:
                    P.pe(lambda e, c8=c8: e.transpose(out=PB[1][:, c8 * 128:(c8 + 1) * 128], in_=mixB4[:, tb * 512 + (c8 - 4) * 128:tb * 512 + (c8 - 3) * 128], identity=idb[:, :]), reads=["mixB4_%d" % tb, "idb"], writes=["PB1"])
            P.dve(lambda e: e.tensor_copy(out=mixT[:, :], in_=PB[1][:, :]), reads=["PB1"], writes=["mixT"])
            hx = xt[tb % 2]
            hk = "xt%d" % (tb % 2)
            P.dma(hx[:, :], xp[t0 + tb * 128:t0 + (tb + 1) * 128, :], writes=[hk])
            for n in range(2):
                for kc in range(8):
                    P.pe(lambda e, n=n, kc=kc: e.matmul(PQ[1][:, :], lhsT=mixT[:, kc * 128:(kc + 1) * 128], rhs=woutb[:, kc * 1024 + n * 512:kc * 1024 + (n + 1) * 512], start=(kc == 0), stop=(kc == 7)), reads=["mixT", "woutb"], writes=["PQ1"])
                P.dve(lambda e, n=n, hx=hx: e.tensor_tensor(out=hx[:, n * 512:(n + 1) * 512], in0=hx[:, n * 512:(n + 1) * 512], in1=PQ[1][:, :], op=ALU.add), reads=[hk, "PQ1"], writes=[hk])
            P.act(lambda e, hx=hx: e.activation(out=xs_[:, :], in_=hx[:, :], func=AF.Square, accum_out=st4[:, 4:5]), reads=[hk], writes=["xs_", "st4"])
            rsqrt_act(st4[:, 4:5], st4[:, 4:5], 1.0 / D, 1e-6, ["st4"], ["st4"])
            P.dve(lambda e, hx=hx: e.scalar_tensor_tensor(out=xs_[:, :], in0=hx[:, :], scalar=st4[:, 4:5], in1=FNW, op0=ALU.mult, op1=ALU.mult), reads=[hk, "st4", "rp"], writes=["xs_"])
            P.dma(yp[t0 + tb * 128:t0 + (tb + 1) * 128, :], xs_[:, :], reads=["xs_"], writes=["yp%d_%d" % (blk, tb)])
        if last_blk:
            for pr in range(4):
                P.pe(lambda e, pr=pr: e.transpose(out=PF[2][:, pr * 128:(pr + 1) * 128], in_=Hb[:, pr * 128:(pr + 1) * 128], identity=IDF), reads=["Hb", "cf"], writes=["PF2"])
            P.dve(lambda e: e.tensor_copy(out=FT[4][:, :], in_=PF[2][:, :]), reads=["PF2"], writes=["FT4"])
            for h in range(8):
                pr, p0 = h // 2, 64 * (h % 2)
                P.dma(p_bmat[h], FT[4][p0:p0 + 64, pr * 128 + p0:pr * 128 + p0 + 64], reads=["FT4"], writes=["p_bmat%d" % h])
        stream_b = P.capture_end()
        if last_blk:
            P.replay([stream_b])

    if DECODE_LAST[0]:
        P.pool(lambda e: e.memset(st4[:, 7:8], 0.0), reads=["BAR", "BVB", "BSG"], writes=["SF%d" % i for i in range(8)] + ["st4"])
        sample_path(SAMPLE_LOCALS)
    P.finish()
    return nc


def inverse_chain(P, X0, x0key, res, IDF, idb):
    Xt, Xk = res["X"]
    Qt, Qk = res["Q"]
    Nt, Nk = res["N"]
    (SQ, sqk), (G0, g0k), (G1, g1k), (TT, ttk) = res["SQ"], res["G0"], res["G1"], res["T"]

    def mm4(out_ps, okey, lhs, lkey, rhs, rkey):
        for h in range(4):
            P.pe(lambda e, h=h: e.matmul(out_ps[:, h * 128:(h + 1) * 128], lhsT=lhs[:, h * 128:(h + 1) * 128], rhs=rhs[:, h * 128:(h + 1) * 128], start=True, stop=True), reads=[lkey, rkey], writes=[okey])

    P.copy(Xt[0][:, :], X0[:, :], reads=[x0key], writes=[Xk[0]])
    P.dve(lambda e: e.tensor_tensor(out=Qt[0][:, :].rearrange("p (h n) -> p h n", h=4), in0=IDF.unsqueeze(1).to_broadcast([128, 4, 128]), in1=X0[:, :].rearrange("p (h n) -> p h n", h=4), op=ALU.subtract), reads=["cf", x0key], writes=[Qk[0]])
    for h in range(4):
        P.pe(lambda e, h=h: e.transpose(out=TT[:, h * 128:(h + 1) * 128], in_=Xt[0][:, h * 128:(h + 1) * 128], identity=idb[:, :]), reads=[Xk[0], "idb"], writes=[ttk])
    P.dve(lambda e: e.tensor_copy(out=Nt[0][:, :], in_=TT[:, 0:512]), reads=[ttk], writes=[Nk[0]])
    mm4(SQ, sqk, Nt[0], Nk[0], Xt[0], Xk[0])
    mm4(G0, g0k, Xt[0], Xk[0], Nt[0], Nk[0])
    P.copy(Xt[1][:, :], SQ[:, 0:512], reads=[sqk], writes=[Xk[1]])
    P.copy(Nt[1][:, :], G0[:, 0:512], reads=[g0k], writes=[Nk[1]])
    xi, ni, qi = 1, 1, 0
    for m in range(5):
        mm4(G1, g1k, Nt[ni], Nk[ni], Qt[qi], Qk[qi])
        P.dve(lambda e, qi=qi: e.tensor_tensor(out=Qt[1 - qi][:, :], in0=Qt[qi][:, :], in1=G1[:, 0:512], op=ALU.add), reads=[Qk[qi], g1k], writes=[Qk[1 - qi]])
        qi = 1 - qi
        if m < 4:
            mm4(SQ, sqk, Nt[ni], Nk[ni], Xt[xi], Xk[xi])
            mm4(G0, g0k, Xt[xi], Xk[xi], Nt[ni], Nk[ni])
            P.copy(Xt[1 - xi][:, :], SQ[:, 0:512], reads=[sqk], writes=[Xk[1 - xi]])
            P.copy(Nt[1 - ni][:, :], G0[:, 0:512], reads=[g0k], writes=[Nk[1 - ni]])
            xi, ni = 1 - xi, 1 - ni
    return Qt[qi], Qk[qi]


_PQ = {}


def PBt_f32(PD, FT):
    return _PQ["t"]


def group_b_block(L):
    pass


def pack_inputs(inp):
    g = lambda k: np.asarray(inp[k], np.float32)
    vrows = np.zeros((128, 128), np.float32)
    vrows[0:8] = g("norm_w")[0].reshape(8, 128)
    cw = g("conv_w")[0]
    for c in range(12):
        for i in range(4):
            vrows[8 + c * 4 + i] = cw[i, c * 128:(c + 1) * 128]
    vrows[56:73] = g("mu")[0].reshape(17, 128)
    vrows[73:77] = g("w0")[0].reshape(4, 128)
    vrows[77:81] = g("a0")[0].reshape(4, 128)
    vrows[81:85] = g("k_k")[0].reshape(4, 128)
    vrows[85:89] = g("k_a")[0].reshape(4, 128)
    vrows[89:93] = g("r_k")[0].reshape(4, 128)
    rowp = np.concatenate([g("ln_w")[0], g("ln_b")[0], np.zeros(1024, np.float32), g("a_norm_w")[0],
                           g("final_norm_w"), g("a_log")[0], g("dt_bias")[0]]).astype(np.float32)
    p = np.arange(512)
    resetm = np.broadcast_to((p % 64 != 0).astype(np.float32), (128, 512)).copy()
    return dict(w_in=g("w_in")[0], w_out=g("w_out")[0], vrows=vrows, rowp=rowp, w2=g("w2")[0], a2=g("a2")[0],
                consts=host_consts(), resetm=resetm)


def kernel(**inputs):
    n = 8
    packed = pack_inputs(inputs)
    xp = np.ascontiguousarray(np.asarray(inputs["x_prompt"], np.float32))
    nc = build(T=2048, NS=16, TB=512)
    in_maps = []
    for i in range(n):
        m = dict(packed)
        m["xp"] = xp[i]
        sl = slice(i * 16, (i + 1) * 16)
        m["xs"] = np.ascontiguousarray(np.asarray(inputs["x_sample"], np.float32)[sl, 0])
        m["sa_mat"] = np.ascontiguousarray(np.asarray(inputs["state_a_mat"], np.float32)[0, sl])
        m["sa_conv"] = np.ascontiguousarray(np.asarray(inputs["state_a_conv"], np.float32)[0, sl])
        m["sb_mat"] = np.ascontiguousarray(np.asarray(inputs["state_b_mat"], np.float32)[0, sl])
        m["sb_shift"] = np.ascontiguousarray(np.asarray(inputs["state_b_shift"], np.float32)[0, sl])
        in_maps.append(m)
    res = run_bass_kernel_spmd(nc, in_maps, core_ids=list(range(n)))
    r = res.results
    f = lambda k: [np.asarray(r[i][k], np.float32) for i in range(n)]
    y_prompt = np.stack(f("yp"))
    p_amat = np.stack(f("p_amat"))[None]
    p_aconv = np.stack([a.reshape(3, 12, 128).reshape(3, 1536) for a in f("p_aconv")])[None]
    p_bmat = np.stack(f("p_bmat"))[None]
    p_bshift = np.stack([a.reshape(2176) for a in f("p_bshift")])[None]
    y_sample = np.concatenate(f("ys"))[:, None, :]
    s_amat = np.concatenate(f("s_amat"))[None]
    s_aconv = np.concatenate(f("s_aconv"))[None]
    s_bmat = np.concatenate(f("s_bmat"))[None]
    s_bshift = np.concatenate(f("s_bshift"))[None]
    return (y_prompt, y_sample, p_amat, p_aconv, p_bmat, p_bshift, s_amat, s_aconv, s_bmat, s_bshift)


def sample_path(L):
    P, NS = L["P"], L["NS"]
    PF, PQ, PB = L["PF"], L["PQ"], L["PB"]
    FT, HT, SF, SPJ, xnTs = L["FT"], L["HT"], L["SF"], L["SPJ"], L["xnTs"]
    xt, xs_, st4, pcol, col = L["xt"], L["xs_"], L["st4"], L["pcol"], L["col"]
    IDF, ONESF, BL, PAIRS, idb = L["IDF"], L["ONESF"], L["BL"], L["PAIRS"], L["idb"]
    wbf, w_in, woutb, w2a2 = L["wbf"], L["w_in"], L["woutb"], L["w2a2"]
    rsqrt_act, b3 = L["rsqrt_act"], L["b3"]
    LNW, LNB, ANW, FNW, DTB, nega = L["LNW"], L["LNB"], L["ANW"], L["FNW"], L["DTB"], L["nega"]
    NW0, CW0, MU0, W00, A00, KK0, KA0, RK0 = 0, 8, 56, 73, 77, 81, 85, 89
    xs_d, sa_mat_d, sa_conv_d, sb_mat_d, sb_shift_d = L["xs_d"], L["sa_mat_d"], L["sa_conv_d"], L["sb_mat_d"], L["sb_shift_d"]
    ys, s_amat, s_aconv, s_bmat, s_bshift = L["ys"], L["s_amat"], L["s_aconv"], L["s_bmat"], L["s_bshift"]
    scr_q, scr_k, scr_v, scr_ab, scr_o, scr_b6, scr_y = L["scr_q"], L["scr_k"], L["scr_v"], L["scr_ab"], L["scr_o"], L["scr_b6"], L["scr_y"]
    N = NS
    R = slice(0, N)

    xa = xt[0]
    P.dma(xa[R, :], xs_d[:, :], writes=["xt0"])
    P.act(lambda e: e.activation(out=xs_[R, :], in_=xa[R, :], func=AF.Square, accum_out=st4[R, 0:1]), reads=["xt0"], writes=["xs_", "st4"])
    rsqrt_act(st4[R, 0:1], st4[R, 0:1], 1.0 / D, 1e-6, ["st4"], ["st4"])
    P.act(lambda e: e.activation(out=xs_[R, :], in_=xa[R, :], func=AF.Copy, scale=st4[R, 0:1]), reads=["xt0", "st4"], writes=["xs_"])
    for half in range(2):
        pf, pk = PF[half], "PF%d" % half
        for q in range(4):
            kc = half * 4 + q
            P.pe(lambda e, pf=pf, q=q, kc=kc: e.transpose(out=pf[:, q * N:(q + 1) * N], in_=xs_[R, kc * 128:(kc + 1) * 128], identity=IDF[R, R]), reads=["xs_", "cf"], writes=[pk])
        P.dve(lambda e, pf=pf, half=half: e.tensor_tensor(out=xnTs[:, half * 4 * N:(half + 1) * 4 * N].rearrange("p (k t) -> p k t", k=4), in0=pf[:, 0:4 * N].rearrange("p (k t) -> p k t", k=4), in1=pcol[:, NW0 + half * 4:NW0 + half * 4 + 4].unsqueeze(2).to_broadcast([128, 4, N]), op=ALU.mult), reads=[pk, "pcol"], writes=["xnTs"])

    chunks = [(c * 128, 128) for c in range(16)] + [(2048, 8)] + [(2056 + j * 128, 128) for j in range(17)]
    for idx, (c0, ncols) in enumerate(chunks):
        s = idx % len(wbf)
        bk_ = "wbf%d" % s
        pf, pk = PF[idx % 2], "PF%d" % (idx % 2)
        P.dma(wbf[s][:, 0:8 * ncols].rearrange("p (k n) -> p k n", k=8), w_in[:, c0:c0 + ncols].rearrange("(k p) n -> p k n", p=128), writes=[bk_], q="pool")
        for kc in range(8):
            P.pe(lambda e, kc=kc, s=s, pf=pf, ncols=ncols: e.matmul(pf[0:ncols, 0:N], lhsT=wbf[s][:, kc * ncols:(kc + 1) * ncols], rhs=xnTs[:, kc * N:(kc + 1) * N], start=(kc == 0), stop=(kc == 7)), reads=[bk_, "xnTs"], writes=[pk])
        P.act(lambda e, pf=pf, ncols=ncols, idx=idx: e.activation(out=SPJ[0:ncols, idx * N:(idx + 1) * N], in_=pf[0:ncols, 0:N], func=AF.Copy), reads=[pk], writes=["SPJ"])
    sp = lambda i, j=None: SPJ[:, i * N:((i + 1) if j is None else j) * N]

    def to_tm(srcs, skey, dst, dkey, ps, pskey):
        for i, src in enumerate(srcs):
            P.pe(lambda e, i=i, src=src: e.transpose(out=ps[R, i * 128:(i + 1) * 128], in_=src, identity=IDF), reads=[skey, "cf"], writes=[pskey])
        n = len(srcs) * 128
        P.dve(lambda e: e.tensor_copy(out=dst[R, 0:n], in_=ps[R, 0:n]), reads=[pskey], writes=[dkey])

    cv = sa_conv_d.rearrange("b i c -> (b i) c")
    P.dma(xt[1][0:3 * N, 0:1024], cv[:, 0:1024], writes=["xt1"])
    P.dma(FT[0][0:3 * N, 0:512], cv[:, 1024:1536], writes=["FT0"])
    for c in range(12):
        src = xt[1][0:3 * N, c * 128:(c + 1) * 128] if c < 8 else FT[0][0:3 * N, (c - 8) * 128:(c - 7) * 128]
        po = c * 3 * N if c < 10 else 512 + (c - 10) * 3 * N
        pq_ = PQ[0] if c < 10 else PQ[1]
        po = po % 512
        P.pe(lambda e, c=c, src=src, po=po, pq_=pq_: e.transpose(out=pq_[:, po:po + 3 * N], in_=src, identity=IDF[0:3 * N, 0:3 * N]), reads=["xt1", "FT0", "cf"], writes=["PQ0", "PQ1"])
    P.dve(lambda e: e.tensor_copy(out=xs_[:, 0:30 * N], in_=PQ[0][:, 0:30 * N]), reads=["PQ0"], writes=["xs_"])
    P.dve(lambda e: e.tensor_copy(out=xs_[:, 30 * N:36 * N], in_=PQ[1][:, 0:6 * N]), reads=["PQ1"], writes=["xs_"])
    acc = FT[2]
    for c in range(12):
        cs3 = xs_[:, c * 3 * N:(c + 1) * 3 * N].rearrange("p (b i) -> p b i", i=3)
        tmp3 = FT[1][:, 0:3 * N].rearrange("p (b i) -> p b i", i=3)
        P.dve(lambda e, cs3=cs3, tmp3=tmp3, c=c: e.tensor_tensor(out=tmp3, in0=cs3, in1=pcol[:, CW0 + 4 * c:CW0 + 4 * c + 3].unsqueeze(1).to_broadcast([128, N, 3]), op=ALU.mult), reads=["xs_", "pcol"], writes=["FT1"])
        P.dve(lambda e, tmp3=tmp3, c=c: e.tensor_reduce(out=acc[:, c * N:(c + 1) * N], in_=tmp3, axis=AX.X, op=ALU.add), reads=["FT1"], writes=["FT2"])
        P.dve(lambda e, c=c: e.scalar_tensor_tensor(out=acc[:, c * N:(c + 1) * N], in0=sp(c), scalar=col(CW0 + 4 * c + 3), in1=acc[:, c * N:(c + 1) * N], op0=ALU.mult, op1=ALU.add), reads=["SPJ", "pcol", "FT2"], writes=["FT2"])
    P.act(lambda e: e.activation(out=acc[:, 0:12 * N], in_=acc[:, 0:12 * N], func=AF.Silu), reads=["FT2"], writes=["FT2"])
    P.act(lambda e: e.activation(out=FT[3][:, 0:8 * N], in_=acc[:, 0:8 * N], func=AF.Square), reads=["FT2"], writes=["FT3"])
    P.pe(lambda e: e.matmul(PF[2][:, 0:8 * N], lhsT=ONESF, rhs=FT[3][:, 0:8 * N], start=True, stop=True), reads=["FT3", "cf"], writes=["PF2"])
    rsqrt_act(FT[3][:, 0:8 * N], PF[2][:, 0:8 * N], 1.0, 1e-12, ["PF2"], ["FT3"])
    P.dve(lambda e: e.scalar_tensor_tensor(out=acc[:, 0:4 * N], in0=acc[:, 0:4 * N], scalar=128 ** -0.5, in1=FT[3][:, 0:4 * N], op0=ALU.mult, op1=ALU.mult), reads=["FT2", "FT3"], writes=["FT2"])
    P.dve(lambda e: e.tensor_tensor(out=acc[:, 4 * N:8 * N], in0=acc[:, 4 * N:8 * N], in1=FT[3][:, 4 * N:8 * N], op=ALU.mult), reads=["FT2", "FT3"], writes=["FT2"])
    for g, (scr, sf, sname) in enumerate(((scr_q, 0, "scr_q"), (scr_k, 1, "scr_k"), (scr_v, 2, "scr_v"))):
        to_tm([acc[:, (g * 4 + i) * N:(g * 4 + i + 1) * N] for i in range(4)], "FT2", SF[sf], "SF%d" % sf, PF[3], "PF3")
        P.dma(scr[:, :], SF[sf][R, 0:512], reads=["SF%d" % sf], writes=[sname])
    P.dma(s_aconv[:, 0:2, :], sa_conv_d[:, 1:3, :], writes=["s_aconv"])
    for g in range(3):
        to_tm([sp(g * 4 + i) for i in range(4)], "SPJ", SF[3], "SF3", PF[3], "PF3")
        P.dma(s_aconv[:, 2, g * 512:(g + 1) * 512], SF[3][R, 0:512], reads=["SF3"], writes=["s_aconv2_%d" % g])
    sga = FT[3]
    P.act(lambda e: e.activation(out=sga[:, 0:4 * N], in_=sp(12, 16), func=AF.Silu), reads=["SPJ"], writes=["FT3"])
    sgaTM = SF[4]
    to_tm([sga[:, i * N:(i + 1) * N] for i in range(4)], "FT3", sgaTM, "SF4", PF[3], "PF3")
    sc = SF[5]
    P.pe(lambda e: e.transpose(out=PF[3][R, 0:8], in_=SPJ[0:8, 16 * N:17 * N], identity=IDF[0:8, 0:8]), reads=["SPJ", "cf"], writes=["PF3"])
    ab3 = sc[R, 16:24].rearrange("p (h s) -> p h s", s=2)
    P.act(lambda e: e.activation(out=ab3[:, :, 1], in_=PF[3][R, 0:4], func=AF.Sigmoid), reads=["PF3"], writes=["SF5"])
    P.dve(lambda e: e.tensor_tensor(out=sc[R, 4:8], in0=PF[3][R, 4:8], in1=DTB[R, :], op=ALU.add), reads=["PF3", "rp"], writes=["SF5"])
    P.act(lambda e: e.activation(out=sc[R, 4:8], in_=sc[R, 4:8], func=AF.Exp), reads=["SF5"], writes=["SF5"])
    P.act(lambda e: e.activation(out=sc[R, 4:8], in_=sc[R, 4:8], func=AF.Ln, bias=1.0), reads=["SF5"], writes=["SF5"])
    P.dve(lambda e: e.tensor_tensor(out=sc[R, 4:8], in0=sc[R, 4:8], in1=nega[R, :], op=ALU.mult), reads=["SF5", "nega"], writes=["SF5"])
    P.act(lambda e: e.activation(out=ab3[:, :, 0], in_=sc[R, 4:8], func=AF.Exp), reads=["SF5"], writes=["SF5"])
    P.dma(scr_ab[:, :], sc[R, 16:24], reads=["SF5"], writes=["scr_ab"])
    VP = FT[8]
    kP, qP, vP, abP = VP[:, 0:64], VP[:, 64:128], VP[:, 128:256], VP[:, 256:260]
    for hi in range(2):
        hs = slice(hi * 64, hi * 64 + 64)
        P.dma(VP[hs, 0:64], scr_k.rearrange("b (h t l) -> (b h) t l", h=4, t=2)[:, hi, :], reads=["scr_k"], writes=["FT8"])
        P.dma(VP[hs, 64:128], scr_q.rearrange("b (h t l) -> (b h) t l", h=4, t=2)[:, hi, :], reads=["scr_q"], writes=["FT8"])
        P.dma(VP[hs, 128:256], scr_v.rearrange("b (h v) -> (b h) v", h=4), reads=["scr_v"], writes=["FT8"])
        P.dma(VP[hs, 256:258], scr_ab.rearrange("b (h s) -> (b h) s", s=2), reads=["scr_ab"], writes=["FT8"])
    P.dve(lambda e: e.tensor_scalar(out=VP[:, 258:259], in0=VP[:, 256:257], scalar1=-1.0, scalar2=None, op0=ALU.mult), reads=["FT8"], writes=["FT8"])
    sav = sa_mat_d.rearrange("b h (t l) v -> (b h) t l v", t=2)
    sov = s_amat.rearrange("b h (t l) v -> (b h) t l v", t=2)
    SLs, prods, red, accs = [(FT[4], "FT4"), (FT[3], "FT3")], [(FT[5], "FT5"), (FT[6], "FT6")], FT[2], FT[7]
    kS, uu, oacc = accs[:, 0:128], accs[:, 128:256], accs[:, 256:384]
    sl3 = lambda t_: t_[:, :].rearrange("p (l v) -> p l v", l=4)
    P.pool(lambda e: e.memset(accs[:, :], 0.0), writes=["FT7"])
    NSL = 16
    for j in range(NSL):
        (SL, slk), (prod, pdk) = SLs[j % 2], prods[j % 2]
        for hi in range(2):
            P.dma(sl3(SL)[hi * 64:hi * 64 + 64], sav[:, hi, 4 * j:4 * j + 4, :], writes=[slk])
        P.dve(lambda e, j=j: e.tensor_tensor(out=sl3(prod), in0=sl3(SL), in1=kP[:, 4 * j:4 * j + 4].unsqueeze(2).to_broadcast([128, 4, 128]), op=ALU.mult), reads=[slk, "FT8"], writes=[pdk])
        P.pool(lambda e, prod=prod: e.tensor_tensor(out=prod[:, 0:256], in0=prod[:, 0:256], in1=prod[:, 256:512], op=ALU.add), reads=[pdk], writes=[pdk])
        P.pool(lambda e, prod=prod: e.tensor_tensor(out=prod[:, 0:128], in0=prod[:, 0:128], in1=prod[:, 128:256], op=ALU.add), reads=[pdk], writes=[pdk])
        P.dve(lambda e: e.tensor_tensor(out=kS, in0=kS, in1=prod[:, 0:128], op=ALU.add), reads=["FT7", pdk], writes=["FT7"])
    P.pe(lambda e: e.matmul(PF[2][:, 0:128], lhsT=PAIRS, rhs=kS, start=True, stop=True), reads=["FT7", "cf"], writes=["PF2"])
    P.dve(lambda e: e.scalar_tensor_tensor(out=uu, in0=PF[2][:, 0:128], scalar=VP[:, 258:259], in1=vP, op0=ALU.mult, op1=ALU.add), reads=["PF2", "FT8", "FT7"], writes=["FT7"])
    P.dve(lambda e: e.tensor_scalar(out=uu, in0=uu, scalar1=VP[:, 257:258], scalar2=None, op0=ALU.mult), reads=["FT7", "FT8"], writes=["FT7"])
    for j in range(NSL):
        (SL, slk), (prod, pdk) = SLs[j % 2], prods[j % 2]
        for hi in range(2):
            P.dma(sl3(SL)[hi * 64:hi * 64 + 64], sav[:, hi, 4 * j:4 * j + 4, :], writes=[slk])
        P.pool(lambda e, j=j: e.tensor_tensor(out=sl3(prod), in0=kP[:, 4 * j:4 * j + 4].unsqueeze(2).to_broadcast([128, 4, 128]), in1=uu.unsqueeze(1).to_broadcast([128, 4, 128]), op=ALU.mult), reads=["FT8", "FT7"], writes=[pdk])
        P.dve(lambda e: e.scalar_tensor_tensor(out=SL[:, :], in0=SL[:, :], scalar=VP[:, 256:257], in1=prod[:, :], op0=ALU.mult, op1=ALU.add), reads=[slk, pdk, "FT8"], writes=[slk])
        for hi in range(2):
            P.dma(sov[:, hi, 4 * j:4 * j + 4, :], sl3(SL)[hi * 64:hi * 64 + 64], reads=[slk], writes=["s_amat%d_%d" % (j, hi)])
        P.dve(lambda e, j=j: e.tensor_tensor(out=sl3(prod), in0=sl3(SL), in1=qP[:, 4 * j:4 * j + 4].unsqueeze(2).to_broadcast([128, 4, 128]), op=ALU.mult), reads=[slk, "FT8"], writes=[pdk])
        P.pool(lambda e, prod=prod: e.tensor_tensor(out=prod[:, 0:256], in0=prod[:, 0:256], in1=prod[:, 256:512], op=ALU.add), reads=[pdk], writes=[pdk])
        P.pool(lambda e, prod=prod: e.tensor_tensor(out=prod[:, 0:128], in0=prod[:, 0:128], in1=prod[:, 128:256], op=ALU.add), reads=[pdk], writes=[pdk])
        P.dve(lambda e: e.tensor_tensor(out=oacc, in0=oacc, in1=prod[:, 0:128], op=ALU.add), reads=["FT7", pdk], writes=["FT7"])
    P.pe(lambda e: e.matmul(PF[2][:, 0:128], lhsT=PAIRS, rhs=oacc, start=True, stop=True), reads=["FT7", "cf"], writes=["PF2"])
    P.dve(lambda e: e.tensor_copy(out=red[0:64, 0:128], in_=PF[2][0:64, 0:128]), reads=["PF2"], writes=["FT2"])
    P.dma(scr_o[:, :], red[0:64, 0:128], reads=["FT2"], writes=["scr_o"])
    osb = SF[6]
    P.dma(osb[R, 0:512], scr_o.rearrange("(b h) v -> b (h v)", h=4), reads=["scr_o"], writes=["SF6"])
    sq = SF[7]
    P.pool(lambda e: e.tensor_tensor(out=sq[R, :], in0=osb[R, :], in1=osb[R, :], op=ALU.mult), reads=["SF6"], writes=["SF7"])
    P.dve(lambda e: e.tensor_reduce(out=sc[R, 40:44], in_=b3(sq[R, :], 4, 128), axis=AX.X, op=ALU.add), reads=["SF7"], writes=["SF5"])
    rsqrt_act(sc[R, 40:44], sc[R, 40:44], 1.0 / 128, 1e-6, ["SF5"], ["SF5"])
    P.dve(lambda e: e.tensor_tensor(out=b3(osb[R, :], 4, 128), in0=b3(osb[R, :], 4, 128), in1=sc[R, 40:44].unsqueeze(2).to_broadcast([N, 4, 128]), op=ALU.mult), reads=["SF6", "SF5"], writes=["SF6"])
    P.dve(lambda e: e.tensor_tensor(out=b3(osb[R, :], 4, 128), in0=b3(osb[R, :], 4, 128), in1=ANW[R, :].unsqueeze(1).to_broadcast([N, 4, 128]), op=ALU.mult), reads=["SF6", "rp"], writes=["SF6"])
    mixs = HT[22]
    P.dve(lambda e: e.tensor_tensor(out=mixs[R, :], in0=osb[R, :], in1=sgaTM[R, :], op=ALU.mult), reads=["SF6", "SF4"], writes=["HT22"])

    pbT = lambda j, k=None: SPJ[:, (17 + j) * N:(17 + (j + 1 if k is None else k)) * N]
    for g in range(5):
        n = 4 if g < 4 else 1
        P.dma(SF[0][R, 0:n * 128], sb_shift_d[:, g * 512:g * 512 + n * 128], writes=["SF0"])
        for i in range(n):
            P.pe(lambda e, g=g, i=i: e.transpose(out=PF[2][:, (g * 4 + i) * N:(g * 4 + i + 1) * N], in_=SF[0][R, i * 128:(i + 1) * 128], identity=IDF[R, R]), reads=["SF0", "cf"], writes=["PF2"])
        to_tm([pbT(g * 4 + i) for i in range(n)], "SPJ", SF[1], "SF1", PF[3], "PF3")
        P.dma(s_bshift[:, g * 512:g * 512 + n * 128], SF[1][R, 0:n * 128], reads=["SF1"], writes=["s_bshift%d" % g])
    xb = FT[9]
    xbj = lambda j, k=None: xb[:, j * N:(j + 1 if k is None else k) * N]
    mu3 = pcol[:, MU0:MU0 + 17].unsqueeze(2).to_broadcast([128, 17, N])
    x3 = xb[:, 0:17 * N].rearrange("p (j t) -> p j t", j=17)
    pb3 = SPJ[:, 17 * N:34 * N].rearrange("p (j t) -> p j t", j=17)
    P.dve(lambda e: e.tensor_tensor(out=xb[:, 0:17 * N], in0=PF[2][:, 0:17 * N], in1=SPJ[:, 17 * N:34 * N], op=ALU.subtract), reads=["PF2", "SPJ"], writes=["FT9"])
    P.dve(lambda e: e.tensor_tensor(out=x3, in0=x3, in1=mu3, op=ALU.mult), reads=["FT9", "pcol"], writes=["FT9"])
    P.dve(lambda e: e.tensor_tensor(out=xb[:, 0:17 * N], in0=xb[:, 0:17 * N], in1=SPJ[:, 17 * N:34 * N], op=ALU.add), reads=["FT9", "SPJ"], writes=["FT9"])
    P.act(lambda e: e.activation(out=xb[0:64, 16 * N:17 * N], in_=xb[0:64, 16 * N:17 * N], func=AF.Tanh), reads=["FT9"], writes=["FT9"])
    W = FT[0]
    wq = lambda qi, pr=None: W[:, (qi * 4 + (0 if pr is None else pr)) * N:(qi * 4 + (4 if pr is None else pr + 1)) * N]
    SG, AIC, KK, KM, BB, BON, SGT, WD = 0, 1, 2, 3, 4, 5, 6, 7
    for pr in range(4):
        P.pe(lambda e, pr=pr: e.matmul(PF[2][:, pr * N:(pr + 1) * N], lhsT=w2a2[0:64, pr * 128:(pr + 1) * 128], rhs=xb[0:64, 16 * N:17 * N], start=True, stop=True), reads=["w2a2", "FT9"], writes=["PF2"])
        P.pe(lambda e, pr=pr: e.matmul(PF[2][:, (4 + pr) * N:(5 + pr) * N], lhsT=w2a2[64:128, pr * 128:(pr + 1) * 128], rhs=xb[64:128, 16 * N:17 * N], start=True, stop=True), reads=["w2a2", "FT9"], writes=["PF2"])
    for pr in range(4):
        P.act(lambda e, pr=pr: e.activation(out=wq(SG, pr), in_=PF[2][:, pr * N:(pr + 1) * N], func=AF.Sigmoid, bias=col(W00 + pr)), reads=["PF2", "pcol"], writes=["FT0"])
        P.act(lambda e, pr=pr: e.activation(out=wq(AIC, pr), in_=PF[2][:, (4 + pr) * N:(5 + pr) * N], func=AF.Sigmoid, bias=col(A00 + pr)), reads=["PF2", "pcol"], writes=["FT0"])
    P.act(lambda e: e.activation(out=wq(WD), in_=wq(SG), func=AF.Exp, scale=-C0), reads=["FT0"], writes=["FT0"])
    pc3 = lambda c0_: pcol[:, c0_:c0_ + 4].unsqueeze(2).to_broadcast([128, 4, N])
    q3 = lambda ap: ap.rearrange("p (a t) -> p a t", a=4)
    rT, kT, vT, gT = xbj(0, 4), xbj(4, 8), xbj(8, 12), xbj(12, 16)
    P.dve(lambda e: e.scalar_tensor_tensor(out=q3(wq(KM)), in0=q3(wq(AIC)), scalar=-1.0, in1=pc3(KA0), op0=ALU.add, op1=ALU.mult), reads=["FT0", "pcol"], writes=["FT0"])
    P.dve(lambda e: e.scalar_tensor_tensor(out=wq(KM), in0=wq(KM), scalar=1.0, in1=kT, op0=ALU.add, op1=ALU.mult), reads=["FT0", "FT9"], writes=["FT0"])
    P.dve(lambda e: e.tensor_tensor(out=q3(wq(KK)), in0=q3(kT), in1=pc3(KK0), op=ALU.mult), reads=["FT9", "pcol"], writes=["FT0"])
    P.act(lambda e: e.activation(out=wq(BB), in_=wq(KK), func=AF.Square), reads=["FT0"], writes=["FT0"])
    P.pe(lambda e: e.matmul(PF[3][:, 0:4 * N], lhsT=BL, rhs=wq(BB), start=True, stop=True), reads=["FT0", "cf"], writes=["PF3"])
    rsqrt_act(wq(BB), PF[3][:, 0:4 * N], 1.0, 1e-12, ["PF3"], ["FT0"])
    P.dve(lambda e: e.tensor_tensor(out=wq(KK), in0=wq(KK), in1=wq(BB), op=ALU.mult), reads=["FT0"], writes=["FT0"])
    P.dve(lambda e: e.tensor_tensor(out=wq(BB), in0=wq(KK), in1=wq(AIC), op=ALU.mult), reads=["FT0"], writes=["FT0"])
    P.dve(lambda e: e.tensor_tensor(out=q3(wq(BON)), in0=q3(rT), in1=pc3(RK0), op=ALU.mult), reads=["FT9", "pcol"], writes=["FT0"])
    P.dve(lambda e: e.tensor_tensor(out=wq(BON), in0=wq(BON), in1=wq(KM), op=ALU.mult), reads=["FT0"], writes=["FT0"])
    P.pe(lambda e: e.matmul(PF[3][:, 0:4 * N], lhsT=BL, rhs=wq(BON), start=True, stop=True), reads=["FT0", "cf"], writes=["PF3"])
    P.dve(lambda e: e.tensor_tensor(out=wq(BON), in0=PF[3][:, 0:4 * N], in1=vT, op=ALU.mult), reads=["PF3", "FT9"], writes=["FT0"])
    P.act(lambda e: e.activation(out=wq(SGT), in_=gT, func=AF.Silu), reads=["FT9"], writes=["FT0"])
    P.dve(lambda e: e.tensor_scalar(out=wq(KK), in0=wq(KK), scalar1=-1.0, scalar2=None, op0=ALU.mult), reads=["FT0"], writes=["FT0"])
    tmsrc = [(wq(WD), "FT0"), (wq(KK), "FT0"), (wq(BB), "FT0"), (wq(KM), "FT0"), (rT, "FT9"), (vT, "FT9")]
    for i, (ap_, key_) in enumerate(tmsrc):
        sf = i % 2
        to_tm([ap_[:, pr * N:(pr + 1) * N] for pr in range(4)], key_, SF[sf], "SF%d" % sf, PF[3], "PF3")
        P.dma(scr_b6[i][:, :], SF[sf][R, 0:512], reads=["SF%d" % sf], writes=["scr_b%d" % i])
    bonTM, sgTM = SF[2], SF[3]
    to_tm([wq(BON, pr) for pr in range(4)], "FT0", bonTM, "SF2", PF[3], "PF3")
    to_tm([wq(SGT, pr) for pr in range(4)], "FT0", sgTM, "SF3", PF[3], "PF3")
    V6 = FT[1]
    for i in range(6):
        P.dma(V6[:, i * 64:(i + 1) * 64], scr_b6[i].rearrange("b (h k) -> (b h) k", h=8), reads=["scr_b%d" % i], writes=["FT1"])
    wP, aP, bP, kP2, rP, vP2 = [V6[:, i * 64:(i + 1) * 64] for i in range(6)]
    sbv = sb_mat_d.rearrange("b h v k -> (b h) (v k)")
    sbo = s_bmat.rearrange("b h v k -> (b h) (v k)")
    S1s, T1s, sa_t, yP = [(FT[4], "FT4"), (FT[3], "FT3")], [(FT[5], "FT5"), (FT[6], "FT6")], FT[2], FT[7]
    v8 = lambda t_: t_[:, :].rearrange("p (v k) -> p v k", v=8)
    kb = lambda ap: ap.unsqueeze(1).to_broadcast([128, 8, 64])
    for j in range(8):
        vsl = slice(8 * j, 8 * j + 8)
        (S1, s1k), (T1, t1k) = S1s[j % 2], T1s[j % 2]
        P.dma(S1[:, :], sbv[:, j * 512:(j + 1) * 512], writes=[s1k])
        P.pool(lambda e, S1=S1, T1=T1: e.tensor_tensor(out=v8(T1), in0=v8(S1), in1=kb(aP), op=ALU.mult), reads=[s1k, "FT1"], writes=[t1k])
        P.dve(lambda e, S1=S1, T1=T1: e.tensor_reduce(out=sa_t[:, 0:8], in_=v8(T1), axis=AX.X, op=ALU.add), reads=[t1k], writes=["FT2"])
        P.dve(lambda e, S1=S1, T1=T1: e.tensor_tensor(out=v8(S1), in0=v8(S1), in1=kb(wP), op=ALU.mult), reads=[s1k, "FT1"], writes=[s1k])
        P.pool(lambda e, S1=S1, T1=T1: e.tensor_tensor(out=v8(T1), in0=sa_t[:, 0:8].unsqueeze(2).to_broadcast([128, 8, 64]), in1=kb(bP), op=ALU.mult), reads=["FT2", "FT1"], writes=[t1k])
        P.dve(lambda e, S1=S1, T1=T1: e.tensor_tensor(out=S1[:, :], in0=S1[:, :], in1=T1[:, :], op=ALU.add), reads=[s1k, t1k], writes=[s1k])
        P.pool(lambda e, vsl=vsl, S1=S1, T1=T1: e.tensor_tensor(out=v8(T1), in0=vP2[:, vsl].unsqueeze(2).to_broadcast([128, 8, 64]), in1=kb(kP2), op=ALU.mult), reads=["FT1"], writes=[t1k])
        P.dve(lambda e, S1=S1, T1=T1: e.tensor_tensor(out=S1[:, :], in0=S1[:, :], in1=T1[:, :], op=ALU.add), reads=[s1k, t1k], writes=[s1k])
        P.dma(sbo[:, j * 512:(j + 1) * 512], S1[:, :], reads=[s1k], writes=["s_bmat%d" % j])
        P.pool(lambda e, S1=S1, T1=T1: e.tensor_tensor(out=v8(T1), in0=v8(S1), in1=kb(rP), op=ALU.mult), reads=[s1k, "FT1"], writes=[t1k])
        P.dve(lambda e, vsl=vsl, S1=S1, T1=T1: e.tensor_reduce(out=yP[:, vsl], in_=v8(T1), axis=AX.X, op=ALU.add), reads=[t1k], writes=["FT7"])
    P.dma(scr_y[:, :], yP[:, 0:64], reads=["FT7"], writes=["scr_y"])
    yb = SF[4]
    P.dma(yb[R, 0:512], scr_y.rearrange("(b h) v -> b (h v)", h=8), reads=["scr_y"], writes=["SF4"])
    y3 = b3(yb[R, :], 8, 64)
    stt = SF[5]
    P.dve(lambda e: e.tensor_reduce(out=stt[R, 0:8], in_=y3, axis=AX.X, op=ALU.add), reads=["SF4"], writes=["SF5"])
    P.dve(lambda e: e.tensor_scalar(out=stt[R, 0:8], in0=stt[R, 0:8], scalar1=1.0 / 64, scalar2=None, op0=ALU.mult), reads=["SF5"], writes=["SF5"])
    P.dve(lambda e: e.tensor_tensor(out=y3, in0=y3, in1=stt[R, 0:8].unsqueeze(2).to_broadcast([N, 8, 64]), op=ALU.subtract), reads=["SF4", "SF5"], writes=["SF4"])
    sq2 = SF[7]
    P.pool(lambda e: e.tensor_tensor(out=sq2[R, :], in0=yb[R, :], in1=yb[R, :], op=ALU.mult), reads=["SF4"], writes=["SF7"])
    P.dve(lambda e: e.tensor_reduce(out=stt[R, 8:16], in_=b3(sq2[R, :], 8, 64), axis=AX.X, op=ALU.add), reads=["SF7"], writes=["SF5"])
    rsqrt_act(stt[R, 8:16], stt[R, 8:16], 1.0 / 64, 64e-5, ["SF5"], ["SF5"])
    P.dve(lambda e: e.tensor_tensor(out=y3, in0=y3, in1=stt[R, 8:16].unsqueeze(2).to_broadcast([N, 8, 64]), op=ALU.mult), reads=["SF4", "SF5"], writes=["SF4"])
    P.dve(lambda e: e.tensor_tensor(out=yb[R, :], in0=yb[R, :], in1=LNW[R, :], op=ALU.mult), reads=["SF4", "rp"], writes=["SF4"])
    P.dve(lambda e: e.tensor_tensor(out=yb[R, :], in0=yb[R, :], in1=LNB[R, :], op=ALU.add), reads=["SF4", "rp"], writes=["SF4"])
    P.dve(lambda e: e.tensor_tensor(out=yb[R, :], in0=yb[R, :], in1=bonTM[R, :], op=ALU.add), reads=["SF4", "SF2"], writes=["SF4"])
    mixb = HT[23]
    P.dve(lambda e: e.tensor_tensor(out=mixb[R, :], in0=yb[R, :], in1=sgTM[R, :], op=ALU.mult), reads=["SF4", "SF3"], writes=["HT23"])
    for c8 in range(8):
        src = mixs[R, c8 * 128:(c8 + 1) * 128] if c8 < 4 else mixb[R, (c8 - 4) * 128:(c8 - 3) * 128]
        P.pe(lambda e, c8=c8, src=src: e.transpose(out=PB[1][:, c8 * N:(c8 + 1) * N], in_=src, identity=idb[R, R]), reads=["HT22", "HT23", "idb"], writes=["PB1"])
    mixTs = HT[24]
    P.dve(lambda e: e.tensor_copy(out=mixTs[:, 0:8 * N], in_=PB[1][:, 0:8 * N]), reads=["PB1"], writes=["HT24"])
    for n in range(2):
        for kc in range(8):
            P.pe(lambda e, n=n, kc=kc: e.matmul(PF[n][R, :], lhsT=mixTs[:, kc * N:(kc + 1) * N], rhs=woutb[:, kc * 1024 + n * 512:kc * 1024 + (n + 1) * 512], start=(kc == 0), stop=(kc == 7)), reads=["HT24", "woutb"], writes=["PF%d" % n])
        P.dve(lambda e, n=n: e.tensor_tensor(out=xa[R, n * 512:(n + 1) * 512], in0=xa[R, n * 512:(n + 1) * 512], in1=PF[n][R, :], op=ALU.add), reads=["xt0", "PF%d" % n], writes=["xt0"])
    P.act(lambda e: e.activation(out=xs_[R, :], in_=xa[R, :], func=AF.Square, accum_out=st4[R, 4:5]), reads=["xt0"], writes=["xs_", "st4"])
    rsqrt_act(st4[R, 4:5], st4[R, 4:5], 1.0 / D, 1e-6, ["st4"], ["st4"])
    P.dve(lambda e: e.scalar_tensor_tensor(out=xs_[R, :], in0=xa[R, :], scalar=st4[R, 4:5], in1=FNW[R, :], op0=ALU.mult, op1=ALU.mult), reads=["xt0", "st4", "rp"], writes=["xs_"])
    P.dma(ys[:, :], xs_[R, :], reads=["xs_"], writes=["ys"])
```

```python
import contextlib
import numpy as np
import concourse.bass as bass
import concourse.mybir as mybir
from concourse.bass_utils import run_bass_kernel_spmd

F32 = mybir.dt.float32
BF16 = mybir.dt.bfloat16
ALU = mybir.AluOpType
AF = mybir.ActivationFunctionType
AX = mybir.AxisListType


class _Rec:
    def __init__(self):
        self.call = None

    def __getattr__(self, name):
        def f(*a, **k):
            self.call = (name, a, k)
            return self
        return f


class Prog:
    ENGS = ["pe", "dve", "act", "pool", "sp"]

    def __init__(self, nc, n_dma_sems=24):
        self.nc = nc
        self.stack = contextlib.ExitStack()
        self.items = {e: [] for e in self.ENGS}
        self.cnt = {e: 0 for e in self.ENGS}
        self.sem = {e: self.stack.enter_context(nc.semaphore("s_" + e)) for e in ["pe", "dve", "act", "pool"]}
        self.dsem = [self.stack.enter_context(nc.semaphore("d%d" % i)) for i in range(n_dma_sems)]
        self.dval = [0] * n_dma_sems
        self.dma_i = 0
        self.n_sw = 8
        self.sw_i = 0
        self.seen = {e: {} for e in self.ENGS}
        self.lastw = {}
        self.readers = {}
        self.n_ops = 0
        self.capture = None
        self.oplist = []
        self.warm_ops = None
        self.n_fill = 0

    def sb(self, name, shape, dtype):
        return self.stack.enter_context(self.nc.sbuf_tensor(name, list(shape), dtype))

    def ps(self, name, shape, dtype):
        return self.stack.enter_context(self.nc.psum_tensor(name, list(shape), dtype))

    _VEC_OPS = ("tensor_tensor", "tensor_copy", "memset")

    def op(self, eng, fn, reads=(), writes=(), is_dma=False, alts=None):
        rec = _Rec()
        fn(rec)
        al = {}
        if alts:
            for e2, fn2 in alts:
                r2 = _Rec()
                fn2(r2)
                al[e2] = r2.call
        if ALT[0] and not is_dma and eng in ("dve", "pool") and rec.call[0] in self._VEC_OPS \
                and not any(k[:2] in ("PF", "PQ", "PB") for k in tuple(reads) + tuple(writes)):
            al.setdefault("pool" if eng == "dve" else "dve", rec.call)
        item = (eng, rec.call, tuple(reads), tuple(writes), is_dma, al)
        if self.capture is not None:
            self.capture.append(item)
            return
        self.oplist.append(item)

    def copy(self, out, in_, reads=(), writes=()):
        self.op("dve", lambda e: e.tensor_copy(out=out, in_=in_), reads, writes,
                alts=[("act", lambda e: e.activation(out=out, in_=in_, func=AF.Copy))] if ALT[0] else None)

    def capture_begin(self):
        self.capture = []

    def capture_end(self):
        c, self.capture = self.capture, None
        return c

    def replay(self, streams):
        items = []
        for si, st in enumerate(streams):
            n = max(len(st), 1)
            for i, it in enumerate(st):
                items.append(((i + 0.5) / n, si, i, it))
        items.sort(key=lambda t: (t[0], t[1], t[2]))
        for _, _, _, it in items:
            self.oplist.append(it)

    def _op(self, eng, call, reads=(), writes=(), is_dma=False):
        if self.n_ops >= MAXOPS[0]:
            return
        need = {}

        def add(tok, same_ok):
            if tok is None:
                return
            key, h, v, teng = tok
            if teng == eng and eng == "pe" and not is_dma_tok(tok) and not same_ok:
                return
            if need.get(key, (None, 0))[1] < v:
                need[key] = (h, v)

        def is_dma_tok(tok):
            return tok[0].startswith("d#")

        for k in reads:
            add(self.lastw.get(k), True)
        for k in writes:
            add(self.lastw.get(k), False)
            for tok in self.readers.get(k, {}).values():
                add(tok, False)
        if is_dma:
            nh = len(self.dsem) - self.n_sw
            if eng == "pool":
                slot = nh + self.sw_i % self.n_sw
                self.sw_i += 1
            else:
                slot = self.dma_i % nh
                self.dma_i += 1
            if self.dval[slot] > 0:
                add(("d#%d" % slot, self.dsem[slot], self.dval[slot], None), True)
            self.dval[slot] += 16
            tok = ("d#%d" % slot, self.dsem[slot], self.dval[slot], None)
            inc = 16
        else:
            self.cnt[eng] += 1
            tok = (eng, self.sem[eng], self.cnt[eng], eng)
            inc = 1
        waits = []
        for key, (h, v) in need.items():
            if self.seen[eng].get(key, 0) < v:
                self.seen[eng][key] = v
                waits.append((h, v))
        for k in writes:
            self.lastw[k] = tok
            self.readers[k] = {}
        for k in reads:
            if k in writes:
                continue
            self.readers.setdefault(k, {})[tok[0]] = tok
        self.items[eng].append((waits, call, tok[1], inc))
        self.n_ops += 1

    def pe(self, fn, reads=(), writes=()):
        self.op("pe", fn, reads, writes)

    def dve(self, fn, reads=(), writes=()):
        self.op("dve", fn, reads, writes)

    def act(self, fn, reads=(), writes=()):
        self.op("act", fn, reads, writes)

    def pool(self, fn, reads=(), writes=()):
        self.op("pool", fn, reads, writes)

    def dma(self, out, in_, reads=(), writes=(), q="sp"):
        self.op(q, lambda e: e.dma_start(out=out, in_=in_), reads, writes, is_dma=True)

    @staticmethod
    def _fd(call):
        name, a_, k_ = call
        out = k_.get("out", a_[0] if a_ else None)
        try:
            shp = list(out.shape)
            n = 1
            for d in shp[1:]:
                n *= int(d)
            return max(n, 1), int(shp[0])
        except Exception:
            return 128, 128

    @staticmethod
    def _act_set(call):
        if call[0] != "activation":
            return None
        f = str(call[2].get("func", "")).split(".")[-1]
        if f in ("Exp", "Ln"):
            return "E"
        if f in ("Silu", "Sigmoid", "Tanh"):
            return f
        return None

    def _dur(self, eng, call, is_dma):
        fd, npart = self._fd(call)
        if is_dma:
            byt = fd * npart * 4
            return 0.15, 2.0 + byt / 120e3
        if eng == "pe":
            t = (0.06 + fd / 1200.0) * PE_SCALE[0]
            return t, t + 0.25 + LAT_EXTRA[0]
        if eng == "dve":
            t = 0.16 + fd / 960.0
            if call[0] == "scalar_tensor_tensor":
                t = 0.16 + fd / 480.0
            return t, t + 0.1 + LAT_EXTRA[0]
        if eng == "act":
            t = 0.22 + fd / 1200.0
            return t, t + 0.1 + LAT_EXTRA[0]
        t = 0.3 + fd / 600.0
        return t, t + 0.1 + LAT_EXTRA[0]

    def schedule(self):
        import heapq
        ops = self.oplist
        n = len(ops)
        if MAXOPS[0] < n:
            ops = ops[:MAXOPS[0]]
            n = len(ops)
        preds = [None] * n
        preds_ps = [None] * n
        lastw, readers = {}, {}
        for i, (eng, call, reads, writes, is_dma, _al) in enumerate(ops):
            ps = set()
            pp = set()
            for k in reads:
                if k in lastw:
                    ps.add(lastw[k])
            for k in writes:
                if k in lastw:
                    ps.add(lastw[k])
                    if k[:2] in ("PF", "PQ", "PB"):
                        pp.add(lastw[k])
                ps.update(readers.get(k, ()))
                if k[:2] in ("PF", "PQ", "PB"):
                    pp.update(readers.get(k, ()))
            ps.discard(i)
            pp.discard(i)
            preds[i] = ps
            preds_ps[i] = pp
            for k in writes:
                lastw[k] = i
                readers[k] = set()
            for k in reads:
                if k not in writes:
                    readers.setdefault(k, set()).add(i)
        succs = [[] for _ in range(n)]
        npred = [0] * n
        for i in range(n):
            npred[i] = len(preds[i])
            for p in preds[i]:
                succs[p].append(i)
        chosen = None
        if not SCHED[0]:
            order = list(range(n))
        else:
            durs = [self._dur(ops[i][0], ops[i][1], ops[i][4]) for i in range(n)]
            tail = [0.0] * n
            for i in range(n - 1, -1, -1):
                t = 0.0
                for sidx in succs[i]:
                    if tail[sidx] > t:
                        t = tail[sidx]
                tail[i] = t + durs[i][1]
            done_t = [0.0] * n
            ready_t = [0.0] * n
            free = {e: 0.0 for e in self.ENGS}
            pend = {e: [] for e in self.ENGS}
            avail = {e: [] for e in self.ENGS}
            act_set = [None]
            act_pick = [None]
            chosen = [None] * n
            engs_of = [[ops[i][0]] + list(ops[i][5].keys()) for i in range(n)]

            def push_ready(i, rt):
                for e2 in engs_of[i]:
                    heapq.heappush(pend[e2], (rt, i))

            for i in range(n):
                if npred[i] == 0:
                    push_ready(i, 0.0)
            order = []
            left = n
            while left:
                best = None
                for e in self.ENGS:
                    f = free[e]
                    while pend[e] and (pend[e][0][0] <= f + LOOKAHEAD[0] or chosen[pend[e][0][1]] is not None):
                        j_ = heapq.heappop(pend[e])[1]
                        if chosen[j_] is None:
                            heapq.heappush(avail[e], ((-tail[j_] if PRIO[0] else 0.0), j_))
                    while avail[e] and chosen[avail[e][0][1]] is not None:
                        heapq.heappop(avail[e])
                    if avail[e] and e == "act" and ACT_TABLES[0]:
                        peek = []
                        while avail[e] and len(peek) < 8:
                            it_ = heapq.heappop(avail[e])
                            if chosen[it_[1]] is None:
                                peek.append(it_)
                        pick = peek[0]
                        for it_ in peek:
                            cs_ = self._act_set(ops[it_[1]][1] if ops[it_[1]][0] == "act" else ops[it_[1]][5]["act"])
                            if ACT_TABLES[0] == 2 and (cs_ is None or cs_ == act_set[0]):
                                pick = it_
                                break
                        for it_ in peek:
                            heapq.heappush(avail[e], it_)
                        act_pick[0] = pick
                        cand = (max(f, ready_t[pick[1]]), pick[1], e, True)
                    elif avail[e]:
                        cand = (max(f, ready_t[avail[e][0][1]]), avail[e][0][1], e, True)
                    elif pend[e]:
                        cand = (pend[e][0][0], pend[e][0][1], e, False)
                    else:
                        continue
                    i_ = cand[1]
                    pen = 0.0
                    if e != ops[i_][0]:
                        pen = max(0.0, self._dur(e, ops[i_][5][e], False)[0] - self._dur(ops[i_][0], ops[i_][1], False)[0])
                    key = (cand[0] + pen, cand[1])
                    if best is None or key < best[0]:
                        best = (key, cand)
                st, i, e, from_avail = best[1]
                if from_avail and e == "act" and ACT_TABLES[0]:
                    tmp_ = []
                    while avail[e]:
                        it_ = heapq.heappop(avail[e])
                        if it_[1] == i:
                            break
                        tmp_.append(it_)
                    for it_ in tmp_:
                        heapq.heappush(avail[e], it_)
                elif from_avail:
                    heapq.heappop(avail[e])
                else:
                    heapq.heappop(pend[e])
                chosen[i] = e
                call_i = ops[i][1] if e == ops[i][0] else ops[i][5][e]
                if e == "pe" and KEEPWARM[0] and self.warm_ops is not None and call_i[0] == "matmul" \
                        and call_i[2].get("start", True) and st - free[e] > 0.7:
                    tp = free[e]
                    for p in preds_ps[i]:
                        tp = max(tp, done_t[p])
                    nfill = min(int((st - tp - 0.25) / 0.17), KEEPWARM[0])
                    if nfill > 0:
                        order.append(("fill", i, nfill))
                        self.n_fill += nfill
                occ, lat = self._dur(e, call_i, ops[i][4])
                if e == "act" and ACT_TABLES[0]:
                    cs_ = self._act_set(call_i)
                    if cs_ is not None and cs_ != act_set[0]:
                        occ += 1.3
                        lat += 1.3
                        act_set[0] = cs_
                if SCHED_TRACE is not None:
                    SCHED_TRACE.append((i, e, st, occ, lat, free[e], ready_t[i]))
                free[e] = st + occ
                done_t[i] = st + lat
                order.append(i)
                left -= 1
                for sidx in succs[i]:
                    ready_t[sidx] = max(ready_t[sidx], (st + occ) if (e == "pe" and ops[sidx][0] == "pe") else done_t[i])
                    npred[sidx] -= 1
                    if npred[sidx] == 0:
                        push_ready(sidx, ready_t[sidx])
            self.est_us = max(done_t) if n else 0.0
        toks = [None] * n
        eng_of = [(chosen[i] if chosen is not None and chosen[i] is not None else ops[i][0]) for i in range(n)]
        for i in order:
            if isinstance(i, tuple):
                _, ri, nfill = i
                out_ap = ops[ri][1][1][0] if ops[ri][1][1] else ops[ri][1][2].get("out")
                try:
                    shp = list(out_ap.shape)
                    if len(shp) != 2 or shp[1] < 64 or str(out_ap.dtype) != str(F32):
                        continue
                    ncol = min(int(shp[1]), 128)
                    dcall = ("matmul", (out_ap[:, 0:ncol],), dict(lhsT=self.warm_ops[:, 0:int(shp[0])], rhs=self.warm_ops[:, 0:ncol], start=True, stop=True))
                except Exception:
                    continue
                for _ in range(nfill):
                    self._emit("pe", dcall, False, [(toks[p], eng_of[p]) for p in preds_ps[ri]])
                continue
            eng, call, reads, writes, is_dma, al = ops[i]
            e = eng_of[i]
            toks[i] = self._emit(e, call if e == eng else al[e], is_dma, [(toks[p], eng_of[p]) for p in preds[i]])

    def _emit(self, eng, call, is_dma, pred_toks):
        need = {}

        def add(tok):
            key, h, v, teng = tok
            if need.get(key, (None, 0))[1] < v:
                need[key] = (h, v)

        for tok, peng in pred_toks:
            if peng == "pe" and eng == "pe" and not tok[0].startswith("d#"):
                continue
            add(tok)
        if is_dma:
            nh = len(self.dsem) - self.n_sw
            if eng == "pool":
                slot = nh + self.sw_i % self.n_sw
                self.sw_i += 1
            else:
                slot = self.dma_i % nh
                self.dma_i += 1
            if self.dval[slot] > 0:
                add(("d#%d" % slot, self.dsem[slot], self.dval[slot], None))
            self.dval[slot] += 16
            tok = ("d#%d" % slot, self.dsem[slot], self.dval[slot], None)
            inc = 16
        else:
            self.cnt[eng] += 1
            tok = (eng, self.sem[eng], self.cnt[eng], eng)
            inc = 1
        waits = []
        for key, (h, v) in need.items():
            if self.seen[eng].get(key, 0) < v:
                self.seen[eng][key] = v
                waits.append((h, v))
        self.items[eng].append((waits, call, tok[1], inc))
        self.n_ops += 1
        return tok

    def finish(self):
        nc = self.nc
        self.schedule()
        final = [(self.dsem[i], self.dval[i]) for i in range(len(self.dsem)) if self.dval[i] > 0]

        def emit(name, e, tail=False):
            for waits, fn, h, inc in self.items[name]:
                for (wh, wv) in waits:
                    e.wait_ge(wh, wv)
                name_, a_, k_ = fn
                getattr(e, name_)(*a_, **k_).then_inc(h, inc)
            if tail:
                for (wh, wv) in final:
                    e.wait_ge(wh, wv)

        with nc.Block() as block:
            @block.tensor
            def _(e):
                emit("pe", e)

            @block.vector
            def _(e):
                emit("dve", e)

            @block.scalar
            def _(e):
                emit("act", e)

            @block.gpsimd
            def _(e):
                emit("pool", e)

            @block.sync
            def _(e):
                emit("sp", e, tail=True)
        self.stack.close()


D = 1024
PW = 4232
C0 = 0.6065306597126334
NCONST = 9
LASTP = None
MAXOPS = [10 ** 9]
SCHED = [True]
PRIO = [True]
ALT = [True]
KEEPWARM = [0]
LOOKAHEAD = [0.1]
ACT_TABLES = [2]
PE_SCALE = [0.6]
LAT_EXTRA = [0.15]
DECODE_LAST = [False]
SCHED_TRACE = None
TRACE_OPS = None


def host_consts():
    p = np.arange(128)
    same = (p[:, None] // 64) == (p[None, :] // 64)
    c = np.zeros((NCONST, 128, 128), np.float32)
    c[0] = np.eye(128)
    c[1] = 1.0
    c[2] = same & (p[:, None] <= p[None, :])
    c[3] = same & (p[:, None] > p[None, :])
    c[4] = same & (p[:, None] < p[None, :])
    c[5] = same
    c[6] = (p[:, None] % 64) == (p[None, :] % 64)
    c[7][:, 0] = p < 64
    c[7][:, 1] = p >= 64
    c[8] = -c[4]
    return c


def build(T=2048, NS=16, TB=512):
    nc = bass.Bass("TRN2", target_bir_lowering=False)

    def din(name, shape):
        return nc.dram_tensor(name, list(shape), F32, kind="ExternalInput").ap()

    def dout(name, shape):
        return nc.dram_tensor(name, list(shape), F32, kind="ExternalOutput").ap()

    xp = din("xp", [T, D])
    w_in = din("w_in", [D, PW])
    w_out = din("w_out", [D, D])
    vrows = din("vrows", [128, 128])
    rowp = din("rowp", [2048 + 128 + 1024 + 8])
    w2d = din("w2", [64, 512])
    a2d = din("a2", [64, 512])
    constd = din("consts", [NCONST, 128, 128])
    resetd = din("resetm", [128, 512])
    yp = dout("yp", [T, D])
    p_amat = dout("p_amat", [4, 128, 128])
    p_aconv = dout("p_aconv", [36, 128])
    p_bmat = dout("p_bmat", [8, 64, 64])
    p_bshift = dout("p_bshift", [17, 128])

    xs_d = din("xs", [NS, D])
    sa_mat_d = din("sa_mat", [NS, 4, 128, 128])
    sa_conv_d = din("sa_conv", [NS, 3, 1536])
    sb_mat_d = din("sb_mat", [NS, 8, 64, 64])
    sb_shift_d = din("sb_shift", [NS, 2176])
    ys = dout("ys", [NS, D])
    s_amat = dout("s_amat", [NS, 4, 128, 128])
    s_aconv = dout("s_aconv", [NS, 3, 1536])
    s_bmat = dout("s_bmat", [NS, 8, 64, 64])
    s_bshift = dout("s_bshift", [NS, 2176])

    def dscr(name, shape):
        return nc.dram_tensor(name, list(shape), F32, kind="Internal").ap()

    scr_q, scr_k, scr_v = dscr("scr_q", [NS, 512]), dscr("scr_k", [NS, 512]), dscr("scr_v", [NS, 512])
    scr_ab = dscr("scr_ab", [NS, 8])
    scr_o = dscr("scr_o", [64, 128])
    scr_b6 = [dscr("scr_b%d" % i, [NS, 512]) for i in range(6)]
    scr_y = dscr("scr_y", [128, 64])

    P = Prog(nc)
    global LASTP
    LASTP = P
    NTB = TB // 128
    assert T % TB == 0

    cf = P.sb("cf", [128, NCONST * 128], F32)
    for i in range(NCONST):
        P.dma(cf[:, i * 128:(i + 1) * 128], constd[i], writes=["cf"])
    CF = lambda i: cf[:, i * 128:(i + 1) * 128]
    IDF, ONESF, BT, MGT, STRICT, BL, PAIRS, NSTRICT = CF(0), CF(1), CF(2), CF(3), CF(4), CF(5), CF(6), CF(8)
    INCL = BT
    CHIND = cf[:, 7 * 128:7 * 128 + 2]
    idb = P.sb("idb", [128, 128], BF16)
    P.dve(lambda e: e.tensor_copy(out=idb[:, :], in_=IDF), reads=["cf"], writes=["idb"])
    P.warm_ops = idb
    onesb = P.sb("onesb", [128, 128], BF16)
    P.dve(lambda e: e.tensor_copy(out=onesb[:, :], in_=ONESF), reads=["cf"], writes=["onesb"])
    blb = P.sb("blb", [128, 128], BF16)
    P.dve(lambda e: e.tensor_copy(out=blb[:, :], in_=BL), reads=["cf"], writes=["blb"])
    resetm = P.sb("resetm_sb", [128, 512], F32)
    P.dma(resetm[:, :], resetd[:, :], writes=["resetm"])

    PF = [P.ps("PF%d" % i, [128, 512], F32) for i in range(4)]
    PQ = [P.ps("PQ%d" % i, [128, 512], F32) for i in range(2)]
    PB = [P.ps("PB%d" % i, [128, 1024], BF16) for i in range(2)]

    vr_t = P.sb("vr_t", [128, 128], F32)
    P.dma(vr_t[:, :], vrows[:, :], writes=["vr_t"])
    pcol = P.sb("pcol", [128, 128], F32)
    P.pe(lambda e: e.transpose(out=PF[0][:, 0:128], in_=vr_t[:, :], identity=IDF), reads=["vr_t", "cf"], writes=["PF0"])
    P.dve(lambda e: e.tensor_copy(out=pcol[:, :], in_=PF[0][:, 0:128]), reads=["PF0"], writes=["pcol"])
    col = lambda i: pcol[:, i:i + 1]
    omu = P.sb("omu", [128, 17], F32)
    P.dve(lambda e: e.tensor_scalar(out=omu[:, :], in0=pcol[:, 56:73], scalar1=-1.0, scalar2=1.0, op0=ALU.mult, op1=ALU.add), reads=["pcol"], writes=["omu"])
    NW0, CW0, MU0, W00, A00, KK0, KA0, RK0 = 0, 8, 56, 73, 77, 81, 85, 89

    RL = 2048 + 128 + 1024 + 8
    rp = P.sb("rp", [128, RL], F32)
    P.dma(rp[:, :], rowp.partition_broadcast(128), writes=["rp"])
    LNW, LNB, ANW, FNW = rp[:, 0:512], rp[:, 512:1024], rp[:, 2048:2176], rp[:, 2176:3200]
    ALOG, DTB = rp[:, 3200:3204], rp[:, 3204:3208]
    nega = P.sb("nega", [128, 4], F32)
    P.act(lambda e: e.activation(out=nega[:, :], in_=ALOG, func=AF.Exp), reads=["rp"], writes=["nega"])
    P.dve(lambda e: e.tensor_scalar(out=nega[:, :], in0=nega[:, :], scalar1=-1.0, scalar2=None, op0=ALU.mult), reads=["nega"], writes=["nega"])

    w2a2 = P.sb("w2a2", [128, 512], F32)
    P.dma(w2a2[0:64, :], w2d[:, :], writes=["w2a2"])
    P.dma(w2a2[64:128, :], a2d[:, :], writes=["w2a2"])

    woutb = P.sb("woutb", [128, 8 * 1024], BF16)
    NWB = 4
    wbf = [P.sb("wbf%d" % i, [128, 1024], BF16) for i in range(NWB)]
    cast_i = [0]

    def cast(out, in_, reads, writes):
        i = cast_i[0]
        cast_i[0] += 1
        if i % 3 != 2:
            P.act(lambda e: e.activation(out=out, in_=in_, func=AF.Copy), reads=reads, writes=writes)
        else:
            P.dve(lambda e: e.tensor_copy(out=out, in_=in_), reads=reads, writes=writes)

    for kc in range(8):
        P.dma(woutb[:, kc * 1024:(kc + 1) * 1024], w_out[kc * 128:(kc + 1) * 128, :], writes=["woutb"], q="pool")

    Sa = P.sb("Sa", [128, 512], F32)
    Sab = P.sb("Sab", [128, 512], BF16)
    Hb = P.sb("Hb", [128, 512], F32)
    Hbb = P.sb("Hbb", [128, 512], BF16)
    for t_, n_ in ((Sa, "Sa"), (Sab, "Sab"), (Hb, "Hb"), (Hbb, "Hbb")):
        P.pool(lambda e, t_=t_: e.memset(t_[:, :], 0.0), writes=[n_])
    ccar = P.sb("ccar", [128, 36], F32)
    P.pool(lambda e: e.memset(ccar[:, :], 0.0), writes=["ccar"])
    bcar = P.sb("bcar", [128, 17], F32)
    P.pool(lambda e: e.memset(bcar[:, :], 0.0), writes=["bcar"])

    xt = [P.sb("xt%d" % i, [128, D], F32) for i in range(2)]
    xs_ = P.sb("xs_", [128, D], F32)
    st4 = P.sb("st4", [128, 8], F32)
    xnT = P.sb("xnT", [128, 8 * TB], BF16)
    cb = [P.sb("cb%d" % i, [128, TB + 4], F32) for i in range(2)]
    FT = [P.sb("FT%d" % i, [128, 512], F32) for i in range(10)]
    HT = [P.sb("HT%d" % i, [128, 512], BF16) for i in range(40)]
    mixB4 = P.sb("mixB4", [128, (TB // 128) * 512], BF16)
    ctmp = P.sb("ctmp", [128, 512], F32)
    mixT = P.sb("mixT", [128, 1024], BF16)
    BLK = [P.sb("BLK%d" % i, [128, (8 if i in (0, 3, 4) else 4) * TB], BF16) for i in range(5)]
    P.pool(lambda e: e.memset(BLK[3][:, :], 0.0), writes=["BLK3"])
    P.pool(lambda e: e.memset(BLK[4][:, :], 0.0), writes=["BLK4"])
    mixA2 = [P.sb("mixA%d" % i, [128, NTB * 512], BF16) for i in range(2)]
    pce = P.sb("pce", [128, 4 * (TB // 64)], F32)
    bon = P.sb("bon", [128, 4 * TB], BF16)

    def rsqrt_act(out, in_, scale, eps, reads, writes):
        P.act(lambda e: e.activation(out=out, in_=in_, func=AF.Ln, scale=scale, bias=eps), reads=reads, writes=writes)
        P.act(lambda e: e.activation(out=out, in_=out, func=AF.Exp, scale=-0.5), reads=writes, writes=writes)

    def b3(ap, h, n):
        return ap.rearrange("p (h n) -> p h n", h=h)

    SPJ = P.sb("SPJ", [128, 35 * NS], F32)
    xnTs = P.sb("xnTs", [128, 8 * NS], BF16)
    arena = P.sb("arena", [128, 4096], F32)

    class _Sub:
        def __init__(self, off):
            self.off = off

        def __getitem__(self, key):
            rows, cols = key
            lo = 0 if cols.start is None else cols.start
            hi = 512 if cols.stop is None else cols.stop
            return arena[rows, self.off + lo:self.off + hi]

    SF = [_Sub(i * 512) for i in range(8)]
    SAMPLE_LOCALS = dict(locals())
    if not DECODE_LAST[0]:
        sample_path(SAMPLE_LOCALS)
    arenab = arena[:, :].bitcast(BF16)
    P.pool(lambda e: e.memset(st4[:, 7:8], 0.0), reads=["SF%d" % i for i in range(8)], writes=["BAR", "BVB", "BSG", "st4"])

    nblk = T // TB
    stream_b = None
    for blk in range(nblk):
        t0 = blk * TB
        last_blk = blk == nblk - 1
        for tb in range(NTB):
            xa = xt[tb % 2]
            xk = "xt%d" % (tb % 2)
            P.dma(xa[:, :], xp[t0 + tb * 128:t0 + (tb + 1) * 128, :], writes=[xk])
            P.act(lambda e, xa=xa: e.activation(out=xs_[:, :], in_=xa[:, :], func=AF.Square, accum_out=st4[:, 0:1]), reads=[xk], writes=["xs_", "st4"])
            rsqrt_act(st4[:, 0:1], st4[:, 0:1], 1.0 / D, 1e-6, ["st4"], ["st4"])
            P.act(lambda e, xa=xa: e.activation(out=xs_[:, :], in_=xa[:, :], func=AF.Copy, scale=st4[:, 0:1]), reads=[xk, "st4"], writes=["xs_"])
            for half in range(2):
                pf = PF[half]
                pk = "PF%d" % half
                for q in range(4):
                    kc = half * 4 + q
                    P.pe(lambda e, pf=pf, q=q, kc=kc: e.transpose(out=pf[:, q * 128:(q + 1) * 128], in_=xs_[:, kc * 128:(kc + 1) * 128], identity=IDF), reads=["xs_", "cf"], writes=[pk])
                out3 = xnT[:, :].rearrange("p (k t) -> p k t", k=8)[:, half * 4:(half + 1) * 4, tb * 128:(tb + 1) * 128]
                in3 = pf[:, :].rearrange("p (k t) -> p k t", k=4)
                nw3 = pcol[:, NW0 + half * 4:NW0 + half * 4 + 4].unsqueeze(2).to_broadcast([128, 4, 128])
                P.dve(lambda e, out3=out3, in3=in3, nw3=nw3: e.tensor_tensor(out=out3, in0=in3, in1=nw3, op=ALU.mult), reads=[pk, "pcol"], writes=["xnT"])

        wchunk_i = [0]

        def proj_chunk(c0, ncols, pf, pk):
            s = wchunk_i[0] % NWB
            wchunk_i[0] += 1
            bk = "wbf%d" % s
            src = w_in[:, c0:c0 + ncols].rearrange("(k p) n -> p k n", p=128)
            dst = wbf[s][:, 0:8 * ncols].rearrange("p (k n) -> p k n", k=8)
            P.dma(dst, src, writes=[bk], q="pool")
            for kc in range(8):
                P.pe(lambda e, kc=kc, s=s: e.matmul(pf[0:ncols, 0:TB], lhsT=wbf[s][:, kc * ncols:(kc + 1) * ncols], rhs=xnT[:, kc * TB:(kc + 1) * TB], start=(kc == 0), stop=(kc == 7)), reads=[bk, "xnT"], writes=[pk])

        mixA, mxk = mixA2[blk % 2], "mixA%d" % (blk % 2)
        KQ, VT, SGA = BLK[0], BLK[1], BLK[2]
        for c in range(12):
            pf, pk = PF[c % 2], "PF%d" % (c % 2)
            cbuf, ck = cb[c % 2], "cb%d" % (c % 2)
            proj_chunk(c * 128, 128, pf, pk)
            car3 = ccar[:, :].rearrange("p (i c) -> p i c", i=3)[:, :, c]
            P.pool(lambda e, cbuf=cbuf, car3=car3: e.tensor_copy(out=cbuf[:, 0:3], in_=car3), reads=["ccar"], writes=[ck])
            P.act(lambda e, cbuf=cbuf, pf=pf: e.activation(out=cbuf[:, 3:3 + TB], in_=pf[:, 0:TB], func=AF.Copy), reads=[pk], writes=[ck])
            P.pool(lambda e, cbuf=cbuf, car3=car3: e.tensor_copy(out=car3, in_=cbuf[:, TB:TB + 3]), reads=[ck], writes=["ccar"])
            acc, ak = FT[c % 2], "FT%d" % (c % 2)
            P.op("dve", lambda e, cbuf=cbuf, acc=acc, c=c: e.tensor_scalar(out=acc[:, 0:TB], in0=cbuf[:, 0:TB], scalar1=col(CW0 + c * 4), scalar2=None, op0=ALU.mult), [ck, "pcol"], [ak],
                 alts=[("act", lambda e, cbuf=cbuf, acc=acc, c=c: e.activation(out=acc[:, 0:TB], in_=cbuf[:, 0:TB], func=AF.Copy, scale=col(CW0 + c * 4)))] if ALT[0] else None)
            P.dve(lambda e, cbuf=cbuf, acc=acc, c=c: e.scalar_tensor_tensor(out=acc[:, 0:TB], in0=cbuf[:, 1:1 + TB], scalar=col(CW0 + c * 4 + 1), in1=acc[:, 0:TB], op0=ALU.mult, op1=ALU.add), reads=[ck, "pcol", ak], writes=[ak])
            P.op("dve", lambda e, cbuf=cbuf, c=c: e.tensor_scalar(out=ctmp[:, 0:TB], in0=cbuf[:, 2:2 + TB], scalar1=col(CW0 + c * 4 + 2), scalar2=None, op0=ALU.mult), [ck, "pcol"], ["ctmp"],
                 alts=[("act", lambda e, cbuf=cbuf, c=c: e.activation(out=ctmp[:, 0:TB], in_=cbuf[:, 2:2 + TB], func=AF.Copy, scale=col(CW0 + c * 4 + 2)))] if ALT[0] else None)
            P.dve(lambda e, cbuf=cbuf, c=c: e.scalar_tensor_tensor(out=ctmp[:, 0:TB], in0=cbuf[:, 3:3 + TB], scalar=col(CW0 + c * 4 + 3), in1=ctmp[:, 0:TB], op0=ALU.mult, op1=ALU.add), reads=[ck, "pcol", "ctmp"], writes=["ctmp"])
            P.dve(lambda e, acc=acc: e.tensor_tensor(out=acc[:, 0:TB], in0=acc[:, 0:TB], in1=ctmp[:, 0:TB], op=ALU.add), reads=[ak, "ctmp"], writes=[ak])
            P.act(lambda e, acc=acc: e.activation(out=acc[:, 0:TB], in_=acc[:, 0:TB], func=AF.Silu), reads=[ak], writes=[ak])
            if c < 8:
                h = c % 4
                isq = c < 4
                sq, sqk = HT[c % 2], "HT%d" % (c % 2)
                P.act(lambda e, acc=acc, sq=sq: e.activation(out=sq[:, 0:TB], in_=acc[:, 0:TB], func=AF.Square), reads=[ak], writes=[sqk])
                p2, p2k = PF[2 + c % 2], "PF%d" % (2 + c % 2)
                P.pe(lambda e, p2=p2, sq=sq: e.matmul(p2[:, 0:TB], lhsT=onesb[:, :], rhs=sq[:, 0:TB], start=True, stop=True), reads=[sqk, "onesb"], writes=[p2k])
                ri, rik = FT[2 + c % 2], "FT%d" % (2 + c % 2)
                rsqrt_act(ri[:, 0:TB], p2[:, 0:TB], 1.0, 1e-12, [p2k], [rik])
                dst = KQ[:, h * 2 * TB:(h + 1) * 2 * TB].rearrange("p (t s n) -> p t s n", t=NTB, s=2)[:, :, 1 if isq else 0, :]
                P.dve(lambda e, acc=acc, ri=ri, dst=dst, isq=isq: e.scalar_tensor_tensor(out=dst, in0=acc[:, 0:TB].rearrange("p (t n) -> p t n", t=NTB), scalar=(128 ** -0.5 if isq else 1.0), in1=ri[:, 0:TB].rearrange("p (t n) -> p t n", t=NTB), op0=ALU.mult, op1=ALU.mult), reads=[ak, rik], writes=["BLK0"])
            else:
                h = c - 8
                P.dve(lambda e, acc=acc, h=h: e.tensor_copy(out=VT[:, h * TB:(h + 1) * TB], in_=acc[:, 0:TB]), reads=[ak], writes=["BLK1"])
        if last_blk:
            P.pe(lambda e: e.transpose(out=PF[0][0:36, 0:128], in_=ccar[:, :], identity=IDF), reads=["ccar", "cf"], writes=["PF0"])
            P.dve(lambda e: e.tensor_copy(out=FT[0][0:36, 0:128], in_=PF[0][0:36, 0:128]), reads=["PF0"], writes=["FT0"])
            P.dma(p_aconv[:, :], FT[0][0:36, 0:128], reads=["FT0"], writes=["p_aconv"])
        for c in range(4):
            pf, pk = PF[c % 2], "PF%d" % (c % 2)
            proj_chunk(1536 + c * 128, 128, pf, pk)
            P.act(lambda e, pf=pf, c=c: e.activation(out=SGA[:, c * TB:(c + 1) * TB], in_=pf[:, 0:TB], func=AF.Silu), reads=[pk], writes=["BLK2"])
        bdT = FT[4]
        proj_chunk(2048, 8, PF[0], "PF0")
        P.act(lambda e: e.activation(out=bdT[0:8, 0:TB], in_=PF[0][0:8, 0:TB], func=AF.Copy), reads=["PF0"], writes=["FT4"])

        P.capture_begin()
        for tb in range(NTB):
            cs = slice(tb * 128, (tb + 1) * 128)
            kq = lambda h, s: KQ[:, h * 2 * TB + tb * 256 + s * 128: h * 2 * TB + tb * 256 + (s + 1) * 128]
            kq2 = lambda h: KQ[:, h * 2 * TB + tb * 256: h * 2 * TB + (tb + 1) * 256]
            sc = FT[5]
            P.pe(lambda e, cs=cs: e.transpose(out=PF[0][:, 0:8], in_=bdT[0:8, cs], identity=IDF[0:8, 0:8]), reads=["FT4", "cf"], writes=["PF0"])
            P.act(lambda e: e.activation(out=sc[:, 0:4], in_=PF[0][:, 0:4], func=AF.Exp, scale=-1.0), reads=["PF0"], writes=["FT5"])
            P.dve(lambda e: e.tensor_scalar(out=sc[:, 0:4], in0=sc[:, 0:4], scalar1=1.0, scalar2=None, op0=ALU.add), reads=["FT5"], writes=["FT5"])
            P.dve(lambda e: e.reciprocal(out=sc[:, 0:4], in_=sc[:, 0:4]), reads=["FT5"], writes=["FT5"])
            P.dve(lambda e: e.tensor_tensor(out=sc[:, 4:8], in0=PF[0][:, 4:8], in1=DTB, op=ALU.add), reads=["PF0", "rp"], writes=["FT5"])
            P.act(lambda e: e.activation(out=sc[:, 4:8], in_=sc[:, 4:8], func=AF.Exp), reads=["FT5"], writes=["FT5"])
            P.act(lambda e: e.activation(out=sc[:, 4:8], in_=sc[:, 4:8], func=AF.Ln, bias=1.0), reads=["FT5"], writes=["FT5"])
            P.dve(lambda e: e.tensor_tensor(out=sc[:, 4:8], in0=sc[:, 4:8], in1=nega[:, :], op=ALU.mult), reads=["FT5", "nega"], writes=["FT5"])
            P.pe(lambda e: e.matmul(PF[0][:, 8:12], lhsT=BT, rhs=sc[:, 4:8], start=True, stop=True), reads=["FT5", "cf"], writes=["PF0"])
            P.pe(lambda e: e.matmul(PF[0][:, 12:16], lhsT=BL, rhs=sc[:, 4:8], start=True, stop=True), reads=["FT5", "cf"], writes=["PF0"])
            P.dve(lambda e: e.tensor_copy(out=sc[:, 8:16], in_=PF[0][:, 8:16]), reads=["PF0"], writes=["FT5"])
            P.act(lambda e: e.activation(out=sc[:, 16:20], in_=sc[:, 8:12], func=AF.Exp), reads=["FT5"], writes=["FT5"])
            P.dve(lambda e: e.tensor_tensor(out=sc[:, 20:24], in0=sc[:, 12:16], in1=sc[:, 8:12], op=ALU.subtract), reads=["FT5"], writes=["FT5"])
            P.act(lambda e: e.activation(out=sc[:, 20:24], in_=sc[:, 20:24], func=AF.Exp), reads=["FT5"], writes=["FT5"])
            gm3 = sc[:, 32:40].rearrange("p (c h) -> p c h", c=2)
            P.dve(lambda e: e.tensor_tensor(out=gm3, in0=sc[:, 4:8].unsqueeze(1).to_broadcast([128, 2, 4]), in1=CHIND.unsqueeze(2).to_broadcast([128, 2, 4]), op=ALU.mult), reads=["FT5", "cf"], writes=["FT5"])
            P.pe(lambda e: e.matmul(PF[0][:, 16:24], lhsT=ONESF, rhs=sc[:, 32:40], start=True, stop=True), reads=["FT5", "cf"], writes=["PF0"])
            P.act(lambda e: e.activation(out=sc[:, 24:32], in_=PF[0][:, 16:24], func=AF.Exp), reads=["PF0"], writes=["FT5"])
            beta_b = sc[:, 0:4].unsqueeze(2).to_broadcast([128, 4, 128])
            Kg, Kd, Vtm = HT[2], HT[3], HT[4]
            for h in range(4):
                P.pe(lambda e, h=h: e.transpose(out=PB[0][:, h * 128:(h + 1) * 128], in_=kq(h, 0), identity=idb[:, :]), reads=["BLK0", "idb"], writes=["PB0"])
                P.pe(lambda e, h=h: e.transpose(out=PB[0][:, 512 + h * 128:512 + (h + 1) * 128], in_=VT[:, h * TB + tb * 128:h * TB + (tb + 1) * 128], identity=idb[:, :]), reads=["BLK1", "idb"], writes=["PB0"])
            P.dve(lambda e: e.tensor_tensor(out=b3(Kg[:, :], 4, 128), in0=b3(PB[0][:, 0:512], 4, 128), in1=sc[:, 16:20].unsqueeze(2).to_broadcast([128, 4, 128]), op=ALU.mult), reads=["PB0", "FT5"], writes=["HT2"])
            P.dve(lambda e: e.tensor_tensor(out=b3(Kd[:, :], 4, 128), in0=b3(PB[0][:, 0:512], 4, 128), in1=sc[:, 20:24].unsqueeze(2).to_broadcast([128, 4, 128]), op=ALU.mult), reads=["PB0", "FT5"], writes=["HT3"])
            P.dve(lambda e: e.tensor_copy(out=Vtm[:, :], in_=PB[0][:, 512:1024]), reads=["PB0"], writes=["HT4"])
            MG, E = FT[6], FT[7]
            P.pool(lambda e: e.tensor_tensor(out=b3(MG[:, :], 4, 128), in0=MGT.unsqueeze(1).to_broadcast([128, 4, 128]), in1=sc[:, 4:8].unsqueeze(2).to_broadcast([128, 4, 128]), op=ALU.mult), reads=["cf", "FT5"], writes=["FT6"])
            for h in range(4):
                P.pe(lambda e, h=h: e.matmul(PF[0][:, h * 128:(h + 1) * 128], lhsT=MG[:, h * 128:(h + 1) * 128], rhs=BT, start=True, stop=True), reads=["FT6", "cf"], writes=["PF0"])
            P.act(lambda e: e.activation(out=E[:, :], in_=PF[0][:, :], func=AF.Exp), reads=["PF0"], writes=["FT7"])
            for h in range(4):
                P.pe(lambda e, h=h: e.matmul((PQ[0] if h < 2 else PF[1])[:, (h % 2) * 256:(h % 2 + 1) * 256], lhsT=kq(h, 0), rhs=kq2(h), start=True, stop=True), reads=["BLK0"], writes=["PQ0" if h < 2 else "PF1"])
            XQ = [HT[5], HT[6]]
            XQa, XQb = (HT[5], HT[6]), (HT[7], HT[8])
            Nn = [HT[9], HT[10]]
            QKT = HT[11]
            E2 = FT[8]
            P.pool(lambda e: e.tensor_tensor(out=b3(E2[:, :], 4, 128), in0=b3(E[:, :], 4, 128), in1=INCL.unsqueeze(1).to_broadcast([128, 4, 128]), op=ALU.mult), reads=["FT7", "cf"], writes=["FT8"])
            pdh = [(PQ[0], "PQ0"), (PF[1], "PF1")]
            pd2 = lambda i: pdh[i][0][:, :].rearrange("p (h s n) -> p h s n", h=2, s=2)
            h2 = lambda t_, i: t_[:, i * 256:(i + 1) * 256].rearrange("p (h n) -> p h n", h=2)
            for i in range(2):
                P.dve(lambda e, i=i: e.tensor_tensor(out=h2(QKT, i), in0=h2(E2, i), in1=pd2(i)[:, :, 1, :], op=ALU.mult), reads=["FT8", pdh[i][1]], writes=["HT11"])
            P.pool(lambda e: e.tensor_tensor(out=b3(E2[:, :], 4, 128), in0=b3(E2[:, :], 4, 128), in1=STRICT.unsqueeze(1).to_broadcast([128, 4, 128]), op=ALU.mult), reads=["FT8", "cf"], writes=["FT8"])
            P.pool(lambda e: e.tensor_tensor(out=b3(E2[:, :], 4, 128), in0=b3(E2[:, :], 4, 128), in1=beta_b, op=ALU.mult), reads=["FT8", "FT5"], writes=["FT8"])
            X0 = FT[9]
            for i in range(2):
                P.dve(lambda e, i=i: e.tensor_tensor(out=h2(X0, i), in0=h2(E2, i), in1=pd2(i)[:, :, 0, :], op=ALU.mult), reads=["FT8", pdh[i][1]], writes=["FT9"])
            resA = dict(X=([HT[5], HT[6]], ["HT5", "HT6"]), Q=([HT[7], HT[8]], ["HT7", "HT8"]), N=([HT[9], HT[10]], ["HT9", "HT10"]),
                        SQ=(PQ[0], "PQ0"), G0=(PF[0], "PF0"), G1=(PF[1], "PF1"), T=(PB[0], "PB0"))
            Tinv = inverse_chain(P, X0, "FT9", resA, IDF, idb)
            TinvT, tk = Tinv
            WT, qgT = HT[12], HT[13]
            U0 = FT[6]
            for h in range(4):
                P.pe(lambda e, h=h: e.matmul(PF[0][:, h * 128:(h + 1) * 128], lhsT=Kg[:, h * 128:(h + 1) * 128], rhs=TinvT[:, h * 128:(h + 1) * 128], start=True, stop=True), reads=["HT2", tk], writes=["PF0"])
            P.copy(WT[:, :], PF[0][:, :], reads=["PF0"], writes=["HT12"])
            for h in range(4):
                P.pe(lambda e, h=h: e.matmul(PF[1][:, h * 128:(h + 1) * 128], lhsT=TinvT[:, h * 128:(h + 1) * 128], rhs=Vtm[:, h * 128:(h + 1) * 128], start=True, stop=True), reads=["HT4", tk], writes=["PF1"])
            P.copy(U0[:, :], PF[1][:, :], reads=["PF1"], writes=["FT6"])
            Dg = FT[7]
            P.pool(lambda e: e.tensor_tensor(out=b3(Dg[:, :], 4, 128), in0=IDF.unsqueeze(1).to_broadcast([128, 4, 128]), in1=sc[:, 16:20].unsqueeze(2).to_broadcast([128, 4, 128]), op=ALU.mult), reads=["cf", "FT5"], writes=["FT7"])
            for h in range(4):
                P.pe(lambda e, h=h: e.matmul(PF[0][:, h * 128:(h + 1) * 128], lhsT=ONESF, rhs=Dg[:, h * 128:(h + 1) * 128], start=True, stop=True), reads=["FT7", "cf"], writes=["PF0"])
            q4 = KQ[:, :].rearrange("p (h t s n) -> p h t s n", h=4, t=NTB, s=2)[:, :, tb, 1, :]
            P.dve(lambda e, q4=q4: e.tensor_tensor(out=b3(qgT[:, :], 4, 128), in0=q4, in1=b3(PF[0][:, :], 4, 128), op=ALU.mult), reads=["BLK0", "PF0"], writes=["HT13"])
            ub = HT[2 + 0]
            ub = HT[5]
            osb = FT[8]
            for c in range(2):
                r0, r1 = c * 64, c * 64 + 64
                for h in range(4):
                    P.pe(lambda e, h=h: e.matmul(PF[0][:, h * 128:(h + 1) * 128], lhsT=WT[:, h * 128:(h + 1) * 128], rhs=Sab[:, h * 128:(h + 1) * 128], start=True, stop=True), reads=["HT12", "Sab"], writes=["PF0"])
                tmpu = FT[9]
                P.dve(lambda e, r0=r0, r1=r1: e.tensor_tensor(out=tmpu[r0:r1, :], in0=U0[r0:r1, :], in1=PF[0][r0:r1, :], op=ALU.subtract), reads=["FT6", "PF0"], writes=["FT9"])
                P.dve(lambda e, r0=r0, r1=r1: e.tensor_tensor(out=b3(ub[r0:r1, :], 4, 128), in0=b3(tmpu[r0:r1, :], 4, 128), in1=sc[r0:r1, 0:4].unsqueeze(2).to_broadcast([64, 4, 128]), op=ALU.mult), reads=["FT9", "FT5"], writes=["HT5"])
                for h in range(4):
                    P.pe(lambda e, h=h: e.matmul(PF[1][:, h * 128:(h + 1) * 128], lhsT=qgT[:, h * 128:(h + 1) * 128], rhs=Sab[:, h * 128:(h + 1) * 128], start=True, stop=False), reads=["HT13", "Sab"], writes=["PF1"])
                    P.pe(lambda e, h=h, r0=r0, r1=r1: e.matmul(PF[1][:, h * 128:(h + 1) * 128], lhsT=QKT[r0:r1, h * 128:(h + 1) * 128], rhs=ub[r0:r1, h * 128:(h + 1) * 128], start=False, stop=True), reads=["HT11", "HT5"], writes=["PF1"])
                P.copy(osb[r0:r1, :], PF[1][r0:r1, :], reads=["PF1"], writes=["FT8"])
                for h in range(4):
                    P.pe(lambda e, h=h, r0=r0, r1=r1: e.matmul(PF[0][:, h * 128:(h + 1) * 128], lhsT=Kd[r0:r1, h * 128:(h + 1) * 128], rhs=ub[r0:r1, h * 128:(h + 1) * 128], start=True, stop=True), reads=["HT3", "HT5"], writes=["PF0"])
                P.dve(lambda e, c=c: e.tensor_tensor(out=b3(Sa[:, :], 4, 128), in0=b3(Sa[:, :], 4, 128), in1=sc[:, 24 + c * 4:28 + c * 4].unsqueeze(2).to_broadcast([128, 4, 128]), op=ALU.mult), reads=["Sa", "FT5"], writes=["Sa"])
                P.dve(lambda e: e.tensor_tensor(out=Sa[:, :], in0=Sa[:, :], in1=PF[0][:, :], op=ALU.add), reads=["Sa", "PF0"], writes=["Sa"])
                P.copy(Sab[:, :], Sa[:, :], reads=["Sa"], writes=["Sab"])
            sq = FT[9]
            P.pool(lambda e: e.tensor_tensor(out=sq[:, :], in0=osb[:, :], in1=osb[:, :], op=ALU.mult), reads=["FT8"], writes=["FT9"])
            P.dve(lambda e: e.tensor_reduce(out=sc[:, 40:44], in_=b3(sq[:, :], 4, 128), axis=AX.X, op=ALU.add), reads=["FT9"], writes=["FT5"])
            rsqrt_act(sc[:, 40:44], sc[:, 40:44], 1.0 / 128, 1e-6, ["FT5"], ["FT5"])
            P.dve(lambda e: e.tensor_tensor(out=b3(osb[:, :], 4, 128), in0=b3(osb[:, :], 4, 128), in1=sc[:, 40:44].unsqueeze(2).to_broadcast([128, 4, 128]), op=ALU.mult), reads=["FT8", "FT5"], writes=["FT8"])
            P.pool(lambda e: e.tensor_tensor(out=b3(osb[:, :], 4, 128), in0=b3(osb[:, :], 4, 128), in1=ANW.unsqueeze(1).to_broadcast([128, 4, 128]), op=ALU.mult), reads=["FT8", "rp"], writes=["FT8"])
            for c4 in range(4):
                P.pe(lambda e, c4=c4, cs=cs: e.transpose(out=PB[0][:, c4 * 128:(c4 + 1) * 128], in_=SGA[:, c4 * TB + tb * 128:c4 * TB + (tb + 1) * 128], identity=idb[:, :]), reads=["BLK2", "idb"], writes=["PB0"])
            P.dve(lambda e: e.tensor_tensor(out=mixA[:, tb * 512:(tb + 1) * 512], in0=osb[:, :], in1=PB[0][:, 0:512], op=ALU.mult), reads=["FT8", "PB0"], writes=[mxk])
        if last_blk:
            P.dma(p_amat.rearrange("h k v -> k h v"), b3(Sa[:, :], 4, 128), reads=["Sa"], writes=["p_amat"])
        stream_a = P.capture_end()
        P.replay([stream_a] + ([stream_b] if stream_b is not None else []))


        AR, BKL, BKH, VBT, SGB = arenab[:, 0:8 * TB], BLK[3], BLK[4], arenab[:, 4096:4096 + 4 * TB], arenab[:, 6144:6144 + 4 * TB]
        NCH = TB // 64
        ar = lambda pr, tb_, s_: AR[:, pr * 2 * TB + tb_ * 256 + s_ * 128: pr * 2 * TB + tb_ * 256 + (s_ + 1) * 128]
        bkh = lambda hh, pr, tb_, s_: (BKL, BKH)[hh][:, pr * 2 * TB + tb_ * 256 + s_ * 128: pr * 2 * TB + tb_ * 256 + (s_ + 1) * 128]
        ar_dst = lambda pr, s_: AR[:, pr * 2 * TB:(pr + 1) * 2 * TB].rearrange("p (t s n) -> p t s n", t=NTB, s=2)[:, :, s_, :]
        bk_dst = lambda hh, pr, s_: (BKL, BKH)[hh][64 * hh:64 * hh + 64, pr * 2 * TB:(pr + 1) * 2 * TB].rearrange("p (t s n) -> p t s n", t=NTB, s=2)[:, :, s_, :]
        t3 = lambda ap: ap.rearrange("p (t n) -> p t n", t=NTB)
        bci = [0]

        def b_chunk(j, dstT, dk):
            i = bci[0] % 2
            bci[0] += 1
            pf, pk = PF[i], "PF%d" % i
            cbuf, ck = cb[i], "cb%d" % i
            proj_chunk(2056 + j * 128, 128, pf, pk)
            P.pool(lambda e: e.tensor_copy(out=cbuf[:, 0:1], in_=bcar[:, j:j + 1]), reads=["bcar"], writes=[ck])
            P.act(lambda e: e.activation(out=cbuf[:, 1:1 + TB], in_=pf[:, 0:TB], func=AF.Copy), reads=[pk], writes=[ck])
            P.pool(lambda e: e.tensor_copy(out=bcar[:, j:j + 1], in_=cbuf[:, TB:TB + 1]), reads=[ck], writes=["bcar"])
            P.op("dve", lambda e: e.tensor_scalar(out=dstT[:, 0:TB], in0=cbuf[:, 0:TB], scalar1=col(MU0 + j), scalar2=None, op0=ALU.mult), [ck, "pcol"], [dk],
                 alts=[("act", lambda e: e.activation(out=dstT[:, 0:TB], in_=cbuf[:, 0:TB], func=AF.Copy, scale=col(MU0 + j)))] if ALT[0] else None)
            P.dve(lambda e: e.scalar_tensor_tensor(out=dstT[:, 0:TB], in0=cbuf[:, 1:1 + TB], scalar=omu[:, j:j + 1], in1=dstT[:, 0:TB], op0=ALU.mult, op1=ALU.add), reads=[dk, ck, "omu"], writes=[dk])

        xb16 = FT[0]
        b_chunk(16, xb16, "FT0")
        P.act(lambda e: e.activation(out=xb16[0:64, 0:TB], in_=xb16[0:64, 0:TB], func=AF.Tanh), reads=["FT0"], writes=["FT0"])
        for pr in range(4):
            sg, aic, Lsg, Pt, Pinv, Pm1, xr, xk, kmod = FT[1], FT[2], FT[3], FT[4], FT[5], FT[6], FT[7], FT[8], FT[9]
            P.pe(lambda e, pr=pr: e.matmul(PF[2][:, 0:TB], lhsT=w2a2[0:64, pr * 128:(pr + 1) * 128], rhs=xb16[0:64, 0:TB], start=True, stop=True), reads=["w2a2", "FT0"], writes=["PF2"])
            P.act(lambda e, pr=pr: e.activation(out=sg[:, 0:TB], in_=PF[2][:, 0:TB], func=AF.Sigmoid, bias=col(W00 + pr)), reads=["PF2", "pcol"], writes=["FT1"])
            P.pe(lambda e, pr=pr: e.matmul(PF[3][:, 0:TB], lhsT=w2a2[64:128, pr * 128:(pr + 1) * 128], rhs=xb16[64:128, 0:TB], start=True, stop=True), reads=["w2a2", "FT0"], writes=["PF3"])
            P.act(lambda e, pr=pr: e.activation(out=aic[:, 0:TB], in_=PF[3][:, 0:TB], func=AF.Sigmoid, bias=col(A00 + pr)), reads=["PF3", "pcol"], writes=["FT2"])
            P.dve(lambda e: e.tensor_tensor_scan(out=Lsg[:, 0:TB], data0=resetm[:, 0:TB], data1=sg[:, 0:TB], initial=0.0, op0=ALU.mult, op1=ALU.add), reads=["FT1", "resetm"], writes=["FT3"])
            P.act(lambda e: e.activation(out=Pt[:, 0:TB], in_=Lsg[:, 0:TB], func=AF.Exp, scale=-C0), reads=["FT3"], writes=["FT4"])
            P.act(lambda e: e.activation(out=Pinv[:, 0:TB], in_=Lsg[:, 0:TB], func=AF.Exp, scale=C0), reads=["FT3"], writes=["FT5"])
            P.dve(lambda e: e.tensor_tensor(out=Pm1[:, 0:TB], in0=Lsg[:, 0:TB], in1=sg[:, 0:TB], op=ALU.subtract), reads=["FT3", "FT1"], writes=["FT6"])
            P.act(lambda e: e.activation(out=Pm1[:, 0:TB], in_=Pm1[:, 0:TB], func=AF.Exp, scale=-C0), reads=["FT6"], writes=["FT6"])
            P.pool(lambda e, pr=pr: e.tensor_copy(out=pce[:, pr * NCH:(pr + 1) * NCH], in_=Pt[:, 0:TB].rearrange("p (c n) -> p c n", n=64)[:, :, 63]), reads=["FT4"], writes=["pce"])
            b_chunk(pr, xr, "FT7")
            P.dve(lambda e, pr=pr: e.tensor_tensor(out=ar_dst(pr, 1), in0=t3(xr[:, 0:TB]), in1=t3(Pt[:, 0:TB]), op=ALU.mult), reads=["FT7", "FT4"], writes=["BAR"])
            b_chunk(4 + pr, xk, "FT8")
            P.dve(lambda e, pr=pr: e.scalar_tensor_tensor(out=kmod[:, 0:TB], in0=aic[:, 0:TB], scalar=-1.0, in1=col(KA0 + pr).to_broadcast([128, TB]), op0=ALU.add, op1=ALU.mult), reads=["FT2", "pcol"], writes=["FT9"])
            P.dve(lambda e: e.scalar_tensor_tensor(out=kmod[:, 0:TB], in0=kmod[:, 0:TB], scalar=1.0, in1=xk[:, 0:TB], op0=ALU.add, op1=ALU.mult), reads=["FT9", "FT8"], writes=["FT9"])
            rkb = HT[1]
            P.dve(lambda e, pr=pr: e.scalar_tensor_tensor(out=rkb[:, 0:TB], in0=xr[:, 0:TB], scalar=col(RK0 + pr), in1=kmod[:, 0:TB], op0=ALU.mult, op1=ALU.mult), reads=["FT7", "FT9", "pcol"], writes=["HT1"])
            P.pe(lambda e: e.matmul(PF[3][:, 0:TB], lhsT=blb[:, :], rhs=rkb[:, 0:TB], start=True, stop=True), reads=["HT1", "blb"], writes=["PF3"])
            P.dve(lambda e, pr=pr: e.tensor_scalar(out=xk[:, 0:TB], in0=xk[:, 0:TB], scalar1=col(KK0 + pr), scalar2=None, op0=ALU.mult), reads=["FT8", "pcol"], writes=["FT8"])
            sqb = HT[0]
            P.act(lambda e: e.activation(out=sqb[:, 0:TB], in_=xk[:, 0:TB], func=AF.Square), reads=["FT8"], writes=["HT0"])
            P.pe(lambda e: e.matmul(PF[2][:, 0:TB], lhsT=blb[:, :], rhs=sqb[:, 0:TB], start=True, stop=True), reads=["HT0", "blb"], writes=["PF2"])
            rsqrt_act(xr[:, 0:TB], PF[2][:, 0:TB], 1.0, 1e-12, ["PF2"], ["FT7"])
            P.dve(lambda e: e.tensor_tensor(out=xk[:, 0:TB], in0=xk[:, 0:TB], in1=xr[:, 0:TB], op=ALU.mult), reads=["FT8", "FT7"], writes=["FT8"])
            P.dve(lambda e, pr=pr: e.scalar_tensor_tensor(out=ar_dst(pr, 0), in0=t3(xk[:, 0:TB]), scalar=-1.0, in1=t3(Pm1[:, 0:TB]), op0=ALU.mult, op1=ALU.mult), reads=["FT8", "FT6"], writes=["BAR"])
            P.pool(lambda e: e.tensor_tensor(out=xk[:, 0:TB], in0=xk[:, 0:TB], in1=aic[:, 0:TB], op=ALU.mult), reads=["FT8", "FT2"], writes=["FT8"])
            for hh in range(2):
                hs = slice(64 * hh, 64 * hh + 64)
                P.dve(lambda e, pr=pr, hh=hh, hs=hs: e.tensor_tensor(out=bk_dst(hh, pr, 0), in0=t3(xk[hs, 0:TB]), in1=t3(Pinv[hs, 0:TB]), op=ALU.mult), reads=["FT8", "FT5"], writes=["BLK%d" % (3 + hh)])
                P.dve(lambda e, pr=pr, hh=hh, hs=hs: e.tensor_tensor(out=bk_dst(hh, pr, 1), in0=t3(kmod[hs, 0:TB]), in1=t3(Pinv[hs, 0:TB]), op=ALU.mult), reads=["FT9", "FT5"], writes=["BLK%d" % (3 + hh)])
            b_chunk(8 + pr, xr, "FT7")
            P.act(lambda e, pr=pr: e.activation(out=VBT[:, pr * TB:(pr + 1) * TB], in_=xr[:, 0:TB], func=AF.Copy), reads=["FT7"], writes=["BVB"])
            P.dve(lambda e, pr=pr: e.tensor_tensor(out=bon[:, pr * TB:(pr + 1) * TB], in0=xr[:, 0:TB], in1=PF[3][:, 0:TB], op=ALU.mult), reads=["FT7", "PF3"], writes=["bon"])
            b_chunk(12 + pr, xk, "FT8")
            P.act(lambda e, pr=pr: e.activation(out=SGB[:, pr * TB:(pr + 1) * TB], in_=xk[:, 0:TB], func=AF.Silu), reads=["FT8"], writes=["BSG"])
        if last_blk:
            P.pe(lambda e: e.transpose(out=PF[0][0:17, 0:128], in_=bcar[:, :], identity=IDF), reads=["bcar", "cf"], writes=["PF0"])
            P.dve(lambda e: e.tensor_copy(out=FT[0][0:17, 0:128], in_=PF[0][0:17, 0:128]), reads=["PF0"], writes=["FT0"])
            P.dma(p_bshift[:, :], FT[0][0:17, 0:128], reads=["FT0"], writes=["p_bshift"])

        P.capture_begin()
        for tb in range(NTB):
            aTM, bTM, kTM, vTM = HT[0], HT[1], HT[26], HT[27]
            bonTM, bonk = (HT[28], "HT28") if tb % 2 == 0 else (HT[38], "HT38")
            sgTM, sgk = (HT[29], "HT29") if tb % 2 == 0 else (HT[39], "HT39")
            def emit_tm(tb=tb, aTM=aTM, bTM=bTM, kTM=kTM, vTM=vTM, bonTM=bonTM, bonk=bonk, sgTM=sgTM, sgk=sgk):
                for s_, dstt, dkey in ((0, bTM, "HT1"), (1, kTM, "HT26")):
                    for pr in range(4):
                        P.pe(lambda e, pr=pr, s_=s_: e.matmul(PF[2][:, pr * 128:(pr + 1) * 128], lhsT=bkh(0, pr, tb, s_), rhs=idb[:, :], start=True, stop=False), reads=["BLK3", "idb"], writes=["PF2"])
                        P.pe(lambda e, pr=pr, s_=s_: e.matmul(PF[2][:, pr * 128:(pr + 1) * 128], lhsT=bkh(1, pr, tb, s_), rhs=idb[:, :], start=False, stop=True), reads=["BLK4", "idb"], writes=["PF2"])
                    P.copy(dstt[:, :], PF[2][:, :], reads=["PF2"], writes=[dkey])
                srcs = [(lambda pr: ar(pr, tb, 0), "BAR", aTM, "HT0"), (lambda pr: VBT[:, pr * TB + tb * 128:pr * TB + (tb + 1) * 128], "BVB", vTM, "HT27"),
                        (lambda pr: bon[:, pr * TB + tb * 128:pr * TB + (tb + 1) * 128], "bon", bonTM, bonk),
                        (lambda pr: SGB[:, pr * TB + tb * 128:pr * TB + (tb + 1) * 128], "BSG", sgTM, sgk)]
                for si, (srcf, skey, dstt, dkey) in enumerate(srcs):
                    half = (si % 2) * 512
                    for pr in range(4):
                        P.pe(lambda e, pr=pr, srcf=srcf, half=half: e.transpose(out=PB[1][:, half + pr * 128:half + (pr + 1) * 128], in_=srcf(pr), identity=idb[:, :]), reads=[skey, "idb"], writes=["PB1"])
                    if True:
                        P.dve(lambda e, dstt=dstt, half=half: e.tensor_copy(out=dstt[:, :], in_=PB[1][:, half:half + 512]), reads=["PB1"], writes=[dkey])
            WmT, U0b, yb = HT[30], FT[0], FT[1]
            Yab = [HT[14], HT[15]]
            Yak = [HT[16], HT[17]]
            Aak = [HT[18], HT[19]]
            AV = HT[20]
            Ub = HT[21]
            for hb in range(2):
                for hl in range(4):
                    h = hb * 4 + hl
                    pr, p0 = h // 2, 64 * (h % 2)
                    P.pe(lambda e, hl=hl, pr=pr, h=h: e.matmul(PF[2 + hl // 2][:, (hl % 2) * 256:(hl % 2 + 1) * 256], lhsT=bkh(h % 2, pr, tb, 0), rhs=AR[:, pr * 2 * TB + tb * 256: pr * 2 * TB + (tb + 1) * 256], start=True, stop=True), reads=["BAR", "BLK3", "BLK4"], writes=["PF%d" % (2 + hl // 2)])
                pd2 = lambda i: PF[2 + i][:, :].rearrange("p (h s n) -> p h s n", h=2, s=2)
                h2 = lambda t_, i: t_[:, i * 256:(i + 1) * 256].rearrange("p (h n) -> p h n", h=2)
                m2 = lambda m_: m_.unsqueeze(1).to_broadcast([128, 2, 128])
                X0 = FT[3]
                for i in range(2):
                    P.dve(lambda e, i=i: e.tensor_tensor(out=h2(X0, i), in0=pd2(i)[:, :, 0, :], in1=m2(NSTRICT), op=ALU.mult), reads=["PF%d" % (2 + i), "cf"], writes=["FT3"])
                    P.dve(lambda e, hb=hb, i=i: e.tensor_tensor(out=h2(Yab[hb], i), in0=pd2(i)[:, :, 1, :], in1=m2(INCL), op=ALU.mult), reads=["PF%d" % (2 + i), "cf"], writes=["HT%d" % (14 + hb)])
                for hl in range(4):
                    h = hb * 4 + hl
                    pr, p0 = h // 2, 64 * (h % 2)
                    P.pe(lambda e, hl=hl, pr=pr, h=h: e.matmul(PF[2 + hl // 2][:, (hl % 2) * 256:(hl % 2 + 1) * 256], lhsT=bkh(h % 2, pr, tb, 1), rhs=AR[:, pr * 2 * TB + tb * 256: pr * 2 * TB + (tb + 1) * 256], start=True, stop=True), reads=["BAR", "BLK3", "BLK4"], writes=["PF%d" % (2 + hl // 2)])
                for i in range(2):
                    P.dve(lambda e, hb=hb, i=i: e.tensor_tensor(out=h2(Aak[hb], i), in0=pd2(i)[:, :, 0, :], in1=m2(STRICT), op=ALU.mult), reads=["PF%d" % (2 + i), "cf"], writes=["HT%d" % (18 + hb)])
                    P.dve(lambda e, hb=hb, i=i: e.tensor_tensor(out=h2(Yak[hb], i), in0=pd2(i)[:, :, 1, :], in1=m2(INCL), op=ALU.mult), reads=["PF%d" % (2 + i), "cf"], writes=["HT%d" % (16 + hb)])
                resB = dict(X=([HT[32], HT[33]], ["HT32", "HT33"]), Q=([HT[34], HT[35]], ["HT34", "HT35"]), N=([HT[36], HT[37]], ["HT36", "HT37"]),
                            SQ=(PQ[1], "PQ1"), G0=(PF[2], "PF2"), G1=(PF[3], "PF3"), T=(PB[1], "PB1"))
                TinvT, tk = inverse_chain(P, X0, "FT3", resB, IDF, idb)
                if hb == 0:
                    emit_tm()
                for hl in range(4):
                    h = hb * 4 + hl
                    pr, p0 = h // 2, 64 * (h % 2)
                    P.pe(lambda e, hl=hl, pr=pr: e.matmul(PF[2][:, hl * 128:(hl + 1) * 128], lhsT=aTM[:, pr * 128:(pr + 1) * 128], rhs=TinvT[:, hl * 128:(hl + 1) * 128], start=True, stop=True), reads=["HT0", tk], writes=["PF2"])
                    P.pe(lambda e, hl=hl, h=h, hb=hb: e.matmul(PF[3][:, hl * 64:(hl + 1) * 64], lhsT=Aak[hb][:, hl * 128:(hl + 1) * 128], rhs=vTM[:, h * 64:(h + 1) * 64], start=True, stop=True), reads=["HT%d" % (18 + hb), "HT27"], writes=["PF3"])
                for hh in range(2):
                    p0 = 64 * hh
                    src = PF[2][p0:p0 + 64, :].rearrange("p (a b n) -> p a b n", a=2, b=2)[:, :, hh, :]
                    dst = WmT[p0:p0 + 64, hb * 256:(hb + 1) * 256].rearrange("p (a n) -> p a n", a=2)
                    P.copy(dst, src, reads=["PF2"], writes=["HT30"])
                P.copy(AV[:, 0:256], PF[3][:, 0:256], reads=["PF3"], writes=["HT20"])
                for hl in range(4):
                    P.pe(lambda e, hl=hl: e.matmul(PF[3][:, 256 + hl * 64:256 + (hl + 1) * 64], lhsT=TinvT[:, hl * 128:(hl + 1) * 128], rhs=AV[:, hl * 64:(hl + 1) * 64], start=True, stop=True), reads=[tk, "HT20"], writes=["PF3"])
                P.copy(U0b[:, hb * 256:(hb + 1) * 256], PF[3][:, 256:512], reads=["PF3"], writes=["FT0"])
            for c in range(2):
                r0, r1 = c * 64, c * 64 + 64
                ci = tb * 2 + c
                hsl = lambda h: slice((h // 2) * 128 + (h % 2) * 64, (h // 2) * 128 + (h % 2) * 64 + 64)
                for h in range(8):
                    pr, p0 = h // 2, 64 * (h % 2)
                    P.pe(lambda e, h=h, pr=pr, p0=p0: e.matmul(PF[2][:, h * 64:(h + 1) * 64], lhsT=WmT[:, pr * 128:(pr + 1) * 128], rhs=Hbb[:, hsl(h)], start=True, stop=True), reads=["HT30", "Hbb"], writes=["PF2"])
                P.dve(lambda e, r0=r0, r1=r1: e.tensor_tensor(out=Ub[r0:r1, :], in0=U0b[r0:r1, :], in1=PF[2][r0:r1, :], op=ALU.add), reads=["FT0", "PF2"], writes=["HT21"])
                for h in range(8):
                    pr, p0, hb, hl = h // 2, 64 * (h % 2), h // 4, h % 4
                    P.pe(lambda e, h=h, pr=pr, p0=p0: e.matmul(PF[3][:, h * 64:(h + 1) * 64], lhsT=ar(pr, tb, 1), rhs=Hbb[:, hsl(h)], start=True, stop=False), reads=["BAR", "Hbb"], writes=["PF3"])
                    P.pe(lambda e, h=h, hb=hb, hl=hl, r0=r0, r1=r1: e.matmul(PF[3][:, h * 64:(h + 1) * 64], lhsT=Yab[hb][r0:r1, hl * 128:(hl + 1) * 128], rhs=Ub[r0:r1, h * 64:(h + 1) * 64], start=False, stop=False), reads=["HT%d" % (14 + hb), "HT21"], writes=["PF3"])
                    P.pe(lambda e, h=h, hb=hb, hl=hl, r0=r0, r1=r1: e.matmul(PF[3][:, h * 64:(h + 1) * 64], lhsT=Yak[hb][r0:r1, hl * 128:(hl + 1) * 128], rhs=vTM[r0:r1, h * 64:(h + 1) * 64], start=False, stop=True), reads=["HT%d" % (16 + hb), "HT27"], writes=["PF3"])
                P.copy(yb[r0:r1, :], PF[3][r0:r1, :], reads=["PF3"], writes=["FT1"])
                for pr in range(4):
                    P.pe(lambda e, pr=pr, r0=r0, r1=r1: e.matmul(PQ[1][:, pr * 128:(pr + 1) * 128], lhsT=bTM[r0:r1, pr * 128:(pr + 1) * 128], rhs=Ub[r0:r1, pr * 128:(pr + 1) * 128], start=True, stop=False), reads=["HT1", "HT21"], writes=["PQ1"])
                    P.pe(lambda e, pr=pr, r0=r0, r1=r1: e.matmul(PQ[1][:, pr * 128:(pr + 1) * 128], lhsT=kTM[r0:r1, pr * 128:(pr + 1) * 128], rhs=vTM[r0:r1, pr * 128:(pr + 1) * 128], start=False, stop=True), reads=["HT26", "HT27"], writes=["PQ1"])
                P.dve(lambda e: e.tensor_tensor(out=Hb[:, :], in0=Hb[:, :], in1=PQ[1][:, 0:512], op=ALU.add), reads=["Hb", "PQ1"], writes=["Hb"])
                pc3 = pce[:, :].rearrange("p (a c) -> p a c", a=4)[:, :, ci]
                P.dve(lambda e, pc3=pc3: e.tensor_tensor(out=b3(Hb[:, :], 4, 128), in0=b3(Hb[:, :], 4, 128), in1=pc3.unsqueeze(2).to_broadcast([128, 4, 128]), op=ALU.mult), reads=["Hb", "pce"], writes=["Hb"])
                P.pool(lambda e: e.tensor_tensor(out=b3(Hbb[:, :], 4, 128), in0=b3(Hb[:, :], 4, 128), in1=BL.unsqueeze(1).to_broadcast([128, 4, 128]), op=ALU.mult), reads=["Hb", "cf"], writes=["Hbb"])
            y3 = b3(yb[:, :], 8, 64)
            stt = FT[2]
            P.dve(lambda e: e.tensor_reduce(out=stt[:, 0:8], in_=y3, axis=AX.X, op=ALU.add), reads=["FT1"], writes=["FT2"])
            P.dve(lambda e: e.tensor_scalar(out=stt[:, 0:8], in0=stt[:, 0:8], scalar1=1.0 / 64, scalar2=None, op0=ALU.mult), reads=["FT2"], writes=["FT2"])
            P.dve(lambda e: e.tensor_tensor(out=y3, in0=y3, in1=stt[:, 0:8].unsqueeze(2).to_broadcast([128, 8, 64]), op=ALU.subtract), reads=["FT1", "FT2"], writes=["FT1"])
            sq2 = FT[0]
            P.pool(lambda e: e.tensor_tensor(out=sq2[:, :], in0=yb[:, :], in1=yb[:, :], op=ALU.mult), reads=["FT1"], writes=["FT0"])
            P.dve(lambda e: e.tensor_reduce(out=stt[:, 8:16], in_=b3(sq2[:, :], 8, 64), axis=AX.X, op=ALU.add), reads=["FT0"], writes=["FT2"])
            rsqrt_act(stt[:, 8:16], stt[:, 8:16], 1.0 / 64, 64e-5, ["FT2"], ["FT2"])
            P.dve(lambda e: e.tensor_tensor(out=y3, in0=y3, in1=stt[:, 8:16].unsqueeze(2).to_broadcast([128, 8, 64]), op=ALU.mult), reads=["FT1", "FT2"], writes=["FT1"])
            P.pool(lambda e: e.tensor_tensor(out=yb[:, :], in0=yb[:, :], in1=LNW, op=ALU.mult), reads=["FT1", "rp"], writes=["FT1"])
            P.pool(lambda e: e.tensor_tensor(out=yb[:, :], in0=yb[:, :], in1=LNB, op=ALU.add), reads=["FT1", "rp"], writes=["FT1"])
            P.dve(lambda e: e.tensor_tensor(out=yb[:, :], in0=yb[:, :], in1=bonTM[:, :], op=ALU.add), reads=["FT1", bonk], writes=["FT1"])
            P.dve(lambda e: e.tensor_tensor(out=mixB4[:, tb * 512:(tb + 1) * 512], in0=yb[:, :], in1=sgTM[:, :], op=ALU.mult), reads=["FT1", sgk], writes=["mixB4_%d" % tb])
        for tb in range(NTB):
            for c8 in range(8):
                if c8 < 4:
                    P.pe(lambda e, c8=c8: e.transpose(out=PB[1][:, c8 * 128:(c8 + 1) * 128], in_=mixA[:, tb * 512 + c8 * 128:tb * 512 + (c8 + 1) * 128], identity=idb[:, :]), reads=[mxk, "idb"], writes=["PB1"])
                else:
                    P.pe(lambda e, c8=c8: e.transpose(out=PB[1][:, c8 * 128:(c8 + 1) * 128], in_=mixB4[:, tb * 512 + (c8 - 4) * 128:tb * 512 + (c8 - 3) * 128], identity=idb[:, :]), reads=["mixB4_%d" % tb, "idb"], writes=["PB1"])
            P.dve(lambda e: e.tensor_copy(out=mixT[:, :], in_=PB[1][:, :]), reads=["PB1"], writes=["mixT"])
            hx = xt[tb % 2]
            hk = "xt%d" % (tb % 2)
            P.dma(hx[:, :], xp[t0 + tb * 128:t0 + (tb + 1) * 128, :], writes=[hk])
            for n in range(2):
                for kc in range(8):
                    P.pe(lambda e, n=n, kc=kc: e.matmul(PQ[1][:, :], lhsT=mixT[:, kc * 128:(kc + 1) * 128], rhs=woutb[:, kc * 1024 + n * 512:kc * 1024 + (n + 1) * 512], start=(kc == 0), stop=(kc == 7)), reads=["mixT", "woutb"], writes=["PQ1"])
                P.dve(lambda e, n=n, hx=hx: e.tensor_tensor(out=hx[:, n * 512:(n + 1) * 512], in0=hx[:, n * 512:(n + 1) * 512], in1=PQ[1][:, :], op=ALU.add), reads=[hk, "PQ1"], writes=[hk])
            P.act(lambda e, hx=hx: e.activation(out=xs_[:, :], in_=hx[:, :], func=AF.Square, accum_out=st4[:, 4:5]), reads=[hk], writes=["xs_", "st4"])
            rsqrt_act(st4[:, 4:5], st4[:, 4:5], 1.0 / D, 1e-6, ["st4"], ["st4"])
            P.dve(lambda e, hx=hx: e.scalar_tensor_tensor(out=xs_[:, :], in0=hx[:, :], scalar=st4[:, 4:5], in1=FNW, op0=ALU.mult, op1=ALU.mult), reads=[hk, "st4", "rp"], writes=["xs_"])
            P.dma(yp[t0 + tb * 128:t0 + (tb + 1) * 128, :], xs_[:, :], reads=["xs_"], writes=["yp%d_%d" % (blk, tb)])
        if last_blk:
            for pr in range(4):
                P.pe(lambda e, pr=pr: e.transpose(out=PF[2][:, pr * 128:(pr + 1) * 128], in_=Hb[:, pr * 128:(pr + 1) * 128], identity=IDF), reads=["Hb", "cf"], writes=["PF2"])
            P.dve(lambda e: e.tensor_copy(out=FT[4][:, :], in_=PF[2][:, :]), reads=["PF2"], writes=["FT4"])
            for h in range(8):
                pr, p0 = h // 2, 64 * (h % 2)
                P.dma(p_bmat[h], FT[4][p0:p0 + 64, pr * 128 + p0:pr * 128 + p0 + 64], reads=["FT4"], writes=["p_bmat%d" % h])
        stream_b = P.capture_end()
        if last_blk:
            P.replay([stream_b])

    if DECODE_LAST[0]:
        P.pool(lambda e: e.memset(st4[:, 7:8], 0.0), reads=["BAR", "BVB", "BSG"], writes=["SF%d" % i for i in range(8)] + ["st4"])
        sample_path(SAMPLE_LOCALS)
    P.finish()
    return nc


def inverse_chain(P, X0, x0key, res, IDF, idb):
    Xt, Xk = res["X"]
    Qt, Qk = res["Q"]
    Nt, Nk = res["N"]
    (SQ, sqk), (G0, g0k), (G1, g1k), (TT, ttk) = res["SQ"], res["G0"], res["G1"], res["T"]

    def mm4(out_ps, okey, lhs, lkey, rhs, rkey):
        for h in range(4):
            P.pe(lambda e, h=h: e.matmul(out_ps[:, h * 128:(h + 1) * 128], lhsT=lhs[:, h * 128:(h + 1) * 128], rhs=rhs[:, h * 128:(h + 1) * 128], start=True, stop=True), reads=[lkey, rkey], writes=[okey])

    P.copy(Xt[0][:, :], X0[:, :], reads=[x0key], writes=[Xk[0]])
    P.dve(lambda e: e.tensor_tensor(out=Qt[0][:, :].rearrange("p (h n) -> p h n", h=4), in0=IDF.unsqueeze(1).to_broadcast([128, 4, 128]), in1=X0[:, :].rearrange("p (h n) -> p h n", h=4), op=ALU.subtract), reads=["cf", x0key], writes=[Qk[0]])
    for h in range(4):
        P.pe(lambda e, h=h: e.transpose(out=TT[:, h * 128:(h + 1) * 128], in_=Xt[0][:, h * 128:(h + 1) * 128], identity=idb[:, :]), reads=[Xk[0], "idb"], writes=[ttk])
    P.dve(lambda e: e.tensor_copy(out=Nt[0][:, :], in_=TT[:, 0:512]), reads=[ttk], writes=[Nk[0]])
    mm4(SQ, sqk, Nt[0], Nk[0], Xt[0], Xk[0])
    mm4(G0, g0k, Xt[0], Xk[0], Nt[0], Nk[0])
    P.copy(Xt[1][:, :], SQ[:, 0:512], reads=[sqk], writes=[Xk[1]])
    P.copy(Nt[1][:, :], G0[:, 0:512], reads=[g0k], writes=[Nk[1]])
    xi, ni, qi = 1, 1, 0
    for m in range(5):
        mm4(G1, g1k, Nt[ni], Nk[ni], Qt[qi], Qk[qi])
        P.dve(lambda e, qi=qi: e.tensor_tensor(out=Qt[1 - qi][:, :], in0=Qt[qi][:, :], in1=G1[:, 0:512], op=ALU.add), reads=[Qk[qi], g1k], writes=[Qk[1 - qi]])
        qi = 1 - qi
        if m < 4:
            mm4(SQ, sqk, Nt[ni], Nk[ni], Xt[xi], Xk[xi])
            mm4(G0, g0k, Xt[xi], Xk[xi], Nt[ni], Nk[ni])
            P.copy(Xt[1 - xi][:, :], SQ[:, 0:512], reads=[sqk], writes=[Xk[1 - xi]])
            P.copy(Nt[1 - ni][:, :], G0[:, 0:512], reads=[g0k], writes=[Nk[1 - ni]])
            xi, ni = 1 - xi, 1 - ni
    return Qt[qi], Qk[qi]


_PQ = {}


def PBt_f32(PD, FT):
    return _PQ["t"]


def group_b_block(L):
    pass


def pack_inputs(inp):
    g = lambda k: np.asarray(inp[k], np.float32)
    vrows = np.zeros((128, 128), np.float32)
    vrows[0:8] = g("norm_w")[0].reshape(8, 128)
    cw = g("conv_w")[0]
    for c in range(12):
        for i in range(4):
            vrows[8 + c * 4 + i] = cw[i, c * 128:(c + 1) * 128]
    vrows[56:73] = g("mu")[0].reshape(17, 128)
    vrows[73:77] = g("w0")[0].reshape(4, 128)
    vrows[77:81] = g("a0")[0].reshape(4, 128)
    vrows[81:85] = g("k_k")[0].reshape(4, 128)
    vrows[85:89] = g("k_a")[0].reshape(4, 128)
    vrows[89:93] = g("r_k")[0].reshape(4, 128)
    rowp = np.concatenate([g("ln_w")[0], g("ln_b")[0], np.zeros(1024, np.float32), g("a_norm_w")[0],
                           g("final_norm_w"), g("a_log")[0], g("dt_bias")[0]]).astype(np.float32)
    p = np.arange(512)
    resetm = np.broadcast_to((p % 64 != 0).astype(np.float32), (128, 512)).copy()
    return dict(w_in=g("w_in")[0], w_out=g("w_out")[0], vrows=vrows, rowp=rowp, w2=g("w2")[0], a2=g("a2")[0],
                consts=host_consts(), resetm=resetm)


def kernel(**inputs):
    n = 8
    packed = pack_inputs(inputs)
    xp = np.ascontiguousarray(np.asarray(inputs["x_prompt"], np.float32))
    nc = build(T=2048, NS=16, TB=512)
    in_maps = []
    for i in range(n):
        m = dict(packed)
        m["xp"] = xp[i]
        sl = slice(i * 16, (i + 1) * 16)
        m["xs"] = np.ascontiguousarray(np.asarray(inputs["x_sample"], np.float32)[sl, 0])
        m["sa_mat"] = np.ascontiguousarray(np.asarray(inputs["state_a_mat"], np.float32)[0, sl])
        m["sa_conv"] = np.ascontiguousarray(np.asarray(inputs["state_a_conv"], np.float32)[0, sl])
        m["sb_mat"] = np.ascontiguousarray(np.asarray(inputs["state_b_mat"], np.float32)[0, sl])
        m["sb_shift"] = np.ascontiguousarray(np.asarray(inputs["state_b_shift"], np.float32)[0, sl])
        in_maps.append(m)
    res = run_bass_kernel_spmd(nc, in_maps, core_ids=list(range(n)))
    r = res.results
    f = lambda k: [np.asarray(r[i][k], np.float32) for i in range(n)]
    y_prompt = np.stack(f("yp"))
    p_amat = np.stack(f("p_amat"))[None]
    p_aconv = np.stack([a.reshape(3, 12, 128).reshape(3, 1536) for a in f("p_aconv")])[None]
    p_bmat = np.stack(f("p_bmat"))[None]
    p_bshift = np.stack([a.reshape(2176) for a in f("p_bshift")])[None]
    y_sample = np.concatenate(f("ys"))[:, None, :]
    s_amat = np.concatenate(f("s_amat"))[None]
    s_aconv = np.concatenate(f("s_aconv"))[None]
    s_bmat = np.concatenate(f("s_bmat"))[None]
    s_bshift = np.concatenate(f("s_bshift"))[None]
    return (y_prompt, y_sample, p_amat, p_aconv, p_bmat, p_bshift, s_amat, s_aconv, s_bmat, s_bshift)


def sample_path(L):
    P, NS = L["P"], L["NS"]
    PF, PQ, PB = L["PF"], L["PQ"], L["PB"]
    FT, HT, SF, SPJ, xnTs = L["FT"], L["HT"], L["SF"], L["SPJ"], L["xnTs"]
    xt, xs_, st4, pcol, col = L["xt"], L["xs_"], L["st4"], L["pcol"], L["col"]
    IDF, ONESF, BL, PAIRS, idb = L["IDF"], L["ONESF"], L["BL"], L["PAIRS"], L["idb"]
    wbf, w_in, woutb, w2a2 = L["wbf"], L["w_in"], L["woutb"], L["w2a2"]
    rsqrt_act, b3 = L["rsqrt_act"], L["b3"]
    LNW, LNB, ANW, FNW, DTB, nega = L["LNW"], L["LNB"], L["ANW"], L["FNW"], L["DTB"], L["nega"]
    NW0, CW0, MU0, W00, A00, KK0, KA0, RK0 = 0, 8, 56, 73, 77, 81, 85, 89
    xs_d, sa_mat_d, sa_conv_d, sb_mat_d, sb_shift_d = L["xs_d"], L["sa_mat_d"], L["sa_conv_d"], L["sb_mat_d"], L["sb_shift_d"]
    ys, s_amat, s_aconv, s_bmat, s_bshift = L["ys"], L["s_amat"], L["s_aconv"], L["s_bmat"], L["s_bshift"]
    scr_q, scr_k, scr_v, scr_ab, scr_o, scr_b6, scr_y = L["scr_q"], L["scr_k"], L["scr_v"], L["scr_ab"], L["scr_o"], L["scr_b6"], L["scr_y"]
    N = NS
    R = slice(0, N)

    xa = xt[0]
    P.dma(xa[R, :], xs_d[:, :], writes=["xt0"])
    P.act(lambda e: e.activation(out=xs_[R, :], in_=xa[R, :], func=AF.Square, accum_out=st4[R, 0:1]), reads=["xt0"], writes=["xs_", "st4"])
    rsqrt_act(st4[R, 0:1], st4[R, 0:1], 1.0 / D, 1e-6, ["st4"], ["st4"])
    P.act(lambda e: e.activation(out=xs_[R, :], in_=xa[R, :], func=AF.Copy, scale=st4[R, 0:1]), reads=["xt0", "st4"], writes=["xs_"])
    for half in range(2):
        pf, pk = PF[half], "PF%d" % half
        for q in range(4):
            kc = half * 4 + q
            P.pe(lambda e, pf=pf, q=q, kc=kc: e.transpose(out=pf[:, q * N:(q + 1) * N], in_=xs_[R, kc * 128:(kc + 1) * 128], identity=IDF[R, R]), reads=["xs_", "cf"], writes=[pk])
        P.dve(lambda e, pf=pf, half=half: e.tensor_tensor(out=xnTs[:, half * 4 * N:(half + 1) * 4 * N].rearrange("p (k t) -> p k t", k=4), in0=pf[:, 0:4 * N].rearrange("p (k t) -> p k t", k=4), in1=pcol[:, NW0 + half * 4:NW0 + half * 4 + 4].unsqueeze(2).to_broadcast([128, 4, N]), op=ALU.mult), reads=[pk, "pcol"], writes=["xnTs"])

    chunks = [(c * 128, 128) for c in range(16)] + [(2048, 8)] + [(2056 + j * 128, 128) for j in range(17)]
    for idx, (c0, ncols) in enumerate(chunks):
        s = idx % len(wbf)
        bk_ = "wbf%d" % s
        pf, pk = PF[idx % 2], "PF%d" % (idx % 2)
        P.dma(wbf[s][:, 0:8 * ncols].rearrange("p (k n) -> p k n", k=8), w_in[:, c0:c0 + ncols].rearrange("(k p) n -> p k n", p=128), writes=[bk_], q="pool")
        for kc in range(8):
            P.pe(lambda e, kc=kc, s=s, pf=pf, ncols=ncols: e.matmul(pf[0:ncols, 0:N], lhsT=wbf[s][:, kc * ncols:(kc + 1) * ncols], rhs=xnTs[:, kc * N:(kc + 1) * N], start=(kc == 0), stop=(kc == 7)), reads=[bk_, "xnTs"], writes=[pk])
        P.act(lambda e, pf=pf, ncols=ncols, idx=idx: e.activation(out=SPJ[0:ncols, idx * N:(idx + 1) * N], in_=pf[0:ncols, 0:N], func=AF.Copy), reads=[pk], writes=["SPJ"])
    sp = lambda i, j=None: SPJ[:, i * N:((i + 1) if j is None else j) * N]

    def to_tm(srcs, skey, dst, dkey, ps, pskey):
        for i, src in enumerate(srcs):
            P.pe(lambda e, i=i, src=src: e.transpose(out=ps[R, i * 128:(i + 1) * 128], in_=src, identity=IDF), reads=[skey, "cf"], writes=[pskey])
        n = len(srcs) * 128
        P.dve(lambda e: e.tensor_copy(out=dst[R, 0:n], in_=ps[R, 0:n]), reads=[pskey], writes=[dkey])

    cv = sa_conv_d.rearrange("b i c -> (b i) c")
    P.dma(xt[1][0:3 * N, 0:1024], cv[:, 0:1024], writes=["xt1"])
    P.dma(FT[0][0:3 * N, 0:512], cv[:, 1024:1536], writes=["FT0"])
    for c in range(12):
        src = xt[1][0:3 * N, c * 128:(c + 1) * 128] if c < 8 else FT[0][0:3 * N, (c - 8) * 128:(c - 7) * 128]
        po = c * 3 * N if c < 10 else 512 + (c - 10) * 3 * N
        pq_ = PQ[0] if c < 10 else PQ[1]
        po = po % 512
        P.pe(lambda e, c=c, src=src, po=po, pq_=pq_: e.transpose(out=pq_[:, po:po + 3 * N], in_=src, identity=IDF[0:3 * N, 0:3 * N]), reads=["xt1", "FT0", "cf"], writes=["PQ0", "PQ1"])
    P.dve(lambda e: e.tensor_copy(out=xs_[:, 0:30 * N], in_=PQ[0][:, 0:30 * N]), reads=["PQ0"], writes=["xs_"])
    P.dve(lambda e: e.tensor_copy(out=xs_[:, 30 * N:36 * N], in_=PQ[1][:, 0:6 * N]), reads=["PQ1"], writes=["xs_"])
    acc = FT[2]
    for c in range(12):
        cs3 = xs_[:, c * 3 * N:(c + 1) * 3 * N].rearrange("p (b i) -> p b i", i=3)
        tmp3 = FT[1][:, 0:3 * N].rearrange("p (b i) -> p b i", i=3)
        P.dve(lambda e, cs3=cs3, tmp3=tmp3, c=c: e.tensor_tensor(out=tmp3, in0=cs3, in1=pcol[:, CW0 + 4 * c:CW0 + 4 * c + 3].unsqueeze(1).to_broadcast([128, N, 3]), op=ALU.mult), reads=["xs_", "pcol"], writes=["FT1"])
        P.dve(lambda e, tmp3=tmp3, c=c: e.tensor_reduce(out=acc[:, c * N:(c + 1) * N], in_=tmp3, axis=AX.X, op=ALU.add), reads=["FT1"], writes=["FT2"])
        P.dve(lambda e, c=c: e.scalar_tensor_tensor(out=acc[:, c * N:(c + 1) * N], in0=sp(c), scalar=col(CW0 + 4 * c + 3), in1=acc[:, c * N:(c + 1) * N], op0=ALU.mult, op1=ALU.add), reads=["SPJ", "pcol", "FT2"], writes=["FT2"])
    P.act(lambda e: e.activation(out=acc[:, 0:12 * N], in_=acc[:, 0:12 * N], func=AF.Silu), reads=["FT2"], writes=["FT2"])
    P.act(lambda e: e.activation(out=FT[3][:, 0:8 * N], in_=acc[:, 0:8 * N], func=AF.Square), reads=["FT2"], writes=["FT3"])
    P.pe(lambda e: e.matmul(PF[2][:, 0:8 * N], lhsT=ONESF, rhs=FT[3][:, 0:8 * N], start=True, stop=True), reads=["FT3", "cf"], writes=["PF2"])
    rsqrt_act(FT[3][:, 0:8 * N], PF[2][:, 0:8 * N], 1.0, 1e-12, ["PF2"], ["FT3"])
    P.dve(lambda e: e.scalar_tensor_tensor(out=acc[:, 0:4 * N], in0=acc[:, 0:4 * N], scalar=128 ** -0.5, in1=FT[3][:, 0:4 * N], op0=ALU.mult, op1=ALU.mult), reads=["FT2", "FT3"], writes=["FT2"])
    P.dve(lambda e: e.tensor_tensor(out=acc[:, 4 * N:8 * N], in0=acc[:, 4 * N:8 * N], in1=FT[3][:, 4 * N:8 * N], op=ALU.mult), reads=["FT2", "FT3"], writes=["FT2"])
    for g, (scr, sf, sname) in enumerate(((scr_q, 0, "scr_q"), (scr_k, 1, "scr_k"), (scr_v, 2, "scr_v"))):
        to_tm([acc[:, (g * 4 + i) * N:(g * 4 + i + 1) * N] for i in range(4)], "FT2", SF[sf], "SF%d" % sf, PF[3], "PF3")
        P.dma(scr[:, :], SF[sf][R, 0:512], reads=["SF%d" % sf], writes=[sname])
    P.dma(s_aconv[:, 0:2, :], sa_conv_d[:, 1:3, :], writes=["s_aconv"])
    for g in range(3):
        to_tm([sp(g * 4 + i) for i in range(4)], "SPJ", SF[3], "SF3", PF[3], "PF3")
        P.dma(s_aconv[:, 2, g * 512:(g + 1) * 512], SF[3][R, 0:512], reads=["SF3"], writes=["s_aconv2_%d" % g])
    sga = FT[3]
    P.act(lambda e: e.activation(out=sga[:, 0:4 * N], in_=sp(12, 16), func=AF.Silu), reads=["SPJ"], writes=["FT3"])
    sgaTM = SF[4]
    to_tm([sga[:, i * N:(i + 1) * N] for i in range(4)], "FT3", sgaTM, "SF4", PF[3], "PF3")
    sc = SF[5]
    P.pe(lambda e: e.transpose(out=PF[3][R, 0:8], in_=SPJ[0:8, 16 * N:17 * N], identity=IDF[0:8, 0:8]), reads=["SPJ", "cf"], writes=["PF3"])
    ab3 = sc[R, 16:24].rearrange("p (h s) -> p h s", s=2)
    P.act(lambda e: e.activation(out=ab3[:, :, 1], in_=PF[3][R, 0:4], func=AF.Sigmoid), reads=["PF3"], writes=["SF5"])
    P.dve(lambda e: e.tensor_tensor(out=sc[R, 4:8], in0=PF[3][R, 4:8], in1=DTB[R, :], op=ALU.add), reads=["PF3", "rp"], writes=["SF5"])
    P.act(lambda e: e.activation(out=sc[R, 4:8], in_=sc[R, 4:8], func=AF.Exp), reads=["SF5"], writes=["SF5"])
    P.act(lambda e: e.activation(out=sc[R, 4:8], in_=sc[R, 4:8], func=AF.Ln, bias=1.0), reads=["SF5"], writes=["SF5"])
    P.dve(lambda e: e.tensor_tensor(out=sc[R, 4:8], in0=sc[R, 4:8], in1=nega[R, :], op=ALU.mult), reads=["SF5", "nega"], writes=["SF5"])
    P.act(lambda e: e.activation(out=ab3[:, :, 0], in_=sc[R, 4:8], func=AF.Exp), reads=["SF5"], writes=["SF5"])
    P.dma(scr_ab[:, :], sc[R, 16:24], reads=["SF5"], writes=["scr_ab"])
    VP = FT[8]
    kP, qP, vP, abP = VP[:, 0:64], VP[:, 64:128], VP[:, 128:256], VP[:, 256:260]
    for hi in range(2):
        hs = slice(hi * 64, hi * 64 + 64)
        P.dma(VP[hs, 0:64], scr_k.rearrange("b (h t l) -> (b h) t l", h=4, t=2)[:, hi, :], reads=["scr_k"], writes=["FT8"])
        P.dma(VP[hs, 64:128], scr_q.rearrange("b (h t l) -> (b h) t l", h=4, t=2)[:, hi, :], reads=["scr_q"], writes=["FT8"])
        P.dma(VP[hs, 128:256], scr_v.rearrange("b (h v) -> (b h) v", h=4), reads=["scr_v"], writes=["FT8"])
        P.dma(VP[hs, 256:258], scr_ab.rearrange("b (h s) -> (b h) s", s=2), reads=["scr_ab"], writes=["FT8"])
    P.dve(lambda e: e.tensor_scalar(out=VP[:, 258:259], in0=VP[:, 256:257], scalar1=-1.0, scalar2=None, op0=ALU.mult), reads=["FT8"], writes=["FT8"])
    sav = sa_mat_d.rearrange("b h (t l) v -> (b h) t l v", t=2)
    sov = s_amat.rearrange("b h (t l) v -> (b h) t l v", t=2)
    SLs, prods, red, accs = [(FT[4], "FT4"), (FT[3], "FT3")], [(FT[5], "FT5"), (FT[6], "FT6")], FT[2], FT[7]
    kS, uu, oacc = accs[:, 0:128], accs[:, 128:256], accs[:, 256:384]
    sl3 = lambda t_: t_[:, :].rearrange("p (l v) -> p l v", l=4)
    P.pool(lambda e: e.memset(accs[:, :], 0.0), writes=["FT7"])
    NSL = 16
    for j in range(NSL):
        (SL, slk), (prod, pdk) = SLs[j % 2], prods[j % 2]
        for hi in range(2):
            P.dma(sl3(SL)[hi * 64:hi * 64 + 64], sav[:, hi, 4 * j:4 * j + 4, :], writes=[slk])
        P.dve(lambda e, j=j: e.tensor_tensor(out=sl3(prod), in0=sl3(SL), in1=kP[:, 4 * j:4 * j + 4].unsqueeze(2).to_broadcast([128, 4, 128]), op=ALU.mult), reads=[slk, "FT8"], writes=[pdk])
        P.pool(lambda e, prod=prod: e.tensor_tensor(out=prod[:, 0:256], in0=prod[:, 0:256], in1=prod[:, 256:512], op=ALU.add), reads=[pdk], writes=[pdk])
        P.pool(lambda e, prod=prod: e.tensor_tensor(out=prod[:, 0:128], in0=prod[:, 0:128], in1=prod[:, 128:256], op=ALU.add), reads=[pdk], writes=[pdk])
        P.dve(lambda e: e.tensor_tensor(out=kS, in0=kS, in1=prod[:, 0:128], op=ALU.add), reads=["FT7", pdk], writes=["FT7"])
    P.pe(lambda e: e.matmul(PF[2][:, 0:128], lhsT=PAIRS, rhs=kS, start=True, stop=True), reads=["FT7", "cf"], writes=["PF2"])
    P.dve(lambda e: e.scalar_tensor_tensor(out=uu, in0=PF[2][:, 0:128], scalar=VP[:, 258:259], in1=vP, op0=ALU.mult, op1=ALU.add), reads=["PF2", "FT8", "FT7"], writes=["FT7"])
    P.dve(lambda e: e.tensor_scalar(out=uu, in0=uu, scalar1=VP[:, 257:258], scalar2=None, op0=ALU.mult), reads=["FT7", "FT8"], writes=["FT7"])
    for j in range(NSL):
        (SL, slk), (prod, pdk) = SLs[j % 2], prods[j % 2]
        for hi in range(2):
            P.dma(sl3(SL)[hi * 64:hi * 64 + 64], sav[:, hi, 4 * j:4 * j + 4, :], writes=[slk])
        P.pool(lambda e, j=j: e.tensor_tensor(out=sl3(prod), in0=kP[:, 4 * j:4 * j + 4].unsqueeze(2).to_broadcast([128, 4, 128]), in1=uu.unsqueeze(1).to_broadcast([128, 4, 128]), op=ALU.mult), reads=["FT8", "FT7"], writes=[pdk])
        P.dve(lambda e: e.scalar_tensor_tensor(out=SL[:, :], in0=SL[:, :], scalar=VP[:, 256:257], in1=prod[:, :], op0=ALU.mult, op1=ALU.add), reads=[slk, pdk, "FT8"], writes=[slk])
        for hi in range(2):
            P.dma(sov[:, hi, 4 * j:4 * j + 4, :], sl3(SL)[hi * 64:hi * 64 + 64], reads=[slk], writes=["s_amat%d_%d" % (j, hi)])
        P.dve(lambda e, j=j: e.tensor_tensor(out=sl3(prod), in0=sl3(SL), in1=qP[:, 4 * j:4 * j + 4].unsqueeze(2).to_broadcast([128, 4, 128]), op=ALU.mult), reads=[slk, "FT8"], writes=[pdk])
        P.pool(lambda e, prod=prod: e.tensor_tensor(out=prod[:, 0:256], in0=prod[:, 0:256], in1=prod[:, 256:512], op=ALU.add), reads=[pdk], writes=[pdk])
        P.pool(lambda e, prod=prod: e.tensor_tensor(out=prod[:, 0:128], in0=prod[:, 0:128], in1=prod[:, 128:256], op=ALU.add), reads=[pdk], writes=[pdk])
        P.dve(lambda e: e.tensor_tensor(out=oacc, in0=oacc, in1=prod[:, 0:128], op=ALU.add), reads=["FT7", pdk], writes=["FT7"])
    P.pe(lambda e: e.matmul(PF[2][:, 0:128], lhsT=PAIRS, rhs=oacc, start=True, stop=True), reads=["FT7", "cf"], writes=["PF2"])
    P.dve(lambda e: e.tensor_copy(out=red[0:64, 0:128], in_=PF[2][0:64, 0:128]), reads=["PF2"], writes=["FT2"])
    P.dma(scr_o[:, :], red[0:64, 0:128], reads=["FT2"], writes=["scr_o"])
    osb = SF[6]
    P.dma(osb[R, 0:512], scr_o.rearrange("(b h) v -> b (h v)", h=4), reads=["scr_o"], writes=["SF6"])
    sq = SF[7]
    P.pool(lambda e: e.tensor_tensor(out=sq[R, :], in0=osb[R, :], in1=osb[R, :], op=ALU.mult), reads=["SF6"], writes=["SF7"])
    P.dve(lambda e: e.tensor_reduce(out=sc[R, 40:44], in_=b3(sq[R, :], 4, 128), axis=AX.X, op=ALU.add), reads=["SF7"], writes=["SF5"])
    rsqrt_act(sc[R, 40:44], sc[R, 40:44], 1.0 / 128, 1e-6, ["SF5"], ["SF5"])
    P.dve(lambda e: e.tensor_tensor(out=b3(osb[R, :], 4, 128), in0=b3(osb[R, :], 4, 128), in1=sc[R, 40:44].unsqueeze(2).to_broadcast([N, 4, 128]), op=ALU.mult), reads=["SF6", "SF5"], writes=["SF6"])
    P.dve(lambda e: e.tensor_tensor(out=b3(osb[R, :], 4, 128), in0=b3(osb[R, :], 4, 128), in1=ANW[R, :].unsqueeze(1).to_broadcast([N, 4, 128]), op=ALU.mult), reads=["SF6", "rp"], writes=["SF6"])
    mixs = HT[22]
    P.dve(lambda e: e.tensor_tensor(out=mixs[R, :], in0=osb[R, :], in1=sgaTM[R, :], op=ALU.mult), reads=["SF6", "SF4"], writes=["HT22"])

    pbT = lambda j, k=None: SPJ[:, (17 + j) * N:(17 + (j + 1 if k is None else k)) * N]
    for g in range(5):
        n = 4 if g < 4 else 1
        P.dma(SF[0][R, 0:n * 128], sb_shift_d[:, g * 512:g * 512 + n * 128], writes=["SF0"])
        for i in range(n):
            P.pe(lambda e, g=g, i=i: e.transpose(out=PF[2][:, (g * 4 + i) * N:(g * 4 + i + 1) * N], in_=SF[0][R, i * 128:(i + 1) * 128], identity=IDF[R, R]), reads=["SF0", "cf"], writes=["PF2"])
        to_tm([pbT(g * 4 + i) for i in range(n)], "SPJ", SF[1], "SF1", PF[3], "PF3")
        P.dma(s_bshift[:, g * 512:g * 512 + n * 128], SF[1][R, 0:n * 128], reads=["SF1"], writes=["s_bshift%d" % g])
    xb = FT[9]
    xbj = lambda j, k=None: xb[:, j * N:(j + 1 if k is None else k) * N]
    mu3 = pcol[:, MU0:MU0 + 17].unsqueeze(2).to_broadcast([128, 17, N])
    x3 = xb[:, 0:17 * N].rearrange("p (j t) -> p j t", j=17)
    pb3 = SPJ[:, 17 * N:34 * N].rearrange("p (j t) -> p j t", j=17)
    P.dve(lambda e: e.tensor_tensor(out=xb[:, 0:17 * N], in0=PF[2][:, 0:17 * N], in1=SPJ[:, 17 * N:34 * N], op=ALU.subtract), reads=["PF2", "SPJ"], writes=["FT9"])
    P.dve(lambda e: e.tensor_tensor(out=x3, in0=x3, in1=mu3, op=ALU.mult), reads=["FT9", "pcol"], writes=["FT9"])
    P.dve(lambda e: e.tensor_tensor(out=xb[:, 0:17 * N], in0=xb[:, 0:17 * N], in1=SPJ[:, 17 * N:34 * N], op=ALU.add), reads=["FT9", "SPJ"], writes=["FT9"])
    P.act(lambda e: e.activation(out=xb[0:64, 16 * N:17 * N], in_=xb[0:64, 16 * N:17 * N], func=AF.Tanh), reads=["FT9"], writes=["FT9"])
    W = FT[0]
    wq = lambda qi, pr=None: W[:, (qi * 4 + (0 if pr is None else pr)) * N:(qi * 4 + (4 if pr is None else pr + 1)) * N]
    SG, AIC, KK, KM, BB, BON, SGT, WD = 0, 1, 2, 3, 4, 5, 6, 7
    for pr in range(4):
        P.pe(lambda e, pr=pr: e.matmul(PF[2][:, pr * N:(pr + 1) * N], lhsT=w2a2[0:64, pr * 128:(pr + 1) * 128], rhs=xb[0:64, 16 * N:17 * N], start=True, stop=True), reads=["w2a2", "FT9"], writes=["PF2"])
        P.pe(lambda e, pr=pr: e.matmul(PF[2][:, (4 + pr) * N:(5 + pr) * N], lhsT=w2a2[64:128, pr * 128:(pr + 1) * 128], rhs=xb[64:128, 16 * N:17 * N], start=True, stop=True), reads=["w2a2", "FT9"], writes=["PF2"])
    for pr in range(4):
        P.act(lambda e, pr=pr: e.activation(out=wq(SG, pr), in_=PF[2][:, pr * N:(pr + 1) * N], func=AF.Sigmoid, bias=col(W00 + pr)), reads=["PF2", "pcol"], writes=["FT0"])
        P.act(lambda e, pr=pr: e.activation(out=wq(AIC, pr), in_=PF[2][:, (4 + pr) * N:(5 + pr) * N], func=AF.Sigmoid, bias=col(A00 + pr)), reads=["PF2", "pcol"], writes=["FT0"])
    P.act(lambda e: e.activation(out=wq(WD), in_=wq(SG), func=AF.Exp, scale=-C0), reads=["FT0"], writes=["FT0"])
    pc3 = lambda c0_: pcol[:, c0_:c0_ + 4].unsqueeze(2).to_broadcast([128, 4, N])
    q3 = lambda ap: ap.rearrange("p (a t) -> p a t", a=4)
    rT, kT, vT, gT = xbj(0, 4), xbj(4, 8), xbj(8, 12), xbj(12, 16)
    P.dve(lambda e: e.scalar_tensor_tensor(out=q3(wq(KM)), in0=q3(wq(AIC)), scalar=-1.0, in1=pc3(KA0), op0=ALU.add, op1=ALU.mult), reads=["FT0", "pcol"], writes=["FT0"])
    P.dve(lambda e: e.scalar_tensor_tensor(out=wq(KM), in0=wq(KM), scalar=1.0, in1=kT, op0=ALU.add, op1=ALU.mult), reads=["FT0", "FT9"], writes=["FT0"])
    P.dve(lambda e: e.tensor_tensor(out=q3(wq(KK)), in0=q3(kT), in1=pc3(KK0), op=ALU.mult), reads=["FT9", "pcol"], writes=["FT0"])
    P.act(lambda e: e.activation(out=wq(BB), in_=wq(KK), func=AF.Square), reads=["FT0"], writes=["FT0"])
    P.pe(lambda e: e.matmul(PF[3][:, 0:4 * N], lhsT=BL, rhs=wq(BB), start=True, stop=True), reads=["FT0", "cf"], writes=["PF3"])
    rsqrt_act(wq(BB), PF[3][:, 0:4 * N], 1.0, 1e-12, ["PF3"], ["FT0"])
    P.dve(lambda e: e.tensor_tensor(out=wq(KK), in0=wq(KK), in1=wq(BB), op=ALU.mult), reads=["FT0"], writes=["FT0"])
    P.dve(lambda e: e.tensor_tensor(out=wq(BB), in0=wq(KK), in1=wq(AIC), op=ALU.mult), reads=["FT0"], writes=["FT0"])
    P.dve(lambda e: e.tensor_tensor(out=q3(wq(BON)), in0=q3(rT), in1=pc3(RK0), op=ALU.mult), reads=["FT9", "pcol"], writes=["FT0"])
    P.dve(lambda e: e.tensor_tensor(out=wq(BON), in0=wq(BON), in1=wq(KM), op=ALU.mult), reads=["FT0"], writes=["FT0"])
    P.pe(lambda e: e.matmul(PF[3][:, 0:4 * N], lhsT=BL, rhs=wq(BON), start=True, stop=True), reads=["FT0", "cf"], writes=["PF3"])
    P.dve(lambda e: e.tensor_tensor(out=wq(BON), in0=PF[3][:, 0:4 * N], in1=vT, op=ALU.mult), reads=["PF3", "FT9"], writes=["FT0"])
    P.act(lambda e: e.activation(out=wq(SGT), in_=gT, func=AF.Silu), reads=["FT9"], writes=["FT0"])
    P.dve(lambda e: e.tensor_scalar(out=wq(KK), in0=wq(KK), scalar1=-1.0, scalar2=None, op0=ALU.mult), reads=["FT0"], writes=["FT0"])
    tmsrc = [(wq(WD), "FT0"), (wq(KK), "FT0"), (wq(BB), "FT0"), (wq(KM), "FT0"), (rT, "FT9"), (vT, "FT9")]
    for i, (ap_, key_) in enumerate(tmsrc):
        sf = i % 2
        to_tm([ap_[:, pr * N:(pr + 1) * N] for pr in range(4)], key_, SF[sf], "SF%d" % sf, PF[3], "PF3")
        P.dma(scr_b6[i][:, :], SF[sf][R, 0:512], reads=["SF%d" % sf], writes=["scr_b%d" % i])
    bonTM, sgTM = SF[2], SF[3]
    to_tm([wq(BON, pr) for pr in range(4)], "FT0", bonTM, "SF2", PF[3], "PF3")
    to_tm([wq(SGT, pr) for pr in range(4)], "FT0", sgTM, "SF3", PF[3], "PF3")
    V6 = FT[1]
    for i in range(6):
        P.dma(V6[:, i * 64:(i + 1) * 64], scr_b6[i].rearrange("b (h k) -> (b h) k", h=8), reads=["scr_b%d" % i], writes=["FT1"])
    wP, aP, bP, kP2, rP, vP2 = [V6[:, i * 64:(i + 1) * 64] for i in range(6)]
    sbv = sb_mat_d.rearrange("b h v k -> (b h) (v k)")
    sbo = s_bmat.rearrange("b h v k -> (b h) (v k)")
    S1s, T1s, sa_t, yP = [(FT[4], "FT4"), (FT[3], "FT3")], [(FT[5], "FT5"), (FT[6], "FT6")], FT[2], FT[7]
    v8 = lambda t_: t_[:, :].rearrange("p (v k) -> p v k", v=8)
    kb = lambda ap: ap.unsqueeze(1).to_broadcast([128, 8, 64])
    for j in range(8):
        vsl = slice(8 * j, 8 * j + 8)
        (S1, s1k), (T1, t1k) = S1s[j % 2], T1s[j % 2]
        P.dma(S1[:, :], sbv[:, j * 512:(j + 1) * 512], writes=[s1k])
        P.pool(lambda e, S1=S1, T1=T1: e.tensor_tensor(out=v8(T1), in0=v8(S1), in1=kb(aP), op=ALU.mult), reads=[s1k, "FT1"], writes=[t1k])
        P.dve(lambda e, S1=S1, T1=T1: e.tensor_reduce(out=sa_t[:, 0:8], in_=v8(T1), axis=AX.X, op=ALU.add), reads=[t1k], writes=["FT2"])
        P.dve(lambda e, S1=S1, T1=T1: e.tensor_tensor(out=v8(S1), in0=v8(S1), in1=kb(wP), op=ALU.mult), reads=[s1k, "FT1"], writes=[s1k])
        P.pool(lambda e, S1=S1, T1=T1: e.tensor_tensor(out=v8(T1), in0=sa_t[:, 0:8].unsqueeze(2).to_broadcast([128, 8, 64]), in1=kb(bP), op=ALU.mult), reads=["FT2", "FT1"], writes=[t1k])
        P.dve(lambda e, S1=S1, T1=T1: e.tensor_tensor(out=S1[:, :], in0=S1[:, :], in1=T1[:, :], op=ALU.add), reads=[s1k, t1k], writes=[s1k])
        P.pool(lambda e, vsl=vsl, S1=S1, T1=T1: e.tensor_tensor(out=v8(T1), in0=vP2[:, vsl].unsqueeze(2).to_broadcast([128, 8, 64]), in1=kb(kP2), op=ALU.mult), reads=["FT1"], writes=[t1k])
        P.dve(lambda e, S1=S1, T1=T1: e.tensor_tensor(out=S1[:, :], in0=S1[:, :], in1=T1[:, :], op=ALU.add), reads=[s1k, t1k], writes=[s1k])
        P.dma(sbo[:, j * 512:(j + 1) * 512], S1[:, :], reads=[s1k], writes=["s_bmat%d" % j])
        P.pool(lambda e, S1=S1, T1=T1: e.tensor_tensor(out=v8(T1), in0=v8(S1), in1=kb(rP), op=ALU.mult), reads=[s1k, "FT1"], writes=[t1k])
        P.dve(lambda e, vsl=vsl, S1=S1, T1=T1: e.tensor_reduce(out=yP[:, vsl], in_=v8(T1), axis=AX.X, op=ALU.add), reads=[t1k], writes=["FT7"])
    P.dma(scr_y[:, :], yP[:, 0:64], reads=["FT7"], writes=["scr_y"])
    yb = SF[4]
    P.dma(yb[R, 0:512], scr_y.rearrange("(b h) v -> b (h v)", h=8), reads=["scr_y"], writes=["SF4"])
    y3 = b3(yb[R, :], 8, 64)
    stt = SF[5]
    P.dve(lambda e: e.tensor_reduce(out=stt[R, 0:8], in_=y3, axis=AX.X, op=ALU.add), reads=["SF4"], writes=["SF5"])
    P.dve(lambda e: e.tensor_scalar(out=stt[R, 0:8], in0=stt[R, 0:8], scalar1=1.0 / 64, scalar2=None, op0=ALU.mult), reads=["SF5"], writes=["SF5"])
    P.dve(lambda e: e.tensor_tensor(out=y3, in0=y3, in1=stt[R, 0:8].unsqueeze(2).to_broadcast([N, 8, 64]), op=ALU.subtract), reads=["SF4", "SF5"], writes=["SF4"])
    sq2 = SF[7]
    P.pool(lambda e: e.tensor_tensor(out=sq2[R, :], in0=yb[R, :], in1=yb[R, :], op=ALU.mult), reads=["SF4"], writes=["SF7"])
    P.dve(lambda e: e.tensor_reduce(out=stt[R, 8:16], in_=b3(sq2[R, :], 8, 64), axis=AX.X, op=ALU.add), reads=["SF7"], writes=["SF5"])
    rsqrt_act(stt[R, 8:16], stt[R, 8:16], 1.0 / 64, 64e-5, ["SF5"], ["SF5"])
    P.dve(lambda e: e.tensor_tensor(out=y3, in0=y3, in1=stt[R, 8:16].unsqueeze(2).to_broadcast([N, 8, 64]), op=ALU.mult), reads=["SF4", "SF5"], writes=["SF4"])
    P.dve(lambda e: e.tensor_tensor(out=yb[R, :], in0=yb[R, :], in1=LNW[R, :], op=ALU.mult), reads=["SF4", "rp"], writes=["SF4"])
    P.dve(lambda e: e.tensor_tensor(out=yb[R, :], in0=yb[R, :], in1=LNB[R, :], op=ALU.add), reads=["SF4", "rp"], writes=["SF4"])
    P.dve(lambda e: e.tensor_tensor(out=yb[R, :], in0=yb[R, :], in1=bonTM[R, :], op=ALU.add), reads=["SF4", "SF2"], writes=["SF4"])
    mixb = HT[23]
    P.dve(lambda e: e.tensor_tensor(out=mixb[R, :], in0=yb[R, :], in1=sgTM[R, :], op=ALU.mult), reads=["SF4", "SF3"], writes=["HT23"])
    for c8 in range(8):
        src = mixs[R, c8 * 128:(c8 + 1) * 128] if c8 < 4 else mixb[R, (c8 - 4) * 128:(c8 - 3) * 128]
        P.pe(lambda e, c8=c8, src=src: e.transpose(out=PB[1][:, c8 * N:(c8 + 1) * N], in_=src, identity=idb[R, R]), reads=["HT22", "HT23", "idb"], writes=["PB1"])
    mixTs = HT[24]
    P.dve(lambda e: e.tensor_copy(out=mixTs[:, 0:8 * N], in_=PB[1][:, 0:8 * N]), reads=["PB1"], writes=["HT24"])
    for n in range(2):
        for kc in range(8):
            P.pe(lambda e, n=n, kc=kc: e.matmul(PF[n][R, :], lhsT=mixTs[:, kc * N:(kc + 1) * N], rhs=woutb[:, kc * 1024 + n * 512:kc * 1024 + (n + 1) * 512], start=(kc == 0), stop=(kc == 7)), reads=["HT24", "woutb"], writes=["PF%d" % n])
        P.dve(lambda e, n=n: e.tensor_tensor(out=xa[R, n * 512:(n + 1) * 512], in0=xa[R, n * 512:(n + 1) * 512], in1=PF[n][R, :], op=ALU.add), reads=["xt0", "PF%d" % n], writes=["xt0"])
    P.act(lambda e: e.activation(out=xs_[R, :], in_=xa[R, :], func=AF.Square, accum_out=st4[R, 4:5]), reads=["xt0"], writes=["xs_", "st4"])
    rsqrt_act(st4[R, 4:5], st4[R, 4:5], 1.0 / D, 1e-6, ["st4"], ["st4"])
    P.dve(lambda e: e.scalar_tensor_tensor(out=xs_[R, :], in0=xa[R, :], scalar=st4[R, 4:5], in1=FNW[R, :], op0=ALU.mult, op1=ALU.mult), reads=["xt0", "st4", "rp"], writes=["xs_"])
    P.dma(ys[:, :], xs_[R, :], reads=["xs_"], writes=["ys"])
```

```python
import contextlib
import numpy as np
import concourse.bass as bass
import concourse.mybir as mybir
from concourse.bass_utils import run_bass_kernel_spmd

F32 = mybir.dt.float32
BF16 = mybir.dt.bfloat16
ALU = mybir.AluOpType
AF = mybir.ActivationFunctionType
AX = mybir.AxisListType


class _Rec:
    def __init__(self):
        self.call = None

    def __getattr__(self, name):
        def f(*a, **k):
            self.call = (name, a, k)
            return self
        return f


class Prog:
    ENGS = ["pe", "dve", "act", "pool", "sp"]

    def __init__(self, nc, n_dma_sems=24):
        self.nc = nc
        self.stack = contextlib.ExitStack()
        self.items = {e: [] for e in self.ENGS}
        self.cnt = {e: 0 for e in self.ENGS}
        self.sem = {e: self.stack.enter_context(nc.semaphore("s_" + e)) for e in ["pe", "dve", "act", "pool"]}
        self.dsem = [self.stack.enter_context(nc.semaphore("d%d" % i)) for i in range(n_dma_sems)]
        self.dval = [0] * n_dma_sems
        self.dma_i = 0
        self.n_sw = 8
        self.sw_i = 0
        self.seen = {e: {} for e in self.ENGS}
        self.lastw = {}
        self.readers = {}
        self.n_ops = 0
        self.capture = None
        self.oplist = []
        self.warm_ops = None
        self.n_fill = 0

    def sb(self, name, shape, dtype):
        return self.stack.enter_context(self.nc.sbuf_tensor(name, list(shape), dtype))

    def ps(self, name, shape, dtype):
        return self.stack.enter_context(self.nc.psum_tensor(name, list(shape), dtype))

    _VEC_OPS = ("tensor_tensor", "tensor_copy", "memset")

    def op(self, eng, fn, reads=(), writes=(), is_dma=False, alts=None):
        rec = _Rec()
        fn(rec)
        al = {}
        if alts:
            for e2, fn2 in alts:
                r2 = _Rec()
                fn2(r2)
                al[e2] = r2.call
        if ALT[0] and not is_dma and eng in ("dve", "pool") and rec.call[0] in self._VEC_OPS \
                and not any(k[:2] in ("PF", "PQ", "PB") for k in tuple(reads) + tuple(writes)):
            al.setdefault("pool" if eng == "dve" else "dve", rec.call)
        item = (eng, rec.call, tuple(reads), tuple(writes), is_dma, al)
        if self.capture is not None:
            self.capture.append(item)
            return
        self.oplist.append(item)

    def copy(self, out, in_, reads=(), writes=()):
        self.op("dve", lambda e: e.tensor_copy(out=out, in_=in_), reads, writes,
                alts=[("act", lambda e: e.activation(out=out, in_=in_, func=AF.Copy))] if ALT[0] else None)

    def capture_begin(self):
        self.capture = []

    def capture_end(self):
        c, self.capture = self.capture, None
        return c

    def replay(self, streams):
        items = []
        for si, st in enumerate(streams):
            n = max(len(st), 1)
            for i, it in enumerate(st):
                items.append(((i + 0.5) / n, si, i, it))
        items.sort(key=lambda t: (t[0], t[1], t[2]))
        for _, _, _, it in items:
            self.oplist.append(it)

    def _op(self, eng, call, reads=(), writes=(), is_dma=False):
        if self.n_ops >= MAXOPS[0]:
            return
        need = {}

        def add(tok, same_ok):
            if tok is None:
                return
            key, h, v, teng = tok
            if teng == eng and eng == "pe" and not is_dma_tok(tok) and not same_ok:
                return
            if need.get(key, (None, 0))[1] < v:
                need[key] = (h, v)

        def is_dma_tok(tok):
            return tok[0].startswith("d#")

        for k in reads:
            add(self.lastw.get(k), True)
        for k in writes:
            add(self.lastw.get(k), False)
            for tok in self.readers.get(k, {}).values():
                add(tok, False)
        if is_dma:
            nh = len(self.dsem) - self.n_sw
            if eng == "pool":
                slot = nh + self.sw_i % self.n_sw
                self.sw_i += 1
            else:
                slot = self.dma_i % nh
                self.dma_i += 1
            if self.dval[slot] > 0:
                add(("d#%d" % slot, self.dsem[slot], self.dval[slot], None), True)
            self.dval[slot] += 16
            tok = ("d#%d" % slot, self.dsem[slot], self.dval[slot], None)
            inc = 16
        else:
            self.cnt[eng] += 1
            tok = (eng, self.sem[eng], self.cnt[eng], eng)
            inc = 1
        waits = []
        for key, (h, v) in need.items():
            if self.seen[eng].get(key, 0) < v:
                self.seen[eng][key] = v
                waits.append((h, v))
        for k in writes:
            self.lastw[k] = tok
            self.readers[k] = {}
        for k in reads:
            if k in writes:
                continue
            self.readers.setdefault(k, {})[tok[0]] = tok
        self.items[eng].append((waits, call, tok[1], inc))
        self.n_ops += 1

    def pe(self, fn, reads=(), writes=()):
        self.op("pe", fn, reads, writes)

    def dve(self, fn, reads=(), writes=()):
        self.op("dve", fn, reads, writes)

    def act(self, fn, reads=(), writes=()):
        self.op("act", fn, reads, writes)

    def pool(self, fn, reads=(), writes=()):
        self.op("pool", fn, reads, writes)

    def dma(self, out, in_, reads=(), writes=(), q="sp"):
        self.op(q, lambda e: e.dma_start(out=out, in_=in_), reads, writes, is_dma=True)

    @staticmethod
    def _fd(call):
        name, a_, k_ = call
        out = k_.get("out", a_[0] if a_ else None)
        try:
            shp = list(out.shape)
            n = 1
            for d in shp[1:]:
                n *= int(d)
            return max(n, 1), int(shp[0])
        except Exception:
            return 128, 128

    @staticmethod
    def _act_set(call):
        if call[0] != "activation":
            return None
        f = str(call[2].get("func", "")).split(".")[-1]
        if f in ("Exp", "Ln"):
            return "E"
        if f in ("Silu", "Sigmoid", "Tanh"):
            return f
        return None

    def _dur(self, eng, call, is_dma):
        fd, npart = self._fd(call)
        if is_dma:
            byt = fd * npart * 4
            if eng == "pool":
                return POOL_DMA_OCC[0], 2.0 + POOL_DMA_OCC[0] + byt / 120e3
            return 0.15, 2.0 + byt / 120e3
        if eng == "pe":
            t = (0.06 + fd / 1200.0) * PE_SCALE[0]
            return t, t + 0.25 + LAT_EXTRA[0]
        if eng == "dve":
            t = 0.16 + fd / 960.0
            if call[0] == "scalar_tensor_tensor":
                t = 0.16 + fd / 480.0
            return t, t + 0.1 + LAT_EXTRA[0]
        if eng == "act":
            t = 0.22 + fd / 1200.0
            return t, t + 0.1 + LAT_EXTRA[0]
        t = 0.3 + fd / 600.0
        return t, t + 0.1 + LAT_EXTRA[0]

    def schedule(self):
        import heapq
        ops = self.oplist
        n = len(ops)
        if MAXOPS[0] < n:
            ops = ops[:MAXOPS[0]]
            n = len(ops)
        preds = [None] * n
        preds_ps = [None] * n
        lastw, readers = {}, {}
        for i, (eng, call, reads, writes, is_dma, _al) in enumerate(ops):
            ps = set()
            pp = set()
            for k in reads:
                if k in lastw:
                    ps.add(lastw[k])
            for k in writes:
                if k in lastw:
                    ps.add(lastw[k])
                    if k[:2] in ("PF", "PQ", "PB"):
                        pp.add(lastw[k])
                ps.update(readers.get(k, ()))
                if k[:2] in ("PF", "PQ", "PB"):
                    pp.update(readers.get(k, ()))
            ps.discard(i)
            pp.discard(i)
            preds[i] = ps
            preds_ps[i] = pp
            for k in writes:
                lastw[k] = i
                readers[k] = set()
            for k in reads:
                if k not in writes:
                    readers.setdefault(k, set()).add(i)
        succs = [[] for _ in range(n)]
        npred = [0] * n
        for i in range(n):
            npred[i] = len(preds[i])
            for p in preds[i]:
                succs[p].append(i)
        chosen = None
        if not SCHED[0]:
            order = list(range(n))
        else:
            durs = [self._dur(ops[i][0], ops[i][1], ops[i][4]) for i in range(n)]
            tail = [0.0] * n
            for i in range(n - 1, -1, -1):
                t = 0.0
                for sidx in succs[i]:
                    if tail[sidx] > t:
                        t = tail[sidx]
                tail[i] = t + durs[i][1]
            done_t = [0.0] * n
            ready_t = [0.0] * n
            free = {e: 0.0 for e in self.ENGS}
            pend = {e: [] for e in self.ENGS}
            avail = {e: [] for e in self.ENGS}
            act_set = [None]
            act_pick = [None]
            chosen = [None] * n
            engs_of = [[ops[i][0]] + list(ops[i][5].keys()) for i in range(n)]

            def push_ready(i, rt):
                for e2 in engs_of[i]:
                    heapq.heappush(pend[e2], (rt, i))

            for i in range(n):
                if npred[i] == 0:
                    push_ready(i, 0.0)
            order = []
            left = n
            while left:
                best = None
                for e in self.ENGS:
                    f = free[e]
                    while pend[e] and (pend[e][0][0] <= f + LOOKAHEAD[0] or chosen[pend[e][0][1]] is not None):
                        j_ = heapq.heappop(pend[e])[1]
                        if chosen[j_] is None:
                            heapq.heappush(avail[e], ((-tail[j_] if PRIO[0] else 0.0), j_))
                    while avail[e] and chosen[avail[e][0][1]] is not None:
                        heapq.heappop(avail[e])
                    if avail[e] and e == "act" and ACT_TABLES[0]:
                        peek = []
                        while avail[e] and len(peek) < 8:
                            it_ = heapq.heappop(avail[e])
                            if chosen[it_[1]] is None:
                                peek.append(it_)
                        pick = peek[0]
                        for it_ in peek:
                            cs_ = self._act_set(ops[it_[1]][1] if ops[it_[1]][0] == "act" else ops[it_[1]][5]["act"])
                            if ACT_TABLES[0] == 2 and (cs_ is None or cs_ == act_set[0]):
                                pick = it_
                                break
                        for it_ in peek:
                            heapq.heappush(avail[e], it_)
                        act_pick[0] = pick
                        cand = (max(f, ready_t[pick[1]]), pick[1], e, True)
                    elif avail[e]:
                        cand = (max(f, ready_t[avail[e][0][1]]), avail[e][0][1], e, True)
                    elif pend[e]:
                        cand = (pend[e][0][0], pend[e][0][1], e, False)
                    else:
                        continue
                    i_ = cand[1]
                    pen = 0.0
                    if e != ops[i_][0]:
                        pen = max(0.0, self._dur(e, ops[i_][5][e], False)[0] - self._dur(ops[i_][0], ops[i_][1], False)[0])
                    key = (cand[0] + pen, cand[1])
                    if best is None or key < best[0]:
                        best = (key, cand)
                st, i, e, from_avail = best[1]
                if from_avail and e == "act" and ACT_TABLES[0]:
                    tmp_ = []
                    while avail[e]:
                        it_ = heapq.heappop(avail[e])
                        if it_[1] == i:
                            break
                        tmp_.append(it_)
                    for it_ in tmp_:
                        heapq.heappush(avail[e], it_)
                elif from_avail:
                    heapq.heappop(avail[e])
                else:
                    heapq.heappop(pend[e])
                chosen[i] = e
                call_i = ops[i][1] if e == ops[i][0] else ops[i][5][e]
                if e == "pe" and KEEPWARM[0] and self.warm_ops is not None and call_i[0] == "matmul" \
                        and call_i[2].get("start", True) and st - free[e] > 0.7:
                    tp = free[e]
                    for p in preds_ps[i]:
                        tp = max(tp, done_t[p])
                    nfill = min(int((st - tp - 0.25) / 0.17), KEEPWARM[0])
                    if nfill > 0:
                        order.append(("fill", i, nfill))
                        self.n_fill += nfill
                occ, lat = self._dur(e, call_i, ops[i][4])
                if e == "act" and ACT_TABLES[0]:
                    cs_ = self._act_set(call_i)
                    if cs_ is not None and cs_ != act_set[0]:
                        occ += 1.3
                        lat += 1.3
                        act_set[0] = cs_
                if SCHED_TRACE is not None:
                    SCHED_TRACE.append((i, e, st, occ, lat, free[e], ready_t[i]))
                free[e] = st + occ
                done_t[i] = st + lat
                order.append(i)
                left -= 1
                for sidx in succs[i]:
                    ready_t[sidx] = max(ready_t[sidx], (st + occ) if (e == "pe" and ops[sidx][0] == "pe") else done_t[i])
                    npred[sidx] -= 1
                    if npred[sidx] == 0:
                        push_ready(sidx, ready_t[sidx])
            self.est_us = max(done_t) if n else 0.0
        toks = [None] * n
        eng_of = [(chosen[i] if chosen is not None and chosen[i] is not None else ops[i][0]) for i in range(n)]
        for i in order:
            if isinstance(i, tuple):
                _, ri, nfill = i
                out_ap = ops[ri][1][1][0] if ops[ri][1][1] else ops[ri][1][2].get("out")
                try:
                    shp = list(out_ap.shape)
                    if len(shp) != 2 or shp[1] < 64 or str(out_ap.dtype) != str(F32):
                        continue
                    ncol = min(int(shp[1]), 128)
                    dcall = ("matmul", (out_ap[:, 0:ncol],), dict(lhsT=self.warm_ops[:, 0:int(shp[0])], rhs=self.warm_ops[:, 0:ncol], start=True, stop=True))
                except Exception:
                    continue
                for _ in range(nfill):
                    self._emit("pe", dcall, False, [(toks[p], eng_of[p]) for p in preds_ps[ri]])
                continue
            eng, call, reads, writes, is_dma, al = ops[i]
            e = eng_of[i]
            toks[i] = self._emit(e, call if e == eng else al[e], is_dma, [(toks[p], eng_of[p]) for p in preds[i]])

    def _emit(self, eng, call, is_dma, pred_toks):
        need = {}

        def add(tok):
            key, h, v, teng = tok
            if need.get(key, (None, 0))[1] < v:
                need[key] = (h, v)

        for tok, peng in pred_toks:
            if peng == "pe" and eng == "pe" and not tok[0].startswith("d#"):
                continue
            add(tok)
        if is_dma:
            nh = len(self.dsem) - self.n_sw
            if eng == "pool":
                slot = nh + self.sw_i % self.n_sw
                self.sw_i += 1
            else:
                slot = self.dma_i % nh
                self.dma_i += 1
            if self.dval[slot] > 0:
                add(("d#%d" % slot, self.dsem[slot], self.dval[slot], None))
            self.dval[slot] += 16
            tok = ("d#%d" % slot, self.dsem[slot], self.dval[slot], None)
            inc = 16
        else:
            self.cnt[eng] += 1
            tok = (eng, self.sem[eng], self.cnt[eng], eng)
            inc = 1
        waits = []
        for key, (h, v) in need.items():
            if self.seen[eng].get(key, 0) < v:
                self.seen[eng][key] = v
                waits.append((h, v))
        self.items[eng].append((waits, call, tok[1], inc))
        self.n_ops += 1
        return tok

    def finish(self):
        nc = self.nc
        self.schedule()
        final = [(self.dsem[i], self.dval[i]) for i in range(len(self.dsem)) if self.dval[i] > 0]

        def emit(name, e, tail=False):
            for waits, fn, h, inc in self.items[name]:
                for (wh, wv) in waits:
                    e.wait_ge(wh, wv)
                name_, a_, k_ = fn
                getattr(e, name_)(*a_, **k_).then_inc(h, inc)
            if tail:
                for (wh, wv) in final:
                    e.wait_ge(wh, wv)

        with nc.Block() as block:
            @block.tensor
            def _(e):
                emit("pe", e)

            @block.vector
            def _(e):
                emit("dve", e)

            @block.scalar
            def _(e):
                emit("act", e)

            @block.gpsimd
            def _(e):
                emit("pool", e)

            @block.sync
            def _(e):
                emit("sp", e, tail=True)
        self.stack.close()


D = 1024
PW = 4232
C0 = 0.6065306597126334
NCONST = 9
LASTP = None
MAXOPS = [10 ** 9]
SCHED = [True]
PRIO = [True]
ALT = [True]
KEEPWARM = [0]
POOL_DMA_OCC = [1.05]
LOOKAHEAD = [0.1]
ACT_TABLES = [2]
PE_SCALE = [0.6]
LAT_EXTRA = [0.15]
DECODE_LAST = [False]
SCHED_TRACE = None
TRACE_OPS = None


def host_consts():
    p = np.arange(128)
    same = (p[:, None] // 64) == (p[None, :] // 64)
    c = np.zeros((NCONST, 128, 128), np.float32)
    c[0] = np.eye(128)
    c[1] = 1.0
    c[2] = same & (p[:, None] <= p[None, :])
    c[3] = same & (p[:, None] > p[None, :])
    c[4] = same & (p[:, None] < p[None, :])
    c[5] = same
    c[6] = (p[:, None] % 64) == (p[None, :] % 64)
    c[7][:, 0] = p < 64
    c[7][:, 1] = p >= 64
    c[8] = -c[4]
    return c


def build(T=2048, NS=16, TB=512):
    nc = bass.Bass("TRN2", target_bir_lowering=False)

    def din(name, shape):
        return nc.dram_tensor(name, list(shape), F32, kind="ExternalInput").ap()

    def dout(name, shape):
        return nc.dram_tensor(name, list(shape), F32, kind="ExternalOutput").ap()

    xp = din("xp", [T, D])
    w_in = din("w_in", [D, PW])
    w_out = din("w_out", [D, D])
    vrows = din("vrows", [128, 128])
    rowp = din("rowp", [2048 + 128 + 1024 + 8])
    w2d = din("w2", [64, 512])
    a2d = din("a2", [64, 512])
    constd = din("consts", [NCONST, 128, 128])
    resetd = din("resetm", [128, 512])
    yp = dout("yp", [T, D])
    p_amat = dout("p_amat", [4, 128, 128])
    p_aconv = dout("p_aconv", [36, 128])
    p_bmat = dout("p_bmat", [8, 64, 64])
    p_bshift = dout("p_bshift", [17, 128])

    xs_d = din("xs", [NS, D])
    sa_mat_d = din("sa_mat", [NS, 4, 128, 128])
    sa_conv_d = din("sa_conv", [NS, 3, 1536])
    sb_mat_d = din("sb_mat", [NS, 8, 64, 64])
    sb_shift_d = din("sb_shift", [NS, 2176])
    ys = dout("ys", [NS, D])
    s_amat = dout("s_amat", [NS, 4, 128, 128])
    s_aconv = dout("s_aconv", [NS, 3, 1536])
    s_bmat = dout("s_bmat", [NS, 8, 64, 64])
    s_bshift = dout("s_bshift", [NS, 2176])

    def dscr(name, shape):
        return nc.dram_tensor(name, list(shape), F32, kind="Internal").ap()

    scr_q, scr_k, scr_v = dscr("scr_q", [NS, 512]), dscr("scr_k", [NS, 512]), dscr("scr_v", [NS, 512])
    scr_ab = dscr("scr_ab", [NS, 8])
    scr_o = dscr("scr_o", [64, 128])
    scr_b6 = [dscr("scr_b%d" % i, [NS, 512]) for i in range(6)]
    scr_y = dscr("scr_y", [128, 64])

    P = Prog(nc)
    global LASTP
    LASTP = P
    NTB = TB // 128
    assert T % TB == 0

    cf = P.sb("cf", [128, NCONST * 128], F32)
    for i in range(NCONST):
        P.dma(cf[:, i * 128:(i + 1) * 128], constd[i], writes=["cf"])
    CF = lambda i: cf[:, i * 128:(i + 1) * 128]
    IDF, ONESF, BT, MGT, STRICT, BL, PAIRS, NSTRICT = CF(0), CF(1), CF(2), CF(3), CF(4), CF(5), CF(6), CF(8)
    INCL = BT
    CHIND = cf[:, 7 * 128:7 * 128 + 2]
    idb = P.sb("idb", [128, 128], BF16)
    P.dve(lambda e: e.tensor_copy(out=idb[:, :], in_=IDF), reads=["cf"], writes=["idb"])
    P.warm_ops = idb
    onesb = P.sb("onesb", [128, 128], BF16)
    P.dve(lambda e: e.tensor_copy(out=onesb[:, :], in_=ONESF), reads=["cf"], writes=["onesb"])
    blb = P.sb("blb", [128, 128], BF16)
    P.dve(lambda e: e.tensor_copy(out=blb[:, :], in_=BL), reads=["cf"], writes=["blb"])
    resetm = P.sb("resetm_sb", [128, 512], F32)
    P.dma(resetm[:, :], resetd[:, :], writes=["resetm"])

    PF = [P.ps("PF%d" % i, [128, 512], F32) for i in range(4)]
    PQ = [P.ps("PQ%d" % i, [128, 512], F32) for i in range(2)]
    PB = [P.ps("PB%d" % i, [128, 1024], BF16) for i in range(2)]

    vr_t = P.sb("vr_t", [128, 128], F32)
    P.dma(vr_t[:, :], vrows[:, :], writes=["vr_t"])
    pcol = P.sb("pcol", [128, 128], F32)
    P.pe(lambda e: e.transpose(out=PF[0][:, 0:128], in_=vr_t[:, :], identity=IDF), reads=["vr_t", "cf"], writes=["PF0"])
    P.dve(lambda e: e.tensor_copy(out=pcol[:, :], in_=PF[0][:, 0:128]), reads=["PF0"], writes=["pcol"])
    col = lambda i: pcol[:, i:i + 1]
    omu = P.sb("omu", [128, 17], F32)
    P.dve(lambda e: e.tensor_scalar(out=omu[:, :], in0=pcol[:, 56:73], scalar1=-1.0, scalar2=1.0, op0=ALU.mult, op1=ALU.add), reads=["pcol"], writes=["omu"])
    NW0, CW0, MU0, W00, A00, KK0, KA0, RK0 = 0, 8, 56, 73, 77, 81, 85, 89

    RL = 2048 + 128 + 1024 + 8
    rp = P.sb("rp", [128, RL], F32)
    P.dma(rp[:, :], rowp.partition_broadcast(128), writes=["rp"])
    LNW, LNB, ANW, FNW = rp[:, 0:512], rp[:, 512:1024], rp[:, 2048:2176], rp[:, 2176:3200]
    ALOG, DTB = rp[:, 3200:3204], rp[:, 3204:3208]
    nega = P.sb("nega", [128, 4], F32)
    P.act(lambda e: e.activation(out=nega[:, :], in_=ALOG, func=AF.Exp), reads=["rp"], writes=["nega"])
    P.dve(lambda e: e.tensor_scalar(out=nega[:, :], in0=nega[:, :], scalar1=-1.0, scalar2=None, op0=ALU.mult), reads=["nega"], writes=["nega"])

    w2a2 = P.sb("w2a2", [128, 512], F32)
    P.dma(w2a2[0:64, :], w2d[:, :], writes=["w2a2"])
    P.dma(w2a2[64:128, :], a2d[:, :], writes=["w2a2"])

    woutb = P.sb("woutb", [128, 8 * 1024], BF16)
    NWB = 4
    wbf = [P.sb("wbf%d" % i, [128, 1024], BF16) for i in range(NWB)]
    cast_i = [0]

    def cast(out, in_, reads, writes):
        i = cast_i[0]
        cast_i[0] += 1
        if i % 3 != 2:
            P.act(lambda e: e.activation(out=out, in_=in_, func=AF.Copy), reads=reads, writes=writes)
        else:
            P.dve(lambda e: e.tensor_copy(out=out, in_=in_), reads=reads, writes=writes)

    for kc in range(8):
        P.dma(woutb[:, kc * 1024:(kc + 1) * 1024], w_out[kc * 128:(kc + 1) * 128, :], writes=["woutb"], q="pool")

    Sa = P.sb("Sa", [128, 512], F32)
    Sab = P.sb("Sab", [128, 512], BF16)
    Hb = P.sb("Hb", [128, 512], F32)
    Hbb = P.sb("Hbb", [128, 512], BF16)
    for t_, n_ in ((Sa, "Sa"), (Sab, "Sab"), (Hb, "Hb"), (Hbb, "Hbb")):
        P.pool(lambda e, t_=t_: e.memset(t_[:, :], 0.0), writes=[n_])
    ccar = P.sb("ccar", [128, 36], F32)
    P.pool(lambda e: e.memset(ccar[:, :], 0.0), writes=["ccar"])
    bcar = P.sb("bcar", [128, 17], F32)
    P.pool(lambda e: e.memset(bcar[:, :], 0.0), writes=["bcar"])

    xt = [P.sb("xt%d" % i, [128, D], F32) for i in range(2)]
    xs_ = P.sb("xs_", [128, D], F32)
    st4 = P.sb("st4", [128, 8], F32)
    xnT = P.sb("xnT", [128, 8 * TB], BF16)
    cb = [P.sb("cb%d" % i, [128, TB + 4], F32) for i in range(2)]
    FT = [P.sb("FT%d" % i, [128, 512], F32) for i in range(10)]
    HT = [P.sb("HT%d" % i, [128, 512], BF16) for i in range(40)]
    mixB4 = P.sb("mixB4", [128, (TB // 128) * 512], BF16)
    ctmp = P.sb("ctmp", [128, 512], F32)
    mixT = P.sb("mixT", [128, 1024], BF16)
    BLK = [P.sb("BLK%d" % i, [128, (8 if i in (0, 3, 4) else 4) * TB], BF16) for i in range(5)]
    P.pool(lambda e: e.memset(BLK[3][:, :], 0.0), writes=["BLK3"])
    P.pool(lambda e: e.memset(BLK[4][:, :], 0.0), writes=["BLK4"])
    mixA2 = [P.sb("mixA%d" % i, [128, NTB * 512], BF16) for i in range(2)]
    pce = P.sb("pce", [128, 4 * (TB // 64)], F32)
    bon = P.sb("bon", [128, 4 * TB], BF16)

    def rsqrt_act(out, in_, scale, eps, reads, writes):
        P.act(lambda e: e.activation(out=out, in_=in_, func=AF.Ln, scale=scale, bias=eps), reads=reads, writes=writes)
        P.act(lambda e: e.activation(out=out, in_=out, func=AF.Exp, scale=-0.5), reads=writes, writes=writes)

    def b3(ap, h, n):
        return ap.rearrange("p (h n) -> p h n", h=h)

    SPJ = P.sb("SPJ", [128, 35 * NS], F32)
    xnTs = P.sb("xnTs", [128, 8 * NS], BF16)
    arena = P.sb("arena", [128, 4096], F32)

    class _Sub:
        def __init__(self, off):
            self.off = off

        def __getitem__(self, key):
            rows, cols = key
            lo = 0 if cols.start is None else cols.start
            hi = 512 if cols.stop is None else cols.stop
            return arena[rows, self.off + lo:self.off + hi]

    SF = [_Sub(i * 512) for i in range(8)]
    SAMPLE_LOCALS = dict(locals())
    if not DECODE_LAST[0]:
        sample_path(SAMPLE_LOCALS)
    arenab = arena[:, :].bitcast(BF16)
    P.pool(lambda e: e.memset(st4[:, 7:8], 0.0), reads=["SF%d" % i for i in range(8)], writes=["BAR", "BVB", "BSG", "st4"])

    nblk = T // TB
    stream_b = None
    for blk in range(nblk):
        t0 = blk * TB
        last_blk = blk == nblk - 1
        for tb in range(NTB):
            xa = xt[tb % 2]
            xk = "xt%d" % (tb % 2)
            P.dma(xa[:, :], xp[t0 + tb * 128:t0 + (tb + 1) * 128, :], writes=[xk])
            P.act(lambda e, xa=xa: e.activation(out=xs_[:, :], in_=xa[:, :], func=AF.Square, accum_out=st4[:, 0:1]), reads=[xk], writes=["xs_", "st4"])
            rsqrt_act(st4[:, 0:1], st4[:, 0:1], 1.0 / D, 1e-6, ["st4"], ["st4"])
            P.act(lambda e, xa=xa: e.activation(out=xs_[:, :], in_=xa[:, :], func=AF.Copy, scale=st4[:, 0:1]), reads=[xk, "st4"], writes=["xs_"])
            for half in range(2):
                pf = PF[half]
                pk = "PF%d" % half
                for q in range(4):
                    kc = half * 4 + q
                    P.pe(lambda e, pf=pf, q=q, kc=kc: e.transpose(out=pf[:, q * 128:(q + 1) * 128], in_=xs_[:, kc * 128:(kc + 1) * 128], identity=IDF), reads=["xs_", "cf"], writes=[pk])
                out3 = xnT[:, :].rearrange("p (k t) -> p k t", k=8)[:, half * 4:(half + 1) * 4, tb * 128:(tb + 1) * 128]
                in3 = pf[:, :].rearrange("p (k t) -> p k t", k=4)
                nw3 = pcol[:, NW0 + half * 4:NW0 + half * 4 + 4].unsqueeze(2).to_broadcast([128, 4, 128])
                P.dve(lambda e, out3=out3, in3=in3, nw3=nw3: e.tensor_tensor(out=out3, in0=in3, in1=nw3, op=ALU.mult), reads=[pk, "pcol"], writes=["xnT"])

        wchunk_i = [0]

        def proj_chunk(c0, ncols, pf, pk):
            s = wchunk_i[0] % NWB
            wchunk_i[0] += 1
            bk = "wbf%d" % s
            src = w_in[:, c0:c0 + ncols].rearrange("(k p) n -> p k n", p=128)
            dst = wbf[s][:, 0:8 * ncols].rearrange("p (k n) -> p k n", k=8)
            P.dma(dst, src, writes=[bk], q="pool")
            for kc in range(8):
                P.pe(lambda e, kc=kc, s=s: e.matmul(pf[0:ncols, 0:TB], lhsT=wbf[s][:, kc * ncols:(kc + 1) * ncols], rhs=xnT[:, kc * TB:(kc + 1) * TB], start=(kc == 0), stop=(kc == 7)), reads=[bk, "xnT"], writes=[pk])

        mixA, mxk = mixA2[blk % 2], "mixA%d" % (blk % 2)
        KQ, VT, SGA = BLK[0], BLK[1], BLK[2]
        for c in range(12):
            pf, pk = PF[c % 2], "PF%d" % (c % 2)
            cbuf, ck = cb[c % 2], "cb%d" % (c % 2)
            proj_chunk(c * 128, 128, pf, pk)
            car3 = ccar[:, :].rearrange("p (i c) -> p i c", i=3)[:, :, c]
            P.pool(lambda e, cbuf=cbuf, car3=car3: e.tensor_copy(out=cbuf[:, 0:3], in_=car3), reads=["ccar"], writes=[ck])
            P.act(lambda e, cbuf=cbuf, pf=pf: e.activation(out=cbuf[:, 3:3 + TB], in_=pf[:, 0:TB], func=AF.Copy), reads=[pk], writes=[ck])
            P.pool(lambda e, cbuf=cbuf, car3=car3: e.tensor_copy(out=car3, in_=cbuf[:, TB:TB + 3]), reads=[ck], writes=["ccar"])
            acc, ak = FT[c % 2], "FT%d" % (c % 2)
            P.op("dve", lambda e, cbuf=cbuf, acc=acc, c=c: e.tensor_scalar(out=acc[:, 0:TB], in0=cbuf[:, 0:TB], scalar1=col(CW0 + c * 4), scalar2=None, op0=ALU.mult), [ck, "pcol"], [ak],
                 alts=[("act", lambda e, cbuf=cbuf, acc=acc, c=c: e.activation(out=acc[:, 0:TB], in_=cbuf[:, 0:TB], func=AF.Copy, scale=col(CW0 + c * 4)))] if ALT[0] else None)
            P.dve(lambda e, cbuf=cbuf, acc=acc, c=c: e.scalar_tensor_tensor(out=acc[:, 0:TB], in0=cbuf[:, 1:1 + TB], scalar=col(CW0 + c * 4 + 1), in1=acc[:, 0:TB], op0=ALU.mult, op1=ALU.add), reads=[ck, "pcol", ak], writes=[ak])
            P.op("dve", lambda e, cbuf=cbuf, c=c: e.tensor_scalar(out=ctmp[:, 0:TB], in0=cbuf[:, 2:2 + TB], scalar1=col(CW0 + c * 4 + 2), scalar2=None, op0=ALU.mult), [ck, "pcol"], ["ctmp"],
                 alts=[("act", lambda e, cbuf=cbuf, c=c: e.activation(out=ctmp[:, 0:TB], in_=cbuf[:, 2:2 + TB], func=AF.Copy, scale=col(CW0 + c * 4 + 2)))] if ALT[0] else None)
            P.dve(lambda e, cbuf=cbuf, c=c: e.scalar_tensor_tensor(out=ctmp[:, 0:TB], in0=cbuf[:, 3:3 + TB], scalar=col(CW0 + c * 4 + 3), in1=ctmp[:, 0:TB], op0=ALU.mult, op1=ALU.add), reads=[ck, "pcol", "ctmp"], writes=["ctmp"])
            P.dve(lambda e, acc=acc: e.tensor_tensor(out=acc[:, 0:TB], in0=acc[:, 0:TB], in1=ctmp[:, 0:TB], op=ALU.add), reads=[ak, "ctmp"], writes=[ak])
            P.act(lambda e, acc=acc: e.activation(out=acc[:, 0:TB], in_=acc[:, 0:TB], func=AF.Silu), reads=[ak], writes=[ak])
            if c < 8:
                h = c % 4
                isq = c < 4
                sq, sqk = HT[c % 2], "HT%d" % (c % 2)
                P.act(lambda e, acc=acc, sq=sq: e.activation(out=sq[:, 0:TB], in_=acc[:, 0:TB], func=AF.Square), reads=[ak], writes=[sqk])
                p2, p2k = PF[2 + c % 2], "PF%d" % (2 + c % 2)
                P.pe(lambda e, p2=p2, sq=sq: e.matmul(p2[:, 0:TB], lhsT=onesb[:, :], rhs=sq[:, 0:TB], start=True, stop=True), reads=[sqk, "onesb"], writes=[p2k])
                ri, rik = FT[2 + c % 2], "FT%d" % (2 + c % 2)
                rsqrt_act(ri[:, 0:TB], p2[:, 0:TB], 1.0, 1e-12, [p2k], [rik])
                dst = KQ[:, h * 2 * TB:(h + 1) * 2 * TB].rearrange("p (t s n) -> p t s n", t=NTB, s=2)[:, :, 1 if isq else 0, :]
                P.dve(lambda e, acc=acc, ri=ri, dst=dst, isq=isq: e.scalar_tensor_tensor(out=dst, in0=acc[:, 0:TB].rearrange("p (t n) -> p t n", t=NTB), scalar=(128 ** -0.5 if isq else 1.0), in1=ri[:, 0:TB].rearrange("p (t n) -> p t n", t=NTB), op0=ALU.mult, op1=ALU.mult), reads=[ak, rik], writes=["BLK0"])
            else:
                h = c - 8
                P.dve(lambda e, acc=acc, h=h: e.tensor_copy(out=VT[:, h * TB:(h + 1) * TB], in_=acc[:, 0:TB]), reads=[ak], writes=["BLK1"])
        if last_blk:
            P.pe(lambda e: e.transpose(out=PF[0][0:36, 0:128], in_=ccar[:, :], identity=IDF), reads=["ccar", "cf"], writes=["PF0"])
            P.dve(lambda e: e.tensor_copy(out=FT[0][0:36, 0:128], in_=PF[0][0:36, 0:128]), reads=["PF0"], writes=["FT0"])
            P.dma(p_aconv[:, :], FT[0][0:36, 0:128], reads=["FT0"], writes=["p_aconv"])
        for c in range(4):
            pf, pk = PF[c % 2], "PF%d" % (c % 2)
            proj_chunk(1536 + c * 128, 128, pf, pk)
            P.act(lambda e, pf=pf, c=c: e.activation(out=SGA[:, c * TB:(c + 1) * TB], in_=pf[:, 0:TB], func=AF.Silu), reads=[pk], writes=["BLK2"])
        bdT = FT[4]
        proj_chunk(2048, 8, PF[0], "PF0")
        P.act(lambda e: e.activation(out=bdT[0:8, 0:TB], in_=PF[0][0:8, 0:TB], func=AF.Copy), reads=["PF0"], writes=["FT4"])

        P.capture_begin()
        for tb in range(NTB):
            cs = slice(tb * 128, (tb + 1) * 128)
            kq = lambda h, s: KQ[:, h * 2 * TB + tb * 256 + s * 128: h * 2 * TB + tb * 256 + (s + 1) * 128]
            kq2 = lambda h: KQ[:, h * 2 * TB + tb * 256: h * 2 * TB + (tb + 1) * 256]
            sc = FT[5]
            P.pe(lambda e, cs=cs: e.transpose(out=PF[0][:, 0:8], in_=bdT[0:8, cs], identity=IDF[0:8, 0:8]), reads=["FT4", "cf"], writes=["PF0"])
            P.act(lambda e: e.activation(out=sc[:, 0:4], in_=PF[0][:, 0:4], func=AF.Exp, scale=-1.0), reads=["PF0"], writes=["FT5"])
            P.dve(lambda e: e.tensor_scalar(out=sc[:, 0:4], in0=sc[:, 0:4], scalar1=1.0, scalar2=None, op0=ALU.add), reads=["FT5"], writes=["FT5"])
            P.dve(lambda e: e.reciprocal(out=sc[:, 0:4], in_=sc[:, 0:4]), reads=["FT5"], writes=["FT5"])
            P.dve(lambda e: e.tensor_tensor(out=sc[:, 4:8], in0=PF[0][:, 4:8], in1=DTB, op=ALU.add), reads=["PF0", "rp"], writes=["FT5"])
            P.act(lambda e: e.activation(out=sc[:, 4:8], in_=sc[:, 4:8], func=AF.Exp), reads=["FT5"], writes=["FT5"])
            P.act(lambda e: e.activation(out=sc[:, 4:8], in_=sc[:, 4:8], func=AF.Ln, bias=1.0), reads=["FT5"], writes=["FT5"])
            P.dve(lambda e: e.tensor_tensor(out=sc[:, 4:8], in0=sc[:, 4:8], in1=nega[:, :], op=ALU.mult), reads=["FT5", "nega"], writes=["FT5"])
            P.pe(lambda e: e.matmul(PF[0][:, 8:12], lhsT=BT, rhs=sc[:, 4:8], start=True, stop=True), reads=["FT5", "cf"], writes=["PF0"])
            P.pe(lambda e: e.matmul(PF[0][:, 12:16], lhsT=BL, rhs=sc[:, 4:8], start=True, stop=True), reads=["FT5", "cf"], writes=["PF0"])
            P.dve(lambda e: e.tensor_copy(out=sc[:, 8:16], in_=PF[0][:, 8:16]), reads=["PF0"], writes=["FT5"])
            P.act(lambda e: e.activation(out=sc[:, 16:20], in_=sc[:, 8:12], func=AF.Exp), reads=["FT5"], writes=["FT5"])
            P.dve(lambda e: e.tensor_tensor(out=sc[:, 20:24], in0=sc[:, 12:16], in1=sc[:, 8:12], op=ALU.subtract), reads=["FT5"], writes=["FT5"])
            P.act(lambda e: e.activation(out=sc[:, 20:24], in_=sc[:, 20:24], func=AF.Exp), reads=["FT5"], writes=["FT5"])
            gm3 = sc[:, 32:40].rearrange("p (c h) -> p c h", c=2)
            P.dve(lambda e: e.tensor_tensor(out=gm3, in0=sc[:, 4:8].unsqueeze(1).to_broadcast([128, 2, 4]), in1=CHIND.unsqueeze(2).to_broadcast([128, 2, 4]), op=ALU.mult), reads=["FT5", "cf"], writes=["FT5"])
            P.pe(lambda e: e.matmul(PF[0][:, 16:24], lhsT=ONESF, rhs=sc[:, 32:40], start=True, stop=True), reads=["FT5", "cf"], writes=["PF0"])
            P.act(lambda e: e.activation(out=sc[:, 24:32], in_=PF[0][:, 16:24], func=AF.Exp), reads=["PF0"], writes=["FT5"])
            beta_b = sc[:, 0:4].unsqueeze(2).to_broadcast([128, 4, 128])
            Kg, Kd, Vtm = HT[2], HT[3], HT[4]
            for h in range(4):
                P.pe(lambda e, h=h: e.transpose(out=PB[0][:, h * 128:(h + 1) * 128], in_=kq(h, 0), identity=idb[:, :]), reads=["BLK0", "idb"], writes=["PB0"])
                P.pe(lambda e, h=h: e.transpose(out=PB[0][:, 512 + h * 128:512 + (h + 1) * 128], in_=VT[:, h * TB + tb * 128:h * TB + (tb + 1) * 128], identity=idb[:, :]), reads=["BLK1", "idb"], writes=["PB0"])
            P.dve(lambda e: e.tensor_tensor(out=b3(Kg[:, :], 4, 128), in0=b3(PB[0][:, 0:512], 4, 128), in1=sc[:, 16:20].unsqueeze(2).to_broadcast([128, 4, 128]), op=ALU.mult), reads=["PB0", "FT5"], writes=["HT2"])
            P.dve(lambda e: e.tensor_tensor(out=b3(Kd[:, :], 4, 128), in0=b3(PB[0][:, 0:512], 4, 128), in1=sc[:, 20:24].unsqueeze(2).to_broadcast([128, 4, 128]), op=ALU.mult), reads=["PB0", "FT5"], writes=["HT3"])
            P.dve(lambda e: e.tensor_copy(out=Vtm[:, :], in_=PB[0][:, 512:1024]), reads=["PB0"], writes=["HT4"])
            MG, E = FT[6], FT[7]
            P.pool(lambda e: e.tensor_tensor(out=b3(MG[:, :], 4, 128), in0=MGT.unsqueeze(1).to_broadcast([128, 4, 128]), in1=sc[:, 4:8].unsqueeze(2).to_broadcast([128, 4, 128]), op=ALU.mult), reads=["cf", "FT5"], writes=["FT6"])
            for h in range(4):
                P.pe(lambda e, h=h: e.matmul(PF[0][:, h * 128:(h + 1) * 128], lhsT=MG[:, h * 128:(h + 1) * 128], rhs=BT, start=True, stop=True), reads=["FT6", "cf"], writes=["PF0"])
            P.act(lambda e: e.activation(out=E[:, :], in_=PF[0][:, :], func=AF.Exp), reads=["PF0"], writes=["FT7"])
            for h in range(4):
                P.pe(lambda e, h=h: e.matmul((PQ[0] if h < 2 else PF[1])[:, (h % 2) * 256:(h % 2 + 1) * 256], lhsT=kq(h, 0), rhs=kq2(h), start=True, stop=True), reads=["BLK0"], writes=["PQ0" if h < 2 else "PF1"])
            XQ = [HT[5], HT[6]]
            XQa, XQb = (HT[5], HT[6]), (HT[7], HT[8])
            Nn = [HT[9], HT[10]]
            QKT = HT[11]
            E2 = FT[8]
            P.pool(lambda e: e.tensor_tensor(out=b3(E2[:, :], 4, 128), in0=b3(E[:, :], 4, 128), in1=INCL.unsqueeze(1).to_broadcast([128, 4, 128]), op=ALU.mult), reads=["FT7", "cf"], writes=["FT8"])
            pdh = [(PQ[0], "PQ0"), (PF[1], "PF1")]
            pd2 = lambda i: pdh[i][0][:, :].rearrange("p (h s n) -> p h s n", h=2, s=2)
            h2 = lambda t_, i: t_[:, i * 256:(i + 1) * 256].rearrange("p (h n) -> p h n", h=2)
            for i in range(2):
                P.dve(lambda e, i=i: e.tensor_tensor(out=h2(QKT, i), in0=h2(E2, i), in1=pd2(i)[:, :, 1, :], op=ALU.mult), reads=["FT8", pdh[i][1]], writes=["HT11"])
            P.pool(lambda e: e.tensor_tensor(out=b3(E2[:, :], 4, 128), in0=b3(E2[:, :], 4, 128), in1=STRICT.unsqueeze(1).to_broadcast([128, 4, 128]), op=ALU.mult), reads=["FT8", "cf"], writes=["FT8"])
            P.pool(lambda e: e.tensor_tensor(out=b3(E2[:, :], 4, 128), in0=b3(E2[:, :], 4, 128), in1=beta_b, op=ALU.mult), reads=["FT8", "FT5"], writes=["FT8"])
            X0 = FT[9]
            for i in range(2):
                P.dve(lambda e, i=i: e.tensor_tensor(out=h2(X0, i), in0=h2(E2, i), in1=pd2(i)[:, :, 0, :], op=ALU.mult), reads=["FT8", pdh[i][1]], writes=["FT9"])
            resA = dict(X=([HT[5], HT[6]], ["HT5", "HT6"]), Q=([HT[7], HT[8]], ["HT7", "HT8"]), N=([HT[9], HT[10]], ["HT9", "HT10"]),
                        SQ=(PQ[0], "PQ0"), G0=(PF[0], "PF0"), G1=(PF[1], "PF1"), T=(PB[0], "PB0"))
            Tinv = inverse_chain(P, X0, "FT9", resA, IDF, idb)
            TinvT, tk = Tinv
            WT, qgT = HT[12], HT[13]
            U0 = FT[6]
            for h in range(4):
                P.pe(lambda e, h=h: e.matmul(PF[0][:, h * 128:(h + 1) * 128], lhsT=Kg[:, h * 128:(h + 1) * 128], rhs=TinvT[:, h * 128:(h + 1) * 128], start=True, stop=True), reads=["HT2", tk], writes=["PF0"])
            P.copy(WT[:, :], PF[0][:, :], reads=["PF0"], writes=["HT12"])
            for h in range(4):
                P.pe(lambda e, h=h: e.matmul(PF[1][:, h * 128:(h + 1) * 128], lhsT=TinvT[:, h * 128:(h + 1) * 128], rhs=Vtm[:, h * 128:(h + 1) * 128], start=True, stop=True), reads=["HT4", tk], writes=["PF1"])
            P.copy(U0[:, :], PF[1][:, :], reads=["PF1"], writes=["FT6"])
            Dg = FT[7]
            P.pool(lambda e: e.tensor_tensor(out=b3(Dg[:, :], 4, 128), in0=IDF.unsqueeze(1).to_broadcast([128, 4, 128]), in1=sc[:, 16:20].unsqueeze(2).to_broadcast([128, 4, 128]), op=ALU.mult), reads=["cf", "FT5"], writes=["FT7"])
            for h in range(4):
                P.pe(lambda e, h=h: e.matmul(PF[0][:, h * 128:(h + 1) * 128], lhsT=ONESF, rhs=Dg[:, h * 128:(h + 1) * 128], start=True, stop=True), reads=["FT7", "cf"], writes=["PF0"])
            q4 = KQ[:, :].rearrange("p (h t s n) -> p h t s n", h=4, t=NTB, s=2)[:, :, tb, 1, :]
            P.dve(lambda e, q4=q4: e.tensor_tensor(out=b3(qgT[:, :], 4, 128), in0=q4, in1=b3(PF[0][:, :], 4, 128), op=ALU.mult), reads=["BLK0", "PF0"], writes=["HT13"])
            ub = HT[2 + 0]
            ub = HT[5]
            osb = FT[8]
            for c in range(2):
                r0, r1 = c * 64, c * 64 + 64
                for h in range(4):
                    P.pe(lambda e, h=h: e.matmul(PF[0][:, h * 128:(h + 1) * 128], lhsT=WT[:, h * 128:(h + 1) * 128], rhs=Sab[:, h * 128:(h + 1) * 128], start=True, stop=True), reads=["HT12", "Sab"], writes=["PF0"])
                tmpu = FT[9]
                P.dve(lambda e, r0=r0, r1=r1: e.tensor_tensor(out=tmpu[r0:r1, :], in0=U0[r0:r1, :], in1=PF[0][r0:r1, :], op=ALU.subtract), reads=["FT6", "PF0"], writes=["FT9"])
                P.dve(lambda e, r0=r0, r1=r1: e.tensor_tensor(out=b3(ub[r0:r1, :], 4, 128), in0=b3(tmpu[r0:r1, :], 4, 128), in1=sc[r0:r1, 0:4].unsqueeze(2).to_broadcast([64, 4, 128]), op=ALU.mult), reads=["FT9", "FT5"], writes=["HT5"])
                for h in range(4):
                    P.pe(lambda e, h=h: e.matmul(PF[1][:, h * 128:(h + 1) * 128], lhsT=qgT[:, h * 128:(h + 1) * 128], rhs=Sab[:, h * 128:(h + 1) * 128], start=True, stop=False), reads=["HT13", "Sab"], writes=["PF1"])
                    P.pe(lambda e, h=h, r0=r0, r1=r1: e.matmul(PF[1][:, h * 128:(h + 1) * 128], lhsT=QKT[r0:r1, h * 128:(h + 1) * 128], rhs=ub[r0:r1, h * 128:(h + 1) * 128], start=False, stop=True), reads=["HT11", "HT5"], writes=["PF1"])
                P.copy(osb[r0:r1, :], PF[1][r0:r1, :], reads=["PF1"], writes=["FT8"])
                for h in range(4):
                    P.pe(lambda e, h=h, r0=r0, r1=r1: e.matmul(PF[0][:, h * 128:(h + 1) * 128], lhsT=Kd[r0:r1, h * 128:(h + 1) * 128], rhs=ub[r0:r1, h * 128:(h + 1) * 128], start=True, stop=True), reads=["HT3", "HT5"], writes=["PF0"])
                P.dve(lambda e, c=c: e.tensor_tensor(out=b3(Sa[:, :], 4, 128), in0=b3(Sa[:, :], 4, 128), in1=sc[:, 24 + c * 4:28 + c * 4].unsqueeze(2).to_broadcast([128, 4, 128]), op=ALU.mult), reads=["Sa", "FT5"], writes=["Sa"])
                P.dve(lambda e: e.tensor_tensor(out=Sa[:, :], in0=Sa[:, :], in1=PF[0][:, :], op=ALU.add), reads=["Sa", "PF0"], writes=["Sa"])
                P.copy(Sab[:, :], Sa[:, :], reads=["Sa"], writes=["Sab"])
            sq = FT[9]
            P.pool(lambda e: e.tensor_tensor(out=sq[:, :], in0=osb[:, :], in1=osb[:, :], op=ALU.mult), reads=["FT8"], writes=["FT9"])
            P.dve(lambda e: e.tensor_reduce(out=sc[:, 40:44], in_=b3(sq[:, :], 4, 128), axis=AX.X, op=ALU.add), reads=["FT9"], writes=["FT5"])
            rsqrt_act(sc[:, 40:44], sc[:, 40:44], 1.0 / 128, 1e-6, ["FT5"], ["FT5"])
            P.dve(lambda e: e.tensor_tensor(out=b3(osb[:, :], 4, 128), in0=b3(osb[:, :], 4, 128), in1=sc[:, 40:44].unsqueeze(2).to_broadcast([128, 4, 128]), op=ALU.mult), reads=["FT8", "FT5"], writes=["FT8"])
            P.pool(lambda e: e.tensor_tensor(out=b3(osb[:, :], 4, 128), in0=b3(osb[:, :], 4, 128), in1=ANW.unsqueeze(1).to_broadcast([128, 4, 128]), op=ALU.mult), reads=["FT8", "rp"], writes=["FT8"])
            for c4 in range(4):
                P.pe(lambda e, c4=c4, cs=cs: e.transpose(out=PB[0][:, c4 * 128:(c4 + 1) * 128], in_=SGA[:, c4 * TB + tb * 128:c4 * TB + (tb + 1) * 128], identity=idb[:, :]), reads=["BLK2", "idb"], writes=["PB0"])
            P.dve(lambda e: e.tensor_tensor(out=mixA[:, tb * 512:(tb + 1) * 512], in0=osb[:, :], in1=PB[0][:, 0:512], op=ALU.mult), reads=["FT8", "PB0"], writes=[mxk])
        if last_blk:
            P.dma(p_amat.rearrange("h k v -> k h v"), b3(Sa[:, :], 4, 128), reads=["Sa"], writes=["p_amat"])
        stream_a = P.capture_end()
        P.replay([stream_a] + ([stream_b] if stream_b is not None else []))


        AR, BKL, BKH, VBT, SGB = arenab[:, 0:8 * TB], BLK[3], BLK[4], arenab[:, 4096:4096 + 4 * TB], arenab[:, 6144:6144 + 4 * TB]
        NCH = TB // 64
        ar = lambda pr, tb_, s_: AR[:, pr * 2 * TB + tb_ * 256 + s_ * 128: pr * 2 * TB + tb_ * 256 + (s_ + 1) * 128]
        bkh = lambda hh, pr, tb_, s_: (BKL, BKH)[hh][:, pr * 2 * TB + tb_ * 256 + s_ * 128: pr * 2 * TB + tb_ * 256 + (s_ + 1) * 128]
        ar_dst = lambda pr, s_: AR[:, pr * 2 * TB:(pr + 1) * 2 * TB].rearrange("p (t s n) -> p t s n", t=NTB, s=2)[:, :, s_, :]
        bk_dst = lambda hh, pr, s_: (BKL, BKH)[hh][64 * hh:64 * hh + 64, pr * 2 * TB:(pr + 1) * 2 * TB].rearrange("p (t s n) -> p t s n", t=NTB, s=2)[:, :, s_, :]
        t3 = lambda ap: ap.rearrange("p (t n) -> p t n", t=NTB)
        bci = [0]

        def b_chunk(j, dstT, dk):
            i = bci[0] % 2
            bci[0] += 1
            pf, pk = PF[i], "PF%d" % i
            cbuf, ck = cb[i], "cb%d" % i
            proj_chunk(2056 + j * 128, 128, pf, pk)
            P.pool(lambda e: e.tensor_copy(out=cbuf[:, 0:1], in_=bcar[:, j:j + 1]), reads=["bcar"], writes=[ck])
            P.act(lambda e: e.activation(out=cbuf[:, 1:1 + TB], in_=pf[:, 0:TB], func=AF.Copy), reads=[pk], writes=[ck])
            P.pool(lambda e: e.tensor_copy(out=bcar[:, j:j + 1], in_=cbuf[:, TB:TB + 1]), reads=[ck], writes=["bcar"])
            P.op("dve", lambda e: e.tensor_scalar(out=dstT[:, 0:TB], in0=cbuf[:, 0:TB], scalar1=col(MU0 + j), scalar2=None, op0=ALU.mult), [ck, "pcol"], [dk],
                 alts=[("act", lambda e: e.activation(out=dstT[:, 0:TB], in_=cbuf[:, 0:TB], func=AF.Copy, scale=col(MU0 + j)))] if ALT[0] else None)
            P.dve(lambda e: e.scalar_tensor_tensor(out=dstT[:, 0:TB], in0=cbuf[:, 1:1 + TB], scalar=omu[:, j:j + 1], in1=dstT[:, 0:TB], op0=ALU.mult, op1=ALU.add), reads=[dk, ck, "omu"], writes=[dk])

        xb16 = FT[0]
        b_chunk(16, xb16, "FT0")
        P.act(lambda e: e.activation(out=xb16[0:64, 0:TB], in_=xb16[0:64, 0:TB], func=AF.Tanh), reads=["FT0"], writes=["FT0"])
        for pr in range(4):
            sg, aic, Lsg, Pt, Pinv, Pm1, xr, xk, kmod = FT[1], FT[2], FT[3], FT[4], FT[5], FT[6], FT[7], FT[8], FT[9]
            P.pe(lambda e, pr=pr: e.matmul(PF[2][:, 0:TB], lhsT=w2a2[0:64, pr * 128:(pr + 1) * 128], rhs=xb16[0:64, 0:TB], start=True, stop=True), reads=["w2a2", "FT0"], writes=["PF2"])
            P.act(lambda e, pr=pr: e.activation(out=sg[:, 0:TB], in_=PF[2][:, 0:TB], func=AF.Sigmoid, bias=col(W00 + pr)), reads=["PF2", "pcol"], writes=["FT1"])
            P.pe(lambda e, pr=pr: e.matmul(PF[3][:, 0:TB], lhsT=w2a2[64:128, pr * 128:(pr + 1) * 128], rhs=xb16[64:128, 0:TB], start=True, stop=True), reads=["w2a2", "FT0"], writes=["PF3"])
            P.act(lambda e, pr=pr: e.activation(out=aic[:, 0:TB], in_=PF[3][:, 0:TB], func=AF.Sigmoid, bias=col(A00 + pr)), reads=["PF3", "pcol"], writes=["FT2"])
            P.dve(lambda e: e.tensor_tensor_scan(out=Lsg[:, 0:TB], data0=resetm[:, 0:TB], data1=sg[:, 0:TB], initial=0.0, op0=ALU.mult, op1=ALU.add), reads=["FT1", "resetm"], writes=["FT3"])
            P.act(lambda e: e.activation(out=Pt[:, 0:TB], in_=Lsg[:, 0:TB], func=AF.Exp, scale=-C0), reads=["FT3"], writes=["FT4"])
            P.act(lambda e: e.activation(out=Pinv[:, 0:TB], in_=Lsg[:, 0:TB], func=AF.Exp, scale=C0), reads=["FT3"], writes=["FT5"])
            P.dve(lambda e: e.tensor_tensor(out=Pm1[:, 0:TB], in0=Lsg[:, 0:TB], in1=sg[:, 0:TB], op=ALU.subtract), reads=["FT3", "FT1"], writes=["FT6"])
            P.act(lambda e: e.activation(out=Pm1[:, 0:TB], in_=Pm1[:, 0:TB], func=AF.Exp, scale=-C0), reads=["FT6"], writes=["FT6"])
            P.pool(lambda e, pr=pr: e.tensor_copy(out=pce[:, pr * NCH:(pr + 1) * NCH], in_=Pt[:, 0:TB].rearrange("p (c n) -> p c n", n=64)[:, :, 63]), reads=["FT4"], writes=["pce"])
            b_chunk(pr, xr, "FT7")
            P.dve(lambda e, pr=pr: e.tensor_tensor(out=ar_dst(pr, 1), in0=t3(xr[:, 0:TB]), in1=t3(Pt[:, 0:TB]), op=ALU.mult), reads=["FT7", "FT4"], writes=["BAR"])
            b_chunk(4 + pr, xk, "FT8")
            P.dve(lambda e, pr=pr: e.scalar_tensor_tensor(out=kmod[:, 0:TB], in0=aic[:, 0:TB], scalar=-1.0, in1=col(KA0 + pr).to_broadcast([128, TB]), op0=ALU.add, op1=ALU.mult), reads=["FT2", "pcol"], writes=["FT9"])
            P.dve(lambda e: e.scalar_tensor_tensor(out=kmod[:, 0:TB], in0=kmod[:, 0:TB], scalar=1.0, in1=xk[:, 0:TB], op0=ALU.add, op1=ALU.mult), reads=["FT9", "FT8"], writes=["FT9"])
            rkb = HT[1]
            P.dve(lambda e, pr=pr: e.scalar_tensor_tensor(out=rkb[:, 0:TB], in0=xr[:, 0:TB], scalar=col(RK0 + pr), in1=kmod[:, 0:TB], op0=ALU.mult, op1=ALU.mult), reads=["FT7", "FT9", "pcol"], writes=["HT1"])
            P.pe(lambda e: e.matmul(PF[3][:, 0:TB], lhsT=blb[:, :], rhs=rkb[:, 0:TB], start=True, stop=True), reads=["HT1", "blb"], writes=["PF3"])
            P.dve(lambda e, pr=pr: e.tensor_scalar(out=xk[:, 0:TB], in0=xk[:, 0:TB], scalar1=col(KK0 + pr), scalar2=None, op0=ALU.mult), reads=["FT8", "pcol"], writes=["FT8"])
            sqb = HT[0]
            P.act(lambda e: e.activation(out=sqb[:, 0:TB], in_=xk[:, 0:TB], func=AF.Square), reads=["FT8"], writes=["HT0"])
            P.pe(lambda e: e.matmul(PF[2][:, 0:TB], lhsT=blb[:, :], rhs=sqb[:, 0:TB], start=True, stop=True), reads=["HT0", "blb"], writes=["PF2"])
            rsqrt_act(xr[:, 0:TB], PF[2][:, 0:TB], 1.0, 1e-12, ["PF2"], ["FT7"])
            P.dve(lambda e: e.tensor_tensor(out=xk[:, 0:TB], in0=xk[:, 0:TB], in1=xr[:, 0:TB], op=ALU.mult), reads=["FT8", "FT7"], writes=["FT8"])
            P.dve(lambda e, pr=pr: e.scalar_tensor_tensor(out=ar_dst(pr, 0), in0=t3(xk[:, 0:TB]), scalar=-1.0, in1=t3(Pm1[:, 0:TB]), op0=ALU.mult, op1=ALU.mult), reads=["FT8", "FT6"], writes=["BAR"])
            P.pool(lambda e: e.tensor_tensor(out=xk[:, 0:TB], in0=xk[:, 0:TB], in1=aic[:, 0:TB], op=ALU.mult), reads=["FT8", "FT2"], writes=["FT8"])
            for hh in range(2):
                hs = slice(64 * hh, 64 * hh + 64)
                P.dve(lambda e, pr=pr, hh=hh, hs=hs: e.tensor_tensor(out=bk_dst(hh, pr, 0), in0=t3(xk[hs, 0:TB]), in1=t3(Pinv[hs, 0:TB]), op=ALU.mult), reads=["FT8", "FT5"], writes=["BLK%d" % (3 + hh)])
                P.dve(lambda e, pr=pr, hh=hh, hs=hs: e.tensor_tensor(out=bk_dst(hh, pr, 1), in0=t3(kmod[hs, 0:TB]), in1=t3(Pinv[hs, 0:TB]), op=ALU.mult), reads=["FT9", "FT5"], writes=["BLK%d" % (3 + hh)])
            b_chunk(8 + pr, xr, "FT7")
            P.act(lambda e, pr=pr: e.activation(out=VBT[:, pr * TB:(pr + 1) * TB], in_=xr[:, 0:TB], func=AF.Copy), reads=["FT7"], writes=["BVB"])
            P.dve(lambda e, pr=pr: e.tensor_tensor(out=bon[:, pr * TB:(pr + 1) * TB], in0=xr[:, 0:TB], in1=PF[3][:, 0:TB], op=ALU.mult), reads=["FT7", "PF3"], writes=["bon"])
            b_chunk(12 + pr, xk, "FT8")
            P.act(lambda e, pr=pr: e.activation(out=SGB[:, pr * TB:(pr + 1) * TB], in_=xk[:, 0:TB], func=AF.Silu), reads=["FT8"], writes=["BSG"])
        if last_blk:
            P.pe(lambda e: e.transpose(out=PF[0][0:17, 0:128], in_=bcar[:, :], identity=IDF), reads=["bcar", "cf"], writes=["PF0"])
            P.dve(lambda e: e.tensor_copy(out=FT[0][0:17, 0:128], in_=PF[0][0:17, 0:128]), reads=["PF0"], writes=["FT0"])
            P.dma(p_bshift[:, :], FT[0][0:17, 0:128], reads=["FT0"], writes=["p_bshift"])

        P.capture_begin()
        for tb in range(NTB):
            aTM, bTM, kTM, vTM = HT[0], HT[1], HT[26], HT[27]
            bonTM, bonk = (HT[28], "HT28") if tb % 2 == 0 else (HT[38], "HT38")
            sgTM, sgk = (HT[29], "HT29") if tb % 2 == 0 else (HT[39], "HT39")
            def emit_tm(tb=tb, aTM=aTM, bTM=bTM, kTM=kTM, vTM=vTM, bonTM=bonTM, bonk=bonk, sgTM=sgTM, sgk=sgk):
                for s_, dstt, dkey in ((0, bTM, "HT1"), (1, kTM, "HT26")):
                    for pr in range(4):
                        P.pe(lambda e, pr=pr, s_=s_: e.matmul(PF[2][:, pr * 128:(pr + 1) * 128], lhsT=bkh(0, pr, tb, s_), rhs=idb[:, :], start=True, stop=False), reads=["BLK3", "idb"], writes=["PF2"])
                        P.pe(lambda e, pr=pr, s_=s_: e.matmul(PF[2][:, pr * 128:(pr + 1) * 128], lhsT=bkh(1, pr, tb, s_), rhs=idb[:, :], start=False, stop=True), reads=["BLK4", "idb"], writes=["PF2"])
                    P.copy(dstt[:, :], PF[2][:, :], reads=["PF2"], writes=[dkey])
                srcs = [(lambda pr: ar(pr, tb, 0), "BAR", aTM, "HT0"), (lambda pr: VBT[:, pr * TB + tb * 128:pr * TB + (tb + 1) * 128], "BVB", vTM, "HT27"),
                        (lambda pr: bon[:, pr * TB + tb * 128:pr * TB + (tb + 1) * 128], "bon", bonTM, bonk),
                        (lambda pr: SGB[:, pr * TB + tb * 128:pr * TB + (tb + 1) * 128], "BSG", sgTM, sgk)]
                for si, (srcf, skey, dstt, dkey) in enumerate(srcs):
                    half = (si % 2) * 512
                    for pr in range(4):
                        P.pe(lambda e, pr=pr, srcf=srcf, half=half: e.transpose(out=PB[1][:, half + pr * 128:half + (pr + 1) * 128], in_=srcf(pr), identity=idb[:, :]), reads=[skey, "idb"], writes=["PB1"])
                    if True:
                        P.dve(lambda e, dstt=dstt, half=half: e.tensor_copy(out=dstt[:, :], in_=PB[1][:, half:half + 512]), reads=["PB1"], writes=[dkey])
            WmT, U0b, yb = HT[30], FT[0], FT[1]
            Yab = [HT[14], HT[15]]
            Yak = [HT[16], HT[17]]
            Aak = [HT[18], HT[19]]
            AV = HT[20]
            Ub = HT[21]
            for hb in range(2):
                for hl in range(4):
                    h = hb * 4 + hl
                    pr, p0 = h // 2, 64 * (h % 2)
                    P.pe(lambda e, hl=hl, pr=pr, h=h: e.matmul(PF[2 + hl // 2][:, (hl % 2) * 256:(hl % 2 + 1) * 256], lhsT=bkh(h % 2, pr, tb, 0), rhs=AR[:, pr * 2 * TB + tb * 256: pr * 2 * TB + (tb + 1) * 256], start=True, stop=True), reads=["BAR", "BLK3", "BLK4"], writes=["PF%d" % (2 + hl // 2)])
                pd2 = lambda i: PF[2 + i][:, :].rearrange("p (h s n) -> p h s n", h=2, s=2)
                h2 = lambda t_, i: t_[:, i * 256:(i + 1) * 256].rearrange("p (h n) -> p h n", h=2)
                m2 = lambda m_: m_.unsqueeze(1).to_broadcast([128, 2, 128])
                X0 = FT[3]
                for i in range(2):
                    P.dve(lambda e, i=i: e.tensor_tensor(out=h2(X0, i), in0=pd2(i)[:, :, 0, :], in1=m2(NSTRICT), op=ALU.mult), reads=["PF%d" % (2 + i), "cf"], writes=["FT3"])
                    P.dve(lambda e, hb=hb, i=i: e.tensor_tensor(out=h2(Yab[hb], i), in0=pd2(i)[:, :, 1, :], in1=m2(INCL), op=ALU.mult), reads=["PF%d" % (2 + i), "cf"], writes=["HT%d" % (14 + hb)])
                for hl in range(4):
                    h = hb * 4 + hl
                    pr, p0 = h // 2, 64 * (h % 2)
                    P.pe(lambda e, hl=hl, pr=pr, h=h: e.matmul(PF[2 + hl // 2][:, (hl % 2) * 256:(hl % 2 + 1) * 256], lhsT=bkh(h % 2, pr, tb, 1), rhs=AR[:, pr * 2 * TB + tb * 256: pr * 2 * TB + (tb + 1) * 256], start=True, stop=True), reads=["BAR", "BLK3", "BLK4"], writes=["PF%d" % (2 + hl // 2)])
                for i in range(2):
                    P.dve(lambda e, hb=hb, i=i: e.tensor_tensor(out=h2(Aak[hb], i), in0=pd2(i)[:, :, 0, :], in1=m2(STRICT), op=ALU.mult), reads=["PF%d" % (2 + i), "cf"], writes=["HT%d" % (18 + hb)])
                    P.dve(lambda e, hb=hb, i=i: e.tensor_tensor(out=h2(Yak[hb], i), in0=pd2(i)[:, :, 1, :], in1=m2(INCL), op=ALU.mult), reads=["PF%d" % (2 + i), "cf"], writes=["HT%d" % (16 + hb)])
                resB = dict(X=([HT[32], HT[33]], ["HT32", "HT33"]), Q=([HT[34], HT[35]], ["HT34", "HT35"]), N=([HT[36], HT[37]], ["HT36", "HT37"]),
                            SQ=(PQ[1], "PQ1"), G0=(PF[2], "PF2"), G1=(PF[3], "PF3"), T=(PB[1], "PB1"))
                TinvT, tk = inverse_chain(P, X0, "FT3", resB, IDF, idb)
                if hb == 0:
                    emit_tm()
                for hl in range(4):
                    h = hb * 4 + hl
                    pr, p0 = h // 2, 64 * (h % 2)
                    P.pe(lambda e, hl=hl, pr=pr: e.matmul(PF[2][:, hl * 128:(hl + 1) * 128], lhsT=aTM[:, pr * 128:(pr + 1) * 128], rhs=TinvT[:, hl * 128:(hl + 1) * 128], start=True, stop=True), reads=["HT0", tk], writes=["PF2"])
                    P.pe(lambda e, hl=hl, h=h, hb=hb: e.matmul(PF[3][:, hl * 64:(hl + 1) * 64], lhsT=Aak[hb][:, hl * 128:(hl + 1) * 128], rhs=vTM[:, h * 64:(h + 1) * 64], start=True, stop=True), reads=["HT%d" % (18 + hb), "HT27"], writes=["PF3"])
                for hh in range(2):
                    p0 = 64 * hh
                    src = PF[2][p0:p0 + 64, :].rearrange("p (a b n) -> p a b n", a=2, b=2)[:, :, hh, :]
                    dst = WmT[p0:p0 + 64, hb * 256:(hb + 1) * 256].rearrange("p (a n) -> p a n", a=2)
                    P.copy(dst, src, reads=["PF2"], writes=["HT30"])
                P.copy(AV[:, 0:256], PF[3][:, 0:256], reads=["PF3"], writes=["HT20"])
                for hl in range(4):
                    P.pe(lambda e, hl=hl: e.matmul(PF[3][:, 256 + hl * 64:256 + (hl + 1) * 64], lhsT=TinvT[:, hl * 128:(hl + 1) * 128], rhs=AV[:, hl * 64:(hl + 1) * 64], start=True, stop=True), reads=[tk, "HT20"], writes=["PF3"])
                P.copy(U0b[:, hb * 256:(hb + 1) * 256], PF[3][:, 256:512], reads=["PF3"], writes=["FT0"])
            for c in range(2):
                r0, r1 = c * 64, c * 64 + 64
                ci = tb * 2 + c
                hsl = lambda h: slice((h // 2) * 128 + (h % 2) * 64, (h // 2) * 128 + (h % 2) * 64 + 64)
                for h in range(8):
                    pr, p0 = h // 2, 64 * (h % 2)
                    P.pe(lambda e, h=h, pr=pr, p0=p0: e.matmul(PF[2][:, h * 64:(h + 1) * 64], lhsT=WmT[:, pr * 128:(pr + 1) * 128], rhs=Hbb[:, hsl(h)], start=True, stop=True), reads=["HT30", "Hbb"], writes=["PF2"])
                P.dve(lambda e, r0=r0, r1=r1: e.tensor_tensor(out=Ub[r0:r1, :], in0=U0b[r0:r1, :], in1=PF[2][r0:r1, :], op=ALU.add), reads=["FT0", "PF2"], writes=["HT21"])
                for h in range(8):
                    pr, p0, hb, hl = h // 2, 64 * (h % 2), h // 4, h % 4
                    P.pe(lambda e, h=h, pr=pr, p0=p0: e.matmul(PF[3][:, h * 64:(h + 1) * 64], lhsT=ar(pr, tb, 1), rhs=Hbb[:, hsl(h)], start=True, stop=False), reads=["BAR", "Hbb"], writes=["PF3"])
                    P.pe(lambda e, h=h, hb=hb, hl=hl, r0=r0, r1=r1: e.matmul(PF[3][:, h * 64:(h + 1) * 64], lhsT=Yab[hb][r0:r1, hl * 128:(hl + 1) * 128], rhs=Ub[r0:r1, h * 64:(h + 1) * 64], start=False, stop=False), reads=["HT%d" % (14 + hb), "HT21"], writes=["PF3"])
                    P.pe(lambda e, h=h, hb=hb, hl=hl, r0=r0, r1=r1: e.matmul(PF[3][:, h * 64:(h + 1) * 64], lhsT=Yak[hb][r0:r1, hl * 128:(hl + 1) * 128], rhs=vTM[r0:r1, h * 64:(h + 1) * 64], start=False, stop=True), reads=["HT%d" % (16 + hb), "HT27"], writes=["PF3"])
                P.copy(yb[r0:r1, :], PF[3][r0:r1, :], reads=["PF3"], writes=["FT1"])
                for pr in range(4):
                    P.pe(lambda e, pr=pr, r0=r0, r1=r1: e.matmul(PQ[1][:, pr * 128:(pr + 1) * 128], lhsT=bTM[r0:r1, pr * 128:(pr + 1) * 128], rhs=Ub[r0:r1, pr * 128:(pr + 1) * 128], start=True, stop=False), reads=["HT1", "HT21"], writes=["PQ1"])
                    P.pe(lambda e, pr=pr, r0=r0, r1=r1: e.matmul(PQ[1][:, pr * 128:(pr + 1) * 128], lhsT=kTM[r0:r1, pr * 128:(pr + 1) * 128], rhs=vTM[r0:r1, pr * 128:(pr + 1) * 128], start=False, stop=True), reads=["HT26", "HT27"], writes=["PQ1"])
                P.dve(lambda e: e.tensor_tensor(out=Hb[:, :], in0=Hb[:, :], in1=PQ[1][:, 0:512], op=ALU.add), reads=["Hb", "PQ1"], writes=["Hb"])
                pc3 = pce[:, :].rearrange("p (a c) -> p a c", a=4)[:, :, ci]
                P.dve(lambda e, pc3=pc3: e.tensor_tensor(out=b3(Hb[:, :], 4, 128), in0=b3(Hb[:, :], 4, 128), in1=pc3.unsqueeze(2).to_broadcast([128, 4, 128]), op=ALU.mult), reads=["Hb", "pce"], writes=["Hb"])
                P.pool(lambda e: e.tensor_tensor(out=b3(Hbb[:, :], 4, 128), in0=b3(Hb[:, :], 4, 128), in1=BL.unsqueeze(1).to_broadcast([128, 4, 128]), op=ALU.mult), reads=["Hb", "cf"], writes=["Hbb"])
            y3 = b3(yb[:, :], 8, 64)
            stt = FT[2]
            P.dve(lambda e: e.tensor_reduce(out=stt[:, 0:8], in_=y3, axis=AX.X, op=ALU.add), reads=["FT1"], writes=["FT2"])
            P.dve(lambda e: e.tensor_scalar(out=stt[:, 0:8], in0=stt[:, 0:8], scalar1=1.0 / 64, scalar2=None, op0=ALU.mult), reads=["FT2"], writes=["FT2"])
            P.dve(lambda e: e.tensor_tensor(out=y3, in0=y3, in1=stt[:, 0:8].unsqueeze(2).to_broadcast([128, 8, 64]), op=ALU.subtract), reads=["FT1", "FT2"], writes=["FT1"])
            sq2 = FT[0]
            P.pool(lambda e: e.tensor_tensor(out=sq2[:, :], in0=yb[:, :], in1=yb[:, :], op=ALU.mult), reads=["FT1"], writes=["FT0"])
            P.dve(lambda e: e.tensor_reduce(out=stt[:, 8:16], in_=b3(sq2[:, :], 8, 64), axis=AX.X, op=ALU.add), reads=["FT0"], writes=["FT2"])
            rsqrt_act(stt[:, 8:16], stt[:, 8:16], 1.0 / 64, 64e-5, ["FT2"], ["FT2"])
            P.dve(lambda e: e.tensor_tensor(out=y3, in0=y3, in1=stt[:, 8:16].unsqueeze(2).to_broadcast([128, 8, 64]), op=ALU.mult), reads=["FT1", "FT2"], writes=["FT1"])
            P.pool(lambda e: e.tensor_tensor(out=yb[:, :], in0=yb[:, :], in1=LNW, op=ALU.mult), reads=["FT1", "rp"], writes=["FT1"])
            P.pool(lambda e: e.tensor_tensor(out=yb[:, :], in0=yb[:, :], in1=LNB, op=ALU.add), reads=["FT1", "rp"], writes=["FT1"])
            P.dve(lambda e: e.tensor_tensor(out=yb[:, :], in0=yb[:, :], in1=bonTM[:, :], op=ALU.add), reads=["FT1", bonk], writes=["FT1"])
            P.dve(lambda e: e.tensor_tensor(out=mixB4[:, tb * 512:(tb + 1) * 512], in0=yb[:, :], in1=sgTM[:, :], op=ALU.mult), reads=["FT1", sgk], writes=["mixB4_%d" % tb])
        for tb in range(NTB):
            for c8 in range(8):
                if c8 < 4:
                    P.pe(lambda e, c8=c8: e.transpose(out=PB[1][:, c8 * 128:(c8 + 1) * 128], in_=mixA[:, tb * 512 + c8 * 128:tb * 512 + (c8 + 1) * 128], identity=idb[:, :]), reads=[mxk, "idb"], writes=["PB1"])
                else:
                    P.pe(lambda e, c8=c8: e.transpose(out=PB[1][:, c8 * 128:(c8 + 1) * 128], in_=mixB4[:, tb * 512 + (c8 - 4) * 128:tb * 512 + (c8 - 3) * 128], identity=idb[:, :]), reads=["mixB4_%d" % tb, "idb"], writes=["PB1"])
            P.dve(lambda e: e.tensor_copy(out=mixT[:, :], in_=PB[1][:, :]), reads=["PB1"], writes=["mixT"])
            hx = xt[tb % 2]
            hk = "xt%d" % (tb % 2)
            P.dma(hx[:, :], xp[t0 + tb * 128:t0 + (tb + 1) * 128, :], writes=[hk])
            for n in range(2):
                for kc in range(8):
                    P.pe(lambda e, n=n, kc=kc: e.matmul(PQ[1][:, :], lhsT=mixT[:, kc * 128:(kc + 1) * 128], rhs=woutb[:, kc * 1024 + n * 512:kc * 1024 + (n + 1) * 512], start=(kc == 0), stop=(kc == 7)), reads=["mixT", "woutb"], writes=["PQ1"])
                P.dve(lambda e, n=n, hx=hx: e.tensor_tensor(out=hx[:, n * 512:(n + 1) * 512], in0=hx[:, n * 512:(n + 1) * 512], in1=PQ[1][:, :], op=ALU.add), reads=[hk, "PQ1"], writes=[hk])
            P.act(lambda e, hx=hx: e.activation(out=xs_[:, :], in_=hx[:, :], func=AF.Square, accum_out=st4[:, 4:5]), reads=[hk], writes=["xs_", "st4"])
            rsqrt_act(st4[:, 4:5], st4[:, 4:5], 1.0 / D, 1e-6, ["st4"], ["st4"])
            P.dve(lambda e, hx=hx: e.scalar_tensor_tensor(out=xs_[:, :], in0=hx[:, :], scalar=st4[:, 4:5], in1=FNW, op0=ALU.mult, op1=ALU.mult), reads=[hk, "st4", "rp"], writes=["xs_"])
            P.dma(yp[t0 + tb * 128:t0 + (tb + 1) * 128, :], xs_[:, :], reads=["xs_"], writes=["yp%d_%d" % (blk, tb)])
        if last_blk:
            for pr in range(4):
                P.pe(lambda e, pr=pr: e.transpose(out=PF[2][:, pr * 128:(pr + 1) * 128], in_=Hb[:, pr * 128:(pr + 1) * 128], identity=IDF), reads=["Hb", "cf"], writes=["PF2"])
            P.dve(lambda e: e.tensor_copy(out=FT[4][:, :], in_=PF[2][:, :]), reads=["PF2"], writes=["FT4"])
            for h in range(8):
                pr, p0 = h // 2, 64 * (h % 2)
                P.dma(p_bmat[h], FT[4][p0:p0 + 64, pr * 128 + p0:pr * 128 + p0 + 64], reads=["FT4"], writes=["p_bmat%d" % h])
        stream_b = P.capture_end()
        if last_blk:
            P.replay([stream_b])

    if DECODE_LAST[0]:
        P.pool(lambda e: e.memset(st4[:, 7:8], 0.0), reads=["BAR", "BVB", "BSG"], writes=["SF%d" % i for i in range(8)] + ["st4"])
        sample_path(SAMPLE_LOCALS)
    P.finish()
    return nc


def inverse_chain(P, X0, x0key, res, IDF, idb):
    Xt, Xk = res["X"]
    Qt, Qk = res["Q"]
    Nt, Nk = res["N"]
    (SQ, sqk), (G0, g0k), (G1, g1k), (TT, ttk) = res["SQ"], res["G0"], res["G1"], res["T"]

    def mm4(out_ps, okey, lhs, lkey, rhs, rkey):
        for h in range(4):
            P.pe(lambda e, h=h: e.matmul(out_ps[:, h * 128:(h + 1) * 128], lhsT=lhs[:, h * 128:(h + 1) * 128], rhs=rhs[:, h * 128:(h + 1) * 128], start=True, stop=True), reads=[lkey, rkey], writes=[okey])

    P.copy(Xt[0][:, :], X0[:, :], reads=[x0key], writes=[Xk[0]])
    P.dve(lambda e: e.tensor_tensor(out=Qt[0][:, :].rearrange("p (h n) -> p h n", h=4), in0=IDF.unsqueeze(1).to_broadcast([128, 4, 128]), in1=X0[:, :].rearrange("p (h n) -> p h n", h=4), op=ALU.subtract), reads=["cf", x0key], writes=[Qk[0]])
    for h in range(4):
        P.pe(lambda e, h=h: e.transpose(out=TT[:, h * 128:(h + 1) * 128], in_=Xt[0][:, h * 128:(h + 1) * 128], identity=idb[:, :]), reads=[Xk[0], "idb"], writes=[ttk])
    P.dve(lambda e: e.tensor_copy(out=Nt[0][:, :], in_=TT[:, 0:512]), reads=[ttk], writes=[Nk[0]])
    mm4(SQ, sqk, Nt[0], Nk[0], Xt[0], Xk[0])
    mm4(G0, g0k, Xt[0], Xk[0], Nt[0], Nk[0])
    P.copy(Xt[1][:, :], SQ[:, 0:512], reads=[sqk], writes=[Xk[1]])
    P.copy(Nt[1][:, :], G0[:, 0:512], reads=[g0k], writes=[Nk[1]])
    xi, ni, qi = 1, 1, 0
    for m in range(5):
        mm4(G1, g1k, Nt[ni], Nk[ni], Qt[qi], Qk[qi])
        P.dve(lambda e, qi=qi: e.tensor_tensor(out=Qt[1 - qi][:, :], in0=Qt[qi][:, :], in1=G1[:, 0:512], op=ALU.add), reads=[Qk[qi], g1k], writes=[Qk[1 - qi]])
        qi = 1 - qi
        if m < 4:
            mm4(SQ, sqk, Nt[ni], Nk[ni], Xt[xi], Xk[xi])
            mm4(G0, g0k, Xt[xi], Xk[xi], Nt[ni], Nk[ni])
            P.copy(Xt[1 - xi][:, :], SQ[:, 0:512], reads=[sqk], writes=[Xk[1 - xi]])
            P.copy(Nt[1 - ni][:, :], G0[:, 0:512], reads=[g0k], writes=[Nk[1 - ni]])
            xi, ni = 1 - xi, 1 - ni
    return Qt[qi], Qk[qi]


_PQ = {}


def PBt_f32(PD, FT):
    return _PQ["t"]


def group_b_block(L):
    pass


def pack_inputs(inp):
    g = lambda k: np.asarray(inp[k], np.float32)
    vrows = np.zeros((128, 128), np.float32)
    vrows[0:8] = g("norm_w")[0].reshape(8, 128)
    cw = g("conv_w")[0]
    for c in range(12):
        for i in range(4):
            vrows[8 + c * 4 + i] = cw[i, c * 128:(c + 1) * 128]
    vrows[56:73] = g("mu")[0].reshape(17, 128)
    vrows[73:77] = g("w0")[0].reshape(4, 128)
    vrows[77:81] = g("a0")[0].reshape(4, 128)
    vrows[81:85] = g("k_k")[0].reshape(4, 128)
    vrows[85:89] = g("k_a")[0].reshape(4, 128)
    vrows[89:93] = g("r_k")[0].reshape(4, 128)
    rowp = np.concatenate([g("ln_w")[0], g("ln_b")[0], np.zeros(1024, np.float32), g("a_norm_w")[0],
                           g("final_norm_w"), g("a_log")[0], g("dt_bias")[0]]).astype(np.float32)
    p = np.arange(512)
    resetm = np.broadcast_to((p % 64 != 0).astype(np.float32), (128, 512)).copy()
    return dict(w_in=g("w_in")[0], w_out=g("w_out")[0], vrows=vrows, rowp=rowp, w2=g("w2")[0], a2=g("a2")[0],
                consts=host_consts(), resetm=resetm)


def kernel(**inputs):
    n = 8
    packed = pack_inputs(inputs)
    xp = np.ascontiguousarray(np.asarray(inputs["x_prompt"], np.float32))
    nc = build(T=2048, NS=16, TB=512)
    in_maps = []
    for i in range(n):
        m = dict(packed)
        m["xp"] = xp[i]
        sl = slice(i * 16, (i + 1) * 16)
        m["xs"] = np.ascontiguousarray(np.asarray(inputs["x_sample"], np.float32)[sl, 0])
        m["sa_mat"] = np.ascontiguousarray(np.asarray(inputs["state_a_mat"], np.float32)[0, sl])
        m["sa_conv"] = np.ascontiguousarray(np.asarray(inputs["state_a_conv"], np.float32)[0, sl])
        m["sb_mat"] = np.ascontiguousarray(np.asarray(inputs["state_b_mat"], np.float32)[0, sl])
        m["sb_shift"] = np.ascontiguousarray(np.asarray(inputs["state_b_shift"], np.float32)[0, sl])
        in_maps.append(m)
    res = run_bass_kernel_spmd(nc, in_maps, core_ids=list(range(n)))
    r = res.results
    f = lambda k: [np.asarray(r[i][k], np.float32) for i in range(n)]
    y_prompt = np.stack(f("yp"))
    p_amat = np.stack(f("p_amat"))[None]
    p_aconv = np.stack([a.reshape(3, 12, 128).reshape(3, 1536) for a in f("p_aconv")])[None]
    p_bmat = np.stack(f("p_bmat"))[None]
    p_bshift = np.stack([a.reshape(2176) for a in f("p_bshift")])[None]
    y_sample = np.concatenate(f("ys"))[:, None, :]
    s_amat = np.concatenate(f("s_amat"))[None]
    s_aconv = np.concatenate(f("s_aconv"))[None]
    s_bmat = np.concatenate(f("s_bmat"))[None]
    s_bshift = np.concatenate(f("s_bshift"))[None]
    return (y_prompt, y_sample, p_amat, p_aconv, p_bmat, p_bshift, s_amat, s_aconv, s_bmat, s_bshift)


def sample_path(L):
    P, NS = L["P"], L["NS"]
    PF, PQ, PB = L["PF"], L["PQ"], L["PB"]
    FT, HT, SF, SPJ, xnTs = L["FT"], L["HT"], L["SF"], L["SPJ"], L["xnTs"]
    xt, xs_, st4, pcol, col = L["xt"], L["xs_"], L["st4"], L["pcol"], L["col"]
    IDF, ONESF, BL, PAIRS, idb = L["IDF"], L["ONESF"], L["BL"], L["PAIRS"], L["idb"]
    wbf, w_in, woutb, w2a2 = L["wbf"], L["w_in"], L["woutb"], L["w2a2"]
    rsqrt_act, b3 = L["rsqrt_act"], L["b3"]
    LNW, LNB, ANW, FNW, DTB, nega = L["LNW"], L["LNB"], L["ANW"], L["FNW"], L["DTB"], L["nega"]
    NW0, CW0, MU0, W00, A00, KK0, KA0, RK0 = 0, 8, 56, 73, 77, 81, 85, 89
    xs_d, sa_mat_d, sa_conv_d, sb_mat_d, sb_shift_d = L["xs_d"], L["sa_mat_d"], L["sa_conv_d"], L["sb_mat_d"], L["sb_shift_d"]
    ys, s_amat, s_aconv, s_bmat, s_bshift = L["ys"], L["s_amat"], L["s_aconv"], L["s_bmat"], L["s_bshift"]
    scr_q, scr_k, scr_v, scr_ab, scr_o, scr_b6, scr_y = L["scr_q"], L["scr_k"], L["scr_v"], L["scr_ab"], L["scr_o"], L["scr_b6"], L["scr_y"]
    N = NS
    R = slice(0, N)

    xa = xt[0]
    P.dma(xa[R, :], xs_d[:, :], writes=["xt0"])
    P.act(lambda e: e.activation(out=xs_[R, :], in_=xa[R, :], func=AF.Square, accum_out=st4[R, 0:1]), reads=["xt0"], writes=["xs_", "st4"])
    rsqrt_act(st4[R, 0:1], st4[R, 0:1], 1.0 / D, 1e-6, ["st4"], ["st4"])
    P.act(lambda e: e.activation(out=xs_[R, :], in_=xa[R, :], func=AF.Copy, scale=st4[R, 0:1]), reads=["xt0", "st4"], writes=["xs_"])
    for half in range(2):
        pf, pk = PF[half], "PF%d" % half
        for q in range(4):
            kc = half * 4 + q
            P.pe(lambda e, pf=pf, q=q, kc=kc: e.transpose(out=pf[:, q * N:(q + 1) * N], in_=xs_[R, kc * 128:(kc + 1) * 128], identity=IDF[R, R]), reads=["xs_", "cf"], writes=[pk])
        P.dve(lambda e, pf=pf, half=half: e.tensor_tensor(out=xnTs[:, half * 4 * N:(half + 1) * 4 * N].rearrange("p (k t) -> p k t", k=4), in0=pf[:, 0:4 * N].rearrange("p (k t) -> p k t", k=4), in1=pcol[:, NW0 + half * 4:NW0 + half * 4 + 4].unsqueeze(2).to_broadcast([128, 4, N]), op=ALU.mult), reads=[pk, "pcol"], writes=["xnTs"])

    chunks = [(c * 128, 128) for c in range(16)] + [(2048, 8)] + [(2056 + j * 128, 128) for j in range(17)]
    for idx, (c0, ncols) in enumerate(chunks):
        s = idx % len(wbf)
        bk_ = "wbf%d" % s
        pf, pk = PF[idx % 2], "PF%d" % (idx % 2)
        P.dma(wbf[s][:, 0:8 * ncols].rearrange("p (k n) -> p k n", k=8), w_in[:, c0:c0 + ncols].rearrange("(k p) n -> p k n", p=128), writes=[bk_], q="pool")
        for kc in range(8):
            P.pe(lambda e, kc=kc, s=s, pf=pf, ncols=ncols: e.matmul(pf[0:ncols, 0:N], lhsT=wbf[s][:, kc * ncols:(kc + 1) * ncols], rhs=xnTs[:, kc * N:(kc + 1) * N], start=(kc == 0), stop=(kc == 7)), reads=[bk_, "xnTs"], writes=[pk])
        P.act(lambda e, pf=pf, ncols=ncols, idx=idx: e.activation(out=SPJ[0:ncols, idx * N:(idx + 1) * N], in_=pf[0:ncols, 0:N], func=AF.Copy), reads=[pk], writes=["SPJ"])
    sp = lambda i, j=None: SPJ[:, i * N:((i + 1) if j is None else j) * N]

    def to_tm(srcs, skey, dst, dkey, ps, pskey):
        for i, src in enumerate(srcs):
            P.pe(lambda e, i=i, src=src: e.transpose(out=ps[R, i * 128:(i + 1) * 128], in_=src, identity=IDF), reads=[skey, "cf"], writes=[pskey])
        n = len(srcs) * 128
        P.dve(lambda e: e.tensor_copy(out=dst[R, 0:n], in_=ps[R, 0:n]), reads=[pskey], writes=[dkey])

    cv = sa_conv_d.rearrange("b i c -> (b i) c")
    P.dma(xt[1][0:3 * N, 0:1024], cv[:, 0:1024], writes=["xt1"])
    P.dma(FT[0][0:3 * N, 0:512], cv[:, 1024:1536], writes=["FT0"])
    for c in range(12):
        src = xt[1][0:3 * N, c * 128:(c + 1) * 128] if c < 8 else FT[0][0:3 * N, (c - 8) * 128:(c - 7) * 128]
        po = c * 3 * N if c < 10 else 512 + (c - 10) * 3 * N
        pq_ = PQ[0] if c < 10 else PQ[1]
        po = po % 512
        P.pe(lambda e, c=c, src=src, po=po, pq_=pq_: e.transpose(out=pq_[:, po:po + 3 * N], in_=src, identity=IDF[0:3 * N, 0:3 * N]), reads=["xt1", "FT0", "cf"], writes=["PQ0", "PQ1"])
    P.dve(lambda e: e.tensor_copy(out=xs_[:, 0:30 * N], in_=PQ[0][:, 0:30 * N]), reads=["PQ0"], writes=["xs_"])
    P.dve(lambda e: e.tensor_copy(out=xs_[:, 30 * N:36 * N], in_=PQ[1][:, 0:6 * N]), reads=["PQ1"], writes=["xs_"])
    acc = FT[2]
    for c in range(12):
        cs3 = xs_[:, c * 3 * N:(c + 1) * 3 * N].rearrange("p (b i) -> p b i", i=3)
        tmp3 = FT[1][:, 0:3 * N].rearrange("p (b i) -> p b i", i=3)
        P.dve(lambda e, cs3=cs3, tmp3=tmp3, c=c: e.tensor_tensor(out=tmp3, in0=cs3, in1=pcol[:, CW0 + 4 * c:CW0 + 4 * c + 3].unsqueeze(1).to_broadcast([128, N, 3]), op=ALU.mult), reads=["xs_", "pcol"], writes=["FT1"])
        P.dve(lambda e, tmp3=tmp3, c=c: e.tensor_reduce(out=acc[:, c * N:(c + 1) * N], in_=tmp3, axis=AX.X, op=ALU.add), reads=["FT1"], writes=["FT2"])
        P.dve(lambda e, c=c: e.scalar_tensor_tensor(out=acc[:, c * N:(c + 1) * N], in0=sp(c), scalar=col(CW0 + 4 * c + 3), in1=acc[:, c * N:(c + 1) * N], op0=ALU.mult, op1=ALU.add), reads=["SPJ", "pcol", "FT2"], writes=["FT2"])
    P.act(lambda e: e.activation(out=acc[:, 0:12 * N], in_=acc[:, 0:12 * N], func=AF.Silu), reads=["FT2"], writes=["FT2"])
    P.act(lambda e: e.activation(out=FT[3][:, 0:8 * N], in_=acc[:, 0:8 * N], func=AF.Square), reads=["FT2"], writes=["FT3"])
    P.pe(lambda e: e.matmul(PF[2][:, 0:8 * N], lhsT=ONESF, rhs=FT[3][:, 0:8 * N], start=True, stop=True), reads=["FT3", "cf"], writes=["PF2"])
    rsqrt_act(FT[3][:, 0:8 * N], PF[2][:, 0:8 * N], 1.0, 1e-12, ["PF2"], ["FT3"])
    P.dve(lambda e: e.scalar_tensor_tensor(out=acc[:, 0:4 * N], in0=acc[:, 0:4 * N], scalar=128 ** -0.5, in1=FT[3][:, 0:4 * N], op0=ALU.mult, op1=ALU.mult), reads=["FT2", "FT3"], writes=["FT2"])
    P.dve(lambda e: e.tensor_tensor(out=acc[:, 4 * N:8 * N], in0=acc[:, 4 * N:8 * N], in1=FT[3][:, 4 * N:8 * N], op=ALU.mult), reads=["FT2", "FT3"], writes=["FT2"])
    for g, (scr, sf, sname) in enumerate(((scr_q, 0, "scr_q"), (scr_k, 1, "scr_k"), (scr_v, 2, "scr_v"))):
        to_tm([acc[:, (g * 4 + i) * N:(g * 4 + i + 1) * N] for i in range(4)], "FT2", SF[sf], "SF%d" % sf, PF[3], "PF3")
        P.dma(scr[:, :], SF[sf][R, 0:512], reads=["SF%d" % sf], writes=[sname])
    P.dma(s_aconv[:, 0:2, :], sa_conv_d[:, 1:3, :], writes=["s_aconv"])
    for g in range(3):
        to_tm([sp(g * 4 + i) for i in range(4)], "SPJ", SF[3], "SF3", PF[3], "PF3")
        P.dma(s_aconv[:, 2, g * 512:(g + 1) * 512], SF[3][R, 0:512], reads=["SF3"], writes=["s_aconv2_%d" % g])
    sga = FT[3]
    P.act(lambda e: e.activation(out=sga[:, 0:4 * N], in_=sp(12, 16), func=AF.Silu), reads=["SPJ"], writes=["FT3"])
    sgaTM = SF[4]
    to_tm([sga[:, i * N:(i + 1) * N] for i in range(4)], "FT3", sgaTM, "SF4", PF[3], "PF3")
    sc = SF[5]
    P.pe(lambda e: e.transpose(out=PF[3][R, 0:8], in_=SPJ[0:8, 16 * N:17 * N], identity=IDF[0:8, 0:8]), reads=["SPJ", "cf"], writes=["PF3"])
    ab3 = sc[R, 16:24].rearrange("p (h s) -> p h s", s=2)
    P.act(lambda e: e.activation(out=ab3[:, :, 1], in_=PF[3][R, 0:4], func=AF.Sigmoid), reads=["PF3"], writes=["SF5"])
    P.dve(lambda e: e.tensor_tensor(out=sc[R, 4:8], in0=PF[3][R, 4:8], in1=DTB[R, :], op=ALU.add), reads=["PF3", "rp"], writes=["SF5"])
    P.act(lambda e: e.activation(out=sc[R, 4:8], in_=sc[R, 4:8], func=AF.Exp), reads=["SF5"], writes=["SF5"])
    P.act(lambda e: e.activation(out=sc[R, 4:8], in_=sc[R, 4:8], func=AF.Ln, bias=1.0), reads=["SF5"], writes=["SF5"])
    P.dve(lambda e: e.tensor_tensor(out=sc[R, 4:8], in0=sc[R, 4:8], in1=nega[R, :], op=ALU.mult), reads=["SF5", "nega"], writes=["SF5"])
    P.act(lambda e: e.activation(out=ab3[:, :, 0], in_=sc[R, 4:8], func=AF.Exp), reads=["SF5"], writes=["SF5"])
    P.dma(scr_ab[:, :], sc[R, 16:24], reads=["SF5"], writes=["scr_ab"])
    VP = FT[8]
    kP, qP, vP, abP = VP[:, 0:64], VP[:, 64:128], VP[:, 128:256], VP[:, 256:260]
    for hi in range(2):
        hs = slice(hi * 64, hi * 64 + 64)
        P.dma(VP[hs, 0:64], scr_k.rearrange("b (h t l) -> (b h) t l", h=4, t=2)[:, hi, :], reads=["scr_k"], writes=["FT8"])
        P.dma(VP[hs, 64:128], scr_q.rearrange("b (h t l) -> (b h) t l", h=4, t=2)[:, hi, :], reads=["scr_q"], writes=["FT8"])
        P.dma(VP[hs, 128:256], scr_v.rearrange("b (h v) -> (b h) v", h=4), reads=["scr_v"], writes=["FT8"])
        P.dma(VP[hs, 256:258], scr_ab.rearrange("b (h s) -> (b h) s", s=2), reads=["scr_ab"], writes=["FT8"])
    P.dve(lambda e: e.tensor_scalar(out=VP[:, 258:259], in0=VP[:, 256:257], scalar1=-1.0, scalar2=None, op0=ALU.mult), reads=["FT8"], writes=["FT8"])
    sav = sa_mat_d.rearrange("b h (t l) v -> (b h) t l v", t=2)
    sov = s_amat.rearrange("b h (t l) v -> (b h) t l v", t=2)
    SLs, prods, red, accs = [(FT[4], "FT4"), (FT[3], "FT3")], [(FT[5], "FT5"), (FT[6], "FT6")], FT[2], FT[7]
    kS, uu, oacc = accs[:, 0:128], accs[:, 128:256], accs[:, 256:384]
    sl3 = lambda t_: t_[:, :].rearrange("p (l v) -> p l v", l=4)
    P.pool(lambda e: e.memset(accs[:, :], 0.0), writes=["FT7"])
    NSL = 16
    for j in range(NSL):
        (SL, slk), (prod, pdk) = SLs[j % 2], prods[j % 2]
        for hi in range(2):
            P.dma(sl3(SL)[hi * 64:hi * 64 + 64], sav[:, hi, 4 * j:4 * j + 4, :], writes=[slk])
        P.dve(lambda e, j=j: e.tensor_tensor(out=sl3(prod), in0=sl3(SL), in1=kP[:, 4 * j:4 * j + 4].unsqueeze(2).to_broadcast([128, 4, 128]), op=ALU.mult), reads=[slk, "FT8"], writes=[pdk])
        P.pool(lambda e, prod=prod: e.tensor_tensor(out=prod[:, 0:256], in0=prod[:, 0:256], in1=prod[:, 256:512], op=ALU.add), reads=[pdk], writes=[pdk])
        P.pool(lambda e, prod=prod: e.tensor_tensor(out=prod[:, 0:128], in0=prod[:, 0:128], in1=prod[:, 128:256], op=ALU.add), reads=[pdk], writes=[pdk])
        P.dve(lambda e: e.tensor_tensor(out=kS, in0=kS, in1=prod[:, 0:128], op=ALU.add), reads=["FT7", pdk], writes=["FT7"])
    P.pe(lambda e: e.matmul(PF[2][:, 0:128], lhsT=PAIRS, rhs=kS, start=True, stop=True), reads=["FT7", "cf"], writes=["PF2"])
    P.dve(lambda e: e.scalar_tensor_tensor(out=uu, in0=PF[2][:, 0:128], scalar=VP[:, 258:259], in1=vP, op0=ALU.mult, op1=ALU.add), reads=["PF2", "FT8", "FT7"], writes=["FT7"])
    P.dve(lambda e: e.tensor_scalar(out=uu, in0=uu, scalar1=VP[:, 257:258], scalar2=None, op0=ALU.mult), reads=["FT7", "FT8"], writes=["FT7"])
    for j in range(NSL):
        (SL, slk), (prod, pdk) = SLs[j % 2], prods[j % 2]
        for hi in range(2):
            P.dma(sl3(SL)[hi * 64:hi * 64 + 64], sav[:, hi, 4 * j:4 * j + 4, :], writes=[slk])
        P.pool(lambda e, j=j: e.tensor_tensor(out=sl3(prod), in0=kP[:, 4 * j:4 * j + 4].unsqueeze(2).to_broadcast([128, 4, 128]), in1=uu.unsqueeze(1).to_broadcast([128, 4, 128]), op=ALU.mult), reads=["FT8", "FT7"], writes=[pdk])
        P.dve(lambda e: e.scalar_tensor_tensor(out=SL[:, :], in0=SL[:, :], scalar=VP[:, 256:257], in1=prod[:, :], op0=ALU.mult, op1=ALU.add), reads=[slk, pdk, "FT8"], writes=[slk])
        for hi in range(2):
            P.dma(sov[:, hi, 4 * j:4 * j + 4, :], sl3(SL)[hi * 64:hi * 64 + 64], reads=[slk], writes=["s_amat%d_%d" % (j, hi)])
        P.dve(lambda e, j=j: e.tensor_tensor(out=sl3(prod), in0=sl3(SL), in1=qP[:, 4 * j:4 * j + 4].unsqueeze(2).to_broadcast([128, 4, 128]), op=ALU.mult), reads=[slk, "FT8"], writes=[pdk])
        P.pool(lambda e, prod=prod: e.tensor_tensor(out=prod[:, 0:256], in0=prod[:, 0:256], in1=prod[:, 256:512], op=ALU.add), reads=[pdk], writes=[pdk])
        P.pool(lambda e, prod=prod: e.tensor_tensor(out=prod[:, 0:128], in0=prod[:, 0:128], in1=prod[:, 128:256], op=ALU.add), reads=[pdk], writes=[pdk])
        P.dve(lambda e: e.tensor_tensor(out=oacc, in0=oacc, in1=prod[:, 0:128], op=ALU.add), reads=["FT7", pdk], writes=["FT7"])
    P.pe(lambda e: e.matmul(PF[2][:, 0:128], lhsT=PAIRS, rhs=oacc, start=True, stop=True), reads=["FT7", "cf"], writes=["PF2"])
    P.dve(lambda e: e.tensor_copy(out=red[0:64, 0:128], in_=PF[2][0:64, 0:128]), reads=["PF2"], writes=["FT2"])
    P.dma(scr_o[:, :], red[0:64, 0:128], reads=["FT2"], writes=["scr_o"])
    osb = SF[6]
    P.dma(osb[R, 0:512], scr_o.rearrange("(b h) v -> b (h v)", h=4), reads=["scr_o"], writes=["SF6"])
    sq = SF[7]
    P.pool(lambda e: e.tensor_tensor(out=sq[R, :], in0=osb[R, :], in1=osb[R, :], op=ALU.mult), reads=["SF6"], writes=["SF7"])
    P.dve(lambda e: e.tensor_reduce(out=sc[R, 40:44], in_=b3(sq[R, :], 4, 128), axis=AX.X, op=ALU.add), reads=["SF7"], writes=["SF5"])
    rsqrt_act(sc[R, 40:44], sc[R, 40:44], 1.0 / 128, 1e-6, ["SF5"], ["SF5"])
    P.dve(lambda e: e.tensor_tensor(out=b3(osb[R, :], 4, 128), in0=b3(osb[R, :], 4, 128), in1=sc[R, 40:44].unsqueeze(2).to_broadcast([N, 4, 128]), op=ALU.mult), reads=["SF6", "SF5"], writes=["SF6"])
    P.dve(lambda e: e.tensor_tensor(out=b3(osb[R, :], 4, 128), in0=b3(osb[R, :], 4, 128), in1=ANW[R, :].unsqueeze(1).to_broadcast([N, 4, 128]), op=ALU.mult), reads=["SF6", "rp"], writes=["SF6"])
    mixs = HT[22]
    P.dve(lambda e: e.tensor_tensor(out=mixs[R, :], in0=osb[R, :], in1=sgaTM[R, :], op=ALU.mult), reads=["SF6", "SF4"], writes=["HT22"])

    pbT = lambda j, k=None: SPJ[:, (17 + j) * N:(17 + (j + 1 if k is None else k)) * N]
    for g in range(5):
        n = 4 if g < 4 else 1
        P.dma(SF[0][R, 0:n * 128], sb_shift_d[:, g * 512:g * 512 + n * 128], writes=["SF0"])
        for i in range(n):
            P.pe(lambda e, g=g, i=i: e.transpose(out=PF[2][:, (g * 4 + i) * N:(g * 4 + i + 1) * N], in_=SF[0][R, i * 128:(i + 1) * 128], identity=IDF[R, R]), reads=["SF0", "cf"], writes=["PF2"])
        to_tm([pbT(g * 4 + i) for i in range(n)], "SPJ", SF[1], "SF1", PF[3], "PF3")
        P.dma(s_bshift[:, g * 512:g * 512 + n * 128], SF[1][R, 0:n * 128], reads=["SF1"], writes=["s_bshift%d" % g])
    xb = FT[9]
    xbj = lambda j, k=None: xb[:, j * N:(j + 1 if k is None else k) * N]
    mu3 = pcol[:, MU0:MU0 + 17].unsqueeze(2).to_broadcast([128, 17, N])
    x3 = xb[:, 0:17 * N].rearrange("p (j t) -> p j t", j=17)
    pb3 = SPJ[:, 17 * N:34 * N].rearrange("p (j t) -> p j t", j=17)
    P.dve(lambda e: e.tensor_tensor(out=xb[:, 0:17 * N], in0=PF[2][:, 0:17 * N], in1=SPJ[:, 17 * N:34 * N], op=ALU.subtract), reads=["PF2", "SPJ"], writes=["FT9"])
    P.dve(lambda e: e.tensor_tensor(out=x3, in0=x3, in1=mu3, op=ALU.mult), reads=["FT9", "pcol"], writes=["FT9"])
    P.dve(lambda e: e.tensor_tensor(out=xb[:, 0:17 * N], in0=xb[:, 0:17 * N], in1=SPJ[:, 17 * N:34 * N], op=ALU.add), reads=["FT9", "SPJ"], writes=["FT9"])
    P.act(lambda e: e.activation(out=xb[0:64, 16 * N:17 * N], in_=xb[0:64, 16 * N:17 * N], func=AF.Tanh), reads=["FT9"], writes=["FT9"])
    W = FT[0]
    wq = lambda qi, pr=None: W[:, (qi * 4 + (0 if pr is None else pr)) * N:(qi * 4 + (4 if pr is None else pr + 1)) * N]
    SG, AIC, KK, KM, BB, BON, SGT, WD = 0, 1, 2, 3, 4, 5, 6, 7
    for pr in range(4):
        P.pe(lambda e, pr=pr: e.matmul(PF[2][:, pr * N:(pr + 1) * N], lhsT=w2a2[0:64, pr * 128:(pr + 1) * 128], rhs=xb[0:64, 16 * N:17 * N], start=True, stop=True), reads=["w2a2", "FT9"], writes=["PF2"])
        P.pe(lambda e, pr=pr: e.matmul(PF[2][:, (4 + pr) * N:(5 + pr) * N], lhsT=w2a2[64:128, pr * 128:(pr + 1) * 128], rhs=xb[64:128, 16 * N:17 * N], start=True, stop=True), reads=["w2a2", "FT9"], writes=["PF2"])
    for pr in range(4):
        P.act(lambda e, pr=pr: e.activation(out=wq(SG, pr), in_=PF[2][:, pr * N:(pr + 1) * N], func=AF.Sigmoid, bias=col(W00 + pr)), reads=["PF2", "pcol"], writes=["FT0"])
        P.act(lambda e, pr=pr: e.activation(out=wq(AIC, pr), in_=PF[2][:, (4 + pr) * N:(5 + pr) * N], func=AF.Sigmoid, bias=col(A00 + pr)), reads=["PF2", "pcol"], writes=["FT0"])
    P.act(lambda e: e.activation(out=wq(WD), in_=wq(SG), func=AF.Exp, scale=-C0), reads=["FT0"], writes=["FT0"])
    pc3 = lambda c0_: pcol[:, c0_:c0_ + 4].unsqueeze(2).to_broadcast([128, 4, N])
    q3 = lambda ap: ap.rearrange("p (a t) -> p a t", a=4)
    rT, kT, vT, gT = xbj(0, 4), xbj(4, 8), xbj(8, 12), xbj(12, 16)
    P.dve(lambda e: e.scalar_tensor_tensor(out=q3(wq(KM)), in0=q3(wq(AIC)), scalar=-1.0, in1=pc3(KA0), op0=ALU.add, op1=ALU.mult), reads=["FT0", "pcol"], writes=["FT0"])
    P.dve(lambda e: e.scalar_tensor_tensor(out=wq(KM), in0=wq(KM), scalar=1.0, in1=kT, op0=ALU.add, op1=ALU.mult), reads=["FT0", "FT9"], writes=["FT0"])
    P.dve(lambda e: e.tensor_tensor(out=q3(wq(KK)), in0=q3(kT), in1=pc3(KK0), op=ALU.mult), reads=["FT9", "pcol"], writes=["FT0"])
    P.act(lambda e: e.activation(out=wq(BB), in_=wq(KK), func=AF.Square), reads=["FT0"], writes=["FT0"])
    P.pe(lambda e: e.matmul(PF[3][:, 0:4 * N], lhsT=BL, rhs=wq(BB), start=True, stop=True), reads=["FT0", "cf"], writes=["PF3"])
    rsqrt_act(wq(BB), PF[3][:, 0:4 * N], 1.0, 1e-12, ["PF3"], ["FT0"])
    P.dve(lambda e: e.tensor_tensor(out=wq(KK), in0=wq(KK), in1=wq(BB), op=ALU.mult), reads=["FT0"], writes=["FT0"])
    P.dve(lambda e: e.tensor_tensor(out=wq(BB), in0=wq(KK), in1=wq(AIC), op=ALU.mult), reads=["FT0"], writes=["FT0"])
    P.dve(lambda e: e.tensor_tensor(out=q3(wq(BON)), in0=q3(rT), in1=pc3(RK0), op=ALU.mult), reads=["FT9", "pcol"], writes=["FT0"])
    P.dve(lambda e: e.tensor_tensor(out=wq(BON), in0=wq(BON), in1=wq(KM), op=ALU.mult), reads=["FT0"], writes=["FT0"])
    P.pe(lambda e: e.matmul(PF[3][:, 0:4 * N], lhsT=BL, rhs=wq(BON), start=True, stop=True), reads=["FT0", "cf"], writes=["PF3"])
    P.dve(lambda e: e.tensor_tensor(out=wq(BON), in0=PF[3][:, 0:4 * N], in1=vT, op=ALU.mult), reads=["PF3", "FT9"], writes=["FT0"])
    P.act(lambda e: e.activation(out=wq(SGT), in_=gT, func=AF.Silu), reads=["FT9"], writes=["FT0"])
    P.dve(lambda e: e.tensor_scalar(out=wq(KK), in0=wq(KK), scalar1=-1.0, scalar2=None, op0=ALU.mult), reads=["FT0"], writes=["FT0"])
    tmsrc = [(wq(WD), "FT0"), (wq(KK), "FT0"), (wq(BB), "FT0"), (wq(KM), "FT0"), (rT, "FT9"), (vT, "FT9")]
    for i, (ap_, key_) in enumerate(tmsrc):
        sf = i % 2
        to_tm([ap_[:, pr * N:(pr + 1) * N] for pr in range(4)], key_, SF[sf], "SF%d" % sf, PF[3], "PF3")
        P.dma(scr_b6[i][:, :], SF[sf][R, 0:512], reads=["SF%d" % sf], writes=["scr_b%d" % i])
    bonTM, sgTM = SF[2], SF[3]
    to_tm([wq(BON, pr) for pr in range(4)], "FT0", bonTM, "SF2", PF[3], "PF3")
    to_tm([wq(SGT, pr) for pr in range(4)], "FT0", sgTM, "SF3", PF[3], "PF3")
    V6 = FT[1]
    for i in range(6):
        P.dma(V6[:, i * 64:(i + 1) * 64], scr_b6[i].rearrange("b (h k) -> (b h) k", h=8), reads=["scr_b%d" % i], writes=["FT1"])
    wP, aP, bP, kP2, rP, vP2 = [V6[:, i * 64:(i + 1) * 64] for i in range(6)]
    sbv = sb_mat_d.rearrange("b h v k -> (b h) (v k)")
    sbo = s_bmat.rearrange("b h v k -> (b h) (v k)")
    S1s, T1s, sa_t, yP = [(FT[4], "FT4"), (FT[3], "FT3")], [(FT[5], "FT5"), (FT[6], "FT6")], FT[2], FT[7]
    v8 = lambda t_: t_[:, :].rearrange("p (v k) -> p v k", v=8)
    kb = lambda ap: ap.unsqueeze(1).to_broadcast([128, 8, 64])
    for j in range(8):
        vsl = slice(8 * j, 8 * j + 8)
        (S1, s1k), (T1, t1k) = S1s[j % 2], T1s[j % 2]
        P.dma(S1[:, :], sbv[:, j * 512:(j + 1) * 512], writes=[s1k])
        P.pool(lambda e, S1=S1, T1=T1: e.tensor_tensor(out=v8(T1), in0=v8(S1), in1=kb(aP), op=ALU.mult), reads=[s1k, "FT1"], writes=[t1k])
        P.dve(lambda e, S1=S1, T1=T1: e.tensor_reduce(out=sa_t[:, 0:8], in_=v8(T1), axis=AX.X, op=ALU.add), reads=[t1k], writes=["FT2"])
        P.dve(lambda e, S1=S1, T1=T1: e.tensor_tensor(out=v8(S1), in0=v8(S1), in1=kb(wP), op=ALU.mult), reads=[s1k, "FT1"], writes=[s1k])
        P.pool(lambda e, S1=S1, T1=T1: e.tensor_tensor(out=v8(T1), in0=sa_t[:, 0:8].unsqueeze(2).to_broadcast([128, 8, 64]), in1=kb(bP), op=ALU.mult), reads=["FT2", "FT1"], writes=[t1k])
        P.dve(lambda e, S1=S1, T1=T1: e.tensor_tensor(out=S1[:, :], in0=S1[:, :], in1=T1[:, :], op=ALU.add), reads=[s1k, t1k], writes=[s1k])
        P.pool(lambda e, vsl=vsl, S1=S1, T1=T1: e.tensor_tensor(out=v8(T1), in0=vP2[:, vsl].unsqueeze(2).to_broadcast([128, 8, 64]), in1=kb(kP2), op=ALU.mult), reads=["FT1"], writes=[t1k])
        P.dve(lambda e, S1=S1, T1=T1: e.tensor_tensor(out=S1[:, :], in0=S1[:, :], in1=T1[:, :], op=ALU.add), reads=[s1k, t1k], writes=[s1k])
        P.dma(sbo[:, j * 512:(j + 1) * 512], S1[:, :], reads=[s1k], writes=["s_bmat%d" % j])
        P.pool(lambda e, S1=S1, T1=T1: e.tensor_tensor(out=v8(T1), in0=v8(S1), in1=kb(rP), op=ALU.mult), reads=[s1k, "FT1"], writes=[t1k])
        P.dve(lambda e, vsl=vsl, S1=S1, T1=T1: e.tensor_reduce(out=yP[:, vsl], in_=v8(T1), axis=AX.X, op=ALU.add), reads=[t1k], writes=["FT7"])
    P.dma(scr_y[:, :], yP[:, 0:64], reads=["FT7"], writes=["scr_y"])
    yb = SF[4]
    P.dma(yb[R, 0:512], scr_y.rearrange("(b h) v -> b (h v)", h=8), reads=["scr_y"], writes=["SF4"])
    y3 = b3(yb[R, :], 8, 64)
    stt = SF[5]
    P.dve(lambda e: e.tensor_reduce(out=stt[R, 0:8], in_=y3, axis=AX.X, op=ALU.add), reads=["SF4"], writes=["SF5"])
    P.dve(lambda e: e.tensor_scalar(out=stt[R, 0:8], in0=stt[R, 0:8], scalar1=1.0 / 64, scalar2=None, op0=ALU.mult), reads=["SF5"], writes=["SF5"])
    P.dve(lambda e: e.tensor_tensor(out=y3, in0=y3, in1=stt[R, 0:8].unsqueeze(2).to_broadcast([N, 8, 64]), op=ALU.subtract), reads=["SF4", "SF5"], writes=["SF4"])
    sq2 = SF[7]
    P.pool(lambda e: e.tensor_tensor(out=sq2[R, :], in0=yb[R, :], in1=yb[R, :], op=ALU.mult), reads=["SF4"], writes=["SF7"])
    P.dve(lambda e: e.tensor_reduce(out=stt[R, 8:16], in_=b3(sq2[R, :], 8, 64), axis=AX.X, op=ALU.add), reads=["SF7"], writes=["SF5"])
    rsqrt_act(stt[R, 8:16], stt[R, 8:16], 1.0 / 64, 64e-5, ["SF5"], ["SF5"])
    P.dve(lambda e: e.tensor_tensor(out=y3, in0=y3, in1=stt[R, 8:16].unsqueeze(2).to_broadcast([N, 8, 64]), op=ALU.mult), reads=["SF4", "SF5"], writes=["SF4"])
    P.dve(lambda e: e.tensor_tensor(out=yb[R, :], in0=yb[R, :], in1=LNW[R, :], op=ALU.mult), reads=["SF4", "rp"], writes=["SF4"])
    P.dve(lambda e: e.tensor_tensor(out=yb[R, :], in0=yb[R, :], in1=LNB[R, :], op=ALU.add), reads=["SF4", "rp"], writes=["SF4"])
    P.dve(lambda e: e.tensor_tensor(out=yb[R, :], in0=yb[R, :], in1=bonTM[R, :], op=ALU.add), reads=["SF4", "SF2"], writes=["SF4"])
    mixb = HT[23]
    P.dve(lambda e: e.tensor_tensor(out=mixb[R, :], in0=yb[R, :], in1=sgTM[R, :], op=ALU.mult), reads=["SF4", "SF3"], writes=["HT23"])
    for c8 in range(8):
        src = mixs[R, c8 * 128:(c8 + 1) * 128] if c8 < 4 else mixb[R, (c8 - 4) * 128:(c8 - 3) * 128]
        P.pe(lambda e, c8=c8, src=src: e.transpose(out=PB[1][:, c8 * N:(c8 + 1) * N], in_=src, identity=idb[R, R]), reads=["HT22", "HT23", "idb"], writes=["PB1"])
    mixTs = HT[24]
    P.dve(lambda e: e.tensor_copy(out=mixTs[:, 0:8 * N], in_=PB[1][:, 0:8 * N]), reads=["PB1"], writes=["HT24"])
    for n in range(2):
        for kc in range(8):
            P.pe(lambda e, n=n, kc=kc: e.matmul(PF[n][R, :], lhsT=mixTs[:, kc * N:(kc + 1) * N], rhs=woutb[:, kc * 1024 + n * 512:kc * 1024 + (n + 1) * 512], start=(kc == 0), stop=(kc == 7)), reads=["HT24", "woutb"], writes=["PF%d" % n])
        P.dve(lambda e, n=n: e.tensor_tensor(out=xa[R, n * 512:(n + 1) * 512], in0=xa[R, n * 512:(n + 1) * 512], in1=PF[n][R, :], op=ALU.add), reads=["xt0", "PF%d" % n], writes=["xt0"])
    P.act(lambda e: e.activation(out=xs_[R, :], in_=xa[R, :], func=AF.Square, accum_out=st4[R, 4:5]), reads=["xt0"], writes=["xs_", "st4"])
    rsqrt_act(st4[R, 4:5], st4[R, 4:5], 1.0 / D, 1e-6, ["st4"], ["st4"])
    P.dve(lambda e: e.scalar_tensor_tensor(out=xs_[R, :], in0=xa[R, :], scalar=st4[R, 4:5], in1=FNW[R, :], op0=ALU.mult, op1=ALU.mult), reads=["xt0", "st4", "rp"], writes=["xs_"])
    P.dma(ys[:, :], xs_[R, :], reads=["xs_"], writes=["ys"])
```
